# Optimizing a Trainium2 kernel written in Bass

```python
import jax, jax.numpy as jnp
from jax import lax
import numpy as np

D_MODEL = 1024
BATCH = 2
SEQ = 8192
DEPTH = 1
DEC_BATCH = 4
DEC_SEQ = 4096
PAST_LEN = 128

N_MEM = 256
MLA_HEADS = 8
QK_NOPE = 64
QK_ROPE = 32
QK_HEAD = QK_NOPE + QK_ROPE
V_HEAD = 64
Q_LORA = 384
KV_LORA = 256
CONV_WIDTH = 512
CONV_K = 3
XA_HEADS = 4
XA_HEAD = 128
N_BRANCH = 3
D_FF = 2816
ROPE_BASE = 10000.0
EPS = 1e-6
Q_BLOCK = 128

IN_SPLITS = (Q_LORA, KV_LORA, QK_ROPE, CONV_WIDTH, CONV_WIDTH, CONV_WIDTH, XA_HEADS * XA_HEAD, N_BRANCH * D_MODEL)
D_IN = Q_LORA + KV_LORA + QK_ROPE + 3 * CONV_WIDTH + XA_HEADS * XA_HEAD + N_BRANCH * D_MODEL

kernel_name = "hybrid_mla_shortconv_memory_encoder"


def _split_points():
    pts, acc = [], 0
    for w in IN_SPLITS[:-1]:
        acc += w
        pts.append(acc)
    return pts


def rmsnorm(x, g):
    xf = x.astype(jnp.float32)
    inv = lax.rsqrt(jnp.mean(xf * xf, axis=-1, keepdims=True) + EPS)
    return (xf * inv).astype(x.dtype) * g


def rope(x, pos):
    half = QK_ROPE // 2
    inv_freq = ROPE_BASE ** (-jnp.arange(half, dtype=jnp.float32) / half)
    ang = pos.astype(jnp.float32)[:, None] * inv_freq[None, :]
    cos = jnp.cos(ang)[:, None, :]
    sin = jnp.sin(ang)[:, None, :]
    xf = x.astype(jnp.float32)
    x1, x2 = xf[..., :half], xf[..., half:]
    return jnp.concatenate([x1 * cos - x2 * sin, x2 * cos + x1 * sin], axis=-1).astype(x.dtype)


def swiglu(x, w_gu, w_down):
    g, u = jnp.split(x @ w_gu, 2, axis=-1)
    return (jax.nn.silu(g) * u) @ w_down


def mla_attention(q, k, v):
    B, S, H, _ = q.shape
    nb = S // Q_BLOCK
    scale = QK_HEAD ** -0.5
    qb = q.reshape(B, nb, Q_BLOCK, H, QK_HEAD).transpose(1, 0, 2, 3, 4)

    def one_block(qblk):
        s = jnp.einsum('bqhd,bkhd->bhqk', qblk, k, preferred_element_type=jnp.float32) * scale
        p = jax.nn.softmax(s, axis=-1)
        return jnp.einsum('bhqk,bkhd->bqhd', p.astype(v.dtype), v)

    o = lax.map(one_block, qb)
    return o.transpose(1, 0, 2, 3, 4).reshape(B, S, H * V_HEAD)


def short_conv(u, w):
    S = u.shape[1]
    pad = CONV_K // 2
    up = jnp.pad(u, ((0, 0), (pad, pad), (0, 0)))
    y = up[:, 0:S] * w[0]
    for j in range(1, CONV_K):
        y = y + up[:, j:j + S] * w[j]
    return y


def cross_attention(q, k, v):
    B, S = q.shape[0], q.shape[1]
    s = jnp.einsum('bqhd,bmhd->bhqm', q, k, preferred_element_type=jnp.float32) * (XA_HEAD ** -0.5)
    p = jax.nn.softmax(s, axis=-1)
    o = jnp.einsum('bhqm,bmhd->bqhd', p.astype(v.dtype), v)
    return o.reshape(B, S, XA_HEADS * XA_HEAD)


def trunk(x, mem, ffn1_norm, ffn1_w_gu, ffn1_w_down, mix_norm, w_in, q_lora_norm, w_uq,
          kv_lora_norm, w_uk, w_uv, mla_q_norm, mla_k_norm, w_o_mla, conv_w, w_o_conv,
          mem_norm, w_mem_kv, xa_q_norm, xa_k_norm, w_o_mem, w_out, ffn2_norm, ffn2_w_gu, ffn2_w_down):
    B, S, _ = x.shape
    pos = jnp.arange(S, dtype=jnp.int32)
    splits = _split_points()
    for l in range(DEPTH):
        x = x + 0.5 * swiglu(rmsnorm(x, ffn1_norm[l]), ffn1_w_gu[l], ffn1_w_down[l])

        h = rmsnorm(x, mix_norm[l])
        c_q, c_kv, k_r, cb, cc, cx, xq, glog = jnp.split(h @ w_in[l], splits, axis=-1)

        q = (rmsnorm(c_q, q_lora_norm[l]) @ w_uq[l]).reshape(B, S, MLA_HEADS, QK_HEAD)
        c_kv = rmsnorm(c_kv, kv_lora_norm[l])
        k_nope = (c_kv @ w_uk[l]).reshape(B, S, MLA_HEADS, QK_NOPE)
        v = (c_kv @ w_uv[l]).reshape(B, S, MLA_HEADS, V_HEAD)
        k = jnp.concatenate([k_nope, jnp.broadcast_to(k_r[:, :, None, :], (B, S, MLA_HEADS, QK_ROPE))], axis=-1)
        q = rmsnorm(q, mla_q_norm[l])
        k = rmsnorm(k, mla_k_norm[l])
        q = jnp.concatenate([q[..., :QK_NOPE], rope(q[..., QK_NOPE:], pos)], axis=-1)
        k = jnp.concatenate([k[..., :QK_NOPE], rope(k[..., QK_NOPE:], pos)], axis=-1)
        y_mla = mla_attention(q, k, v) @ w_o_mla[l]

        y_conv = (cb * short_conv(cc * cx, conv_w[l])) @ w_o_conv[l]

        m = rmsnorm(mem, mem_norm[l])
        mk, mv = jnp.split(m @ w_mem_kv[l], 2, axis=-1)
        Bm, M = mem.shape[0], mem.shape[1]
        mk = rmsnorm(mk.reshape(Bm, M, XA_HEADS, XA_HEAD), xa_k_norm[l])
        mv = mv.reshape(Bm, M, XA_HEADS, XA_HEAD)
        xq = rmsnorm(xq.reshape(B, S, XA_HEADS, XA_HEAD), xa_q_norm[l])
        y_mem = cross_attention(xq, mk, mv) @ w_o_mem[l]

        gates = jax.nn.sigmoid(glog.reshape(B, S, N_BRANCH, D_MODEL))
        merged = gates[:, :, 0] * y_mla + gates[:, :, 1] * y_conv + gates[:, :, 2] * y_mem
        x = x + merged @ w_out[l]

        x = x + 0.5 * swiglu(rmsnorm(x, ffn2_norm[l]), ffn2_w_gu[l], ffn2_w_down[l])
    return x


def setup_inputs(seed: int = 0) -> dict:
    key = jax.random.key(seed)
    ks = jax.random.split(key, 32)

    def dense(k, shape, fan_in):
        return jax.random.normal(k, shape, jnp.float32) * (fan_in ** -0.5)

    def gain(k, n):
        return 1.0 + 0.02 * jax.random.normal(k, (DEPTH, n), jnp.float32)

    L = DEPTH
    return {
        "x_prompt": jax.random.normal(ks[0], (BATCH, SEQ, D_MODEL), jnp.float32),
        "x_sample": jax.random.normal(ks[1], (DEC_BATCH, DEC_SEQ, D_MODEL), jnp.float32),
        "mem_prompt": jax.random.normal(ks[2], (BATCH, N_MEM, D_MODEL), jnp.float32),
        "mem_sample": jax.random.normal(ks[3], (DEC_BATCH, N_MEM, D_MODEL), jnp.float32),
        "ffn1_norm": gain(ks[4], D_MODEL),
        "ffn1_w_gu": dense(ks[5], (L, D_MODEL, 2 * D_FF), D_MODEL),
        "ffn1_w_down": dense(ks[6], (L, D_FF, D_MODEL), D_FF),
        "mix_norm": gain(ks[7], D_MODEL),
        "w_in": dense(ks[8], (L, D_MODEL, D_IN), D_MODEL),
        "q_lora_norm": gain(ks[9], Q_LORA),
        "w_uq": dense(ks[10], (L, Q_LORA, MLA_HEADS * QK_HEAD), Q_LORA),
        "kv_lora_norm": gain(ks[11], KV_LORA),
        "w_uk": dense(ks[12], (L, KV_LORA, MLA_HEADS * QK_NOPE), KV_LORA),
        "w_uv": dense(ks[13], (L, KV_LORA, MLA_HEADS * V_HEAD), KV_LORA),
        "mla_q_norm": gain(ks[14], QK_HEAD),
        "mla_k_norm": gain(ks[15], QK_HEAD),
        "w_o_mla": dense(ks[16], (L, MLA_HEADS * V_HEAD, D_MODEL), MLA_HEADS * V_HEAD),
        "conv_w": dense(ks[17], (L, CONV_K, CONV_WIDTH), CONV_K),
        "w_o_conv": dense(ks[18], (L, CONV_WIDTH, D_MODEL), CONV_WIDTH),
        "mem_norm": gain(ks[19], D_MODEL),
        "w_mem_kv": dense(ks[20], (L, D_MODEL, 2 * XA_HEADS * XA_HEAD), D_MODEL),
        "xa_q_norm": gain(ks[21], XA_HEAD),
        "xa_k_norm": gain(ks[22], XA_HEAD),
        "w_o_mem": dense(ks[23], (L, XA_HEADS * XA_HEAD, D_MODEL), XA_HEADS * XA_HEAD),
        "w_out": dense(ks[24], (L, D_MODEL, D_MODEL), D_MODEL),
        "ffn2_norm": gain(ks[25], D_MODEL),
        "ffn2_w_gu": dense(ks[26], (L, D_MODEL, 2 * D_FF), D_MODEL),
        "ffn2_w_down": dense(ks[27], (L, D_FF, D_MODEL), D_FF),
    }


def reference(x_prompt, x_sample, mem_prompt, mem_sample, ffn1_norm, ffn1_w_gu, ffn1_w_down,
              mix_norm, w_in, q_lora_norm, w_uq, kv_lora_norm, w_uk, w_uv, mla_q_norm, mla_k_norm,
              w_o_mla, conv_w, w_o_conv, mem_norm, w_mem_kv, xa_q_norm, xa_k_norm, w_o_mem, w_out,
              ffn2_norm, ffn2_w_gu, ffn2_w_down):
    y_prompt = trunk(x_prompt, mem_prompt, ffn1_norm, ffn1_w_gu, ffn1_w_down, mix_norm, w_in,
                     q_lora_norm, w_uq, kv_lora_norm, w_uk, w_uv, mla_q_norm, mla_k_norm, w_o_mla,
                     conv_w, w_o_conv, mem_norm, w_mem_kv, xa_q_norm, xa_k_norm, w_o_mem, w_out,
                     ffn2_norm, ffn2_w_gu, ffn2_w_down)
    y_sample = trunk(x_sample, mem_sample, ffn1_norm, ffn1_w_gu, ffn1_w_down, mix_norm, w_in,
                     q_lora_norm, w_uq, kv_lora_norm, w_uk, w_uv, mla_q_norm, mla_k_norm, w_o_mla,
                     conv_w, w_o_conv, mem_norm, w_mem_kv, xa_q_norm, xa_k_norm, w_o_mem, w_out,
                     ffn2_norm, ffn2_w_gu, ffn2_w_down)
    return (y_prompt, y_sample)
```

```python
import numpy as np
from contextlib import ExitStack
import concourse.bass as bass
import concourse.mybir as mybir
from concourse.bass_utils import run_bass_kernel_spmd

F32 = mybir.dt.float32
BF16 = mybir.dt.bfloat16
I32 = mybir.dt.int32
AF = mybir.ActivationFunctionType
ALU = mybir.AluOpType

D = 1024
KC = 8
T = 512
NT = 8
FF = 2816
NJ = 22
EPS = 1e-6
SAME_ENG_SYNC = True
MAGIC = 1597463007.0

G_FFN1, G_MIX, G_FFN2, G_MEM, G_QL, G_KVL = 0, 8, 16, 24, 32, 35
G_MQ, G_MQS, G_MK, G_KR, G_KRS, G_XQ, G_XK, G_CONV = 37, 38, 39, 40, 41, 42, 43, 44
NG = 56


class Buf:
    __slots__ = ("w", "r", "name")

    def __init__(self, name=""):
        self.w = None
        self.r = {}
        self.name = name


class Eng:
    def __init__(self, e, sem, name, is_pe=False):
        self.e = e
        self.sem = sem
        self.n = 0
        self.seen = {}
        self.name = name
        self.is_pe = is_pe

    def wait(self, tok):
        if tok is None:
            return
        sem, val = tok
        if sem is self.sem and (self.is_pe or not SAME_ENG_SYNC):
            return
        k = id(sem)
        if self.seen.get(k, 0) >= val:
            return
        self.e.wait_ge(sem, val)
        self.seen[k] = val


class Prog:
    def __init__(self):
        self.nc = bass.Bass("TRN2", target_bir_lowering=False)
        self.es = ExitStack()
        nc = self.nc
        self.semcount = {}
        self.sems = []
        self.PE = Eng(nc.tensor, self.sem("pe"), "pe", is_pe=True)
        self.ACT = Eng(nc.scalar, self.sem("act"), "act")
        self.DVE = Eng(nc.vector, self.sem("dve"), "dve")
        self.POOL = Eng(nc.gpsimd, self.sem("pool"), "pool")
        self.SP = Eng(nc.sync, self.sem("sp"), "sp")
        self.engs = [self.PE, self.ACT, self.DVE, self.POOL, self.SP]
        self.bank_rr = 0

    def sem(self, name):
        s = self.es.enter_context(self.nc.semaphore(f"{name}_n{len(self.sems)}"))
        self.semcount[id(s)] = 0
        self.sems.append(s)
        return s

    def deps(self, E, reads, writes):
        for b in reads:
            E.wait(b.w)
        for b in writes:
            E.wait(b.w)
            for t in list(b.r.values()):
                E.wait(t)

    def done(self, tok, reads, writes):
        for b in reads:
            b.r[id(tok[0])] = tok
        for b in writes:
            b.w = tok
            b.r = {}

    def op(self, E, fn, reads=(), writes=()):
        self.deps(E, reads, writes)
        ins = fn()
        E.n += 1
        ins.then_inc(E.sem, 1)
        self.semcount[id(E.sem)] = E.n
        self.done((E.sem, E.n), reads, writes)

    def mm(self, fns, reads, writes):
        E = self.PE
        self.deps(E, reads, writes)
        ins = None
        for f in fns:
            ins = f()
        E.n += 1
        ins.then_inc(E.sem, 1)
        self.semcount[id(E.sem)] = E.n
        self.done((E.sem, E.n), reads, writes)

    def mmf(self, items, reads, writes):
        E = self.PE
        self.deps(E, reads, writes)
        allr = list(reads)
        ins = None
        for f, rb in items:
            for b in rb:
                E.wait(b.w)
            allr += rb
            ins = f()
        E.n += 1
        ins.then_inc(E.sem, 1)
        self.semcount[id(E.sem)] = E.n
        self.done((E.sem, E.n), allr, writes)

    def dma(self, Q, out, in_, sem, reads=(), writes=(), **kw):
        self.deps(Q, reads, writes)
        Q.e.dma_start(out=out, in_=in_, **kw).then_inc(sem, 16)
        self.semcount[id(sem)] += 16
        self.done((sem, self.semcount[id(sem)]), reads, writes)

    def barrier(self):
        for E in self.engs:
            for s in self.sems:
                c = self.semcount[id(s)]
                if c > 0:
                    E.wait((s, c)) if s is not E.sem else None


class Stream:
    def __init__(self, P, es, name, width, nslots, srcs):
        self.P = P
        self.srcs = srcs
        self.n = nslots
        self.slots = [es.enter_context(P.nc.sbuf_tensor(f"{name}_s{i}_{len(P.sems)}", [128, width], BF16)) for i in range(nslots)]
        self.bufs = [Buf(f"{name}{i}") for i in range(nslots)]
        self.sems = [P.sem(f"{name}_q{i}") for i in range(nslots)]
        self.issued = 0
        self.pos = 0

    def _issue(self):
        i = self.issued
        s = i % self.n
        ap, db = self.srcs[i]
        self.P.dma(self.P.SP, self.slots[s][:, :], ap, self.sems[s], reads=[db], writes=[self.bufs[s]])
        self.issued += 1

    def next(self):
        i = self.pos
        while self.issued < min(i + self.n, len(self.srcs)):
            self._issue()
        self.pos += 1
        s = i % self.n
        return self.slots[s], self.bufs[s]


import os as _os
STOP = float(_os.environ.get("KSTOP", "9"))
KSUB = int(_os.environ.get("KSUB", "99"))
KATT = int(_os.environ.get("KATT", "99"))
KK = int(_os.environ.get("KK", "99"))


class _Stop(Exception):
    pass


def build():
    stacks = []
    P = Prog()
    try:
        _build(P, stacks)
    except _Stop:
        P.barrier()
        for st in reversed(stacks):
            st.close()
        P.es.close()
    return P.nc


def _build(P, stacks):
    nc = P.nc
    es = P.es
    PE, ACT, DVE, POOL, SP = P.PE, P.ACT, P.DVE, P.POOL, P.SP

    def din(name, shape, dt=F32):
        return nc.dram_tensor(name, shape, dt, kind="ExternalInput")

    xT = din("xT", [NT, 128, KC * T])
    xh = din("xh", [128, KC * 4])
    memT = din("memT", [2, 128, KC * 256])
    gains_d = din("gains", [128, NG])
    rope_d = din("ropeT", [128, 2, NT * T])
    wsh = {
        "w1gu": [NJ, 128, KC * 256], "w1d": [8, 128, NJ * 128],
        "w2gu": [NJ, 128, KC * 256], "w2d": [8, 128, NJ * 128],
        "wina": [14, 128, KC * 128], "winb": [32, 128, KC * 128],
        "wuq": [8, 128, 3 * 192], "wuk": [1, 128, 2 * 512], "wuv": [1, 128, 2 * 512],
        "wo3": [8, 128, 16 * 128], "wout": [8, 128, KC * 128],
        "wmk": [4, 128, KC * 128], "wmv": [1, 128, KC * 512],
    }
    wf = {k: din(k, v) for k, v in wsh.items()}
    wb = {k: nc.dram_tensor(k + "_b", v, BF16) for k, v in wsh.items()}
    wbuf = {k: Buf(k) for k in wsh}
    yT = nc.dram_tensor("yT", [NT, 128, KC * T], F32, kind="ExternalOutput")

    x1s = nc.dram_tensor("x1s", [NT, 128, KC * T], F32)
    x1s_buf = [Buf(f"x1s{i}") for i in range(NT)]
    Us = nc.dram_tensor("Us", [128, 4, 2, 2048], BF16)
    Us_buf = Buf("Us")
    AOs = nc.dram_tensor("AOs", [64, 8, NT * T], BF16)
    AOs_buf = Buf("AOs")
    LROWS = 320
    latP = [nc.dram_tensor(f"latP{i}", [LROWS, 1024], BF16) for i in range(2)]
    latS = [nc.dram_tensor(f"latS{i}", [LROWS, 1024], BF16) for i in range(2)]
    gatP = [nc.dram_tensor(f"gatP{i}", [4 * LROWS, 1024], BF16) for i in range(2)]
    gatS = [nc.dram_tensor(f"gatS{i}", [2 * LROWS, 1024], BF16) for i in range(2)]
    lat_buf = [Buf("latP"), Buf("latS")]
    gat_buf = [Buf("gatP"), Buf("gatS")]

    uniq = [0]

    def sb(stack, name, shape, dt):
        uniq[0] += 1
        return stack.enter_context(nc.sbuf_tensor(f"{name}_u{uniq[0]}", shape, dt))

    cast_order = ["w1gu", "w1d", "wina", "wmk", "wmv", "wuq", "wuk", "wuv", "winb", "wo3", "wout", "w2gu", "w2d"]
    for k in cast_order:
        s = P.sem("c_" + k)
        n0, _, wd = wsh[k]
        bb = max(d_ for d_ in range(1, 1025) if wd % d_ == 0)
        src = wf[k].ap().rearrange("n p (a b) -> (n p a) b", b=bb)
        dst = wb[k].ap().rearrange("n p (a b) -> (n p a) b", b=bb)
        rows = src.shape[0]
        step = 4096
        for r0 in range(0, rows, step):
            r1 = min(rows, r0 + step)
            P.dma(POOL, dst[r0:r1, :], src[r0:r1, :], s, writes=[wbuf[k]] if r0 + step >= rows else [], max_dma_last_dim=4096)
            if r0 + step < rows:
                pass

    if STOP <= 0.1:
        raise _Stop()
    gains = sb(es, "gains_sb", [128, NG], F32)
    gains_b = Buf("gains")
    ONESB = sb(es, "onesb", [128, 128], BF16)
    ONESF = sb(es, "onesf", [128, 64], F32)
    ones_b = Buf("ones")
    s_misc = P.sem("misc")
    P.dma(SP, gains[:, :], gains_d.ap(), s_misc, writes=[gains_b])
    P.op(DVE, lambda: nc.vector.memset(ONESB[:, :], 1.0), writes=[ones_b])
    P.op(DVE, lambda: nc.vector.memset(ONESF[:, :], 1.0), writes=[ones_b])
    MK = sb(es, "MK", [128, 2, 4, 256], BF16)
    MV = sb(es, "MV", [128, 2, 2, 512], BF16)
    MK_b, MV_b = Buf("MK"), Buf("MV")
    UH = sb(es, "UH", [128, 4, 4], BF16)
    UHb = Buf()

    psum = [es.enter_context(nc.psum_tensor(f"ps{i}", [128, 512], F32)) for i in range(8)]
    psb = [Buf(f"ps{i}") for i in range(8)]
    bank_pool = list(range(8))

    def bank():
        i = bank_pool[P.bank_rr % len(bank_pool)]
        P.bank_rr += 1
        return psum[i], psb[i]

    def g(col, p0=0, p1=128):
        return gains[p0:p1, col:col + 1]

    def rsqrt(tmp, ps_ap, ps_b, out_ap, out_b, inv_n, post=None):
        V, Y, TT = tmp["V"], tmp["Y"], tmp["T"]
        vb, yb, tb = tmp["Vb"], tmp["Yb"], tmp["Tb"]
        shp = ps_ap.shape
        np_, n = shp[0], shp[1]
        p0 = tmp.get("p0", 0)
        v = V[p0:p0 + np_, 0:n]
        y = Y[p0:p0 + np_, 0:n]
        t = TT[p0:p0 + np_, 0:n]
        P.op(DVE, lambda: nc.vector.tensor_scalar(out=v, in0=ps_ap, scalar1=inv_n, scalar2=EPS, op0=ALU.mult, op1=ALU.add),
             reads=[ps_b], writes=[vb])
        P.op(DVE, lambda: nc.vector.tensor_scalar(out=y.bitcast(I32), in0=v.bitcast(I32), scalar1=-0.5, scalar2=MAGIC,
                                                  op0=ALU.mult, op1=ALU.add), reads=[vb], writes=[yb])
        for it in range(2):
            P.op(DVE, lambda: nc.vector.tensor_tensor(out=t, in0=y, in1=y, op=ALU.mult), reads=[yb], writes=[tb])
            P.op(DVE, lambda: nc.vector.scalar_tensor_tensor(out=t, in0=t, scalar=-0.5, in1=v, op0=ALU.mult, op1=ALU.mult),
                 reads=[tb, vb], writes=[tb])
            last = it == 1
            o = out_ap if last else y
            ob = out_b if last else yb
            if last and post is not None:
                P.op(DVE, lambda: nc.vector.scalar_tensor_tensor(out=y, in0=t, scalar=1.5, in1=y, op0=ALU.add, op1=ALU.mult),
                     reads=[tb, yb], writes=[yb])
                P.op(DVE, lambda: nc.vector.tensor_scalar(out=o, in0=y, scalar1=post, scalar2=None, op0=ALU.mult),
                     reads=[yb], writes=[ob])
            else:
                P.op(DVE, lambda: nc.vector.scalar_tensor_tensor(out=o, in0=t, scalar=1.5, in1=y, op0=ALU.add, op1=ALU.mult),
                     reads=[tb, yb], writes=[ob])

    def mk_tmp(stack, tag, n=T):
        return {"V": sb(stack, "tV" + tag, [128, n], F32), "Y": sb(stack, "tY" + tag, [128, n], F32),
                "T": sb(stack, "tT" + tag, [128, n], F32), "Vb": Buf(), "Yb": Buf(), "Tb": Buf()}

    class TileCtx:
        pass

    def alloc_tile_ctx(stack):
        c = TileCtx()
        c.X = [sb(stack, f"X{i}", [128, KC, T], F32) for i in range(2)]
        c.Xb = [[Buf(f"X{i}_{k}") for k in range(KC)] for i in range(2)]
        c.Xsem = [P.sem(f"X{i}") for i in range(2)]
        c.XsemS = [P.sem(f"XS{i}") for i in range(2)]
        c.XN = sb(stack, "XN", [128, KC, T], BF16)
        c.XNb = [Buf(f"XN{k}") for k in range(KC)]
        c.H = sb(stack, "H", [128, NJ, T], BF16)
        c.Hb = [Buf(f"H{j}") for j in range(NJ)]
        c.SG = [sb(stack, f"SG{i}", [128, T], F32) for i in range(2)]
        c.SGb = [Buf(), Buf()]
        c.RINV = sb(stack, "RINV", [128, T], F32)
        c.RINVb = Buf("rinv")
        c.tmp = mk_tmp(stack, "a")
        return c

    def norm_to_xn(c, xs, n, gcol):
        X, Xb = c.X[xs], c.Xb[xs]
        for k0 in range(0, KC, 4):
            P.op(ACT, lambda k0=k0: nc.scalar.activation(out=c.H[:, k0:k0 + 4, 0:n], in_=X[:, k0:k0 + 4, 0:n], func=AF.Square),
                 reads=Xb[k0:k0 + 4], writes=c.Hb[k0:k0 + 4])
        ps, pb = bank()
        P.mmf([(lambda k=k: nc.tensor.matmul(ps[:, 0:n], lhsT=ONESB[:, :], rhs=c.H[:, k, 0:n], start=(k == 0), stop=(k == KC - 1)), [c.Hb[k]])
               for k in range(KC)], reads=[ones_b], writes=[pb])
        rsqrt(c.tmp, ps[:, 0:n], pb, c.RINV[:, 0:n], c.RINVb, 1.0 / D)
        for k in range(KC):
            P.op(DVE, lambda k=k: nc.vector.scalar_tensor_tensor(out=c.XN[:, k, 0:n], in0=X[:, k, 0:n], scalar=g(gcol + k),
                                                                 in1=c.RINV[:, 0:n], op0=ALU.mult, op1=ALU.mult),
                 reads=[Xb[k], c.RINVb, gains_b], writes=[c.XNb[k]])

    def ffn(c, xs, n, gcol, gu_stream, d_stream):
        X, Xb = c.X[xs], c.Xb[xs]
        if KSUB <= 1:
            raise _Stop()
        norm_to_xn(c, xs, n, gcol)
        if KSUB <= 2:
            raise _Stop()
        for j in range(NJ):
            wt, wtb = gu_stream.next()
            w3 = wt[:, :].rearrange("p (k c) -> p k c", c=256)
            pg, pgb = bank()
            pu, pub = bank()
            P.mmf([(lambda k=k: nc.tensor.matmul(pg[:, 0:n], lhsT=w3[:, k, 0:128], rhs=c.XN[:, k, 0:n], start=(k == 0), stop=(k == KC - 1)), [c.XNb[k]])
                   for k in range(KC)], reads=[wtb], writes=[pgb])
            P.mmf([(lambda k=k: nc.tensor.matmul(pu[:, 0:n], lhsT=w3[:, k, 128:256], rhs=c.XN[:, k, 0:n], start=(k == 0), stop=(k == KC - 1)), [c.XNb[k]])
                   for k in range(KC)], reads=[wtb], writes=[pub])
            sg, sgb = c.SG[j % 2], c.SGb[j % 2]
            P.op(ACT, lambda: nc.scalar.activation(out=sg[:, 0:n], in_=pg[:, 0:n], func=AF.Silu), reads=[pgb], writes=[sgb])
            P.op(DVE, lambda: nc.vector.tensor_tensor(out=c.H[:, j, 0:n], in0=sg[:, 0:n], in1=pu[:, 0:n], op=ALU.mult),
                 reads=[sgb, pub], writes=[c.Hb[j]])
        if KSUB <= 3:
            raise _Stop()
        for fc in range(KC):
            wt, wtb = d_stream.next()
            w3 = wt[:, :].rearrange("p (j c) -> p j c", c=128)
            pd, pdb = bank()
            P.mmf([(lambda j=j: nc.tensor.matmul(pd[:, 0:n], lhsT=w3[:, j, :], rhs=c.H[:, j, 0:n], start=(j == 0), stop=(j == NJ - 1)), [c.Hb[j]])
                   for j in range(NJ)], reads=[wtb], writes=[pdb])
            P.op(DVE, lambda: nc.vector.scalar_tensor_tensor(out=X[:, fc, 0:n], in0=pd[:, 0:n], scalar=0.5, in1=X[:, fc, 0:n],
                                                             op0=ALU.mult, op1=ALU.add), reads=[pdb, Xb[fc]], writes=[Xb[fc]])

    def proj(c, blk_stream, n):
        wt, wtb = blk_stream.next()
        w3 = wt[:, :].rearrange("p (k c) -> p k c", c=128)
        ps, pb = bank()
        P.mmf([(lambda k=k: nc.tensor.matmul(ps[:, 0:n], lhsT=w3[:, k, :], rhs=c.XN[:, k, 0:n], start=(k == 0), stop=(k == KC - 1)), [c.XNb[k]])
               for k in range(KC)], reads=[wtb], writes=[pb])
        return ps, pb

    esA = ExitStack()
    stacks.append(esA)
    CQ = sb(esA, "CQ", [128, 3, NT * T], BF16)
    CQb = [Buf(f"CQ{i}") for i in range(NT)]
    es1 = ExitStack()
    stacks.append(es1)
    c = alloc_tile_ctx(es1)
    order1 = (["h"] if not _os.environ.get("KSKIPH") else []) + list(range(NT))
    gu_srcs, d_srcs, blk_srcs = [], [], []
    for ti in order1:
        gu_srcs += [(wb["w1gu"][j], wbuf["w1gu"]) for j in range(NJ)]
        d_srcs += [(wb["w1d"][f], wbuf["w1d"]) for f in range(8)]
        if ti == "h":
            blk_srcs += [(wb["wina"][m], wbuf["wina"]) for m in range(6, 14)]
        else:
            blk_srcs += [(wb["wina"][m], wbuf["wina"]) for m in range(14)]
    mem_blk = []
    for ctx in range(2):
        mem_blk += [(wb["wmk"][h], wbuf["wmk"]) for h in range(4)]
    blk_srcs = blk_srcs + mem_blk
    gu1 = Stream(P, es1, "gu", KC * 256, 4, gu_srcs)
    d1 = Stream(P, es1, "wd", NJ * 128, 2, d_srcs)
    blk1 = Stream(P, es1, "blk", KC * 128, 6, blk_srcs)
    RAW = sb(es1, "RAW", [128, 3, T], F32)
    RAWb = [Buf() for _ in range(3)]
    SQ3 = sb(es1, "SQ3", [128, 3, T], BF16)
    SQ3b = [Buf() for _ in range(3)]
    RV2 = c.RINV
    RV2b = c.RINVb
    tmp2 = c.tmp
    CKVN = sb(es1, "CKVN", [128, 2, T], BF16)
    CKVNb = Buf()
    s_ckvn = P.sem("ckvn")
    KRO = sb(es1, "KRO", [32, 2, T], BF16)
    KROb = Buf()
    s_kro = P.sem("kro")
    KA = sb(es1, "KA", [32, T], F32)
    KB = sb(es1, "KB", [32, T], F32)
    KAb, KBb = Buf(), Buf()
    ROPE = sb(es1, "ROPE", [32, 2, T], F32)
    ROPEb = Buf()
    s_rope = P.sem("rope")
    UT = sb(es1, "UT", [128, 4, T], BF16)
    UTb = Buf()
    s_ut = P.sem("ut")
    CCT = sb(es1, "CCT", [128, T], F32)
    CCTb = Buf()
    if STOP <= 0.3:
        raise _Stop()
    for idx, ti in enumerate(order1):
        if (STOP <= 0.4 and idx == 1) or (STOP <= 0.5 and idx == 2):
            raise _Stop()
        halo = ti == "h"
        n = 128 if halo else T
        xs = idx % 2
        X, Xb = c.X[xs], c.Xb[xs]
        def load_x(idx_, ti_):
            xs_ = idx_ % 2
            if ti_ == "h":
                P.op(DVE, lambda: nc.vector.memset(c.X[xs_][:, :, 0:128], 0.0), writes=c.Xb[xs_])
                P.dma(SP, c.X[xs_][:, :, 0:4], xh.ap().rearrange("p (k t) -> p k t", t=4), c.Xsem[xs_], writes=c.Xb[xs_])
            else:
                P.dma(SP, c.X[xs_][:, :, :], xT[ti_].rearrange("p (k t) -> p k t", t=T), c.Xsem[xs_], writes=c.Xb[xs_])
        if idx == 0:
            load_x(0, order1[0])
        if idx + 1 < len(order1):
            load_x(idx + 1, order1[idx + 1])
        if not halo:
            P.dma(SP, ROPE[:, :, :], rope_d[0:32, :, ti * T:(ti + 1) * T], s_rope, writes=[ROPEb])
        ffn(c, xs, n, G_FFN1, gu1, d1)
        if KSUB <= 4:
            raise _Stop()
        if not halo:
            P.dma(POOL, x1s[ti].rearrange("p (k t) -> p k t", t=T), X[:, :, :], c.XsemS[xs], reads=Xb, writes=[x1s_buf[ti]])
        if KSUB <= 5:
            raise _Stop()
        norm_to_xn(c, xs, n, G_MIX)
        if KSUB <= 6:
            raise _Stop()
        if not halo:
            ch, tl = ti // 4, ti % 4
            for i in range(3):
                ps, pb = proj(c, blk1, n)
                P.op(ACT, lambda: nc.scalar.activation(out=SQ3[:, i, :], in_=ps[:, :], func=AF.Square), reads=[pb], writes=[SQ3b[i]])
                P.op(ACT, lambda: nc.scalar.activation(out=RAW[:, i, :], in_=ps[:, :], func=AF.Copy), reads=[pb], writes=[RAWb[i]])
            p2, p2b = bank()
            P.mm([lambda i=i: nc.tensor.matmul(p2[:, :], lhsT=ONESB[:, :], rhs=SQ3[:, i, :], start=(i == 0), stop=(i == 2)) for i in range(3)],
                 reads=[ones_b] + SQ3b, writes=[p2b])
            rsqrt(tmp2, p2[:, :], p2b, RV2[:, :], RV2b, 1.0 / 384)
            for i in range(3):
                P.op(DVE, lambda i=i: nc.vector.scalar_tensor_tensor(out=CQ[:, i, ti * T:(ti + 1) * T], in0=RAW[:, i, :], scalar=g(G_QL + i),
                                                                     in1=RV2[:, :], op0=ALU.mult, op1=ALU.mult),
                     reads=[RAWb[i], RV2b, gains_b], writes=[CQb[ti]])
            if KSUB <= 7:
                raise _Stop()
            for i in range(2):
                ps, pb = proj(c, blk1, n)
                P.op(ACT, lambda: nc.scalar.activation(out=SQ3[:, i, :], in_=ps[:, :], func=AF.Square), reads=[pb], writes=[SQ3b[i]])
                P.op(ACT, lambda: nc.scalar.activation(out=RAW[:, i, :], in_=ps[:, :], func=AF.Copy), reads=[pb], writes=[RAWb[i]])
            p2, p2b = bank()
            P.mm([lambda i=i: nc.tensor.matmul(p2[:, :], lhsT=ONESB[:, :], rhs=SQ3[:, i, :], start=(i == 0), stop=(i == 1)) for i in range(2)],
                 reads=[ones_b] + SQ3b[0:2], writes=[p2b])
            rsqrt(tmp2, p2[:, :], p2b, RV2[:, :], RV2b, 1.0 / 256)
            for i in range(2):
                P.op(DVE, lambda i=i: nc.vector.scalar_tensor_tensor(out=CKVN[:, i, :], in0=RAW[:, i, :], scalar=g(G_KVL + i),
                                                                     in1=RV2[:, :], op0=ALU.mult, op1=ALU.mult),
                     reads=[RAWb[i], RV2b, gains_b], writes=[CKVNb])
            lat = [latP, latS][ch][tl // 2]
            tl2 = tl % 2
            P.dma(POOL, lat[0:256, tl2 * T:(tl2 + 1) * T].rearrange("(k p) t -> p k t", p=128), CKVN[:, :, :], s_ckvn,
                  reads=[CKVNb], writes=[lat_buf[ch]])
            if KSUB <= 8:
                raise _Stop()
            wt, wtb = blk1.next()
            w3 = wt[:, :].rearrange("p (k c) -> p k c", c=128)
            pk, pkb = bank()
            pq, pqb = bank()
            P.mm([lambda k=k: nc.tensor.matmul(pk[0:32, :], lhsT=w3[:, k, 0:32], rhs=c.XN[:, k, :], start=(k == 0), stop=(k == KC - 1))
                  for k in range(KC)], reads=[wtb] + c.XNb, writes=[pkb])
            P.mm([lambda k=k: nc.tensor.matmul(pq[0:32, :], lhsT=w3[:, k, 32:64], rhs=c.XN[:, k, :], start=(k == 0), stop=(k == KC - 1))
                  for k in range(KC)], reads=[wtb] + c.XNb, writes=[pqb])
            P.op(ACT, lambda: nc.scalar.activation(out=KRO[:, 1, :], in_=pk[0:32, :], func=AF.Copy), reads=[pkb], writes=[KROb])
            P.op(DVE, lambda: nc.vector.scalar_tensor_tensor(out=KA[:, :], in0=pk[0:32, :], scalar=g(G_KR, 0, 32), in1=ROPE[:, 0, :],
                                                             op0=ALU.mult, op1=ALU.mult), reads=[pkb, ROPEb, gains_b, KROb], writes=[KAb])
            P.op(DVE, lambda: nc.vector.scalar_tensor_tensor(out=KB[:, :], in0=pq[0:32, :], scalar=g(G_KRS, 0, 32), in1=ROPE[:, 1, :],
                                                             op0=ALU.mult, op1=ALU.mult), reads=[pqb, ROPEb, gains_b], writes=[KBb])
            P.op(DVE, lambda: nc.vector.tensor_tensor(out=KRO[:, 0, :], in0=KA[:, :], in1=KB[:, :], op=ALU.add),
                 reads=[KAb, KBb], writes=[KROb])
            P.dma(POOL, lat[256:320, tl2 * T:(tl2 + 1) * T].rearrange("(a p) t -> p a t", p=32), KRO[:, :, :], s_kro,
                  reads=[KROb], writes=[lat_buf[ch]])
        if KSUB <= 9:
            raise _Stop()
        pcc = []
        for i in range(4):
            pcc.append(proj(c, blk1, n))
            if i >= 1:
                pass
        for i in range(4):
            pc, pcb = pcc[i]
            px, pxb = proj(c, blk1, n)
            P.op(ACT, lambda: nc.scalar.activation(out=CCT[:, 0:n], in_=pc[:, 0:n], func=AF.Copy), reads=[pcb], writes=[CCTb])
            if halo:
                P.op(DVE, lambda: nc.vector.tensor_tensor(out=UH[:, i, :], in0=CCT[:, 0:4], in1=px[:, 0:4], op=ALU.mult),
                     reads=[CCTb, pxb], writes=[UHb])
            else:
                P.op(DVE, lambda: nc.vector.tensor_tensor(out=UT[:, i, :], in0=CCT[:, :], in1=px[:, :], op=ALU.mult),
                     reads=[CCTb, pxb], writes=[UTb])
        if not halo:
            P.dma(POOL, Us[:, :, ch, tl * T:(tl + 1) * T], UT[:, :, :], s_ut, reads=[UTb], writes=[Us_buf])

    if STOP <= 1:
        raise _Stop()
    s_cc = P.sem("cc")
    P.deps(POOL, [lat_buf[0], lat_buf[1]], [gat_buf[0], gat_buf[1]])
    for hf in range(2):
        nc.gpsimd.collective_compute("AllGather", ALU.bypass, replica_groups=[[0, 1, 2, 3], [4, 5, 6, 7]],
                                     ins=[latP[hf].ap().opt()], outs=[gatP[hf].ap().opt()]).then_inc(s_cc)
        P.semcount[id(s_cc)] += 1
    gat_buf[0].w = (s_cc, P.semcount[id(s_cc)])
    for hf in range(2):
        nc.gpsimd.collective_compute("AllGather", ALU.bypass, replica_groups=[[0, 1], [2, 3], [4, 5], [6, 7]],
                                     ins=[latS[hf].ap().opt()], outs=[gatS[hf].ap().opt()]).then_inc(s_cc)
        P.semcount[id(s_cc)] += 1
    gat_buf[1].w = (s_cc, P.semcount[id(s_cc)])

    s_wmv = P.sem("wmv")
    WMVbs = c.Hb[8:16]
    P.dma(SP, c.H[:, 8:16, :], wb["wmv"][0].rearrange("p (k c) -> p k c", c=512), s_wmv, reads=[wbuf["wmv"]], writes=WMVbs)

    for ctx in range(2):
        MEMX = c.X[1][:, :, 0:256]
        MEMXb = c.Xb[1][0]
        P.dma(SP, MEMX, memT[ctx].rearrange("p (k m) -> p k m", m=256), c.Xsem[1], writes=c.Xb[1])
        for k in range(KC):
            P.op(ACT, lambda k=k: nc.scalar.activation(out=c.H[:, k, 0:256], in_=MEMX[:, k, :], func=AF.Square),
                 reads=[MEMXb], writes=[c.Hb[k]])
        ps, pb = bank()
        P.mm([lambda k=k: nc.tensor.matmul(ps[:, 0:256], lhsT=ONESB[:, :], rhs=c.H[:, k, 0:256], start=(k == 0), stop=(k == KC - 1))
              for k in range(KC)], reads=[ones_b] + c.Hb[0:KC], writes=[pb])
        rsqrt(c.tmp, ps[:, 0:256], pb, c.RINV[:, 0:256], c.RINVb, 1.0 / D)
        for k in range(KC):
            P.op(DVE, lambda k=k: nc.vector.scalar_tensor_tensor(out=c.XN[:, k, 0:256], in0=MEMX[:, k, :], scalar=g(G_MEM + k),
                                                                 in1=c.RINV[:, 0:256], op0=ALU.mult, op1=ALU.mult),
                 reads=[MEMXb, c.RINVb, gains_b], writes=[c.XNb[k]])
        for h in range(4):
            ps, pb = proj(c, blk1, 256)
            P.op(ACT, lambda: nc.scalar.activation(out=SQ3[:, 0, 0:256], in_=ps[:, 0:256], func=AF.Square), reads=[pb], writes=[SQ3b[0]])
            P.op(ACT, lambda: nc.scalar.activation(out=RAW[:, 0, 0:256], in_=ps[:, 0:256], func=AF.Copy), reads=[pb], writes=[RAWb[0]])
            p2, p2b = bank()
            P.mm([lambda: nc.tensor.matmul(p2[:, 0:256], lhsT=ONESB[:, :], rhs=SQ3[:, 0, 0:256], start=True, stop=True)],
                 reads=[ones_b, SQ3b[0]], writes=[p2b])
            rsqrt(tmp2, p2[:, 0:256], p2b, RV2[:, 0:256], RV2b, 1.0 / 128)
            P.op(DVE, lambda: nc.vector.scalar_tensor_tensor(out=MK[:, ctx, h, :], in0=RAW[:, 0, 0:256], scalar=g(G_XK), in1=RV2[:, 0:256],
                                                             op0=ALU.mult, op1=ALU.mult), reads=[RAWb[0], RV2b, gains_b], writes=[MK_b])
        wmv3 = c.H[:, 8:16, :]
        for mc in range(2):
            ps, pb = bank()
            P.mm([lambda k=k: nc.tensor.matmul(ps[:, :], lhsT=c.XN[:, k, mc * 128:(mc + 1) * 128], rhs=wmv3[:, k, :],
                                               start=(k == 0), stop=(k == KC - 1)) for k in range(KC)],
                 reads=WMVbs + c.XNb, writes=[pb])
            P.op(ACT, lambda: nc.scalar.activation(out=MV[:, ctx, mc, :], in_=ps[:, :], func=AF.Copy), reads=[pb], writes=[MV_b])

    P.barrier()
    es1.close()
    stacks.pop()
    if STOP <= 2:
        raise _Stop()

    es2 = ExitStack()
    stacks.append(es2)
    SMAX = 8192
    CKV = sb(es2, "CKV", [128, 2, SMAX], BF16)
    CKVb = Buf()
    s_ckv = P.sem("ckv")
    KT = [sb(es2, f"KT{i}", [96, SMAX], BF16) for i in range(2)]
    KTb = [Buf(), Buf()]
    KTrb = [Buf(), Buf()]
    s_kt = [P.sem("kt0"), P.sem("kt1")]
    KRR = sb(es2, "KRR", [96, 2048], BF16)
    KRRb = Buf()
    s_krr = P.sem("krr")
    VG = sb(es2, "VG", [128, SMAX // 128, 4, 65], BF16)
    VGb = Buf()
    QT = [sb(es2, f"QT{i}", [96, T], BF16) for i in range(2)]
    QTb = [Buf(), Buf()]
    NPT = 4
    PT = [sb(es2, f"PT{i}", [128, T], BF16) for i in range(NPT)]
    PTb = [Buf() for _ in range(NPT)]
    SQK = [sb(es2, f"SQK{i}", [64, T], BF16) for i in range(2)]
    SQKb = [Buf(), Buf()]
    SQQ = sb(es2, "SQQ", [96, T], BF16)
    SQQb = Buf()
    tq = mk_tmp(es2, "q")
    RQ = sb(es2, "RQ", [96, T], F32)
    RQb = Buf()
    QA = sb(es2, "QA", [96, T], F32)
    QB = sb(es2, "QB", [96, T], F32)
    QAb, QBb = Buf(), Buf()
    ROQ = sb(es2, "ROQ", [96, 2, T], F32)
    ROQb = Buf()
    s_roq = P.sem("roq")
    KRSS = sb(es2, "KRSS", [128, 64], F32)
    KRSSb = Buf()
    RK = [sb(es2, f"RK{i}", [128, 64], F32) for i in range(2)]
    RKb = [Buf(), Buf()]
    tk = mk_tmp(es2, "k", 64)
    SSK = sb(es2, "SSK", [128, 64], F32)
    SSKb = Buf()
    REC = sb(es2, "REC", [65, T], F32)
    RECb = Buf()
    BC = sb(es2, "BC", [64, T], F32)
    BCb = Buf()
    AOT = [sb(es2, f"AOT{i}", [64, T], BF16) for i in range(2)]
    AOTb = [Buf(), Buf()]
    s_aot = [P.sem("aot0"), P.sem("aot1")]
    P.op(DVE, lambda: nc.vector.memset(VG[:, :, :, 64:65], 1.0), writes=[VGb])
    WUQ = sb(es2, "WUQ", [128, 8, 3 * 192], BF16)
    WUK = sb(es2, "WUK", [128, 2 * 512], BF16)
    WUV = sb(es2, "WUV", [128, 2 * 512], BF16)
    wsm_b = Buf("wsmall")
    s_ws = P.sem("wsm")
    P.dma(SP, WUQ[:, :, :], wb["wuq"].ap().rearrange("h p x -> p h x"), s_ws, reads=[wbuf["wuq"]], writes=[wsm_b])
    P.dma(SP, WUK[:, :], wb["wuk"][0], s_ws, reads=[wbuf["wuk"]], writes=[wsm_b])
    P.dma(SP, WUV[:, :], wb["wuv"][0], s_ws, reads=[wbuf["wuv"]], writes=[wsm_b])

    wuk3 = WUK[:, :].rearrange("p (k c) -> p k c", c=512)
    wuv3 = WUV[:, :].rearrange("p (k c) -> p k c", c=512)
    ST_BANKS = [0, 1, 2]
    O_BANKS = [3, 4]
    bank_pool[:] = [5, 6, 7]
    st_rr = 0
    o_rr = 0
    pt_rr = 0
    qt_rr = 0
    kt_rr = 0
    aot_rr = 0
    SCALE = 96.0 ** -0.5

    rr = {"st": 0, "o": 0, "pt": 0, "qt": 0, "aot": 0}
    pend_q = [None]
    pend_fin = [None]
    LOOK = 2

    def q_gen(h, ti):
        P.dma(SP, ROQ[64:96, :, :], rope_d[64:96, :, ti * T:(ti + 1) * T], s_roq, writes=[ROQb])
        pq, pqb = bank()
        pqs, pqsb = bank()
        P.mm([lambda k=k: nc.tensor.matmul(pq[0:96, :], lhsT=WUQ[:, h, k * 192:k * 192 + 96], rhs=CQ[:, k, ti * T:(ti + 1) * T],
                                           start=(k == 0), stop=(k == 2)) for k in range(3)],
             reads=[wsm_b, CQb[ti]], writes=[pqb])
        P.mm([lambda k=k: nc.tensor.matmul(pqs[0:96, :], lhsT=WUQ[:, h, k * 192 + 96:k * 192 + 192], rhs=CQ[:, k, ti * T:(ti + 1) * T],
                                           start=(k == 0), stop=(k == 2)) for k in range(3)],
             reads=[wsm_b, CQb[ti]], writes=[pqsb])
        P.op(ACT, lambda: nc.scalar.activation(out=SQQ[:, :], in_=pq[0:96, :], func=AF.Square), reads=[pqb], writes=[SQQb])
        p2, p2b = bank()
        P.mm([lambda: nc.tensor.matmul(p2[0:96, :], lhsT=ONESB[0:96, 0:96], rhs=SQQ[:, :], start=True, stop=True)],
             reads=[ones_b, SQQb], writes=[p2b])
        rsqrt(tq, p2[0:96, :], p2b, RQ[:, :], RQb, 1.0 / 96)
        qi = rr["qt"] % 2
        rr["qt"] += 1
        qtile, qtb = QT[qi], QTb[qi]
        P.op(DVE, lambda: nc.vector.scalar_tensor_tensor(out=qtile[0:64, :], in0=pq[0:64, :], scalar=g(G_MQ, 0, 64), in1=RQ[0:64, :],
                                                         op0=ALU.mult, op1=ALU.mult), reads=[pqb, RQb, gains_b], writes=[qtb])
        P.op(DVE, lambda: nc.vector.scalar_tensor_tensor(out=QA[64:96, :], in0=pq[64:96, :], scalar=g(G_MQ, 64, 96), in1=ROQ[64:96, 0, :],
                                                         op0=ALU.mult, op1=ALU.mult), reads=[pqb, ROQb, gains_b], writes=[QAb])
        P.op(DVE, lambda: nc.vector.scalar_tensor_tensor(out=QB[64:96, :], in0=pqs[64:96, :], scalar=g(G_MQS, 64, 96), in1=ROQ[64:96, 1, :],
                                                         op0=ALU.mult, op1=ALU.mult), reads=[pqsb, ROQb, gains_b], writes=[QBb])
        P.op(DVE, lambda: nc.vector.tensor_tensor(out=QA[64:96, :], in0=QA[64:96, :], in1=QB[64:96, :], op=ALU.add),
             reads=[QAb, QBb], writes=[QAb])
        P.op(DVE, lambda: nc.vector.tensor_tensor(out=qtile[64:96, :], in0=QA[64:96, :], in1=RQ[64:96, :], op=ALU.mult),
             reads=[QAb, RQb], writes=[qtb])
        return qtile, qtb

    def main_store(h, hh, ti, NCH, ktile, ktb, ki, rk, rkb, qtile, qtb):
        ob = O_BANKS[rr["o"] % 2]
        rr["o"] += 1
        po, pob = psum[ob], psb[ob]
        P.deps(PE, [], [pob])
        pend = []
        for step in range(NCH + LOOK):
            if step < NCH:
                cc = step
                sbk = ST_BANKS[rr["st"] % 3]
                rr["st"] += 1
                pst, pstb = psum[sbk], psb[sbk]
                P.mm([lambda: nc.tensor.matmul(pst[:, :], lhsT=ktile[0:96, cc * 128:(cc + 1) * 128], rhs=qtile[0:96, :], start=True, stop=True)],
                     reads=[ktb, KTrb[ki], qtb], writes=[pstb])
                pi = rr["pt"] % NPT
                rr["pt"] += 1
                P.op(ACT, lambda: nc.scalar.activation(out=PT[pi][:, :], in_=pst[:, :], func=AF.Exp, scale=rk[:, cc:cc + 1]),
                     reads=[pstb, rkb], writes=[PTb[pi]])
                pend.append((cc, pi))
            if step >= LOOK:
                cc, pi = pend.pop(0)
                last = cc == NCH - 1
                if not last:
                    PE.wait(PTb[pi].w)
                    PE.wait(VGb.w)
                    nc.tensor.matmul(po[0:65, :], lhsT=VG[:, cc, hh, :], rhs=PT[pi][:, :], start=(cc == 0), stop=False)
                    PTb[pi].r[id(PE.sem)] = (PE.sem, PE.n + 1)
                else:
                    P.mm([lambda: nc.tensor.matmul(po[0:65, :], lhsT=VG[:, cc, hh, :], rhs=PT[pi][:, :], start=(cc == 0), stop=True)],
                         reads=[PTb[pi], VGb], writes=[pob])
            if step == 8 and pend_fin[0] is not None:
                finish(*pend_fin[0])
                pend_fin[0] = None
        pend_fin[0] = (h, ti, po, pob)

    def finish(h, ti, po, pob):
        P.op(DVE, lambda: nc.vector.reciprocal(out=REC[64:65, :], in_=po[64:65, :]), reads=[pob], writes=[RECb])
        pbc, pbcb = bank()
        P.mm([lambda: nc.tensor.matmul(pbc[0:64, :], lhsT=ONESF[64:65, 0:64], rhs=REC[64:65, :], start=True, stop=True)],
             reads=[ones_b, RECb], writes=[pbcb])
        P.op(ACT, lambda: nc.scalar.activation(out=BC[:, :], in_=pbc[0:64, :], func=AF.Copy), reads=[pbcb], writes=[BCb])
        ai = rr["aot"] % 2
        rr["aot"] += 1
        P.op(DVE, lambda: nc.vector.tensor_tensor(out=AOT[ai][:, :], in0=po[0:64, :], in1=BC[:, :], op=ALU.mult),
             reads=[pob, BCb], writes=[AOTb[ai]])
        P.dma(POOL, AOs[:, h, ti * T:(ti + 1) * T], AOT[ai][:, :], s_aot[ai], reads=[AOTb[ai]], writes=[AOs_buf])

    for ctx in range(2):
        S = 8192 if ctx == 0 else 4096
        R = 4 if ctx == 0 else 2
        NCH = S // 128
        NKT = S // T
        gat = [gatP, gatS][ctx]
        gb = gat_buf[ctx]
        gvs = [gat[hf].ap().rearrange("(r x) t -> r x t", x=LROWS) for hf in range(2)]
        for r in range(R):
            for hf in range(2):
                c0 = r * 2048 + hf * 1024
                P.dma(SP, CKV[:, :, c0:c0 + 1024], gvs[hf][r, 0:256, :].rearrange("(k p) t -> p k t", p=128), s_ckv,
                      reads=[gb], writes=[CKVb])
                for i in range(2):
                    P.dma(SP, KT[i][64:96, c0:c0 + 1024], gvs[hf][r, 256:288, :], s_kt[i], reads=[gb], writes=[KTrb[i], KTb[i]])
        for r in range(R):
            for hf in range(2):
                P.dma(SP, KRR[64:96, hf * 1024:(hf + 1) * 1024], gvs[hf][r, 288:320, :], s_krr, reads=[gb], writes=[KRRb])
            for kt in range(4):
                P.op(ACT, lambda kt=kt: nc.scalar.activation(out=KRR[64:96, kt * T:(kt + 1) * T], in_=KRR[64:96, kt * T:(kt + 1) * T], func=AF.Square),
                     reads=[KRRb], writes=[KRRb])
            ps, pb = bank()
            fns = [lambda cc=cc: nc.tensor.matmul(ps[:, cc:cc + 1], lhsT=KRR[64:96, cc * 128:(cc + 1) * 128], rhs=ONESB[64:96, 0:1], start=True, stop=True)
                   for cc in range(16)]
            P.mm(fns, reads=[KRRb, ones_b], writes=[pb])
            P.op(DVE, lambda: nc.vector.tensor_copy(out=KRSS[:, r * 16:(r + 1) * 16], in_=ps[:, 0:16]), reads=[pb], writes=[KRSSb])

        if KATT <= 1:
            raise _Stop()
        for hg in range(2):
            for cc in range(NCH):
                ps, pb = bank()
                P.mm([lambda k=k: nc.tensor.matmul(ps[:, 0:256], lhsT=CKV[:, k, cc * 128:(cc + 1) * 128], rhs=wuv3[:, k, hg * 256:(hg + 1) * 256],
                                                   start=(k == 0), stop=(k == 1)) for k in range(2)],
                     reads=[CKVb, wsm_b], writes=[pb])
                eng = ACT if cc % 2 == 0 else DVE
                if eng is ACT:
                    P.op(ACT, lambda: nc.scalar.activation(out=VG[:, cc, :, 0:64], in_=ps[:, 0:256].rearrange("p (h d) -> p h d", d=64), func=AF.Copy),
                         reads=[pb], writes=[VGb])
                else:
                    P.op(DVE, lambda: nc.vector.tensor_copy(out=VG[:, cc, :, 0:64], in_=ps[:, 0:256].rearrange("p (h d) -> p h d", d=64)),
                         reads=[pb], writes=[VGb])
            if KATT <= 2:
                raise _Stop()
            for hh in range(4):
                h = hg * 4 + hh
                ki = kt_rr % 2
                kt_rr += 1
                ktile, ktb = KT[ki], KTb[ki]
                pss, pssb = psum[0], psb[0]
                def ss_mm(kt_):
                    sq_, sqb_ = SQK[kt_ % 2], SQKb[kt_ % 2]
                    P.mm([lambda a=a: nc.tensor.matmul(pss[:, kt_ * 4 + a:kt_ * 4 + a + 1], lhsT=sq_[0:64, a * 128:(a + 1) * 128], rhs=ONESB[0:64, 0:1],
                                                       start=True, stop=True) for a in range(4)],
                         reads=[sqb_, ones_b], writes=[pssb])
                prev_kt = None
                for kt in range(NKT):
                    ps, pb = bank()
                    P.mm([lambda k=k: nc.tensor.matmul(ps[0:64, :], lhsT=wuk3[:, k, h * 64:(h + 1) * 64], rhs=CKV[:, k, kt * T:(kt + 1) * T],
                                                       start=(k == 0), stop=(k == 1)) for k in range(2)],
                         reads=[CKVb, wsm_b], writes=[pb])
                    sq, sqb = SQK[kt % 2], SQKb[kt % 2]
                    P.op(ACT, lambda: nc.scalar.activation(out=sq[:, :], in_=ps[0:64, :], func=AF.Square), reads=[pb], writes=[sqb])
                    P.op(DVE, lambda: nc.vector.tensor_scalar(out=ktile[0:64, kt * T:(kt + 1) * T], in0=ps[0:64, :], scalar1=g(G_MK, 0, 64),
                                                              scalar2=None, op0=ALU.mult), reads=[pb, gains_b, sqb], writes=[ktb])
                    if prev_kt is not None:
                        ss_mm(prev_kt)
                    prev_kt = kt
                ss_mm(prev_kt)
                if KK >= 4:
                    P.op(DVE, lambda: nc.vector.tensor_tensor(out=SSK[:, 0:NCH], in0=pss[:, 0:NCH], in1=KRSS[:, 0:NCH], op=ALU.add),
                         reads=[pssb, KRSSb], writes=[SSKb])
                rk, rkb = RK[ki], RKb[ki]
                if KK >= 5:
                    rsqrt(tk, SSK[:, 0:NCH], SSKb, rk[:, 0:NCH], rkb, 1.0 / 96, post=SCALE)
                if KATT <= 3:
                    raise _Stop()
                for qt in range(4):
                    ti = ctx * 4 + qt
                    if pend_q[0] is None:
                        pend_q[0] = q_gen(h, ti)
                    qtile, qtb = pend_q[0]
                    nxt = None
                    if qt < 3:
                        nxt = (h, ti + 1)
                    elif h < 7:
                        nxt = (h + 1, ctx * 4)
                    pend_q[0] = q_gen(*nxt) if nxt is not None else None
                    main_store(h, hh, ti, NCH, ktile, ktb, ki, rk, rkb, qtile, qtb)
                if KATT <= 7:
                    raise _Stop()

    if pend_fin[0] is not None:
        finish(*pend_fin[0])
        pend_fin[0] = None
    P.barrier()
    es2.close()
    esA.close()
    stacks.pop()
    stacks.pop()
    if STOP <= 3:
        raise _Stop()
    bank_pool[:] = list(range(8))

    es3 = ExitStack()
    stacks.append(es3)
    c = alloc_tile_ctx(es3)
    gu_srcs, d_srcs, blk_srcs, wo_srcs = [], [], [], []
    for ti in range(NT):
        gu_srcs += [(wb["w2gu"][j], wbuf["w2gu"]) for j in range(NJ)]
        d_srcs += [(wb["w2d"][f], wbuf["w2d"]) for f in range(8)]
        blk_srcs += [(wb["winb"][m], wbuf["winb"]) for m in range(8)]
        for fc in range(8):
            blk_srcs += [(wb["winb"][8 + gg * 8 + fc], wbuf["winb"]) for gg in range(3)]
        blk_srcs += [(wb["wout"][f], wbuf["wout"]) for f in range(8)]
        wo_srcs += [(wb["wo3"][f], wbuf["wo3"]) for f in range(8)]
    gu3 = Stream(P, es3, "gu3", KC * 256, 3, gu_srcs)
    d3 = Stream(P, es3, "wd3", NJ * 128, 2, d_srcs)
    blk3 = Stream(P, es3, "blk3", KC * 128, 4, blk_srcs)
    wo3s = Stream(P, es3, "wo3", 16 * 128, 2, wo_srcs)
    UW = sb(es3, "UW", [128, 4, T + 2], BF16)
    UWb = Buf()
    s_uw = P.sem("uw")
    AOX = sb(es3, "AOX", [64, 8, T], BF16)
    AOXb = Buf()
    s_aox = P.sem("aox")
    CVT = sb(es3, "CVT", [128, T], F32)
    CVTb = Buf()
    CV = sb(es3, "CV", [128, 4, T], BF16)
    CVb = Buf()
    XQ = sb(es3, "XQ", [128, 4, T], BF16)
    XQb = Buf()
    RAWX = sb(es3, "RAWX", [128, T], F32)
    RAWXb = Buf()
    SQX = sb(es3, "SQX", [128, T], BF16)
    SQXb = Buf()
    RV3 = c.RINV
    RV3b = c.RINVb
    tmp3 = c.tmp
    XA = sb(es3, "XA", [128, 4, T], BF16)
    XAb = Buf()
    PM = [sb(es3, f"PM{i}", [128, T], BF16) for i in range(2)]
    PMb = [Buf(), Buf()]
    RD = sb(es3, "RD", [128, T], F32)
    RDb = Buf()
    TH = [sb(es3, f"TH{i}", [128, T], F32) for i in range(3)]
    THb = [Buf() for _ in range(3)]
    M0 = sb(es3, "M0", [128, T], F32)
    M1 = sb(es3, "M1", [128, T], F32)
    M0b, M1b = Buf(), Buf()
    MG = sb(es3, "MG", [128, KC, T], BF16)
    MGb = [Buf() for _ in range(KC)]
    s_y = [P.sem("y0"), P.sem("y1")]
    XSC = 128.0 ** -0.5

    for ti in range(NT):
        ctx, tl = ti // 4, ti % 4
        xs = ti % 2
        X, Xb = c.X[xs], c.Xb[xs]
        def load_x3(ti_):
            xs_ = ti_ % 2
            P.dma(SP, c.X[xs_][:, :, :], x1s[ti_].rearrange("p (k t) -> p k t", t=T), c.Xsem[xs_], reads=[x1s_buf[ti_]], writes=c.Xb[xs_])
        if ti == 0:
            load_x3(0)
        if ti + 1 < NT:
            load_x3(ti + 1)
        lo = max(tl * T - 1, 0)
        hi = min(tl * T + T + 1, 2048)
        d0 = lo - (tl * T - 1)
        P.dma(SP, UW[:, :, d0:d0 + (hi - lo)], Us[:, :, ctx, lo:hi], s_uw, reads=[Us_buf], writes=[UWb])
        if tl == 0:
            P.op(DVE, lambda: nc.vector.tensor_copy(out=UW[:, :, 0:1], in_=UH[:, :, 2 * ctx:2 * ctx + 1]), reads=[UHb], writes=[UWb])
        if tl == 3:
            P.op(DVE, lambda: nc.vector.tensor_copy(out=UW[:, :, T + 1:T + 2], in_=UH[:, :, 2 * ctx + 1:2 * ctx + 2]), reads=[UHb], writes=[UWb])
        P.dma(SP, AOX[:, :, :], AOs[:, :, ti * T:(ti + 1) * T], s_aox, reads=[AOs_buf], writes=[AOXb])
        norm_to_xn(c, xs, T, G_MIX)
        for i in range(4):
            ps, pb = proj(c, blk3, T)
            P.op(DVE, lambda: nc.vector.tensor_scalar(out=CVT[:, :], in0=UW[:, i, 0:T], scalar1=g(G_CONV + i * 3 + 0), scalar2=None, op0=ALU.mult),
                 reads=[UWb, gains_b], writes=[CVTb])
            P.op(DVE, lambda: nc.vector.scalar_tensor_tensor(out=CVT[:, :], in0=UW[:, i, 1:T + 1], scalar=g(G_CONV + i * 3 + 1), in1=CVT[:, :],
                                                             op0=ALU.mult, op1=ALU.add), reads=[UWb, CVTb, gains_b], writes=[CVTb])
            P.op(DVE, lambda: nc.vector.scalar_tensor_tensor(out=CVT[:, :], in0=UW[:, i, 2:T + 2], scalar=g(G_CONV + i * 3 + 2), in1=CVT[:, :],
                                                             op0=ALU.mult, op1=ALU.add), reads=[UWb, CVTb, gains_b], writes=[CVTb])
            P.op(DVE, lambda: nc.vector.tensor_tensor(out=CV[:, i, :], in0=CVT[:, :], in1=ps[:, :], op=ALU.mult),
                 reads=[CVTb, pb], writes=[CVb])
        for h in range(4):
            ps, pb = proj(c, blk3, T)
            P.op(ACT, lambda: nc.scalar.activation(out=SQX[:, :], in_=ps[:, :], func=AF.Square), reads=[pb], writes=[SQXb])
            P.op(ACT, lambda: nc.scalar.activation(out=RAWX[:, :], in_=ps[:, :], func=AF.Copy), reads=[pb], writes=[RAWXb])
            p2, p2b = bank()
            P.mm([lambda: nc.tensor.matmul(p2[:, :], lhsT=ONESB[:, :], rhs=SQX[:, :], start=True, stop=True)], reads=[ones_b, SQXb], writes=[p2b])
            rsqrt(tmp3, p2[:, :], p2b, RV3[:, :], RV3b, 1.0 / 128)
            P.op(DVE, lambda: nc.vector.scalar_tensor_tensor(out=XQ[:, h, :], in0=RAWX[:, :], scalar=g(G_XQ), in1=RV3[:, :],
                                                             op0=ALU.mult, op1=ALU.mult), reads=[RAWXb, RV3b, gains_b], writes=[XQb])
        for h in range(4):
            for mc in range(2):
                ps, pb = bank()
                P.mm([lambda: nc.tensor.matmul(ps[:, :], lhsT=MK[:, ctx, h, mc * 128:(mc + 1) * 128], rhs=XQ[:, h, :], start=True, stop=True)],
                     reads=[MK_b, XQb], writes=[pb])
                P.op(ACT, lambda: nc.scalar.activation(out=PM[mc][:, :], in_=ps[:, :], func=AF.Exp, scale=XSC), reads=[pb], writes=[PMb[mc]])
            po, pob = bank()
            pdn, pdnb = bank()
            P.mm([lambda mc=mc: nc.tensor.matmul(po[:, :], lhsT=MV[:, ctx, mc, h * 128:(h + 1) * 128], rhs=PM[mc][:, :], start=(mc == 0), stop=(mc == 1))
                  for mc in range(2)], reads=[MV_b] + PMb, writes=[pob])
            P.mm([lambda mc=mc: nc.tensor.matmul(pdn[:, :], lhsT=ONESB[:, :], rhs=PM[mc][:, :], start=(mc == 0), stop=(mc == 1))
                  for mc in range(2)], reads=[ones_b] + PMb, writes=[pdnb])
            P.op(DVE, lambda: nc.vector.reciprocal(out=RD[:, :], in_=pdn[:, :]), reads=[pdnb], writes=[RDb])
            P.op(DVE, lambda: nc.vector.tensor_tensor(out=XA[:, h, :], in0=po[:, :], in1=RD[:, :], op=ALU.mult), reads=[pob, RDb], writes=[XAb])
        for fc in range(KC):
            wt, wtb = wo3s.next()
            w3 = wt[:, :].rearrange("p (k c) -> p k c", c=128)
            ys = []
            py, pyb = bank()
            P.mm([lambda h=h: nc.tensor.matmul(py[:, :], lhsT=w3[0:64, h, :], rhs=AOX[:, h, :], start=(h == 0), stop=(h == 7)) for h in range(8)],
                 reads=[wtb, AOXb], writes=[pyb])
            ys.append((py, pyb))
            py, pyb = bank()
            P.mm([lambda i=i: nc.tensor.matmul(py[:, :], lhsT=w3[:, 8 + i, :], rhs=CV[:, i, :], start=(i == 0), stop=(i == 3)) for i in range(4)],
                 reads=[wtb, CVb], writes=[pyb])
            ys.append((py, pyb))
            py, pyb = bank()
            P.mm([lambda i=i: nc.tensor.matmul(py[:, :], lhsT=w3[:, 12 + i, :], rhs=XA[:, i, :], start=(i == 0), stop=(i == 3)) for i in range(4)],
                 reads=[wtb, XAb], writes=[pyb])
            ys.append((py, pyb))
            for gg in range(3):
                pg, pgb = proj(c, blk3, T)
                P.op(ACT, lambda: nc.scalar.activation(out=TH[gg][:, :], in_=pg[:, :], func=AF.Tanh, scale=0.5), reads=[pgb], writes=[THb[gg]])
            P.op(DVE, lambda: nc.vector.scalar_tensor_tensor(out=M0[:, :], in0=TH[0][:, :], scalar=1.0, in1=ys[0][0][:, :], op0=ALU.add, op1=ALU.mult),
                 reads=[THb[0], ys[0][1]], writes=[M0b])
            P.op(DVE, lambda: nc.vector.scalar_tensor_tensor(out=M1[:, :], in0=TH[1][:, :], scalar=1.0, in1=ys[1][0][:, :], op0=ALU.add, op1=ALU.mult),
                 reads=[THb[1], ys[1][1]], writes=[M1b])
            P.op(DVE, lambda: nc.vector.tensor_tensor(out=M0[:, :], in0=M0[:, :], in1=M1[:, :], op=ALU.add), reads=[M0b, M1b], writes=[M0b])
            P.op(DVE, lambda: nc.vector.scalar_tensor_tensor(out=M1[:, :], in0=TH[2][:, :], scalar=1.0, in1=ys[2][0][:, :], op0=ALU.add, op1=ALU.mult),
                 reads=[THb[2], ys[2][1]], writes=[M1b])
            P.op(DVE, lambda: nc.vector.tensor_tensor(out=MG[:, fc, :], in0=M0[:, :], in1=M1[:, :], op=ALU.add), reads=[M0b, M1b], writes=[MGb[fc]])
        for fo in range(KC):
            wt, wtb = blk3.next()
            w3 = wt[:, :].rearrange("p (k c) -> p k c", c=128)
            ps, pb = bank()
            P.mm([lambda k=k: nc.tensor.matmul(ps[:, :], lhsT=w3[:, k, :], rhs=MG[:, k, :], start=(k == 0), stop=(k == KC - 1)) for k in range(KC)],
                 reads=[wtb] + MGb, writes=[pb])
            P.op(DVE, lambda: nc.vector.scalar_tensor_tensor(out=X[:, fo, :], in0=ps[:, :], scalar=0.5, in1=X[:, fo, :], op0=ALU.mult, op1=ALU.add),
                 reads=[pb, Xb[fo]], writes=[Xb[fo]])
        ffn(c, xs, T, G_FFN2, gu3, d3)
        P.dma(POOL, yT[ti].rearrange("p (k t) -> p k t", t=T), X[:, :, :], c.XsemS[xs], reads=Xb, writes=[])

    P.barrier()
    es3.close()
    es.close()


def _kblocks(W, cols):
    K = W.shape[0]
    kc = K // 128
    out = np.empty((len(cols), 128, kc, 128), np.float32)
    Wr = W.reshape(kc, 128, W.shape[1])
    for m, c0 in enumerate(cols):
        out[m] = Wr[:, :, c0:c0 + 128].transpose(1, 0, 2)
    return out.reshape(len(cols), 128, kc * 128)


def _prep_weights(inp):
    f = lambda a: np.ascontiguousarray(np.asarray(a, np.float32))
    out = {}
    for tag, gu, dn in (("w1", "ffn1_w_gu", "ffn1_w_down"), ("w2", "ffn2_w_gu", "ffn2_w_down")):
        W = f(inp[gu][0]).reshape(KC, 128, 2 * FF)
        gate = W[:, :, :FF].reshape(KC, 128, NJ, 128)
        up = W[:, :, FF:].reshape(KC, 128, NJ, 128)
        st = np.stack([gate, up], axis=3)
        out[tag + "gu"] = np.ascontiguousarray(st.transpose(2, 1, 0, 3, 4)).reshape(NJ, 128, KC * 256)
        Wd = f(inp[dn][0]).reshape(NJ, 128, 8, 128)
        out[tag + "d"] = np.ascontiguousarray(Wd.transpose(2, 1, 0, 3)).reshape(8, 128, NJ * 128)
    Win = f(inp["w_in"][0])
    perm = np.concatenate([np.arange(16, 32), np.arange(0, 16)])
    kr = Win[:, 640:672]
    Wkr = np.concatenate([kr, kr[:, perm], np.zeros((D, 64), np.float32)], axis=1)
    Wa = np.concatenate([Win[:, 0:640], Wkr, Win[:, 1184:2208]], axis=1)
    out["wina"] = _kblocks(Wa, [i * 128 for i in range(14)])
    Wb = np.concatenate([Win[:, 672:1184], Win[:, 2208:2720], Win[:, 2720:5792]], axis=1)
    out["winb"] = _kblocks(Wb, [i * 128 for i in range(32)])
    Wuq = f(inp["w_uq"][0])
    wuq = np.empty((8, 128, 3, 192), np.float32)
    Wr = Wuq.reshape(3, 128, 768)
    for h in range(8):
        blk = Wr[:, :, h * 96:(h + 1) * 96]
        sw = np.concatenate([blk[:, :, :64], blk[:, :, 64:][:, :, perm]], axis=2)
        wuq[h] = np.concatenate([blk, sw], axis=2).transpose(1, 0, 2)
    out["wuq"] = wuq.reshape(8, 128, 3 * 192)
    out["wuk"] = np.ascontiguousarray(f(inp["w_uk"][0]).reshape(2, 128, 512).transpose(1, 0, 2)).reshape(1, 128, 1024)
    out["wuv"] = np.ascontiguousarray(f(inp["w_uv"][0]).reshape(2, 128, 512).transpose(1, 0, 2)).reshape(1, 128, 1024)
    wo3 = np.zeros((8, 128, 16, 128), np.float32)
    Wm = f(inp["w_o_mla"][0]).reshape(8, 64, 8, 128)
    Wc = f(inp["w_o_conv"][0]).reshape(4, 128, 8, 128)
    Wx = f(inp["w_o_mem"][0]).reshape(4, 128, 8, 128)
    wo3[:, 0:64, 0:8, :] = Wm.transpose(2, 1, 0, 3)
    wo3[:, :, 8:12, :] = Wc.transpose(2, 1, 0, 3)
    wo3[:, :, 12:16, :] = Wx.transpose(2, 1, 0, 3)
    out["wo3"] = wo3.reshape(8, 128, 16 * 128)
    out["wout"] = _kblocks(f(inp["w_out"][0]), [i * 128 for i in range(8)])
    Wmkv = f(inp["w_mem_kv"][0])
    out["wmk"] = _kblocks(Wmkv, [i * 128 for i in range(4)])
    out["wmv"] = np.ascontiguousarray(Wmkv[:, 512:].reshape(8, 128, 512).transpose(1, 0, 2)).reshape(1, 128, 8 * 512)
    G = np.zeros((128, NG), np.float32)
    def colk(v, c0):
        v = f(v).reshape(-1, 128)
        for k in range(v.shape[0]):
            G[:, c0 + k] = v[k]
    colk(inp["ffn1_norm"][0], G_FFN1)
    colk(inp["mix_norm"][0], G_MIX)
    colk(inp["ffn2_norm"][0], G_FFN2)
    colk(inp["mem_norm"][0], G_MEM)
    colk(inp["q_lora_norm"][0], G_QL)
    colk(inp["kv_lora_norm"][0], G_KVL)
    mq = f(inp["mla_q_norm"][0])
    mk = f(inp["mla_k_norm"][0])
    G[0:96, G_MQ] = mq
    G[0:64, G_MQS] = mq[:64]
    G[64:96, G_MQS] = mq[64:][perm]
    G[0:64, G_MK] = mk[:64]
    G[0:32, G_KR] = mk[64:]
    G[0:32, G_KRS] = mk[64:][perm]
    G[:, G_XQ] = f(inp["xa_q_norm"][0])
    G[:, G_XK] = f(inp["xa_k_norm"][0])
    cw = f(inp["conv_w"][0])
    for i in range(4):
        for tap in range(3):
            G[:, G_CONV + i * 3 + tap] = cw[tap, i * 128:(i + 1) * 128]
    out["gains"] = G
    return out


def _rope_table(pos):
    half = 16
    inv_freq = (10000.0 ** (-np.arange(half, dtype=np.float32) / half)).astype(np.float32)
    ang = pos.astype(np.float32)[None, :] * inv_freq[:, None]
    cos = np.cos(ang).astype(np.float32)
    sin = np.sin(ang).astype(np.float32)
    c32 = np.concatenate([cos, cos], 0)
    s32 = np.concatenate([-sin, sin], 0)
    tab = np.stack([c32, s32], axis=1)
    return np.ascontiguousarray(np.tile(tab, (4, 1, 1)))


_NC_CACHE = {}


def kernel(**inputs):
    xp = np.asarray(inputs["x_prompt"], np.float32)
    xsm = np.asarray(inputs["x_sample"], np.float32)
    mp = np.asarray(inputs["mem_prompt"], np.float32)
    ms = np.asarray(inputs["mem_sample"], np.float32)
    W = _prep_weights(inputs)
    if "nc" not in _NC_CACHE:
        _NC_CACHE["nc"] = build()
    nc = _NC_CACHE["nc"]
    in_maps = []
    for cidx in range(8):
        ps_, pq_ = cidx // 4, cidx % 4
        ss_, sh_ = cidx // 2, cidx % 2
        xpc = xp[ps_, pq_ * 2048:(pq_ + 1) * 2048]
        xsc = xsm[ss_, sh_ * 2048:(sh_ + 1) * 2048]
        xc = np.concatenate([xpc, xsc], 0)
        xt = xc.reshape(NT, T, KC, 128).transpose(0, 3, 2, 1)
        halo = np.zeros((4, D), np.float32)
        if pq_ > 0:
            halo[0] = xp[ps_, pq_ * 2048 - 1]
        if pq_ < 3:
            halo[1] = xp[ps_, (pq_ + 1) * 2048]
        if sh_ > 0:
            halo[2] = xsm[ss_, sh_ * 2048 - 1]
        if sh_ < 1:
            halo[3] = xsm[ss_, (sh_ + 1) * 2048]
        hl = halo.reshape(4, KC, 128).transpose(2, 1, 0)
        memc = np.stack([mp[ps_], ms[ss_]], 0)
        memt = memc.reshape(2, 256, KC, 128).transpose(0, 3, 2, 1)
        pos = np.concatenate([np.arange(pq_ * 2048, (pq_ + 1) * 2048), np.arange(sh_ * 2048, (sh_ + 1) * 2048)])
        m = {
            "xT": np.ascontiguousarray(xt).reshape(NT, 128, KC * T),
            "xh": np.ascontiguousarray(hl).reshape(128, KC * 4),
            "memT": np.ascontiguousarray(memt).reshape(2, 128, KC * 256),
            "ropeT": _rope_table(pos),
        }
        m.update(W)
        in_maps.append(m)
    res = run_bass_kernel_spmd(nc, in_maps, core_ids=list(range(8)))
    yp = np.empty_like(xp)
    ysm = np.empty_like(xsm)
    for cidx in range(8):
        ps_, pq_ = cidx // 4, cidx % 4
        ss_, sh_ = cidx // 2, cidx % 2
        y = np.asarray(res.results[cidx]["yT"]).reshape(NT, 128, KC, T).transpose(0, 3, 2, 1).reshape(NT * T, D)
        yp[ps_, pq_ * 2048:(pq_ + 1) * 2048] = y[:2048]
        ysm[ss_, sh_ * 2048:(sh_ + 1) * 2048] = y[2048:]
    return (yp, ysm)
```

```python
import numpy as np
from contextlib import ExitStack
import concourse.bass as bass
import concourse.mybir as mybir
from concourse.bass_utils import run_bass_kernel_spmd

F32 = mybir.dt.float32
BF16 = mybir.dt.bfloat16
I32 = mybir.dt.int32
AF = mybir.ActivationFunctionType
ALU = mybir.AluOpType

D = 1024
KC = 8
T = 512
NT = 8
FF = 2816
NJ = 22
EPS = 1e-6
SAME_ENG_SYNC = True
MAGIC = 1597463007.0

G_FFN1, G_MIX, G_FFN2, G_MEM, G_QL, G_KVL = 0, 8, 16, 24, 32, 35
G_MQ, G_MQS, G_MK, G_KR, G_KRS, G_XQ, G_XK, G_CONV = 37, 38, 39, 40, 41, 42, 43, 44
NG = 56


class Buf:
    __slots__ = ("w", "r", "name")

    def __init__(self, name=""):
        self.w = None
        self.r = {}
        self.name = name


class Eng:
    def __init__(self, e, sem, name, is_pe=False):
        self.e = e
        self.sem = sem
        self.n = 0
        self.seen = {}
        self.name = name
        self.is_pe = is_pe

    def wait(self, tok):
        if tok is None:
            return
        sem, val = tok
        if sem is self.sem and (self.is_pe or not SAME_ENG_SYNC):
            return
        k = id(sem)
        if self.seen.get(k, 0) >= val:
            return
        self.e.wait_ge(sem, val)
        self.seen[k] = val


class Prog:
    def __init__(self):
        self.nc = bass.Bass("TRN2", target_bir_lowering=False)
        self.es = ExitStack()
        nc = self.nc
        self.semcount = {}
        self.sems = []
        self.PE = Eng(nc.tensor, self.sem("pe"), "pe", is_pe=True)
        self.ACT = Eng(nc.scalar, self.sem("act"), "act")
        self.DVE = Eng(nc.vector, self.sem("dve"), "dve")
        self.POOL = Eng(nc.gpsimd, self.sem("pool"), "pool")
        self.SP = Eng(nc.sync, self.sem("sp"), "sp")
        self.engs = [self.PE, self.ACT, self.DVE, self.POOL, self.SP]
        self.bank_rr = 0

    def sem(self, name):
        s = self.es.enter_context(self.nc.semaphore(f"{name}_n{len(self.sems)}"))
        self.semcount[id(s)] = 0
        self.sems.append(s)
        return s

    def deps(self, E, reads, writes):
        for b in reads:
            E.wait(b.w)
        for b in writes:
            E.wait(b.w)
            for t in list(b.r.values()):
                E.wait(t)

    def done(self, tok, reads, writes):
        for b in reads:
            b.r[id(tok[0])] = tok
        for b in writes:
            b.w = tok
            b.r = {}

    def op(self, E, fn, reads=(), writes=()):
        self.deps(E, reads, writes)
        ins = fn()
        E.n += 1
        ins.then_inc(E.sem, 1)
        self.semcount[id(E.sem)] = E.n
        self.done((E.sem, E.n), reads, writes)

    def mm(self, fns, reads, writes):
        E = self.PE
        self.deps(E, reads, writes)
        ins = None
        for f in fns:
            ins = f()
        E.n += 1
        ins.then_inc(E.sem, 1)
        self.semcount[id(E.sem)] = E.n
        self.done((E.sem, E.n), reads, writes)

    def mmf(self, items, reads, writes):
        E = self.PE
        self.deps(E, reads, writes)
        allr = list(reads)
        ins = None
        for f, rb in items:
            for b in rb:
                E.wait(b.w)
            allr += rb
            ins = f()
        E.n += 1
        ins.then_inc(E.sem, 1)
        self.semcount[id(E.sem)] = E.n
        self.done((E.sem, E.n), allr, writes)

    def dma(self, Q, out, in_, sem, reads=(), writes=(), **kw):
        self.deps(Q, reads, writes)
        Q.e.dma_start(out=out, in_=in_, **kw).then_inc(sem, 16)
        self.semcount[id(sem)] += 16
        self.done((sem, self.semcount[id(sem)]), reads, writes)

    def barrier(self):
        for E in self.engs:
            for s in self.sems:
                c = self.semcount[id(s)]
                if c > 0:
                    E.wait((s, c)) if s is not E.sem else None


class Stream:
    def __init__(self, P, es, name, width, nslots, srcs):
        self.P = P
        self.srcs = srcs
        self.n = nslots
        self.slots = [es.enter_context(P.nc.sbuf_tensor(f"{name}_s{i}_{len(P.sems)}", [128, width], BF16)) for i in range(nslots)]
        self.bufs = [Buf(f"{name}{i}") for i in range(nslots)]
        self.sems = [P.sem(f"{name}_q{i}") for i in range(nslots)]
        self.issued = 0
        self.pos = 0

    def _issue(self):
        i = self.issued
        s = i % self.n
        ap, db = self.srcs[i]
        self.P.dma(self.P.SP, self.slots[s][:, :], ap, self.sems[s], reads=[db], writes=[self.bufs[s]])
        self.issued += 1

    def next(self):
        i = self.pos
        while self.issued < min(i + self.n, len(self.srcs)):
            self._issue()
        self.pos += 1
        s = i % self.n
        return self.slots[s], self.bufs[s]


import os as _os
STOP = float(_os.environ.get("KSTOP", "9"))
KSUB = int(_os.environ.get("KSUB", "99"))
KATT = int(_os.environ.get("KATT", "99"))
KK = int(_os.environ.get("KK", "99"))


class _Stop(Exception):
    pass


def build():
    stacks = []
    P = Prog()
    try:
        _build(P, stacks)
    except _Stop:
        P.barrier()
        for st in reversed(stacks):
            st.close()
        P.es.close()
    return P.nc


def _build(P, stacks):
    nc = P.nc
    es = P.es
    PE, ACT, DVE, POOL, SP = P.PE, P.ACT, P.DVE, P.POOL, P.SP

    def din(name, shape, dt=F32):
        return nc.dram_tensor(name, shape, dt, kind="ExternalInput")

    xT = din("xT", [NT, 128, KC * T])
    xh = din("xh", [128, KC * 4])
    memT = din("memT", [2, 128, KC * 256])
    gains_d = din("gains", [128, NG])
    rope_d = din("ropeT", [128, 2, NT * T])
    wsh = {
        "w1gu": [NJ, 128, KC * 256], "w1d": [8, 128, NJ * 128],
        "w2gu": [NJ, 128, KC * 256], "w2d": [8, 128, NJ * 128],
        "wina": [14, 128, KC * 128], "winb": [32, 128, KC * 128],
        "wuq": [8, 128, 3 * 192], "wuk": [1, 128, 2 * 512], "wuv": [1, 128, 2 * 512],
        "wo3": [8, 128, 16 * 128], "wout": [8, 128, KC * 128],
        "wmk": [4, 128, KC * 128], "wmv": [1, 128, KC * 512],
    }
    wf = {k: din(k, v) for k, v in wsh.items()}
    wb = {k: nc.dram_tensor(k + "_b", v, BF16) for k, v in wsh.items()}
    wbuf = {k: Buf(k) for k in wsh}
    yT = nc.dram_tensor("yT", [NT, 128, KC * T], F32, kind="ExternalOutput")

    x1s = nc.dram_tensor("x1s", [NT, 128, KC * T], F32)
    x1s_buf = [Buf(f"x1s{i}") for i in range(NT)]
    Us = nc.dram_tensor("Us", [128, 4, 2, 2048], BF16)
    Us_buf = Buf("Us")
    AOs = nc.dram_tensor("AOs", [64, 8, NT * T], BF16)
    AOs_buf = Buf("AOs")
    LROWS = 320
    latP = [nc.dram_tensor(f"latP{i}", [LROWS, 1024], BF16) for i in range(2)]
    latS = [nc.dram_tensor(f"latS{i}", [LROWS, 1024], BF16) for i in range(2)]
    gatP = [nc.dram_tensor(f"gatP{i}", [4 * LROWS, 1024], BF16) for i in range(2)]
    gatS = [nc.dram_tensor(f"gatS{i}", [2 * LROWS, 1024], BF16) for i in range(2)]
    lat_buf = [Buf("latP"), Buf("latS")]
    gat_buf = [Buf("gatP"), Buf("gatS")]

    uniq = [0]

    def sb(stack, name, shape, dt):
        uniq[0] += 1
        return stack.enter_context(nc.sbuf_tensor(f"{name}_u{uniq[0]}", shape, dt))

    cast_order = ["w1gu", "w1d", "wina", "wmk", "wmv", "wuq", "wuk", "wuv", "winb", "wo3", "wout", "w2gu", "w2d"]
    for k in cast_order:
        s = P.sem("c_" + k)
        n0, _, wd = wsh[k]
        bb = max(d_ for d_ in range(1, 1025) if wd % d_ == 0)
        src = wf[k].ap().rearrange("n p (a b) -> (n p a) b", b=bb)
        dst = wb[k].ap().rearrange("n p (a b) -> (n p a) b", b=bb)
        rows = src.shape[0]
        step = 4096
        for r0 in range(0, rows, step):
            r1 = min(rows, r0 + step)
            P.dma(POOL, dst[r0:r1, :], src[r0:r1, :], s, writes=[wbuf[k]] if r0 + step >= rows else [], max_dma_last_dim=4096)
            if r0 + step < rows:
                pass

    if STOP <= 0.1:
        raise _Stop()
    gains = sb(es, "gains_sb", [128, NG], F32)
    gains_b = Buf("gains")
    ONESB = sb(es, "onesb", [128, 128], BF16)
    ONESF = sb(es, "onesf", [128, 64], F32)
    ones_b = Buf("ones")
    s_misc = P.sem("misc")
    P.dma(SP, gains[:, :], gains_d.ap(), s_misc, writes=[gains_b])
    P.op(DVE, lambda: nc.vector.memset(ONESB[:, :], 1.0), writes=[ones_b])
    P.op(DVE, lambda: nc.vector.memset(ONESF[:, :], 1.0), writes=[ones_b])
    MK = sb(es, "MK", [128, 2, 4, 256], BF16)
    MV = sb(es, "MV", [128, 2, 2, 512], BF16)
    MK_b, MV_b = Buf("MK"), Buf("MV")
    UH = sb(es, "UH", [128, 4, 4], BF16)
    UHb = Buf()

    psum = [es.enter_context(nc.psum_tensor(f"ps{i}", [128, 512], F32)) for i in range(8)]
    psb = [Buf(f"ps{i}") for i in range(8)]
    bank_pool = list(range(8))

    def bank():
        i = bank_pool[P.bank_rr % len(bank_pool)]
        P.bank_rr += 1
        return psum[i], psb[i]

    def g(col, p0=0, p1=128):
        return gains[p0:p1, col:col + 1]

    def rsqrt(tmp, ps_ap, ps_b, out_ap, out_b, inv_n, post=None):
        V, Y, TT = tmp["V"], tmp["Y"], tmp["T"]
        vb, yb, tb = tmp["Vb"], tmp["Yb"], tmp["Tb"]
        shp = ps_ap.shape
        np_, n = shp[0], shp[1]
        p0 = tmp.get("p0", 0)
        v = V[p0:p0 + np_, 0:n]
        y = Y[p0:p0 + np_, 0:n]
        t = TT[p0:p0 + np_, 0:n]
        P.op(DVE, lambda: nc.vector.tensor_scalar(out=v, in0=ps_ap, scalar1=inv_n, scalar2=EPS, op0=ALU.mult, op1=ALU.add),
             reads=[ps_b], writes=[vb])
        P.op(DVE, lambda: nc.vector.tensor_scalar(out=y.bitcast(I32), in0=v.bitcast(I32), scalar1=-0.5, scalar2=MAGIC,
                                                  op0=ALU.mult, op1=ALU.add), reads=[vb], writes=[yb])
        for it in range(2):
            P.op(DVE, lambda: nc.vector.tensor_tensor(out=t, in0=y, in1=y, op=ALU.mult), reads=[yb], writes=[tb])
            P.op(DVE, lambda: nc.vector.scalar_tensor_tensor(out=t, in0=t, scalar=-0.5, in1=v, op0=ALU.mult, op1=ALU.mult),
                 reads=[tb, vb], writes=[tb])
            last = it == 1
            o = out_ap if last else y
            ob = out_b if last else yb
            if last and post is not None:
                P.op(DVE, lambda: nc.vector.scalar_tensor_tensor(out=y, in0=t, scalar=1.5, in1=y, op0=ALU.add, op1=ALU.mult),
                     reads=[tb, yb], writes=[yb])
                P.op(DVE, lambda: nc.vector.tensor_scalar(out=o, in0=y, scalar1=post, scalar2=None, op0=ALU.mult),
                     reads=[yb], writes=[ob])
            else:
                P.op(DVE, lambda: nc.vector.scalar_tensor_tensor(out=o, in0=t, scalar=1.5, in1=y, op0=ALU.add, op1=ALU.mult),
                     reads=[tb, yb], writes=[ob])

    def mk_tmp(stack, tag, n=T):
        return {"V": sb(stack, "tV" + tag, [128, n], F32), "Y": sb(stack, "tY" + tag, [128, n], F32),
                "T": sb(stack, "tT" + tag, [128, n], F32), "Vb": Buf(), "Yb": Buf(), "Tb": Buf()}

    class TileCtx:
        pass

    def alloc_tile_ctx(stack):
        c = TileCtx()
        c.X = [sb(stack, f"X{i}", [128, KC, T], F32) for i in range(2)]
        c.Xb = [[Buf(f"X{i}_{k}") for k in range(KC)] for i in range(2)]
        c.Xsem = [P.sem(f"X{i}") for i in range(2)]
        c.XsemS = [P.sem(f"XS{i}") for i in range(2)]
        c.XN = sb(stack, "XN", [128, KC, T], BF16)
        c.XNb = [Buf(f"XN{k}") for k in range(KC)]
        c.H = sb(stack, "H", [128, NJ, T], BF16)
        c.Hb = [Buf(f"H{j}") for j in range(NJ)]
        c.SG = [sb(stack, f"SG{i}", [128, T], F32) for i in range(2)]
        c.SGb = [Buf(), Buf()]
        c.RINV = sb(stack, "RINV", [128, T], F32)
        c.RINVb = Buf("rinv")
        c.tmp = mk_tmp(stack, "a")
        return c

    def norm_to_xn(c, xs, n, gcol):
        X, Xb = c.X[xs], c.Xb[xs]
        for k0 in range(0, KC, 4):
            P.op(ACT, lambda k0=k0: nc.scalar.activation(out=c.H[:, k0:k0 + 4, 0:n], in_=X[:, k0:k0 + 4, 0:n], func=AF.Square),
                 reads=Xb[k0:k0 + 4], writes=c.Hb[k0:k0 + 4])
        ps, pb = bank()
        P.mmf([(lambda k=k: nc.tensor.matmul(ps[:, 0:n], lhsT=ONESB[:, :], rhs=c.H[:, k, 0:n], start=(k == 0), stop=(k == KC - 1)), [c.Hb[k]])
               for k in range(KC)], reads=[ones_b], writes=[pb])
        rsqrt(c.tmp, ps[:, 0:n], pb, c.RINV[:, 0:n], c.RINVb, 1.0 / D)
        for k in range(KC):
            P.op(DVE, lambda k=k: nc.vector.scalar_tensor_tensor(out=c.XN[:, k, 0:n], in0=X[:, k, 0:n], scalar=g(gcol + k),
                                                                 in1=c.RINV[:, 0:n], op0=ALU.mult, op1=ALU.mult),
                 reads=[Xb[k], c.RINVb, gains_b], writes=[c.XNb[k]])

    def ffn(c, xs, n, gcol, gu_stream, d_stream):
        X, Xb = c.X[xs], c.Xb[xs]
        if KSUB <= 1:
            raise _Stop()
        norm_to_xn(c, xs, n, gcol)
        if KSUB <= 2:
            raise _Stop()
        for j in range(NJ):
            wt, wtb = gu_stream.next()
            w3 = wt[:, :].rearrange("p (k c) -> p k c", c=256)
            pg, pgb = bank()
            pu, pub = bank()
            P.mmf([(lambda k=k: nc.tensor.matmul(pg[:, 0:n], lhsT=w3[:, k, 0:128], rhs=c.XN[:, k, 0:n], start=(k == 0), stop=(k == KC - 1)), [c.XNb[k]])
                   for k in range(KC)], reads=[wtb], writes=[pgb])
            P.mmf([(lambda k=k: nc.tensor.matmul(pu[:, 0:n], lhsT=w3[:, k, 128:256], rhs=c.XN[:, k, 0:n], start=(k == 0), stop=(k == KC - 1)), [c.XNb[k]])
                   for k in range(KC)], reads=[wtb], writes=[pub])
            sg, sgb = c.SG[j % 2], c.SGb[j % 2]
            P.op(ACT, lambda: nc.scalar.activation(out=sg[:, 0:n], in_=pg[:, 0:n], func=AF.Silu), reads=[pgb], writes=[sgb])
            P.op(DVE, lambda: nc.vector.tensor_tensor(out=c.H[:, j, 0:n], in0=sg[:, 0:n], in1=pu[:, 0:n], op=ALU.mult),
                 reads=[sgb, pub], writes=[c.Hb[j]])
        if KSUB <= 3:
            raise _Stop()
        for fc in range(KC):
            wt, wtb = d_stream.next()
            w3 = wt[:, :].rearrange("p (j c) -> p j c", c=128)
            pd, pdb = bank()
            P.mmf([(lambda j=j: nc.tensor.matmul(pd[:, 0:n], lhsT=w3[:, j, :], rhs=c.H[:, j, 0:n], start=(j == 0), stop=(j == NJ - 1)), [c.Hb[j]])
                   for j in range(NJ)], reads=[wtb], writes=[pdb])
            P.op(DVE, lambda: nc.vector.scalar_tensor_tensor(out=X[:, fc, 0:n], in0=pd[:, 0:n], scalar=0.5, in1=X[:, fc, 0:n],
                                                             op0=ALU.mult, op1=ALU.add), reads=[pdb, Xb[fc]], writes=[Xb[fc]])

    def proj(c, blk_stream, n):
        wt, wtb = blk_stream.next()
        w3 = wt[:, :].rearrange("p (k c) -> p k c", c=128)
        ps, pb = bank()
        P.mmf([(lambda k=k: nc.tensor.matmul(ps[:, 0:n], lhsT=w3[:, k, :], rhs=c.XN[:, k, 0:n], start=(k == 0), stop=(k == KC - 1)), [c.XNb[k]])
               for k in range(KC)], reads=[wtb], writes=[pb])
        return ps, pb

    esA = ExitStack()
    stacks.append(esA)
    CQ = sb(esA, "CQ", [128, 3, NT * T], BF16)
    CQb = [Buf(f"CQ{i}") for i in range(NT)]
    es1 = ExitStack()
    stacks.append(es1)
    c = alloc_tile_ctx(es1)
    order1 = (["h"] if not _os.environ.get("KSKIPH") else []) + list(range(NT))
    gu_srcs, d_srcs, blk_srcs = [], [], []
    for ti in order1:
        gu_srcs += [(wb["w1gu"][j], wbuf["w1gu"]) for j in range(NJ)]
        d_srcs += [(wb["w1d"][f], wbuf["w1d"]) for f in range(8)]
        if ti == "h":
            blk_srcs += [(wb["wina"][m], wbuf["wina"]) for m in range(6, 14)]
        else:
            blk_srcs += [(wb["wina"][m], wbuf["wina"]) for m in range(14)]
    mem_blk = []
    for ctx in range(2):
        mem_blk += [(wb["wmk"][h], wbuf["wmk"]) for h in range(4)]
    blk_srcs = blk_srcs + mem_blk
    gu1 = Stream(P, es1, "gu", KC * 256, 4, gu_srcs)
    d1 = Stream(P, es1, "wd", NJ * 128, 2, d_srcs)
    blk1 = Stream(P, es1, "blk", KC * 128, 6, blk_srcs)
    RAW = sb(es1, "RAW", [128, 3, T], F32)
    RAWb = [Buf() for _ in range(3)]
    SQ3 = sb(es1, "SQ3", [128, 3, T], BF16)
    SQ3b = [Buf() for _ in range(3)]
    RV2 = c.RINV
    RV2b = c.RINVb
    tmp2 = c.tmp
    CKVN = sb(es1, "CKVN", [128, 2, T], BF16)
    CKVNb = Buf()
    s_ckvn = P.sem("ckvn")
    KRO = sb(es1, "KRO", [32, 2, T], BF16)
    KROb = Buf()
    s_kro = P.sem("kro")
    KA = sb(es1, "KA", [32, T], F32)
    KB = sb(es1, "KB", [32, T], F32)
    KAb, KBb = Buf(), Buf()
    ROPE = sb(es1, "ROPE", [32, 2, T], F32)
    ROPEb = Buf()
    s_rope = P.sem("rope")
    UT = sb(es1, "UT", [128, 4, T], BF16)
    UTb = Buf()
    s_ut = P.sem("ut")
    CCT = sb(es1, "CCT", [128, T], F32)
    CCTb = Buf()
    if STOP <= 0.3:
        raise _Stop()
    for idx, ti in enumerate(order1):
        if (STOP <= 0.4 and idx == 1) or (STOP <= 0.5 and idx == 2):
            raise _Stop()
        halo = ti == "h"
        n = 128 if halo else T
        xs = idx % 2
        X, Xb = c.X[xs], c.Xb[xs]
        def load_x(idx_, ti_):
            xs_ = idx_ % 2
            if ti_ == "h":
                P.op(DVE, lambda: nc.vector.memset(c.X[xs_][:, :, 0:128], 0.0), writes=c.Xb[xs_])
                P.dma(SP, c.X[xs_][:, :, 0:4], xh.ap().rearrange("p (k t) -> p k t", t=4), c.Xsem[xs_], writes=c.Xb[xs_])
            else:
                P.dma(SP, c.X[xs_][:, :, :], xT[ti_].rearrange("p (k t) -> p k t", t=T), c.Xsem[xs_], writes=c.Xb[xs_])
        if idx == 0:
            load_x(0, order1[0])
        if idx + 1 < len(order1):
            load_x(idx + 1, order1[idx + 1])
        if not halo:
            P.dma(SP, ROPE[:, :, :], rope_d[0:32, :, ti * T:(ti + 1) * T], s_rope, writes=[ROPEb])
        ffn(c, xs, n, G_FFN1, gu1, d1)
        if KSUB <= 4:
            raise _Stop()
        if not halo:
            P.dma(POOL, x1s[ti].rearrange("p (k t) -> p k t", t=T), X[:, :, :], c.XsemS[xs], reads=Xb, writes=[x1s_buf[ti]])
        if KSUB <= 5:
            raise _Stop()
        norm_to_xn(c, xs, n, G_MIX)
        if KSUB <= 6:
            raise _Stop()
        if not halo:
            ch, tl = ti // 4, ti % 4
            for i in range(3):
                ps, pb = proj(c, blk1, n)
                P.op(ACT, lambda: nc.scalar.activation(out=SQ3[:, i, :], in_=ps[:, :], func=AF.Square), reads=[pb], writes=[SQ3b[i]])
                P.op(ACT, lambda: nc.scalar.activation(out=RAW[:, i, :], in_=ps[:, :], func=AF.Copy), reads=[pb], writes=[RAWb[i]])
            p2, p2b = bank()
            P.mm([lambda i=i: nc.tensor.matmul(p2[:, :], lhsT=ONESB[:, :], rhs=SQ3[:, i, :], start=(i == 0), stop=(i == 2)) for i in range(3)],
                 reads=[ones_b] + SQ3b, writes=[p2b])
            rsqrt(tmp2, p2[:, :], p2b, RV2[:, :], RV2b, 1.0 / 384)
            for i in range(3):
                P.op(DVE, lambda i=i: nc.vector.scalar_tensor_tensor(out=CQ[:, i, ti * T:(ti + 1) * T], in0=RAW[:, i, :], scalar=g(G_QL + i),
                                                                     in1=RV2[:, :], op0=ALU.mult, op1=ALU.mult),
                     reads=[RAWb[i], RV2b, gains_b], writes=[CQb[ti]])
            if KSUB <= 7:
                raise _Stop()
            for i in range(2):
                ps, pb = proj(c, blk1, n)
                P.op(ACT, lambda: nc.scalar.activation(out=SQ3[:, i, :], in_=ps[:, :], func=AF.Square), reads=[pb], writes=[SQ3b[i]])
                P.op(ACT, lambda: nc.scalar.activation(out=RAW[:, i, :], in_=ps[:, :], func=AF.Copy), reads=[pb], writes=[RAWb[i]])
            p2, p2b = bank()
            P.mm([lambda i=i: nc.tensor.matmul(p2[:, :], lhsT=ONESB[:, :], rhs=SQ3[:, i, :], start=(i == 0), stop=(i == 1)) for i in range(2)],
                 reads=[ones_b] + SQ3b[0:2], writes=[p2b])
            rsqrt(tmp2, p2[:, :], p2b, RV2[:, :], RV2b, 1.0 / 256)
            for i in range(2):
                P.op(DVE, lambda i=i: nc.vector.scalar_tensor_tensor(out=CKVN[:, i, :], in0=RAW[:, i, :], scalar=g(G_KVL + i),
                                                                     in1=RV2[:, :], op0=ALU.mult, op1=ALU.mult),
                     reads=[RAWb[i], RV2b, gains_b], writes=[CKVNb])
            lat = [latP, latS][ch][tl // 2]
            tl2 = tl % 2
            P.dma(POOL, lat[0:256, tl2 * T:(tl2 + 1) * T].rearrange("(k p) t -> p k t", p=128), CKVN[:, :, :], s_ckvn,
                  reads=[CKVNb], writes=[lat_buf[ch]])
            if KSUB <= 8:
                raise _Stop()
            wt, wtb = blk1.next()
            w3 = wt[:, :].rearrange("p (k c) -> p k c", c=128)
            pk, pkb = bank()
            pq, pqb = bank()
            P.mm([lambda k=k: nc.tensor.matmul(pk[0:32, :], lhsT=w3[:, k, 0:32], rhs=c.XN[:, k, :], start=(k == 0), stop=(k == KC - 1))
                  for k in range(KC)], reads=[wtb] + c.XNb, writes=[pkb])
            P.mm([lambda k=k: nc.tensor.matmul(pq[0:32, :], lhsT=w3[:, k, 32:64], rhs=c.XN[:, k, :], start=(k == 0), stop=(k == KC - 1))
                  for k in range(KC)], reads=[wtb] + c.XNb, writes=[pqb])
            P.op(ACT, lambda: nc.scalar.activation(out=KRO[:, 1, :], in_=pk[0:32, :], func=AF.Copy), reads=[pkb], writes=[KROb])
            P.op(DVE, lambda: nc.vector.scalar_tensor_tensor(out=KA[:, :], in0=pk[0:32, :], scalar=g(G_KR, 0, 32), in1=ROPE[:, 0, :],
                                                             op0=ALU.mult, op1=ALU.mult), reads=[pkb, ROPEb, gains_b, KROb], writes=[KAb])
            P.op(DVE, lambda: nc.vector.scalar_tensor_tensor(out=KB[:, :], in0=pq[0:32, :], scalar=g(G_KRS, 0, 32), in1=ROPE[:, 1, :],
                                                             op0=ALU.mult, op1=ALU.mult), reads=[pqb, ROPEb, gains_b], writes=[KBb])
            P.op(DVE, lambda: nc.vector.tensor_tensor(out=KRO[:, 0, :], in0=KA[:, :], in1=KB[:, :], op=ALU.add),
                 reads=[KAb, KBb], writes=[KROb])
            P.dma(POOL, lat[256:320, tl2 * T:(tl2 + 1) * T].rearrange("(a p) t -> p a t", p=32), KRO[:, :, :], s_kro,
                  reads=[KROb], writes=[lat_buf[ch]])
        if KSUB <= 9:
            raise _Stop()
        pcc = []
        for i in range(4):
            pcc.append(proj(c, blk1, n))
            if i >= 1:
                pass
        for i in range(4):
            pc, pcb = pcc[i]
            px, pxb = proj(c, blk1, n)
            P.op(ACT, lambda: nc.scalar.activation(out=CCT[:, 0:n], in_=pc[:, 0:n], func=AF.Copy), reads=[pcb], writes=[CCTb])
            if halo:
                P.op(DVE, lambda: nc.vector.tensor_tensor(out=UH[:, i, :], in0=CCT[:, 0:4], in1=px[:, 0:4], op=ALU.mult),
                     reads=[CCTb, pxb], writes=[UHb])
            else:
                P.op(DVE, lambda: nc.vector.tensor_tensor(out=UT[:, i, :], in0=CCT[:, :], in1=px[:, :], op=ALU.mult),
                     reads=[CCTb, pxb], writes=[UTb])
        if not halo:
            P.dma(POOL, Us[:, :, ch, tl * T:(tl + 1) * T], UT[:, :, :], s_ut, reads=[UTb], writes=[Us_buf])

    if STOP <= 1:
        raise _Stop()
    s_cc = P.sem("cc")
    P.deps(POOL, [lat_buf[0], lat_buf[1]], [gat_buf[0], gat_buf[1]])
    for hf in range(2):
        nc.gpsimd.collective_compute("AllGather", ALU.bypass, replica_groups=[[0, 1, 2, 3], [4, 5, 6, 7]],
                                     ins=[latP[hf].ap().opt()], outs=[gatP[hf].ap().opt()]).then_inc(s_cc)
        P.semcount[id(s_cc)] += 1
    gat_buf[0].w = (s_cc, P.semcount[id(s_cc)])
    for hf in range(2):
        nc.gpsimd.collective_compute("AllGather", ALU.bypass, replica_groups=[[0, 1], [2, 3], [4, 5], [6, 7]],
                                     ins=[latS[hf].ap().opt()], outs=[gatS[hf].ap().opt()]).then_inc(s_cc)
        P.semcount[id(s_cc)] += 1
    gat_buf[1].w = (s_cc, P.semcount[id(s_cc)])

    s_wmv = P.sem("wmv")
    WMVbs = c.Hb[8:16]
    P.dma(SP, c.H[:, 8:16, :], wb["wmv"][0].rearrange("p (k c) -> p k c", c=512), s_wmv, reads=[wbuf["wmv"]], writes=WMVbs)

    for ctx in range(2):
        MEMX = c.X[1][:, :, 0:256]
        MEMXb = c.Xb[1][0]
        P.dma(SP, MEMX, memT[ctx].rearrange("p (k m) -> p k m", m=256), c.Xsem[1], writes=c.Xb[1])
        for k in range(KC):
            P.op(ACT, lambda k=k: nc.scalar.activation(out=c.H[:, k, 0:256], in_=MEMX[:, k, :], func=AF.Square),
                 reads=[MEMXb], writes=[c.Hb[k]])
        ps, pb = bank()
        P.mm([lambda k=k: nc.tensor.matmul(ps[:, 0:256], lhsT=ONESB[:, :], rhs=c.H[:, k, 0:256], start=(k == 0), stop=(k == KC - 1))
              for k in range(KC)], reads=[ones_b] + c.Hb[0:KC], writes=[pb])
        rsqrt(c.tmp, ps[:, 0:256], pb, c.RINV[:, 0:256], c.RINVb, 1.0 / D)
        for k in range(KC):
            P.op(DVE, lambda k=k: nc.vector.scalar_tensor_tensor(out=c.XN[:, k, 0:256], in0=MEMX[:, k, :], scalar=g(G_MEM + k),
                                                                 in1=c.RINV[:, 0:256], op0=ALU.mult, op1=ALU.mult),
                 reads=[MEMXb, c.RINVb, gains_b], writes=[c.XNb[k]])
        for h in range(4):
            ps, pb = proj(c, blk1, 256)
            P.op(ACT, lambda: nc.scalar.activation(out=SQ3[:, 0, 0:256], in_=ps[:, 0:256], func=AF.Square), reads=[pb], writes=[SQ3b[0]])
            P.op(ACT, lambda: nc.scalar.activation(out=RAW[:, 0, 0:256], in_=ps[:, 0:256], func=AF.Copy), reads=[pb], writes=[RAWb[0]])
            p2, p2b = bank()
            P.mm([lambda: nc.tensor.matmul(p2[:, 0:256], lhsT=ONESB[:, :], rhs=SQ3[:, 0, 0:256], start=True, stop=True)],
                 reads=[ones_b, SQ3b[0]], writes=[p2b])
            rsqrt(tmp2, p2[:, 0:256], p2b, RV2[:, 0:256], RV2b, 1.0 / 128)
            P.op(DVE, lambda: nc.vector.scalar_tensor_tensor(out=MK[:, ctx, h, :], in0=RAW[:, 0, 0:256], scalar=g(G_XK), in1=RV2[:, 0:256],
                                                             op0=ALU.mult, op1=ALU.mult), reads=[RAWb[0], RV2b, gains_b], writes=[MK_b])
        wmv3 = c.H[:, 8:16, :]
        for mc in range(2):
            ps, pb = bank()
            P.mm([lambda k=k: nc.tensor.matmul(ps[:, :], lhsT=c.XN[:, k, mc * 128:(mc + 1) * 128], rhs=wmv3[:, k, :],
                                               start=(k == 0), stop=(k == KC - 1)) for k in range(KC)],
                 reads=WMVbs + c.XNb, writes=[pb])
            P.op(ACT, lambda: nc.scalar.activation(out=MV[:, ctx, mc, :], in_=ps[:, :], func=AF.Copy), reads=[pb], writes=[MV_b])

    P.barrier()
    es1.close()
    stacks.pop()
    if STOP <= 2:
        raise _Stop()

    es2 = ExitStack()
    stacks.append(es2)
    SMAX = 8192
    CKV = sb(es2, "CKV", [128, 2, SMAX], BF16)
    CKVb = Buf()
    s_ckv = P.sem("ckv")
    KT = [sb(es2, f"KT{i}", [96, SMAX], BF16) for i in range(2)]
    KTb = [Buf(), Buf()]
    KTrb = [Buf(), Buf()]
    s_kt = [P.sem("kt0"), P.sem("kt1")]
    KRR = sb(es2, "KRR", [96, 2048], BF16)
    KRRb = Buf()
    s_krr = P.sem("krr")
    VG = sb(es2, "VG", [128, SMAX // 128, 4, 65], BF16)
    VGb = Buf()
    QT = [sb(es2, f"QT{i}", [96, T], BF16) for i in range(2)]
    QTb = [Buf(), Buf()]
    NPT = 4
    PT = [sb(es2, f"PT{i}", [128, T], BF16) for i in range(NPT)]
    PTb = [Buf() for _ in range(NPT)]
    SQK = [sb(es2, f"SQK{i}", [64, T], BF16) for i in range(2)]
    SQKb = [Buf(), Buf()]
    SQQ = sb(es2, "SQQ", [96, T], BF16)
    SQQb = Buf()
    tq = mk_tmp(es2, "q")
    RQ = sb(es2, "RQ", [96, T], F32)
    RQb = Buf()
    QA = sb(es2, "QA", [96, T], F32)
    QB = sb(es2, "QB", [96, T], F32)
    QAb, QBb = Buf(), Buf()
    ROQ = sb(es2, "ROQ", [96, 2, T], F32)
    ROQb = Buf()
    s_roq = P.sem("roq")
    KRSS = sb(es2, "KRSS", [128, 64], F32)
    KRSSb = Buf()
    RK = [sb(es2, f"RK{i}", [128, 64], F32) for i in range(2)]
    RKb = [Buf(), Buf()]
    tk = mk_tmp(es2, "k", 64)
    SSK = sb(es2, "SSK", [128, 64], F32)
    SSKb = Buf()
    REC = sb(es2, "REC", [65, T], F32)
    RECb = Buf()
    BC = sb(es2, "BC", [64, T], F32)
    BCb = Buf()
    AOT = [sb(es2, f"AOT{i}", [64, T], BF16) for i in range(2)]
    AOTb = [Buf(), Buf()]
    s_aot = [P.sem("aot0"), P.sem("aot1")]
    P.op(DVE, lambda: nc.vector.memset(VG[:, :, :, 64:65], 1.0), writes=[VGb])
    WUQ = sb(es2, "WUQ", [128, 8, 3 * 192], BF16)
    WUK = sb(es2, "WUK", [128, 2 * 512], BF16)
    WUV = sb(es2, "WUV", [128, 2 * 512], BF16)
    wsm_b = Buf("wsmall")
    s_ws = P.sem("wsm")
    P.dma(SP, WUQ[:, :, :], wb["wuq"].ap().rearrange("h p x -> p h x"), s_ws, reads=[wbuf["wuq"]], writes=[wsm_b])
    P.dma(SP, WUK[:, :], wb["wuk"][0], s_ws, reads=[wbuf["wuk"]], writes=[wsm_b])
    P.dma(SP, WUV[:, :], wb["wuv"][0], s_ws, reads=[wbuf["wuv"]], writes=[wsm_b])

    wuk3 = WUK[:, :].rearrange("p (k c) -> p k c", c=512)
    wuv3 = WUV[:, :].rearrange("p (k c) -> p k c", c=512)
    ST_BANKS = [0, 1, 2]
    O_BANKS = [3, 4]
    bank_pool[:] = [5, 6, 7]
    st_rr = 0
    o_rr = 0
    pt_rr = 0
    qt_rr = 0
    kt_rr = 0
    aot_rr = 0
    SCALE = 96.0 ** -0.5

    rr = {"st": 0, "o": 0, "pt": 0, "qt": 0, "aot": 0}
    pend_q = [None]
    pend_fin = [None]
    LOOK = 2

    def q_gen(h, ti):
        P.dma(SP, ROQ[64:96, :, :], rope_d[64:96, :, ti * T:(ti + 1) * T], s_roq, writes=[ROQb])
        pq, pqb = bank()
        pqs, pqsb = bank()
        P.mm([lambda k=k: nc.tensor.matmul(pq[0:96, :], lhsT=WUQ[:, h, k * 192:k * 192 + 96], rhs=CQ[:, k, ti * T:(ti + 1) * T],
                                           start=(k == 0), stop=(k == 2)) for k in range(3)],
             reads=[wsm_b, CQb[ti]], writes=[pqb])
        P.mm([lambda k=k: nc.tensor.matmul(pqs[0:96, :], lhsT=WUQ[:, h, k * 192 + 96:k * 192 + 192], rhs=CQ[:, k, ti * T:(ti + 1) * T],
                                           start=(k == 0), stop=(k == 2)) for k in range(3)],
             reads=[wsm_b, CQb[ti]], writes=[pqsb])
        P.op(ACT, lambda: nc.scalar.activation(out=SQQ[:, :], in_=pq[0:96, :], func=AF.Square), reads=[pqb], writes=[SQQb])
        p2, p2b = bank()
        P.mm([lambda: nc.tensor.matmul(p2[0:96, :], lhsT=ONESB[0:96, 0:96], rhs=SQQ[:, :], start=True, stop=True)],
             reads=[ones_b, SQQb], writes=[p2b])
        rsqrt(tq, p2[0:96, :], p2b, RQ[:, :], RQb, 1.0 / 96)
        qi = rr["qt"] % 2
        rr["qt"] += 1
        qtile, qtb = QT[qi], QTb[qi]
        P.op(DVE, lambda: nc.vector.scalar_tensor_tensor(out=qtile[0:64, :], in0=pq[0:64, :], scalar=g(G_MQ, 0, 64), in1=RQ[0:64, :],
                                                         op0=ALU.mult, op1=ALU.mult), reads=[pqb, RQb, gains_b], writes=[qtb])
        P.op(DVE, lambda: nc.vector.scalar_tensor_tensor(out=QA[64:96, :], in0=pq[64:96, :], scalar=g(G_MQ, 64, 96), in1=ROQ[64:96, 0, :],
                                                         op0=ALU.mult, op1=ALU.mult), reads=[pqb, ROQb, gains_b], writes=[QAb])
        P.op(DVE, lambda: nc.vector.scalar_tensor_tensor(out=QB[64:96, :], in0=pqs[64:96, :], scalar=g(G_MQS, 64, 96), in1=ROQ[64:96, 1, :],
                                                         op0=ALU.mult, op1=ALU.mult), reads=[pqsb, ROQb, gains_b], writes=[QBb])
        P.op(DVE, lambda: nc.vector.tensor_tensor(out=QA[64:96, :], in0=QA[64:96, :], in1=QB[64:96, :], op=ALU.add),
             reads=[QAb, QBb], writes=[QAb])
        P.op(DVE, lambda: nc.vector.tensor_tensor(out=qtile[64:96, :], in0=QA[64:96, :], in1=RQ[64:96, :], op=ALU.mult),
             reads=[QAb, RQb], writes=[qtb])
        return qtile, qtb

    def main_store(h, hh, ti, NCH, ktile, ktb, ki, rk, rkb, qtile, qtb):
        ob = O_BANKS[rr["o"] % 2]
        rr["o"] += 1
        po, pob = psum[ob], psb[ob]
        P.deps(PE, [], [pob])
        pend = []
        for step in range(NCH + LOOK):
            if step < NCH:
                cc = step
                sbk = ST_BANKS[rr["st"] % 3]
                rr["st"] += 1
                pst, pstb = psum[sbk], psb[sbk]
                P.mm([lambda: nc.tensor.matmul(pst[:, :], lhsT=ktile[0:96, cc * 128:(cc + 1) * 128], rhs=qtile[0:96, :], start=True, stop=True)],
                     reads=[ktb, KTrb[ki], qtb], writes=[pstb])
                pi = rr["pt"] % NPT
                rr["pt"] += 1
                P.op(ACT, lambda: nc.scalar.activation(out=PT[pi][:, :], in_=pst[:, :], func=AF.Exp, scale=rk[:, cc:cc + 1]),
                     reads=[pstb, rkb], writes=[PTb[pi]])
                pend.append((cc, pi))
            if step >= LOOK:
                cc, pi = pend.pop(0)
                last = cc == NCH - 1
                if not last:
                    PE.wait(PTb[pi].w)
                    PE.wait(VGb.w)
                    nc.tensor.matmul(po[0:65, :], lhsT=VG[:, cc, hh, :], rhs=PT[pi][:, :], start=(cc == 0), stop=False)
                    PTb[pi].r[id(PE.sem)] = (PE.sem, PE.n + 1)
                else:
                    P.mm([lambda: nc.tensor.matmul(po[0:65, :], lhsT=VG[:, cc, hh, :], rhs=PT[pi][:, :], start=(cc == 0), stop=True)],
                         reads=[PTb[pi], VGb], writes=[pob])
            if step == 22 and pend_fin[0] is not None:
                finish(*pend_fin[0])
                pend_fin[0] = None
        pend_fin[0] = (h, ti, po, pob)

    def finish(h, ti, po, pob):
        P.op(DVE, lambda: nc.vector.reciprocal(out=REC[64:65, :], in_=po[64:65, :]), reads=[pob], writes=[RECb])
        pbc, pbcb = bank()
        P.mm([lambda: nc.tensor.matmul(pbc[0:64, :], lhsT=ONESF[64:65, 0:64], rhs=REC[64:65, :], start=True, stop=True)],
             reads=[ones_b, RECb], writes=[pbcb])
        P.op(ACT, lambda: nc.scalar.activation(out=BC[:, :], in_=pbc[0:64, :], func=AF.Copy), reads=[pbcb], writes=[BCb])
        ai = rr["aot"] % 2
        rr["aot"] += 1
        P.op(DVE, lambda: nc.vector.tensor_tensor(out=AOT[ai][:, :], in0=po[0:64, :], in1=BC[:, :], op=ALU.mult),
             reads=[pob, BCb], writes=[AOTb[ai]])
        P.dma(POOL, AOs[:, h, ti * T:(ti + 1) * T], AOT[ai][:, :], s_aot[ai], reads=[AOTb[ai]], writes=[AOs_buf])

    for ctx in range(2):
        S = 8192 if ctx == 0 else 4096
        R = 4 if ctx == 0 else 2
        NCH = S // 128
        NKT = S // T
        gat = [gatP, gatS][ctx]
        gb = gat_buf[ctx]
        gvs = [gat[hf].ap().rearrange("(r x) t -> r x t", x=LROWS) for hf in range(2)]
        for r in range(R):
            for hf in range(2):
                c0 = r * 2048 + hf * 1024
                P.dma(SP, CKV[:, :, c0:c0 + 1024], gvs[hf][r, 0:256, :].rearrange("(k p) t -> p k t", p=128), s_ckv,
                      reads=[gb], writes=[CKVb])
                for i in range(2):
                    P.dma(SP, KT[i][64:96, c0:c0 + 1024], gvs[hf][r, 256:288, :], s_kt[i], reads=[gb], writes=[KTrb[i], KTb[i]])
        for r in range(R):
            for hf in range(2):
                P.dma(SP, KRR[64:96, hf * 1024:(hf + 1) * 1024], gvs[hf][r, 288:320, :], s_krr, reads=[gb], writes=[KRRb])
            for kt in range(4):
                P.op(ACT, lambda kt=kt: nc.scalar.activation(out=KRR[64:96, kt * T:(kt + 1) * T], in_=KRR[64:96, kt * T:(kt + 1) * T], func=AF.Square),
                     reads=[KRRb], writes=[KRRb])
            ps, pb = bank()
            fns = [lambda cc=cc: nc.tensor.matmul(ps[:, cc:cc + 1], lhsT=KRR[64:96, cc * 128:(cc + 1) * 128], rhs=ONESB[64:96, 0:1], start=True, stop=True)
                   for cc in range(16)]
            P.mm(fns, reads=[KRRb, ones_b], writes=[pb])
            P.op(DVE, lambda: nc.vector.tensor_copy(out=KRSS[:, r * 16:(r + 1) * 16], in_=ps[:, 0:16]), reads=[pb], writes=[KRSSb])

        if KATT <= 1:
            raise _Stop()
        for hg in range(2):
            for cc in range(NCH):
                ps, pb = bank()
                P.mm([lambda k=k: nc.tensor.matmul(ps[:, 0:256], lhsT=CKV[:, k, cc * 128:(cc + 1) * 128], rhs=wuv3[:, k, hg * 256:(hg + 1) * 256],
                                                   start=(k == 0), stop=(k == 1)) for k in range(2)],
                     reads=[CKVb, wsm_b], writes=[pb])
                eng = ACT if cc % 2 == 0 else DVE
                if eng is ACT:
                    P.op(ACT, lambda: nc.scalar.activation(out=VG[:, cc, :, 0:64], in_=ps[:, 0:256].rearrange("p (h d) -> p h d", d=64), func=AF.Copy),
                         reads=[pb], writes=[VGb])
                else:
                    P.op(DVE, lambda: nc.vector.tensor_copy(out=VG[:, cc, :, 0:64], in_=ps[:, 0:256].rearrange("p (h d) -> p h d", d=64)),
                         reads=[pb], writes=[VGb])
            if KATT <= 2:
                raise _Stop()
            for hh in range(4):
                h = hg * 4 + hh
                ki = kt_rr % 2
                kt_rr += 1
                ktile, ktb = KT[ki], KTb[ki]
                pss, pssb = psum[0], psb[0]
                def ss_mm(kt_):
                    sq_, sqb_ = SQK[kt_ % 2], SQKb[kt_ % 2]
                    P.mm([lambda a=a: nc.tensor.matmul(pss[:, kt_ * 4 + a:kt_ * 4 + a + 1], lhsT=sq_[0:64, a * 128:(a + 1) * 128], rhs=ONESB[0:64, 0:1],
                                                       start=True, stop=True) for a in range(4)],
                         reads=[sqb_, ones_b], writes=[pssb])
                prev_kt = None
                for kt in range(NKT):
                    ps, pb = bank()
                    P.mm([lambda k=k: nc.tensor.matmul(ps[0:64, :], lhsT=wuk3[:, k, h * 64:(h + 1) * 64], rhs=CKV[:, k, kt * T:(kt + 1) * T],
                                                       start=(k == 0), stop=(k == 1)) for k in range(2)],
                         reads=[CKVb, wsm_b], writes=[pb])
                    sq, sqb = SQK[kt % 2], SQKb[kt % 2]
                    P.op(ACT, lambda: nc.scalar.activation(out=sq[:, :], in_=ps[0:64, :], func=AF.Square), reads=[pb], writes=[sqb])
                    P.op(DVE, lambda: nc.vector.tensor_scalar(out=ktile[0:64, kt * T:(kt + 1) * T], in0=ps[0:64, :], scalar1=g(G_MK, 0, 64),
                                                              scalar2=None, op0=ALU.mult), reads=[pb, gains_b, sqb], writes=[ktb])
                    if prev_kt is not None:
                        ss_mm(prev_kt)
                    prev_kt = kt
                ss_mm(prev_kt)
                if KK >= 4:
                    P.op(DVE, lambda: nc.vector.tensor_tensor(out=SSK[:, 0:NCH], in0=pss[:, 0:NCH], in1=KRSS[:, 0:NCH], op=ALU.add),
                         reads=[pssb, KRSSb], writes=[SSKb])
                rk, rkb = RK[ki], RKb[ki]
                if KK >= 5:
                    rsqrt(tk, SSK[:, 0:NCH], SSKb, rk[:, 0:NCH], rkb, 1.0 / 96, post=SCALE)
                if KATT <= 3:
                    raise _Stop()
                for qt in range(4):
                    ti = ctx * 4 + qt
                    if pend_q[0] is None:
                        pend_q[0] = q_gen(h, ti)
                    qtile, qtb = pend_q[0]
                    nxt = None
                    if qt < 3:
                        nxt = (h, ti + 1)
                    elif h < 7:
                        nxt = (h + 1, ctx * 4)
                    pend_q[0] = q_gen(*nxt) if nxt is not None else None
                    main_store(h, hh, ti, NCH, ktile, ktb, ki, rk, rkb, qtile, qtb)
                if KATT <= 7:
                    raise _Stop()

    if pend_fin[0] is not None:
        finish(*pend_fin[0])
        pend_fin[0] = None
    P.barrier()
    es2.close()
    esA.close()
    stacks.pop()
    stacks.pop()
    if STOP <= 3:
        raise _Stop()
    bank_pool[:] = list(range(8))

    es3 = ExitStack()
    stacks.append(es3)
    c = alloc_tile_ctx(es3)
    gu_srcs, d_srcs, blk_srcs, wo_srcs = [], [], [], []
    for ti in range(NT):
        gu_srcs += [(wb["w2gu"][j], wbuf["w2gu"]) for j in range(NJ)]
        d_srcs += [(wb["w2d"][f], wbuf["w2d"]) for f in range(8)]
        blk_srcs += [(wb["winb"][m], wbuf["winb"]) for m in range(8)]
        for fc in range(8):
            blk_srcs += [(wb["winb"][8 + gg * 8 + fc], wbuf["winb"]) for gg in range(3)]
        blk_srcs += [(wb["wout"][f], wbuf["wout"]) for f in range(8)]
        wo_srcs += [(wb["wo3"][f], wbuf["wo3"]) for f in range(8)]
    gu3 = Stream(P, es3, "gu3", KC * 256, 3, gu_srcs)
    d3 = Stream(P, es3, "wd3", NJ * 128, 2, d_srcs)
    blk3 = Stream(P, es3, "blk3", KC * 128, 4, blk_srcs)
    wo3s = Stream(P, es3, "wo3", 16 * 128, 2, wo_srcs)
    UW = sb(es3, "UW", [128, 4, T + 2], BF16)
    UWb = Buf()
    s_uw = P.sem("uw")
    AOX = sb(es3, "AOX", [64, 8, T], BF16)
    AOXb = Buf()
    s_aox = P.sem("aox")
    CVT = sb(es3, "CVT", [128, T], F32)
    CVTb = Buf()
    CV = sb(es3, "CV", [128, 4, T], BF16)
    CVb = Buf()
    XQ = sb(es3, "XQ", [128, 4, T], BF16)
    XQb = Buf()
    RAWX = sb(es3, "RAWX", [128, T], F32)
    RAWXb = Buf()
    SQX = sb(es3, "SQX", [128, T], BF16)
    SQXb = Buf()
    RV3 = c.RINV
    RV3b = c.RINVb
    tmp3 = c.tmp
    XA = sb(es3, "XA", [128, 4, T], BF16)
    XAb = Buf()
    PM = [sb(es3, f"PM{i}", [128, T], BF16) for i in range(2)]
    PMb = [Buf(), Buf()]
    RD = sb(es3, "RD", [128, T], F32)
    RDb = Buf()
    TH = [sb(es3, f"TH{i}", [128, T], F32) for i in range(3)]
    THb = [Buf() for _ in range(3)]
    M0 = sb(es3, "M0", [128, T], F32)
    M1 = sb(es3, "M1", [128, T], F32)
    M0b, M1b = Buf(), Buf()
    MG = sb(es3, "MG", [128, KC, T], BF16)
    MGb = [Buf() for _ in range(KC)]
    s_y = [P.sem("y0"), P.sem("y1")]
    XSC = 128.0 ** -0.5

    for ti in range(NT):
        ctx, tl = ti // 4, ti % 4
        xs = ti % 2
        X, Xb = c.X[xs], c.Xb[xs]
        def load_x3(ti_):
            xs_ = ti_ % 2
            P.dma(SP, c.X[xs_][:, :, :], x1s[ti_].rearrange("p (k t) -> p k t", t=T), c.Xsem[xs_], reads=[x1s_buf[ti_]], writes=c.Xb[xs_])
        if ti == 0:
            load_x3(0)
        if ti + 1 < NT:
            load_x3(ti + 1)
        lo = max(tl * T - 1, 0)
        hi = min(tl * T + T + 1, 2048)
        d0 = lo - (tl * T - 1)
        P.dma(SP, UW[:, :, d0:d0 + (hi - lo)], Us[:, :, ctx, lo:hi], s_uw, reads=[Us_buf], writes=[UWb])
        if tl == 0:
            P.op(DVE, lambda: nc.vector.tensor_copy(out=UW[:, :, 0:1], in_=UH[:, :, 2 * ctx:2 * ctx + 1]), reads=[UHb], writes=[UWb])
        if tl == 3:
            P.op(DVE, lambda: nc.vector.tensor_copy(out=UW[:, :, T + 1:T + 2], in_=UH[:, :, 2 * ctx + 1:2 * ctx + 2]), reads=[UHb], writes=[UWb])
        P.dma(SP, AOX[:, :, :], AOs[:, :, ti * T:(ti + 1) * T], s_aox, reads=[AOs_buf], writes=[AOXb])
        norm_to_xn(c, xs, T, G_MIX)
        for i in range(4):
            ps, pb = proj(c, blk3, T)
            P.op(DVE, lambda: nc.vector.tensor_scalar(out=CVT[:, :], in0=UW[:, i, 0:T], scalar1=g(G_CONV + i * 3 + 0), scalar2=None, op0=ALU.mult),
                 reads=[UWb, gains_b], writes=[CVTb])
            P.op(DVE, lambda: nc.vector.scalar_tensor_tensor(out=CVT[:, :], in0=UW[:, i, 1:T + 1], scalar=g(G_CONV + i * 3 + 1), in1=CVT[:, :],
                                                             op0=ALU.mult, op1=ALU.add), reads=[UWb, CVTb, gains_b], writes=[CVTb])
            P.op(DVE, lambda: nc.vector.scalar_tensor_tensor(out=CVT[:, :], in0=UW[:, i, 2:T + 2], scalar=g(G_CONV + i * 3 + 2), in1=CVT[:, :],
                                                             op0=ALU.mult, op1=ALU.add), reads=[UWb, CVTb, gains_b], writes=[CVTb])
            P.op(DVE, lambda: nc.vector.tensor_tensor(out=CV[:, i, :], in0=CVT[:, :], in1=ps[:, :], op=ALU.mult),
                 reads=[CVTb, pb], writes=[CVb])
        for h in range(4):
            ps, pb = proj(c, blk3, T)
            P.op(ACT, lambda: nc.scalar.activation(out=SQX[:, :], in_=ps[:, :], func=AF.Square), reads=[pb], writes=[SQXb])
            P.op(ACT, lambda: nc.scalar.activation(out=RAWX[:, :], in_=ps[:, :], func=AF.Copy), reads=[pb], writes=[RAWXb])
            p2, p2b = bank()
            P.mm([lambda: nc.tensor.matmul(p2[:, :], lhsT=ONESB[:, :], rhs=SQX[:, :], start=True, stop=True)], reads=[ones_b, SQXb], writes=[p2b])
            rsqrt(tmp3, p2[:, :], p2b, RV3[:, :], RV3b, 1.0 / 128)
            P.op(DVE, lambda: nc.vector.scalar_tensor_tensor(out=XQ[:, h, :], in0=RAWX[:, :], scalar=g(G_XQ), in1=RV3[:, :],
                                                             op0=ALU.mult, op1=ALU.mult), reads=[RAWXb, RV3b, gains_b], writes=[XQb])
        for h in range(4):
            for mc in range(2):
                ps, pb = bank()
                P.mm([lambda: nc.tensor.matmul(ps[:, :], lhsT=MK[:, ctx, h, mc * 128:(mc + 1) * 128], rhs=XQ[:, h, :], start=True, stop=True)],
                     reads=[MK_b, XQb], writes=[pb])
                P.op(ACT, lambda: nc.scalar.activation(out=PM[mc][:, :], in_=ps[:, :], func=AF.Exp, scale=XSC), reads=[pb], writes=[PMb[mc]])
            po, pob = bank()
            pdn, pdnb = bank()
            P.mm([lambda mc=mc: nc.tensor.matmul(po[:, :], lhsT=MV[:, ctx, mc, h * 128:(h + 1) * 128], rhs=PM[mc][:, :], start=(mc == 0), stop=(mc == 1))
                  for mc in range(2)], reads=[MV_b] + PMb, writes=[pob])
            P.mm([lambda mc=mc: nc.tensor.matmul(pdn[:, :], lhsT=ONESB[:, :], rhs=PM[mc][:, :], start=(mc == 0), stop=(mc == 1))
                  for mc in range(2)], reads=[ones_b] + PMb, writes=[pdnb])
            P.op(DVE, lambda: nc.vector.reciprocal(out=RD[:, :], in_=pdn[:, :]), reads=[pdnb], writes=[RDb])
            P.op(DVE, lambda: nc.vector.tensor_tensor(out=XA[:, h, :], in0=po[:, :], in1=RD[:, :], op=ALU.mult), reads=[pob, RDb], writes=[XAb])
        for fc in range(KC):
            wt, wtb = wo3s.next()
            w3 = wt[:, :].rearrange("p (k c) -> p k c", c=128)
            ys = []
            py, pyb = bank()
            P.mm([lambda h=h: nc.tensor.matmul(py[:, :], lhsT=w3[0:64, h, :], rhs=AOX[:, h, :], start=(h == 0), stop=(h == 7)) for h in range(8)],
                 reads=[wtb, AOXb], writes=[pyb])
            ys.append((py, pyb))
            py, pyb = bank()
            P.mm([lambda i=i: nc.tensor.matmul(py[:, :], lhsT=w3[:, 8 + i, :], rhs=CV[:, i, :], start=(i == 0), stop=(i == 3)) for i in range(4)],
                 reads=[wtb, CVb], writes=[pyb])
            ys.append((py, pyb))
            py, pyb = bank()
            P.mm([lambda i=i: nc.tensor.matmul(py[:, :], lhsT=w3[:, 12 + i, :], rhs=XA[:, i, :], start=(i == 0), stop=(i == 3)) for i in range(4)],
                 reads=[wtb, XAb], writes=[pyb])
            ys.append((py, pyb))
            for gg in range(3):
                pg, pgb = proj(c, blk3, T)
                P.op(ACT, lambda: nc.scalar.activation(out=TH[gg][:, :], in_=pg[:, :], func=AF.Tanh, scale=0.5), reads=[pgb], writes=[THb[gg]])
            P.op(DVE, lambda: nc.vector.scalar_tensor_tensor(out=M0[:, :], in0=TH[0][:, :], scalar=1.0, in1=ys[0][0][:, :], op0=ALU.add, op1=ALU.mult),
                 reads=[THb[0], ys[0][1]], writes=[M0b])
            P.op(DVE, lambda: nc.vector.scalar_tensor_tensor(out=M1[:, :], in0=TH[1][:, :], scalar=1.0, in1=ys[1][0][:, :], op0=ALU.add, op1=ALU.mult),
                 reads=[THb[1], ys[1][1]], writes=[M1b])
            P.op(DVE, lambda: nc.vector.tensor_tensor(out=M0[:, :], in0=M0[:, :], in1=M1[:, :], op=ALU.add), reads=[M0b, M1b], writes=[M0b])
            P.op(DVE, lambda: nc.vector.scalar_tensor_tensor(out=M1[:, :], in0=TH[2][:, :], scalar=1.0, in1=ys[2][0][:, :], op0=ALU.add, op1=ALU.mult),
                 reads=[THb[2], ys[2][1]], writes=[M1b])
            P.op(DVE, lambda: nc.vector.tensor_tensor(out=MG[:, fc, :], in0=M0[:, :], in1=M1[:, :], op=ALU.add), reads=[M0b, M1b], writes=[MGb[fc]])
        for fo in range(KC):
            wt, wtb = blk3.next()
            w3 = wt[:, :].rearrange("p (k c) -> p k c", c=128)
            ps, pb = bank()
            P.mm([lambda k=k: nc.tensor.matmul(ps[:, :], lhsT=w3[:, k, :], rhs=MG[:, k, :], start=(k == 0), stop=(k == KC - 1)) for k in range(KC)],
                 reads=[wtb] + MGb, writes=[pb])
            P.op(DVE, lambda: nc.vector.scalar_tensor_tensor(out=X[:, fo, :], in0=ps[:, :], scalar=0.5, in1=X[:, fo, :], op0=ALU.mult, op1=ALU.add),
                 reads=[pb, Xb[fo]], writes=[Xb[fo]])
        ffn(c, xs, T, G_FFN2, gu3, d3)
        P.dma(POOL, yT[ti].rearrange("p (k t) -> p k t", t=T), X[:, :, :], c.XsemS[xs], reads=Xb, writes=[])

    P.barrier()
    es3.close()
    es.close()


def _kblocks(W, cols):
    K = W.shape[0]
    kc = K // 128
    out = np.empty((len(cols), 128, kc, 128), np.float32)
    Wr = W.reshape(kc, 128, W.shape[1])
    for m, c0 in enumerate(cols):
        out[m] = Wr[:, :, c0:c0 + 128].transpose(1, 0, 2)
    return out.reshape(len(cols), 128, kc * 128)


def _prep_weights(inp):
    f = lambda a: np.ascontiguousarray(np.asarray(a, np.float32))
    out = {}
    for tag, gu, dn in (("w1", "ffn1_w_gu", "ffn1_w_down"), ("w2", "ffn2_w_gu", "ffn2_w_down")):
        W = f(inp[gu][0]).reshape(KC, 128, 2 * FF)
        gate = W[:, :, :FF].reshape(KC, 128, NJ, 128)
        up = W[:, :, FF:].reshape(KC, 128, NJ, 128)
        st = np.stack([gate, up], axis=3)
        out[tag + "gu"] = np.ascontiguousarray(st.transpose(2, 1, 0, 3, 4)).reshape(NJ, 128, KC * 256)
        Wd = f(inp[dn][0]).reshape(NJ, 128, 8, 128)
        out[tag + "d"] = np.ascontiguousarray(Wd.transpose(2, 1, 0, 3)).reshape(8, 128, NJ * 128)
    Win = f(inp["w_in"][0])
    perm = np.concatenate([np.arange(16, 32), np.arange(0, 16)])
    kr = Win[:, 640:672]
    Wkr = np.concatenate([kr, kr[:, perm], np.zeros((D, 64), np.float32)], axis=1)
    Wa = np.concatenate([Win[:, 0:640], Wkr, Win[:, 1184:2208]], axis=1)
    out["wina"] = _kblocks(Wa, [i * 128 for i in range(14)])
    Wb = np.concatenate([Win[:, 672:1184], Win[:, 2208:2720], Win[:, 2720:5792]], axis=1)
    out["winb"] = _kblocks(Wb, [i * 128 for i in range(32)])
    Wuq = f(inp["w_uq"][0])
    wuq = np.empty((8, 128, 3, 192), np.float32)
    Wr = Wuq.reshape(3, 128, 768)
    for h in range(8):
        blk = Wr[:, :, h * 96:(h + 1) * 96]
        sw = np.concatenate([blk[:, :, :64], blk[:, :, 64:][:, :, perm]], axis=2)
        wuq[h] = np.concatenate([blk, sw], axis=2).transpose(1, 0, 2)
    out["wuq"] = wuq.reshape(8, 128, 3 * 192)
    out["wuk"] = np.ascontiguousarray(f(inp["w_uk"][0]).reshape(2, 128, 512).transpose(1, 0, 2)).reshape(1, 128, 1024)
    out["wuv"] = np.ascontiguousarray(f(inp["w_uv"][0]).reshape(2, 128, 512).transpose(1, 0, 2)).reshape(1, 128, 1024)
    wo3 = np.zeros((8, 128, 16, 128), np.float32)
    Wm = f(inp["w_o_mla"][0]).reshape(8, 64, 8, 128)
    Wc = f(inp["w_o_conv"][0]).reshape(4, 128, 8, 128)
    Wx = f(inp["w_o_mem"][0]).reshape(4, 128, 8, 128)
    wo3[:, 0:64, 0:8, :] = Wm.transpose(2, 1, 0, 3)
    wo3[:, :, 8:12, :] = Wc.transpose(2, 1, 0, 3)
    wo3[:, :, 12:16, :] = Wx.transpose(2, 1, 0, 3)
    out["wo3"] = wo3.reshape(8, 128, 16 * 128)
    out["wout"] = _kblocks(f(inp["w_out"][0]), [i * 128 for i in range(8)])
    Wmkv = f(inp["w_mem_kv"][0])
    out["wmk"] = _kblocks(Wmkv, [i * 128 for i in range(4)])
    out["wmv"] = np.ascontiguousarray(Wmkv[:, 512:].reshape(8, 128, 512).transpose(1, 0, 2)).reshape(1, 128, 8 * 512)
    G = np.zeros((128, NG), np.float32)
    def colk(v, c0):
        v = f(v).reshape(-1, 128)
        for k in range(v.shape[0]):
            G[:, c0 + k] = v[k]
    colk(inp["ffn1_norm"][0], G_FFN1)
    colk(inp["mix_norm"][0], G_MIX)
    colk(inp["ffn2_norm"][0], G_FFN2)
    colk(inp["mem_norm"][0], G_MEM)
    colk(inp["q_lora_norm"][0], G_QL)
    colk(inp["kv_lora_norm"][0], G_KVL)
    mq = f(inp["mla_q_norm"][0])
    mk = f(inp["mla_k_norm"][0])
    G[0:96, G_MQ] = mq
    G[0:64, G_MQS] = mq[:64]
    G[64:96, G_MQS] = mq[64:][perm]
    G[0:64, G_MK] = mk[:64]
    G[0:32, G_KR] = mk[64:]
    G[0:32, G_KRS] = mk[64:][perm]
    G[:, G_XQ] = f(inp["xa_q_norm"][0])
    G[:, G_XK] = f(inp["xa_k_norm"][0])
    cw = f(inp["conv_w"][0])
    for i in range(4):
        for tap in range(3):
            G[:, G_CONV + i * 3 + tap] = cw[tap, i * 128:(i + 1) * 128]
    out["gains"] = G
    return out


def _rope_table(pos):
    half = 16
    inv_freq = (10000.0 ** (-np.arange(half, dtype=np.float32) / half)).astype(np.float32)
    ang = pos.astype(np.float32)[None, :] * inv_freq[:, None]
    cos = np.cos(ang).astype(np.float32)
    sin = np.sin(ang).astype(np.float32)
    c32 = np.concatenate([cos, cos], 0)
    s32 = np.concatenate([-sin, sin], 0)
    tab = np.stack([c32, s32], axis=1)
    return np.ascontiguousarray(np.tile(tab, (4, 1, 1)))


_NC_CACHE = {}


def kernel(**inputs):
    xp = np.asarray(inputs["x_prompt"], np.float32)
    xsm = np.asarray(inputs["x_sample"], np.float32)
    mp = np.asarray(inputs["mem_prompt"], np.float32)
    ms = np.asarray(inputs["mem_sample"], np.float32)
    W = _prep_weights(inputs)
    if "nc" not in _NC_CACHE:
        _NC_CACHE["nc"] = build()
    nc = _NC_CACHE["nc"]
    in_maps = []
    for cidx in range(8):
        ps_, pq_ = cidx // 4, cidx % 4
        ss_, sh_ = cidx // 2, cidx % 2
        xpc = xp[ps_, pq_ * 2048:(pq_ + 1) * 2048]
        xsc = xsm[ss_, sh_ * 2048:(sh_ + 1) * 2048]
        xc = np.concatenate([xpc, xsc], 0)
        xt = xc.reshape(NT, T, KC, 128).transpose(0, 3, 2, 1)
        halo = np.zeros((4, D), np.float32)
        if pq_ > 0:
            halo[0] = xp[ps_, pq_ * 2048 - 1]
        if pq_ < 3:
            halo[1] = xp[ps_, (pq_ + 1) * 2048]
        if sh_ > 0:
            halo[2] = xsm[ss_, sh_ * 2048 - 1]
        if sh_ < 1:
            halo[3] = xsm[ss_, (sh_ + 1) * 2048]
        hl = halo.reshape(4, KC, 128).transpose(2, 1, 0)
        memc = np.stack([mp[ps_], ms[ss_]], 0)
        memt = memc.reshape(2, 256, KC, 128).transpose(0, 3, 2, 1)
        pos = np.concatenate([np.arange(pq_ * 2048, (pq_ + 1) * 2048), np.arange(sh_ * 2048, (sh_ + 1) * 2048)])
        m = {
            "xT": np.ascontiguousarray(xt).reshape(NT, 128, KC * T),
            "xh": np.ascontiguousarray(hl).reshape(128, KC * 4),
            "memT": np.ascontiguousarray(memt).reshape(2, 128, KC * 256),
            "ropeT": _rope_table(pos),
        }
        m.update(W)
        in_maps.append(m)
    res = run_bass_kernel_spmd(nc, in_maps, core_ids=list(range(8)))
    yp = np.empty_like(xp)
    ysm = np.empty_like(xsm)
    for cidx in range(8):
        ps_, pq_ = cidx // 4, cidx % 4
        ss_, sh_ = cidx // 2, cidx % 2
        y = np.asarray(res.results[cidx]["yT"]).reshape(NT, 128, KC, T).transpose(0, 3, 2, 1).reshape(NT * T, D)
        yp[ps_, pq_ * 2048:(pq_ + 1) * 2048] = y[:2048]
        ysm[ss_, sh_ * 2048:(sh_ + 1) * 2048] = y[2048:]
    return (yp, ysm)
```

```python
import numpy as np
from contextlib import ExitStack
import concourse.bass as bass
import concourse.mybir as mybir
from concourse.bass_utils import run_bass_kernel_spmd

F32 = mybir.dt.float32
BF16 = mybir.dt.bfloat16
I32 = mybir.dt.int32
AF = mybir.ActivationFunctionType
ALU = mybir.AluOpType

D = 1024
KC = 8
T = 512
NT = 8
FF = 2816
NJ = 22
EPS = 1e-6
SAME_ENG_SYNC = True
MAGIC = 1597463007.0

G_FFN1, G_MIX, G_FFN2, G_MEM, G_QL, G_KVL = 0, 8, 16, 24, 32, 35
G_MQ, G_MQS, G_MK, G_KR, G_KRS, G_XQ, G_XK, G_CONV = 37, 38, 39, 40, 41, 42, 43, 44
NG = 56


class Buf:
    __slots__ = ("w", "r", "name")

    def __init__(self, name=""):
        self.w = None
        self.r = {}
        self.name = name


class Eng:
    def __init__(self, e, sem, name, is_pe=False):
        self.e = e
        self.sem = sem
        self.n = 0
        self.seen = {}
        self.name = name
        self.is_pe = is_pe

    def wait(self, tok):
        if tok is None:
            return
        sem, val = tok
        if sem is self.sem and (self.is_pe or not SAME_ENG_SYNC):
            return
        k = id(sem)
        if self.seen.get(k, 0) >= val:
            return
        self.e.wait_ge(sem, val)
        self.seen[k] = val


class Prog:
    def __init__(self):
        self.nc = bass.Bass("TRN2", target_bir_lowering=False)
        self.es = ExitStack()
        nc = self.nc
        self.semcount = {}
        self.sems = []
        self.PE = Eng(nc.tensor, self.sem("pe"), "pe", is_pe=True)
        self.ACT = Eng(nc.scalar, self.sem("act"), "act")
        self.DVE = Eng(nc.vector, self.sem("dve"), "dve")
        self.POOL = Eng(nc.gpsimd, self.sem("pool"), "pool")
        self.SP = Eng(nc.sync, self.sem("sp"), "sp")
        self.engs = [self.PE, self.ACT, self.DVE, self.POOL, self.SP]
        self.bank_rr = 0

    def sem(self, name):
        s = self.es.enter_context(self.nc.semaphore(f"{name}_n{len(self.sems)}"))
        self.semcount[id(s)] = 0
        self.sems.append(s)
        return s

    def deps(self, E, reads, writes):
        for b in reads:
            E.wait(b.w)
        for b in writes:
            E.wait(b.w)
            for t in list(b.r.values()):
                E.wait(t)

    def done(self, tok, reads, writes):
        for b in reads:
            b.r[id(tok[0])] = tok
        for b in writes:
            b.w = tok
            b.r = {}

    def op(self, E, fn, reads=(), writes=()):
        self.deps(E, reads, writes)
        ins = fn()
        E.n += 1
        ins.then_inc(E.sem, 1)
        self.semcount[id(E.sem)] = E.n
        self.done((E.sem, E.n), reads, writes)

    def mm(self, fns, reads, writes):
        E = self.PE
        self.deps(E, reads, writes)
        ins = None
        for f in fns:
            ins = f()
        E.n += 1
        ins.then_inc(E.sem, 1)
        self.semcount[id(E.sem)] = E.n
        self.done((E.sem, E.n), reads, writes)

    def mmf(self, items, reads, writes):
        E = self.PE
        self.deps(E, reads, writes)
        allr = list(reads)
        ins = None
        for f, rb in items:
            for b in rb:
                E.wait(b.w)
            allr += rb
            ins = f()
        E.n += 1
        ins.then_inc(E.sem, 1)
        self.semcount[id(E.sem)] = E.n
        self.done((E.sem, E.n), allr, writes)

    def dma(self, Q, out, in_, sem, reads=(), writes=(), **kw):
        self.deps(Q, reads, writes)
        Q.e.dma_start(out=out, in_=in_, **kw).then_inc(sem, 16)
        self.semcount[id(sem)] += 16
        self.done((sem, self.semcount[id(sem)]), reads, writes)

    def barrier(self):
        for E in self.engs:
            for s in self.sems:
                c = self.semcount[id(s)]
                if c > 0:
                    E.wait((s, c)) if s is not E.sem else None


class Stream:
    def __init__(self, P, es, name, width, nslots, srcs):
        self.P = P
        self.srcs = srcs
        self.n = nslots
        self.slots = [es.enter_context(P.nc.sbuf_tensor(f"{name}_s{i}_{len(P.sems)}", [128, width], BF16)) for i in range(nslots)]
        self.bufs = [Buf(f"{name}{i}") for i in range(nslots)]
        self.sems = [P.sem(f"{name}_q{i}") for i in range(nslots)]
        self.issued = 0
        self.pos = 0

    def _issue(self):
        i = self.issued
        s = i % self.n
        ap, db = self.srcs[i]
        self.P.dma(self.P.SP, self.slots[s][:, :], ap, self.sems[s], reads=[db], writes=[self.bufs[s]])
        self.issued += 1

    def next(self):
        i = self.pos
        while self.issued < min(i + self.n, len(self.srcs)):
            self._issue()
        self.pos += 1
        s = i % self.n
        return self.slots[s], self.bufs[s]


import os as _os
STOP = float(_os.environ.get("KSTOP", "9"))
KSUB = int(_os.environ.get("KSUB", "99"))
KATT = int(_os.environ.get("KATT", "99"))
KK = int(_os.environ.get("KK", "99"))


class _Stop(Exception):
    pass


def build():
    stacks = []
    P = Prog()
    try:
        _build(P, stacks)
    except _Stop:
        P.barrier()
        for st in reversed(stacks):
            st.close()
        P.es.close()
    return P.nc


def _build(P, stacks):
    nc = P.nc
    es = P.es
    PE, ACT, DVE, POOL, SP = P.PE, P.ACT, P.DVE, P.POOL, P.SP

    def din(name, shape, dt=F32):
        return nc.dram_tensor(name, shape, dt, kind="ExternalInput")

    xT = din("xT", [NT, 128, KC * T])
    xh = din("xh", [128, KC * 4])
    memT = din("memT", [2, 128, KC * 256])
    gains_d = din("gains", [128, NG])
    rope_d = din("ropeT", [128, 2, NT * T])
    wsh = {
        "w1gu": [NJ, 128, KC * 256], "w1d": [8, 128, NJ * 128],
        "w2gu": [NJ, 128, KC * 256], "w2d": [8, 128, NJ * 128],
        "wina": [14, 128, KC * 128], "winb": [32, 128, KC * 128],
        "wuq": [8, 128, 3 * 192], "wuk": [1, 128, 2 * 512], "wuv": [1, 128, 2 * 512],
        "wo3": [8, 128, 16 * 128], "wout": [8, 128, KC * 128],
        "wmk": [4, 128, KC * 128], "wmv": [1, 128, KC * 512],
    }
    wf = {k: din(k, v) for k, v in wsh.items()}
    wb = {k: nc.dram_tensor(k + "_b", v, BF16) for k, v in wsh.items()}
    wbuf = {k: Buf(k) for k in wsh}
    yT = nc.dram_tensor("yT", [NT, 128, KC * T], F32, kind="ExternalOutput")

    x1s = nc.dram_tensor("x1s", [NT, 128, KC * T], F32)
    x1s_buf = [Buf(f"x1s{i}") for i in range(NT)]
    Us = nc.dram_tensor("Us", [128, 4, 2, 2048], BF16)
    Us_buf = Buf("Us")
    AOs = nc.dram_tensor("AOs", [64, 8, NT * T], BF16)
    AOs_buf = Buf("AOs")
    LROWS = 320
    latP = [nc.dram_tensor(f"latP{i}", [LROWS, 1024], BF16) for i in range(2)]
    latS = [nc.dram_tensor(f"latS{i}", [LROWS, 1024], BF16) for i in range(2)]
    gatP = [nc.dram_tensor(f"gatP{i}", [4 * LROWS, 1024], BF16) for i in range(2)]
    gatS = [nc.dram_tensor(f"gatS{i}", [2 * LROWS, 1024], BF16) for i in range(2)]
    lat_buf = [Buf("latP"), Buf("latS")]
    gat_buf = [Buf("gatP"), Buf("gatS")]

    uniq = [0]

    def sb(stack, name, shape, dt):
        uniq[0] += 1
        return stack.enter_context(nc.sbuf_tensor(f"{name}_u{uniq[0]}", shape, dt))

    cast_order = ["w1gu", "w1d", "wina", "wmk", "wmv", "wuq", "wuk", "wuv", "winb", "wo3", "wout", "w2gu", "w2d"]
    for k in cast_order:
        s = P.sem("c_" + k)
        n0, _, wd = wsh[k]
        bb = max(d_ for d_ in range(1, 1025) if wd % d_ == 0)
        src = wf[k].ap().rearrange("n p (a b) -> (n p a) b", b=bb)
        dst = wb[k].ap().rearrange("n p (a b) -> (n p a) b", b=bb)
        rows = src.shape[0]
        step = 4096
        for r0 in range(0, rows, step):
            r1 = min(rows, r0 + step)
            P.dma(POOL, dst[r0:r1, :], src[r0:r1, :], s, writes=[wbuf[k]] if r0 + step >= rows else [], max_dma_last_dim=4096)
            if r0 + step < rows:
                pass

    if STOP <= 0.1:
        raise _Stop()
    gains = sb(es, "gains_sb", [128, NG], F32)
    gains_b = Buf("gains")
    ONESB = sb(es, "onesb", [128, 128], BF16)
    ONESF = sb(es, "onesf", [128, 64], F32)
    ones_b = Buf("ones")
    s_misc = P.sem("misc")
    P.dma(SP, gains[:, :], gains_d.ap(), s_misc, writes=[gains_b])
    P.op(DVE, lambda: nc.vector.memset(ONESB[:, :], 1.0), writes=[ones_b])
    P.op(DVE, lambda: nc.vector.memset(ONESF[:, :], 1.0), writes=[ones_b])
    MK = sb(es, "MK", [128, 2, 4, 256], BF16)
    MV = sb(es, "MV", [128, 2, 2, 512], BF16)
    MK_b, MV_b = Buf("MK"), Buf("MV")
    UH = sb(es, "UH", [128, 4, 4], BF16)
    UHb = Buf()

    psum = [es.enter_context(nc.psum_tensor(f"ps{i}", [128, 512], F32)) for i in range(8)]
    psb = [Buf(f"ps{i}") for i in range(8)]
    bank_pool = list(range(8))

    def bank():
        i = bank_pool[P.bank_rr % len(bank_pool)]
        P.bank_rr += 1
        return psum[i], psb[i]

    def g(col, p0=0, p1=128):
        return gains[p0:p1, col:col + 1]

    def rsqrt(tmp, ps_ap, ps_b, out_ap, out_b, inv_n, post=None):
        V, Y, TT = tmp["V"], tmp["Y"], tmp["T"]
        vb, yb, tb = tmp["Vb"], tmp["Yb"], tmp["Tb"]
        shp = ps_ap.shape
        np_, n = shp[0], shp[1]
        p0 = tmp.get("p0", 0)
        v = V[p0:p0 + np_, 0:n]
        y = Y[p0:p0 + np_, 0:n]
        t = TT[p0:p0 + np_, 0:n]
        P.op(DVE, lambda: nc.vector.tensor_scalar(out=v, in0=ps_ap, scalar1=inv_n, scalar2=EPS, op0=ALU.mult, op1=ALU.add),
             reads=[ps_b], writes=[vb])
        P.op(DVE, lambda: nc.vector.tensor_scalar(out=y.bitcast(I32), in0=v.bitcast(I32), scalar1=-0.5, scalar2=MAGIC,
                                                  op0=ALU.mult, op1=ALU.add), reads=[vb], writes=[yb])
        for it in range(2):
            P.op(DVE, lambda: nc.vector.tensor_tensor(out=t, in0=y, in1=y, op=ALU.mult), reads=[yb], writes=[tb])
            P.op(DVE, lambda: nc.vector.scalar_tensor_tensor(out=t, in0=t, scalar=-0.5, in1=v, op0=ALU.mult, op1=ALU.mult),
                 reads=[tb, vb], writes=[tb])
            last = it == 1
            o = out_ap if last else y
            ob = out_b if last else yb
            if last and post is not None:
                P.op(DVE, lambda: nc.vector.scalar_tensor_tensor(out=y, in0=t, scalar=1.5, in1=y, op0=ALU.add, op1=ALU.mult),
                     reads=[tb, yb], writes=[yb])
                P.op(DVE, lambda: nc.vector.tensor_scalar(out=o, in0=y, scalar1=post, scalar2=None, op0=ALU.mult),
                     reads=[yb], writes=[ob])
            else:
                P.op(DVE, lambda: nc.vector.scalar_tensor_tensor(out=o, in0=t, scalar=1.5, in1=y, op0=ALU.add, op1=ALU.mult),
                     reads=[tb, yb], writes=[ob])

    def mk_tmp(stack, tag, n=T):
        return {"V": sb(stack, "tV" + tag, [128, n], F32), "Y": sb(stack, "tY" + tag, [128, n], F32),
                "T": sb(stack, "tT" + tag, [128, n], F32), "Vb": Buf(), "Yb": Buf(), "Tb": Buf()}

    class TileCtx:
        pass

    def alloc_tile_ctx(stack):
        c = TileCtx()
        c.X = [sb(stack, f"X{i}", [128, KC, T], F32) for i in range(2)]
        c.Xb = [[Buf(f"X{i}_{k}") for k in range(KC)] for i in range(2)]
        c.Xsem = [P.sem(f"X{i}") for i in range(2)]
        c.XsemS = [P.sem(f"XS{i}") for i in range(2)]
        c.XNs = [sb(stack, f"XN{i}", [128, KC, T], BF16) for i in range(2)]
        c.XNbs = [[Buf(f"XN{i}_{k}") for k in range(KC)] for i in range(2)]
        c.XN = c.XNs[0]
        c.XNb = c.XNbs[0]
        c.SQ = sb(stack, "SQn", [128, KC, T], BF16)
        c.SQb = [Buf(f"SQ{k}") for k in range(KC)]
        c.H = sb(stack, "H", [128, NJ, T], BF16)
        c.Hb = [Buf(f"H{j}") for j in range(NJ)]
        c.SG = [sb(stack, f"SG{i}", [128, T], F32) for i in range(2)]
        c.SGb = [Buf(), Buf()]
        c.RINV = sb(stack, "RINV", [128, T], F32)
        c.RINVb = Buf("rinv")
        c.tmp = mk_tmp(stack, "a")
        return c

    def norm_to_xn(c, xs, n, gcol, xi=0):
        X, Xb = c.X[xs], c.Xb[xs]
        XN, XNb = c.XNs[xi], c.XNbs[xi]
        for k0 in range(0, KC, 4):
            P.op(ACT, lambda k0=k0: nc.scalar.activation(out=c.SQ[:, k0:k0 + 4, 0:n], in_=X[:, k0:k0 + 4, 0:n], func=AF.Square),
                 reads=Xb[k0:k0 + 4], writes=c.SQb[k0:k0 + 4])
        ps, pb = bank()
        P.mmf([(lambda k=k: nc.tensor.matmul(ps[:, 0:n], lhsT=ONESB[:, :], rhs=c.SQ[:, k, 0:n], start=(k == 0), stop=(k == KC - 1)), [c.SQb[k]])
               for k in range(KC)], reads=[ones_b], writes=[pb])
        rsqrt(c.tmp, ps[:, 0:n], pb, c.RINV[:, 0:n], c.RINVb, 1.0 / D)
        for k in range(KC):
            P.op(DVE, lambda k=k: nc.vector.scalar_tensor_tensor(out=XN[:, k, 0:n], in0=X[:, k, 0:n], scalar=g(gcol + k),
                                                                 in1=c.RINV[:, 0:n], op0=ALU.mult, op1=ALU.mult),
                 reads=[Xb[k], c.RINVb, gains_b], writes=[XNb[k]])

    def ffn(c, xs, n, gcol, gu_stream, d_stream, xi=0, do_norm=True, mid_hook=None):
        X, Xb = c.X[xs], c.Xb[xs]
        XN, XNb = c.XNs[xi], c.XNbs[xi]
        if do_norm:
            norm_to_xn(c, xs, n, gcol, xi)
        for j in range(NJ):
            wt, wtb = gu_stream.next()
            w3 = wt[:, :].rearrange("p (k c) -> p k c", c=256)
            pg, pgb = bank()
            pu, pub = bank()
            P.mmf([(lambda k=k: nc.tensor.matmul(pg[:, 0:n], lhsT=w3[:, k, 0:128], rhs=XN[:, k, 0:n], start=(k == 0), stop=(k == KC - 1)), [XNb[k]])
                   for k in range(KC)], reads=[wtb], writes=[pgb])
            P.mmf([(lambda k=k: nc.tensor.matmul(pu[:, 0:n], lhsT=w3[:, k, 128:256], rhs=XN[:, k, 0:n], start=(k == 0), stop=(k == KC - 1)), [XNb[k]])
                   for k in range(KC)], reads=[wtb], writes=[pub])
            sg, sgb = c.SG[j % 2], c.SGb[j % 2]
            P.op(ACT, lambda: nc.scalar.activation(out=sg[:, 0:n], in_=pg[:, 0:n], func=AF.Silu), reads=[pgb], writes=[sgb])
            P.op(DVE, lambda: nc.vector.tensor_tensor(out=c.H[:, j, 0:n], in0=sg[:, 0:n], in1=pu[:, 0:n], op=ALU.mult),
                 reads=[sgb, pub], writes=[c.Hb[j]])
        if mid_hook is not None:
            mid_hook()
        for fc in range(KC):
            wt, wtb = d_stream.next()
            w3 = wt[:, :].rearrange("p (j c) -> p j c", c=128)
            pd, pdb = bank()
            P.mmf([(lambda j=j: nc.tensor.matmul(pd[:, 0:n], lhsT=w3[:, j, :], rhs=c.H[:, j, 0:n], start=(j == 0), stop=(j == NJ - 1)), [c.Hb[j]])
                   for j in range(NJ)], reads=[wtb], writes=[pdb])
            P.op(DVE, lambda: nc.vector.scalar_tensor_tensor(out=X[:, fc, 0:n], in0=pd[:, 0:n], scalar=0.5, in1=X[:, fc, 0:n],
                                                             op0=ALU.mult, op1=ALU.add), reads=[pdb, Xb[fc]], writes=[Xb[fc]])

    def proj(c, blk_stream, n, xi=0):
        wt, wtb = blk_stream.next()
        w3 = wt[:, :].rearrange("p (k c) -> p k c", c=128)
        ps, pb = bank()
        P.mmf([(lambda k=k: nc.tensor.matmul(ps[:, 0:n], lhsT=w3[:, k, :], rhs=c.XNs[xi][:, k, 0:n], start=(k == 0), stop=(k == KC - 1)), [c.XNbs[xi][k]])
               for k in range(KC)], reads=[wtb], writes=[pb])
        return ps, pb

    esA = ExitStack()
    stacks.append(esA)
    CQ = sb(esA, "CQ", [128, 3, NT * T], BF16)
    CQb = [Buf(f"CQ{i}") for i in range(NT)]
    es1 = ExitStack()
    stacks.append(es1)
    c = alloc_tile_ctx(es1)
    order1 = (["h"] if not _os.environ.get("KSKIPH") else []) + list(range(NT))
    gu_srcs, d_srcs, blk_srcs = [], [], []
    for ti in order1:
        gu_srcs += [(wb["w1gu"][j], wbuf["w1gu"]) for j in range(NJ)]
        d_srcs += [(wb["w1d"][f], wbuf["w1d"]) for f in range(8)]
        if ti == "h":
            blk_srcs += [(wb["wina"][m], wbuf["wina"]) for m in range(6, 14)]
        else:
            blk_srcs += [(wb["wina"][m], wbuf["wina"]) for m in range(14)]
    mem_blk = []
    for ctx in range(2):
        mem_blk += [(wb["wmk"][h], wbuf["wmk"]) for h in range(4)]
    blk_srcs = blk_srcs + mem_blk
    gu1 = Stream(P, es1, "gu", KC * 256, 3, gu_srcs)
    d1 = Stream(P, es1, "wd", NJ * 128, 2, d_srcs)
    blk1 = Stream(P, es1, "blk", KC * 128, 4, blk_srcs)
    RAW = sb(es1, "RAW", [128, 3, T], F32)
    RAWb = [Buf() for _ in range(3)]
    SQ3 = sb(es1, "SQ3", [128, 3, T], BF16)
    SQ3b = [Buf() for _ in range(3)]
    RV2 = c.RINV
    RV2b = c.RINVb
    tmp2 = c.tmp
    CKVN = sb(es1, "CKVN", [128, 2, T], BF16)
    CKVNb = Buf()
    s_ckvn = P.sem("ckvn")
    KRO = sb(es1, "KRO", [32, 2, T], BF16)
    KROb = Buf()
    s_kro = P.sem("kro")
    KA = sb(es1, "KA", [32, T], F32)
    KB = sb(es1, "KB", [32, T], F32)
    KAb, KBb = Buf(), Buf()
    ROPE = sb(es1, "ROPE", [32, 2, T], F32)
    ROPEb = Buf()
    s_rope = P.sem("rope")
    UT = sb(es1, "UT", [128, 4, T], BF16)
    UTb = Buf()
    s_ut = P.sem("ut")
    CCT = sb(es1, "CCT", [128, T], F32)
    CCTb = Buf()
    if STOP <= 0.3:
        raise _Stop()
    for idx, ti in enumerate(order1):
        if (STOP <= 0.4 and idx == 1) or (STOP <= 0.5 and idx == 2):
            raise _Stop()
        halo = ti == "h"
        n = 128 if halo else T
        xs = idx % 2
        X, Xb = c.X[xs], c.Xb[xs]
        def load_x(idx_, ti_):
            xs_ = idx_ % 2
            if ti_ == "h":
                P.op(DVE, lambda: nc.vector.memset(c.X[xs_][:, :, 0:128], 0.0), writes=c.Xb[xs_])
                P.dma(SP, c.X[xs_][:, :, 0:4], xh.ap().rearrange("p (k t) -> p k t", t=4), c.Xsem[xs_], writes=c.Xb[xs_])
            else:
                P.dma(SP, c.X[xs_][:, :, :], xT[ti_].rearrange("p (k t) -> p k t", t=T), c.Xsem[xs_], writes=c.Xb[xs_])
        if idx == 0:
            load_x(0, order1[0])
        if idx + 1 < len(order1):
            load_x(idx + 1, order1[idx + 1])
        if not halo:
            P.dma(SP, ROPE[:, :, :], rope_d[0:32, :, ti * T:(ti + 1) * T], s_rope, writes=[ROPEb])
        xi = idx % 2
        if idx == 0:
            norm_to_xn(c, xs, n, G_FFN1, xi)

        def hook1(idx=idx):
            if idx + 1 < len(order1):
                norm_to_xn(c, (idx + 1) % 2, T, G_FFN1, (idx + 1) % 2)
        ffn(c, xs, n, G_FFN1, gu1, d1, xi=xi, do_norm=False, mid_hook=hook1)
        if KSUB <= 4:
            raise _Stop()
        if not halo:
            P.dma(POOL, x1s[ti].rearrange("p (k t) -> p k t", t=T), X[:, :, :], c.XsemS[xs], reads=Xb, writes=[x1s_buf[ti]])
        if KSUB <= 5:
            raise _Stop()
        norm_to_xn(c, xs, n, G_MIX, xi)
        if KSUB <= 6:
            raise _Stop()
        if not halo:
            ch, tl = ti // 4, ti % 4
            for i in range(3):
                ps, pb = proj(c, blk1, n, xi)
                P.op(ACT, lambda: nc.scalar.activation(out=SQ3[:, i, :], in_=ps[:, :], func=AF.Square), reads=[pb], writes=[SQ3b[i]])
                P.op(ACT, lambda: nc.scalar.activation(out=RAW[:, i, :], in_=ps[:, :], func=AF.Copy), reads=[pb], writes=[RAWb[i]])
            p2, p2b = bank()
            P.mm([lambda i=i: nc.tensor.matmul(p2[:, :], lhsT=ONESB[:, :], rhs=SQ3[:, i, :], start=(i == 0), stop=(i == 2)) for i in range(3)],
                 reads=[ones_b] + SQ3b, writes=[p2b])
            rsqrt(tmp2, p2[:, :], p2b, RV2[:, :], RV2b, 1.0 / 384)
            for i in range(3):
                P.op(DVE, lambda i=i: nc.vector.scalar_tensor_tensor(out=CQ[:, i, ti * T:(ti + 1) * T], in0=RAW[:, i, :], scalar=g(G_QL + i),
                                                                     in1=RV2[:, :], op0=ALU.mult, op1=ALU.mult),
                     reads=[RAWb[i], RV2b, gains_b], writes=[CQb[ti]])
            if KSUB <= 7:
                raise _Stop()
            for i in range(2):
                ps, pb = proj(c, blk1, n, xi)
                P.op(ACT, lambda: nc.scalar.activation(out=SQ3[:, i, :], in_=ps[:, :], func=AF.Square), reads=[pb], writes=[SQ3b[i]])
                P.op(ACT, lambda: nc.scalar.activation(out=RAW[:, i, :], in_=ps[:, :], func=AF.Copy), reads=[pb], writes=[RAWb[i]])
            p2, p2b = bank()
            P.mm([lambda i=i: nc.tensor.matmul(p2[:, :], lhsT=ONESB[:, :], rhs=SQ3[:, i, :], start=(i == 0), stop=(i == 1)) for i in range(2)],
                 reads=[ones_b] + SQ3b[0:2], writes=[p2b])
            rsqrt(tmp2, p2[:, :], p2b, RV2[:, :], RV2b, 1.0 / 256)
            for i in range(2):
                P.op(DVE, lambda i=i: nc.vector.scalar_tensor_tensor(out=CKVN[:, i, :], in0=RAW[:, i, :], scalar=g(G_KVL + i),
                                                                     in1=RV2[:, :], op0=ALU.mult, op1=ALU.mult),
                     reads=[RAWb[i], RV2b, gains_b], writes=[CKVNb])
            lat = [latP, latS][ch][tl // 2]
            tl2 = tl % 2
            P.dma(POOL, lat[0:256, tl2 * T:(tl2 + 1) * T].rearrange("(k p) t -> p k t", p=128), CKVN[:, :, :], s_ckvn,
                  reads=[CKVNb], writes=[lat_buf[ch]])
            if KSUB <= 8:
                raise _Stop()
            wt, wtb = blk1.next()
            w3 = wt[:, :].rearrange("p (k c) -> p k c", c=128)
            pk, pkb = bank()
            pq, pqb = bank()
            P.mm([lambda k=k: nc.tensor.matmul(pk[0:32, :], lhsT=w3[:, k, 0:32], rhs=c.XNs[xi][:, k, :], start=(k == 0), stop=(k == KC - 1))
                  for k in range(KC)], reads=[wtb] + c.XNbs[xi], writes=[pkb])
            P.mm([lambda k=k: nc.tensor.matmul(pq[0:32, :], lhsT=w3[:, k, 32:64], rhs=c.XNs[xi][:, k, :], start=(k == 0), stop=(k == KC - 1))
                  for k in range(KC)], reads=[wtb] + c.XNbs[xi], writes=[pqb])
            P.op(ACT, lambda: nc.scalar.activation(out=KRO[:, 1, :], in_=pk[0:32, :], func=AF.Copy), reads=[pkb], writes=[KROb])
            P.op(DVE, lambda: nc.vector.scalar_tensor_tensor(out=KA[:, :], in0=pk[0:32, :], scalar=g(G_KR, 0, 32), in1=ROPE[:, 0, :],
                                                             op0=ALU.mult, op1=ALU.mult), reads=[pkb, ROPEb, gains_b, KROb], writes=[KAb])
            P.op(DVE, lambda: nc.vector.scalar_tensor_tensor(out=KB[:, :], in0=pq[0:32, :], scalar=g(G_KRS, 0, 32), in1=ROPE[:, 1, :],
                                                             op0=ALU.mult, op1=ALU.mult), reads=[pqb, ROPEb, gains_b], writes=[KBb])
            P.op(DVE, lambda: nc.vector.tensor_tensor(out=KRO[:, 0, :], in0=KA[:, :], in1=KB[:, :], op=ALU.add),
                 reads=[KAb, KBb], writes=[KROb])
            P.dma(POOL, lat[256:320, tl2 * T:(tl2 + 1) * T].rearrange("(a p) t -> p a t", p=32), KRO[:, :, :], s_kro,
                  reads=[KROb], writes=[lat_buf[ch]])
        if KSUB <= 9:
            raise _Stop()
        pcc = []
        for i in range(4):
            pcc.append(proj(c, blk1, n, xi))
            if i >= 1:
                pass
        for i in range(4):
            pc, pcb = pcc[i]
            px, pxb = proj(c, blk1, n, xi)
            P.op(ACT, lambda: nc.scalar.activation(out=CCT[:, 0:n], in_=pc[:, 0:n], func=AF.Copy), reads=[pcb], writes=[CCTb])
            if halo:
                P.op(DVE, lambda: nc.vector.tensor_tensor(out=UH[:, i, :], in0=CCT[:, 0:4], in1=px[:, 0:4], op=ALU.mult),
                     reads=[CCTb, pxb], writes=[UHb])
            else:
                P.op(DVE, lambda: nc.vector.tensor_tensor(out=UT[:, i, :], in0=CCT[:, :], in1=px[:, :], op=ALU.mult),
                     reads=[CCTb, pxb], writes=[UTb])
        if not halo:
            P.dma(POOL, Us[:, :, ch, tl * T:(tl + 1) * T], UT[:, :, :], s_ut, reads=[UTb], writes=[Us_buf])

    if STOP <= 1:
        raise _Stop()
    s_cc = P.sem("cc")
    P.deps(POOL, [lat_buf[0], lat_buf[1]], [gat_buf[0], gat_buf[1]])
    for hf in range(2):
        nc.gpsimd.collective_compute("AllGather", ALU.bypass, replica_groups=[[0, 1, 2, 3], [4, 5, 6, 7]],
                                     ins=[latP[hf].ap().opt()], outs=[gatP[hf].ap().opt()]).then_inc(s_cc)
        P.semcount[id(s_cc)] += 1
    gat_buf[0].w = (s_cc, P.semcount[id(s_cc)])
    for hf in range(2):
        nc.gpsimd.collective_compute("AllGather", ALU.bypass, replica_groups=[[0, 1], [2, 3], [4, 5], [6, 7]],
                                     ins=[latS[hf].ap().opt()], outs=[gatS[hf].ap().opt()]).then_inc(s_cc)
        P.semcount[id(s_cc)] += 1
    gat_buf[1].w = (s_cc, P.semcount[id(s_cc)])

    s_wmv = P.sem("wmv")
    WMVbs = c.Hb[8:16]
    P.dma(SP, c.H[:, 8:16, :], wb["wmv"][0].rearrange("p (k c) -> p k c", c=512), s_wmv, reads=[wbuf["wmv"]], writes=WMVbs)

    for ctx in range(2):
        MEMX = c.X[1][:, :, 0:256]
        MEMXb = c.Xb[1][0]
        P.dma(SP, MEMX, memT[ctx].rearrange("p (k m) -> p k m", m=256), c.Xsem[1], writes=c.Xb[1])
        for k in range(KC):
            P.op(ACT, lambda k=k: nc.scalar.activation(out=c.H[:, k, 0:256], in_=MEMX[:, k, :], func=AF.Square),
                 reads=[MEMXb], writes=[c.Hb[k]])
        ps, pb = bank()
        P.mm([lambda k=k: nc.tensor.matmul(ps[:, 0:256], lhsT=ONESB[:, :], rhs=c.H[:, k, 0:256], start=(k == 0), stop=(k == KC - 1))
              for k in range(KC)], reads=[ones_b] + c.Hb[0:KC], writes=[pb])
        rsqrt(c.tmp, ps[:, 0:256], pb, c.RINV[:, 0:256], c.RINVb, 1.0 / D)
        for k in range(KC):
            P.op(DVE, lambda k=k: nc.vector.scalar_tensor_tensor(out=c.XN[:, k, 0:256], in0=MEMX[:, k, :], scalar=g(G_MEM + k),
                                                                 in1=c.RINV[:, 0:256], op0=ALU.mult, op1=ALU.mult),
                 reads=[MEMXb, c.RINVb, gains_b], writes=[c.XNb[k]])
        for h in range(4):
            ps, pb = proj(c, blk1, 256)
            P.op(ACT, lambda: nc.scalar.activation(out=SQ3[:, 0, 0:256], in_=ps[:, 0:256], func=AF.Square), reads=[pb], writes=[SQ3b[0]])
            P.op(ACT, lambda: nc.scalar.activation(out=RAW[:, 0, 0:256], in_=ps[:, 0:256], func=AF.Copy), reads=[pb], writes=[RAWb[0]])
            p2, p2b = bank()
            P.mm([lambda: nc.tensor.matmul(p2[:, 0:256], lhsT=ONESB[:, :], rhs=SQ3[:, 0, 0:256], start=True, stop=True)],
                 reads=[ones_b, SQ3b[0]], writes=[p2b])
            rsqrt(tmp2, p2[:, 0:256], p2b, RV2[:, 0:256], RV2b, 1.0 / 128)
            P.op(DVE, lambda: nc.vector.scalar_tensor_tensor(out=MK[:, ctx, h, :], in0=RAW[:, 0, 0:256], scalar=g(G_XK), in1=RV2[:, 0:256],
                                                             op0=ALU.mult, op1=ALU.mult), reads=[RAWb[0], RV2b, gains_b], writes=[MK_b])
        wmv3 = c.H[:, 8:16, :]
        for mc in range(2):
            ps, pb = bank()
            P.mm([lambda k=k: nc.tensor.matmul(ps[:, :], lhsT=c.XN[:, k, mc * 128:(mc + 1) * 128], rhs=wmv3[:, k, :],
                                               start=(k == 0), stop=(k == KC - 1)) for k in range(KC)],
                 reads=WMVbs + c.XNb, writes=[pb])
            P.op(ACT, lambda: nc.scalar.activation(out=MV[:, ctx, mc, :], in_=ps[:, :], func=AF.Copy), reads=[pb], writes=[MV_b])

    P.barrier()
    es1.close()
    stacks.pop()
    if STOP <= 2:
        raise _Stop()

    es2 = ExitStack()
    stacks.append(es2)
    SMAX = 8192
    CKV = sb(es2, "CKV", [128, 2, SMAX], BF16)
    CKVb = Buf()
    s_ckv = P.sem("ckv")
    KT = [sb(es2, f"KT{i}", [96, SMAX], BF16) for i in range(2)]
    KTb = [Buf(), Buf()]
    KTrb = [Buf(), Buf()]
    s_kt = [P.sem("kt0"), P.sem("kt1")]
    KRR = sb(es2, "KRR", [96, 2048], BF16)
    KRRb = Buf()
    s_krr = P.sem("krr")
    VG = sb(es2, "VG", [128, SMAX // 128, 4, 65], BF16)
    VGb = Buf()
    QT = [sb(es2, f"QT{i}", [96, T], BF16) for i in range(2)]
    QTb = [Buf(), Buf()]
    NPT = 4
    PT = [sb(es2, f"PT{i}", [128, T], BF16) for i in range(NPT)]
    PTb = [Buf() for _ in range(NPT)]
    SQK = [sb(es2, f"SQK{i}", [64, T], BF16) for i in range(2)]
    SQKb = [Buf(), Buf()]
    SQQ = sb(es2, "SQQ", [96, T], BF16)
    SQQb = Buf()
    tq = mk_tmp(es2, "q")
    RQ = sb(es2, "RQ", [96, T], F32)
    RQb = Buf()
    QA = sb(es2, "QA", [96, T], F32)
    QB = sb(es2, "QB", [96, T], F32)
    QAb, QBb = Buf(), Buf()
    ROQ = sb(es2, "ROQ", [96, 2, T], F32)
    ROQb = Buf()
    s_roq = P.sem("roq")
    KRSS = sb(es2, "KRSS", [128, 64], F32)
    KRSSb = Buf()
    RK = [sb(es2, f"RK{i}", [128, 64], F32) for i in range(2)]
    RKb = [Buf(), Buf()]
    tk = mk_tmp(es2, "k", 64)
    SSK = sb(es2, "SSK", [128, 64], F32)
    SSKb = Buf()
    REC = sb(es2, "REC", [65, T], F32)
    RECb = Buf()
    BC = sb(es2, "BC", [64, T], F32)
    BCb = Buf()
    AOT = [sb(es2, f"AOT{i}", [64, T], BF16) for i in range(2)]
    AOTb = [Buf(), Buf()]
    s_aot = [P.sem("aot0"), P.sem("aot1")]
    P.op(DVE, lambda: nc.vector.memset(VG[:, :, :, 64:65], 1.0), writes=[VGb])
    WUQ = sb(es2, "WUQ", [128, 8, 3 * 192], BF16)
    WUK = sb(es2, "WUK", [128, 2 * 512], BF16)
    WUV = sb(es2, "WUV", [128, 2 * 512], BF16)
    wsm_b = Buf("wsmall")
    s_ws = P.sem("wsm")
    P.dma(SP, WUQ[:, :, :], wb["wuq"].ap().rearrange("h p x -> p h x"), s_ws, reads=[wbuf["wuq"]], writes=[wsm_b])
    P.dma(SP, WUK[:, :], wb["wuk"][0], s_ws, reads=[wbuf["wuk"]], writes=[wsm_b])
    P.dma(SP, WUV[:, :], wb["wuv"][0], s_ws, reads=[wbuf["wuv"]], writes=[wsm_b])

    wuk3 = WUK[:, :].rearrange("p (k c) -> p k c", c=512)
    wuv3 = WUV[:, :].rearrange("p (k c) -> p k c", c=512)
    ST_BANKS = [0, 1, 2]
    O_BANKS = [3, 4]
    bank_pool[:] = [5, 6, 7]
    st_rr = 0
    o_rr = 0
    pt_rr = 0
    qt_rr = 0
    kt_rr = 0
    aot_rr = 0
    SCALE = 96.0 ** -0.5

    rr = {"st": 0, "o": 0, "pt": 0, "qt": 0, "aot": 0}
    pend_q = [None]
    pend_fin = [None]
    LOOK = 2

    def q_gen(h, ti):
        P.dma(SP, ROQ[64:96, :, :], rope_d[64:96, :, ti * T:(ti + 1) * T], s_roq, writes=[ROQb])
        pq, pqb = bank()
        pqs, pqsb = bank()
        P.mm([lambda k=k: nc.tensor.matmul(pq[0:96, :], lhsT=WUQ[:, h, k * 192:k * 192 + 96], rhs=CQ[:, k, ti * T:(ti + 1) * T],
                                           start=(k == 0), stop=(k == 2)) for k in range(3)],
             reads=[wsm_b, CQb[ti]], writes=[pqb])
        P.mm([lambda k=k: nc.tensor.matmul(pqs[0:96, :], lhsT=WUQ[:, h, k * 192 + 96:k * 192 + 192], rhs=CQ[:, k, ti * T:(ti + 1) * T],
                                           start=(k == 0), stop=(k == 2)) for k in range(3)],
             reads=[wsm_b, CQb[ti]], writes=[pqsb])
        P.op(ACT, lambda: nc.scalar.activation(out=SQQ[:, :], in_=pq[0:96, :], func=AF.Square), reads=[pqb], writes=[SQQb])
        p2, p2b = bank()
        P.mm([lambda: nc.tensor.matmul(p2[0:96, :], lhsT=ONESB[0:96, 0:96], rhs=SQQ[:, :], start=True, stop=True)],
             reads=[ones_b, SQQb], writes=[p2b])
        rsqrt(tq, p2[0:96, :], p2b, RQ[:, :], RQb, 1.0 / 96)
        qi = rr["qt"] % 2
        rr["qt"] += 1
        qtile, qtb = QT[qi], QTb[qi]
        P.op(DVE, lambda: nc.vector.scalar_tensor_tensor(out=qtile[0:64, :], in0=pq[0:64, :], scalar=g(G_MQ, 0, 64), in1=RQ[0:64, :],
                                                         op0=ALU.mult, op1=ALU.mult), reads=[pqb, RQb, gains_b], writes=[qtb])
        P.op(DVE, lambda: nc.vector.scalar_tensor_tensor(out=QA[64:96, :], in0=pq[64:96, :], scalar=g(G_MQ, 64, 96), in1=ROQ[64:96, 0, :],
                                                         op0=ALU.mult, op1=ALU.mult), reads=[pqb, ROQb, gains_b], writes=[QAb])
        P.op(DVE, lambda: nc.vector.scalar_tensor_tensor(out=QB[64:96, :], in0=pqs[64:96, :], scalar=g(G_MQS, 64, 96), in1=ROQ[64:96, 1, :],
                                                         op0=ALU.mult, op1=ALU.mult), reads=[pqsb, ROQb, gains_b], writes=[QBb])
        P.op(DVE, lambda: nc.vector.tensor_tensor(out=QA[64:96, :], in0=QA[64:96, :], in1=QB[64:96, :], op=ALU.add),
             reads=[QAb, QBb], writes=[QAb])
        P.op(DVE, lambda: nc.vector.tensor_tensor(out=qtile[64:96, :], in0=QA[64:96, :], in1=RQ[64:96, :], op=ALU.mult),
             reads=[QAb, RQb], writes=[qtb])
        return qtile, qtb

    def main_store(h, hh, ti, NCH, ktile, ktb, ki, rk, rkb, qtile, qtb):
        ob = O_BANKS[rr["o"] % 2]
        rr["o"] += 1
        po, pob = psum[ob], psb[ob]
        P.deps(PE, [], [pob])
        pend = []
        for step in range(NCH + LOOK):
            if step < NCH:
                cc = step
                sbk = ST_BANKS[rr["st"] % 3]
                rr["st"] += 1
                pst, pstb = psum[sbk], psb[sbk]
                P.mm([lambda: nc.tensor.matmul(pst[:, :], lhsT=ktile[0:96, cc * 128:(cc + 1) * 128], rhs=qtile[0:96, :], start=True, stop=True)],
                     reads=[ktb, KTrb[ki], qtb], writes=[pstb])
                pi = rr["pt"] % NPT
                rr["pt"] += 1
                P.op(ACT, lambda: nc.scalar.activation(out=PT[pi][:, :], in_=pst[:, :], func=AF.Exp, scale=rk[:, cc:cc + 1]),
                     reads=[pstb, rkb], writes=[PTb[pi]])
                pend.append((cc, pi))
            if step >= LOOK:
                cc, pi = pend.pop(0)
                last = cc == NCH - 1
                if not last:
                    PE.wait(PTb[pi].w)
                    PE.wait(VGb.w)
                    nc.tensor.matmul(po[0:65, :], lhsT=VG[:, cc, hh, :], rhs=PT[pi][:, :], start=(cc == 0), stop=False)
                    PTb[pi].r[id(PE.sem)] = (PE.sem, PE.n + 1)
                else:
                    P.mm([lambda: nc.tensor.matmul(po[0:65, :], lhsT=VG[:, cc, hh, :], rhs=PT[pi][:, :], start=(cc == 0), stop=True)],
                         reads=[PTb[pi], VGb], writes=[pob])
            if step == 22 and pend_fin[0] is not None:
                finish(*pend_fin[0])
                pend_fin[0] = None
        pend_fin[0] = (h, ti, po, pob)

    def finish(h, ti, po, pob):
        P.op(DVE, lambda: nc.vector.reciprocal(out=REC[64:65, :], in_=po[64:65, :]), reads=[pob], writes=[RECb])
        pbc, pbcb = bank()
        P.mm([lambda: nc.tensor.matmul(pbc[0:64, :], lhsT=ONESF[64:65, 0:64], rhs=REC[64:65, :], start=True, stop=True)],
             reads=[ones_b, RECb], writes=[pbcb])
        P.op(ACT, lambda: nc.scalar.activation(out=BC[:, :], in_=pbc[0:64, :], func=AF.Copy), reads=[pbcb], writes=[BCb])
        ai = rr["aot"] % 2
        rr["aot"] += 1
        P.op(DVE, lambda: nc.vector.tensor_tensor(out=AOT[ai][:, :], in0=po[0:64, :], in1=BC[:, :], op=ALU.mult),
             reads=[pob, BCb], writes=[AOTb[ai]])
        P.dma(POOL, AOs[:, h, ti * T:(ti + 1) * T], AOT[ai][:, :], s_aot[ai], reads=[AOTb[ai]], writes=[AOs_buf])

    for ctx in range(2):
        S = 8192 if ctx == 0 else 4096
        R = 4 if ctx == 0 else 2
        NCH = S // 128
        NKT = S // T
        gat = [gatP, gatS][ctx]
        gb = gat_buf[ctx]
        gvs = [gat[hf].ap().rearrange("(r x) t -> r x t", x=LROWS) for hf in range(2)]
        for r in range(R):
            for hf in range(2):
                c0 = r * 2048 + hf * 1024
                P.dma(SP, CKV[:, :, c0:c0 + 1024], gvs[hf][r, 0:256, :].rearrange("(k p) t -> p k t", p=128), s_ckv,
                      reads=[gb], writes=[CKVb])
                for i in range(2):
                    P.dma(SP, KT[i][64:96, c0:c0 + 1024], gvs[hf][r, 256:288, :], s_kt[i], reads=[gb], writes=[KTrb[i], KTb[i]])
        for r in range(R):
            for hf in range(2):
                P.dma(SP, KRR[64:96, hf * 1024:(hf + 1) * 1024], gvs[hf][r, 288:320, :], s_krr, reads=[gb], writes=[KRRb])
            for kt in range(4):
                P.op(ACT, lambda kt=kt: nc.scalar.activation(out=KRR[64:96, kt * T:(kt + 1) * T], in_=KRR[64:96, kt * T:(kt + 1) * T], func=AF.Square),
                     reads=[KRRb], writes=[KRRb])
            ps, pb = bank()
            fns = [lambda cc=cc: nc.tensor.matmul(ps[:, cc:cc + 1], lhsT=KRR[64:96, cc * 128:(cc + 1) * 128], rhs=ONESB[64:96, 0:1], start=True, stop=True)
                   for cc in range(16)]
            P.mm(fns, reads=[KRRb, ones_b], writes=[pb])
            P.op(DVE, lambda: nc.vector.tensor_copy(out=KRSS[:, r * 16:(r + 1) * 16], in_=ps[:, 0:16]), reads=[pb], writes=[KRSSb])

        if KATT <= 1:
            raise _Stop()
        for hg in range(2):
            for cc in range(NCH):
                ps, pb = bank()
                P.mm([lambda k=k: nc.tensor.matmul(ps[:, 0:256], lhsT=CKV[:, k, cc * 128:(cc + 1) * 128], rhs=wuv3[:, k, hg * 256:(hg + 1) * 256],
                                                   start=(k == 0), stop=(k == 1)) for k in range(2)],
                     reads=[CKVb, wsm_b], writes=[pb])
                eng = ACT if cc % 2 == 0 else DVE
                if eng is ACT:
                    P.op(ACT, lambda: nc.scalar.activation(out=VG[:, cc, :, 0:64], in_=ps[:, 0:256].rearrange("p (h d) -> p h d", d=64), func=AF.Copy),
                         reads=[pb], writes=[VGb])
                else:
                    P.op(DVE, lambda: nc.vector.tensor_copy(out=VG[:, cc, :, 0:64], in_=ps[:, 0:256].rearrange("p (h d) -> p h d", d=64)),
                         reads=[pb], writes=[VGb])
            if KATT <= 2:
                raise _Stop()
            for hh in range(4):
                h = hg * 4 + hh
                ki = kt_rr % 2
                kt_rr += 1
                ktile, ktb = KT[ki], KTb[ki]
                pss, pssb = psum[0], psb[0]
                def ss_mm(kt_):
                    sq_, sqb_ = SQK[kt_ % 2], SQKb[kt_ % 2]
                    P.mm([lambda a=a: nc.tensor.matmul(pss[:, kt_ * 4 + a:kt_ * 4 + a + 1], lhsT=sq_[0:64, a * 128:(a + 1) * 128], rhs=ONESB[0:64, 0:1],
                                                       start=True, stop=True) for a in range(4)],
                         reads=[sqb_, ones_b], writes=[pssb])
                prev_kt = None
                for kt in range(NKT):
                    ps, pb = bank()
                    P.mm([lambda k=k: nc.tensor.matmul(ps[0:64, :], lhsT=wuk3[:, k, h * 64:(h + 1) * 64], rhs=CKV[:, k, kt * T:(kt + 1) * T],
                                                       start=(k == 0), stop=(k == 1)) for k in range(2)],
                         reads=[CKVb, wsm_b], writes=[pb])
                    sq, sqb = SQK[kt % 2], SQKb[kt % 2]
                    P.op(ACT, lambda: nc.scalar.activation(out=sq[:, :], in_=ps[0:64, :], func=AF.Square), reads=[pb], writes=[sqb])
                    P.op(DVE, lambda: nc.vector.tensor_scalar(out=ktile[0:64, kt * T:(kt + 1) * T], in0=ps[0:64, :], scalar1=g(G_MK, 0, 64),
                                                              scalar2=None, op0=ALU.mult), reads=[pb, gains_b, sqb], writes=[ktb])
                    if prev_kt is not None:
                        ss_mm(prev_kt)
                    prev_kt = kt
                ss_mm(prev_kt)
                if KK >= 4:
                    P.op(DVE, lambda: nc.vector.tensor_tensor(out=SSK[:, 0:NCH], in0=pss[:, 0:NCH], in1=KRSS[:, 0:NCH], op=ALU.add),
                         reads=[pssb, KRSSb], writes=[SSKb])
                rk, rkb = RK[ki], RKb[ki]
                if KK >= 5:
                    rsqrt(tk, SSK[:, 0:NCH], SSKb, rk[:, 0:NCH], rkb, 1.0 / 96, post=SCALE)
                if KATT <= 3:
                    raise _Stop()
                for qt in range(4):
                    ti = ctx * 4 + qt
                    if pend_q[0] is None:
                        pend_q[0] = q_gen(h, ti)
                    qtile, qtb = pend_q[0]
                    nxt = None
                    if qt < 3:
                        nxt = (h, ti + 1)
                    elif h < 7:
                        nxt = (h + 1, ctx * 4)
                    pend_q[0] = q_gen(*nxt) if nxt is not None else None
                    main_store(h, hh, ti, NCH, ktile, ktb, ki, rk, rkb, qtile, qtb)
                if KATT <= 7:
                    raise _Stop()

    if pend_fin[0] is not None:
        finish(*pend_fin[0])
        pend_fin[0] = None
    P.barrier()
    es2.close()
    esA.close()
    stacks.pop()
    stacks.pop()
    if STOP <= 3:
        raise _Stop()
    bank_pool[:] = list(range(8))

    es3 = ExitStack()
    stacks.append(es3)
    c = alloc_tile_ctx(es3)
    gu_srcs, d_srcs, blk_srcs, wo_srcs = [], [], [], []
    for ti in range(NT):
        gu_srcs += [(wb["w2gu"][j], wbuf["w2gu"]) for j in range(NJ)]
        d_srcs += [(wb["w2d"][f], wbuf["w2d"]) for f in range(8)]
        blk_srcs += [(wb["winb"][m], wbuf["winb"]) for m in range(8)]
        for fc in range(8):
            blk_srcs += [(wb["winb"][8 + gg * 8 + fc], wbuf["winb"]) for gg in range(3)]
        blk_srcs += [(wb["wout"][f], wbuf["wout"]) for f in range(8)]
        wo_srcs += [(wb["wo3"][f], wbuf["wo3"]) for f in range(8)]
    gu3 = Stream(P, es3, "gu3", KC * 256, 3, gu_srcs)
    d3 = Stream(P, es3, "wd3", NJ * 128, 2, d_srcs)
    blk3 = Stream(P, es3, "blk3", KC * 128, 4, blk_srcs)
    wo3s = Stream(P, es3, "wo3", 16 * 128, 2, wo_srcs)
    UW = sb(es3, "UW", [128, 4, T + 2], BF16)
    UWb = Buf()
    s_uw = P.sem("uw")
    AOX = sb(es3, "AOX", [64, 8, T], BF16)
    AOXb = Buf()
    s_aox = P.sem("aox")
    CVT = sb(es3, "CVT", [128, T], F32)
    CVTb = Buf()
    CV = sb(es3, "CV", [128, 4, T], BF16)
    CVb = Buf()
    XQ = sb(es3, "XQ", [128, 4, T], BF16)
    XQb = Buf()
    RAWX = sb(es3, "RAWX", [128, T], F32)
    RAWXb = Buf()
    SQX = sb(es3, "SQX", [128, T], BF16)
    SQXb = Buf()
    RV3 = c.RINV
    RV3b = c.RINVb
    tmp3 = c.tmp
    XA = sb(es3, "XA", [128, 4, T], BF16)
    XAb = Buf()
    PM = [sb(es3, f"PM{i}", [128, T], BF16) for i in range(2)]
    PMb = [Buf(), Buf()]
    RD = sb(es3, "RD", [128, T], F32)
    RDb = Buf()
    TH = [sb(es3, f"TH{i}", [128, T], F32) for i in range(3)]
    THb = [Buf() for _ in range(3)]
    M0 = sb(es3, "M0", [128, T], F32)
    M1 = sb(es3, "M1", [128, T], F32)
    M0b, M1b = Buf(), Buf()
    MG = sb(es3, "MG", [128, KC, T], BF16)
    MGb = [Buf() for _ in range(KC)]
    s_y = [P.sem("y0"), P.sem("y1")]
    XSC = 128.0 ** -0.5

    for ti in range(NT):
        ctx, tl = ti // 4, ti % 4
        xs = ti % 2
        X, Xb = c.X[xs], c.Xb[xs]
        def load_x3(ti_):
            xs_ = ti_ % 2
            P.dma(SP, c.X[xs_][:, :, :], x1s[ti_].rearrange("p (k t) -> p k t", t=T), c.Xsem[xs_], reads=[x1s_buf[ti_]], writes=c.Xb[xs_])
        if ti == 0:
            load_x3(0)
        if ti + 1 < NT:
            load_x3(ti + 1)
        lo = max(tl * T - 1, 0)
        hi = min(tl * T + T + 1, 2048)
        d0 = lo - (tl * T - 1)
        P.dma(SP, UW[:, :, d0:d0 + (hi - lo)], Us[:, :, ctx, lo:hi], s_uw, reads=[Us_buf], writes=[UWb])
        if tl == 0:
            P.op(DVE, lambda: nc.vector.tensor_copy(out=UW[:, :, 0:1], in_=UH[:, :, 2 * ctx:2 * ctx + 1]), reads=[UHb], writes=[UWb])
        if tl == 3:
            P.op(DVE, lambda: nc.vector.tensor_copy(out=UW[:, :, T + 1:T + 2], in_=UH[:, :, 2 * ctx + 1:2 * ctx + 2]), reads=[UHb], writes=[UWb])
        P.dma(SP, AOX[:, :, :], AOs[:, :, ti * T:(ti + 1) * T], s_aox, reads=[AOs_buf], writes=[AOXb])
        xi = ti % 2
        if ti == 0:
            norm_to_xn(c, xs, T, G_MIX, xi)
        for i in range(4):
            ps, pb = proj(c, blk3, T, xi)
            P.op(DVE, lambda: nc.vector.tensor_scalar(out=CVT[:, :], in0=UW[:, i, 0:T], scalar1=g(G_CONV + i * 3 + 0), scalar2=None, op0=ALU.mult),
                 reads=[UWb, gains_b], writes=[CVTb])
            P.op(DVE, lambda: nc.vector.scalar_tensor_tensor(out=CVT[:, :], in0=UW[:, i, 1:T + 1], scalar=g(G_CONV + i * 3 + 1), in1=CVT[:, :],
                                                             op0=ALU.mult, op1=ALU.add), reads=[UWb, CVTb, gains_b], writes=[CVTb])
            P.op(DVE, lambda: nc.vector.scalar_tensor_tensor(out=CVT[:, :], in0=UW[:, i, 2:T + 2], scalar=g(G_CONV + i * 3 + 2), in1=CVT[:, :],
                                                             op0=ALU.mult, op1=ALU.add), reads=[UWb, CVTb, gains_b], writes=[CVTb])
            P.op(DVE, lambda: nc.vector.tensor_tensor(out=CV[:, i, :], in0=CVT[:, :], in1=ps[:, :], op=ALU.mult),
                 reads=[CVTb, pb], writes=[CVb])
        for h in range(4):
            ps, pb = proj(c, blk3, T, xi)
            P.op(ACT, lambda: nc.scalar.activation(out=SQX[:, :], in_=ps[:, :], func=AF.Square), reads=[pb], writes=[SQXb])
            P.op(ACT, lambda: nc.scalar.activation(out=RAWX[:, :], in_=ps[:, :], func=AF.Copy), reads=[pb], writes=[RAWXb])
            p2, p2b = bank()
            P.mm([lambda: nc.tensor.matmul(p2[:, :], lhsT=ONESB[:, :], rhs=SQX[:, :], start=True, stop=True)], reads=[ones_b, SQXb], writes=[p2b])
            rsqrt(tmp3, p2[:, :], p2b, RV3[:, :], RV3b, 1.0 / 128)
            P.op(DVE, lambda: nc.vector.scalar_tensor_tensor(out=XQ[:, h, :], in0=RAWX[:, :], scalar=g(G_XQ), in1=RV3[:, :],
                                                             op0=ALU.mult, op1=ALU.mult), reads=[RAWXb, RV3b, gains_b], writes=[XQb])
        for h in range(4):
            for mc in range(2):
                ps, pb = bank()
                P.mm([lambda: nc.tensor.matmul(ps[:, :], lhsT=MK[:, ctx, h, mc * 128:(mc + 1) * 128], rhs=XQ[:, h, :], start=True, stop=True)],
                     reads=[MK_b, XQb], writes=[pb])
                P.op(ACT, lambda: nc.scalar.activation(out=PM[mc][:, :], in_=ps[:, :], func=AF.Exp, scale=XSC), reads=[pb], writes=[PMb[mc]])
            po, pob = bank()
            pdn, pdnb = bank()
            P.mm([lambda mc=mc: nc.tensor.matmul(po[:, :], lhsT=MV[:, ctx, mc, h * 128:(h + 1) * 128], rhs=PM[mc][:, :], start=(mc == 0), stop=(mc == 1))
                  for mc in range(2)], reads=[MV_b] + PMb, writes=[pob])
            P.mm([lambda mc=mc: nc.tensor.matmul(pdn[:, :], lhsT=ONESB[:, :], rhs=PM[mc][:, :], start=(mc == 0), stop=(mc == 1))
                  for mc in range(2)], reads=[ones_b] + PMb, writes=[pdnb])
            P.op(DVE, lambda: nc.vector.reciprocal(out=RD[:, :], in_=pdn[:, :]), reads=[pdnb], writes=[RDb])
            P.op(DVE, lambda: nc.vector.tensor_tensor(out=XA[:, h, :], in0=po[:, :], in1=RD[:, :], op=ALU.mult), reads=[pob, RDb], writes=[XAb])
        for fc in range(KC):
            wt, wtb = wo3s.next()
            w3 = wt[:, :].rearrange("p (k c) -> p k c", c=128)
            ys = []
            py, pyb = bank()
            P.mm([lambda h=h: nc.tensor.matmul(py[:, :], lhsT=w3[0:64, h, :], rhs=AOX[:, h, :], start=(h == 0), stop=(h == 7)) for h in range(8)],
                 reads=[wtb, AOXb], writes=[pyb])
            ys.append((py, pyb))
            py, pyb = bank()
            P.mm([lambda i=i: nc.tensor.matmul(py[:, :], lhsT=w3[:, 8 + i, :], rhs=CV[:, i, :], start=(i == 0), stop=(i == 3)) for i in range(4)],
                 reads=[wtb, CVb], writes=[pyb])
            ys.append((py, pyb))
            py, pyb = bank()
            P.mm([lambda i=i: nc.tensor.matmul(py[:, :], lhsT=w3[:, 12 + i, :], rhs=XA[:, i, :], start=(i == 0), stop=(i == 3)) for i in range(4)],
                 reads=[wtb, XAb], writes=[pyb])
            ys.append((py, pyb))
            for gg in range(3):
                pg, pgb = proj(c, blk3, T, xi)
                P.op(ACT, lambda: nc.scalar.activation(out=TH[gg][:, :], in_=pg[:, :], func=AF.Tanh, scale=0.5), reads=[pgb], writes=[THb[gg]])
            P.op(DVE, lambda: nc.vector.scalar_tensor_tensor(out=M0[:, :], in0=TH[0][:, :], scalar=1.0, in1=ys[0][0][:, :], op0=ALU.add, op1=ALU.mult),
                 reads=[THb[0], ys[0][1]], writes=[M0b])
            P.op(DVE, lambda: nc.vector.scalar_tensor_tensor(out=M1[:, :], in0=TH[1][:, :], scalar=1.0, in1=ys[1][0][:, :], op0=ALU.add, op1=ALU.mult),
                 reads=[THb[1], ys[1][1]], writes=[M1b])
            P.op(DVE, lambda: nc.vector.tensor_tensor(out=M0[:, :], in0=M0[:, :], in1=M1[:, :], op=ALU.add), reads=[M0b, M1b], writes=[M0b])
            P.op(DVE, lambda: nc.vector.scalar_tensor_tensor(out=M1[:, :], in0=TH[2][:, :], scalar=1.0, in1=ys[2][0][:, :], op0=ALU.add, op1=ALU.mult),
                 reads=[THb[2], ys[2][1]], writes=[M1b])
            P.op(DVE, lambda: nc.vector.tensor_tensor(out=MG[:, fc, :], in0=M0[:, :], in1=M1[:, :], op=ALU.add), reads=[M0b, M1b], writes=[MGb[fc]])
        for fo in range(KC):
            wt, wtb = blk3.next()
            w3 = wt[:, :].rearrange("p (k c) -> p k c", c=128)
            ps, pb = bank()
            P.mm([lambda k=k: nc.tensor.matmul(ps[:, :], lhsT=w3[:, k, :], rhs=MG[:, k, :], start=(k == 0), stop=(k == KC - 1)) for k in range(KC)],
                 reads=[wtb] + MGb, writes=[pb])
            P.op(DVE, lambda: nc.vector.scalar_tensor_tensor(out=X[:, fo, :], in0=ps[:, :], scalar=0.5, in1=X[:, fo, :], op0=ALU.mult, op1=ALU.add),
                 reads=[pb, Xb[fo]], writes=[Xb[fo]])
        def hook3(ti=ti):
            if ti + 1 < NT:
                norm_to_xn(c, (ti + 1) % 2, T, G_MIX, (ti + 1) % 2)
        ffn(c, xs, T, G_FFN2, gu3, d3, xi=xi, do_norm=True, mid_hook=hook3)
        P.dma(POOL, yT[ti].rearrange("p (k t) -> p k t", t=T), X[:, :, :], c.XsemS[xs], reads=Xb, writes=[])

    P.barrier()
    es3.close()
    es.close()


def _kblocks(W, cols):
    K = W.shape[0]
    kc = K // 128
    out = np.empty((len(cols), 128, kc, 128), np.float32)
    Wr = W.reshape(kc, 128, W.shape[1])
    for m, c0 in enumerate(cols):
        out[m] = Wr[:, :, c0:c0 + 128].transpose(1, 0, 2)
    return out.reshape(len(cols), 128, kc * 128)


def _prep_weights(inp):
    f = lambda a: np.ascontiguousarray(np.asarray(a, np.float32))
    out = {}
    for tag, gu, dn in (("w1", "ffn1_w_gu", "ffn1_w_down"), ("w2", "ffn2_w_gu", "ffn2_w_down")):
        W = f(inp[gu][0]).reshape(KC, 128, 2 * FF)
        gate = W[:, :, :FF].reshape(KC, 128, NJ, 128)
        up = W[:, :, FF:].reshape(KC, 128, NJ, 128)
        st = np.stack([gate, up], axis=3)
        out[tag + "gu"] = np.ascontiguousarray(st.transpose(2, 1, 0, 3, 4)).reshape(NJ, 128, KC * 256)
        Wd = f(inp[dn][0]).reshape(NJ, 128, 8, 128)
        out[tag + "d"] = np.ascontiguousarray(Wd.transpose(2, 1, 0, 3)).reshape(8, 128, NJ * 128)
    Win = f(inp["w_in"][0])
    perm = np.concatenate([np.arange(16, 32), np.arange(0, 16)])
    kr = Win[:, 640:672]
    Wkr = np.concatenate([kr, kr[:, perm], np.zeros((D, 64), np.float32)], axis=1)
    Wa = np.concatenate([Win[:, 0:640], Wkr, Win[:, 1184:2208]], axis=1)
    out["wina"] = _kblocks(Wa, [i * 128 for i in range(14)])
    Wb = np.concatenate([Win[:, 672:1184], Win[:, 2208:2720], Win[:, 2720:5792]], axis=1)
    out["winb"] = _kblocks(Wb, [i * 128 for i in range(32)])
    Wuq = f(inp["w_uq"][0])
    wuq = np.empty((8, 128, 3, 192), np.float32)
    Wr = Wuq.reshape(3, 128, 768)
    for h in range(8):
        blk = Wr[:, :, h * 96:(h + 1) * 96]
        sw = np.concatenate([blk[:, :, :64], blk[:, :, 64:][:, :, perm]], axis=2)
        wuq[h] = np.concatenate([blk, sw], axis=2).transpose(1, 0, 2)
    out["wuq"] = wuq.reshape(8, 128, 3 * 192)
    out["wuk"] = np.ascontiguousarray(f(inp["w_uk"][0]).reshape(2, 128, 512).transpose(1, 0, 2)).reshape(1, 128, 1024)
    out["wuv"] = np.ascontiguousarray(f(inp["w_uv"][0]).reshape(2, 128, 512).transpose(1, 0, 2)).reshape(1, 128, 1024)
    wo3 = np.zeros((8, 128, 16, 128), np.float32)
    Wm = f(inp["w_o_mla"][0]).reshape(8, 64, 8, 128)
    Wc = f(inp["w_o_conv"][0]).reshape(4, 128, 8, 128)
    Wx = f(inp["w_o_mem"][0]).reshape(4, 128, 8, 128)
    wo3[:, 0:64, 0:8, :] = Wm.transpose(2, 1, 0, 3)
    wo3[:, :, 8:12, :] = Wc.transpose(2, 1, 0, 3)
    wo3[:, :, 12:16, :] = Wx.transpose(2, 1, 0, 3)
    out["wo3"] = wo3.reshape(8, 128, 16 * 128)
    out["wout"] = _kblocks(f(inp["w_out"][0]), [i * 128 for i in range(8)])
    Wmkv = f(inp["w_mem_kv"][0])
    out["wmk"] = _kblocks(Wmkv, [i * 128 for i in range(4)])
    out["wmv"] = np.ascontiguousarray(Wmkv[:, 512:].reshape(8, 128, 512).transpose(1, 0, 2)).reshape(1, 128, 8 * 512)
    G = np.zeros((128, NG), np.float32)
    def colk(v, c0):
        v = f(v).reshape(-1, 128)
        for k in range(v.shape[0]):
            G[:, c0 + k] = v[k]
    colk(inp["ffn1_norm"][0], G_FFN1)
    colk(inp["mix_norm"][0], G_MIX)
    colk(inp["ffn2_norm"][0], G_FFN2)
    colk(inp["mem_norm"][0], G_MEM)
    colk(inp["q_lora_norm"][0], G_QL)
    colk(inp["kv_lora_norm"][0], G_KVL)
    mq = f(inp["mla_q_norm"][0])
    mk = f(inp["mla_k_norm"][0])
    G[0:96, G_MQ] = mq
    G[0:64, G_MQS] = mq[:64]
    G[64:96, G_MQS] = mq[64:][perm]
    G[0:64, G_MK] = mk[:64]
    G[0:32, G_KR] = mk[64:]
    G[0:32, G_KRS] = mk[64:][perm]
    G[:, G_XQ] = f(inp["xa_q_norm"][0])
    G[:, G_XK] = f(inp["xa_k_norm"][0])
    cw = f(inp["conv_w"][0])
    for i in range(4):
        for tap in range(3):
            G[:, G_CONV + i * 3 + tap] = cw[tap, i * 128:(i + 1) * 128]
    out["gains"] = G
    return out


def _rope_table(pos):
    half = 16
    inv_freq = (10000.0 ** (-np.arange(half, dtype=np.float32) / half)).astype(np.float32)
    ang = pos.astype(np.float32)[None, :] * inv_freq[:, None]
    cos = np.cos(ang).astype(np.float32)
    sin = np.sin(ang).astype(np.float32)
    c32 = np.concatenate([cos, cos], 0)
    s32 = np.concatenate([-sin, sin], 0)
    tab = np.stack([c32, s32], axis=1)
    return np.ascontiguousarray(np.tile(tab, (4, 1, 1)))


_NC_CACHE = {}


def kernel(**inputs):
    xp = np.asarray(inputs["x_prompt"], np.float32)
    xsm = np.asarray(inputs["x_sample"], np.float32)
    mp = np.asarray(inputs["mem_prompt"], np.float32)
    ms = np.asarray(inputs["mem_sample"], np.float32)
    W = _prep_weights(inputs)
    if "nc" not in _NC_CACHE:
        _NC_CACHE["nc"] = build()
    nc = _NC_CACHE["nc"]
    in_maps = []
    for cidx in range(8):
        ps_, pq_ = cidx // 4, cidx % 4
        ss_, sh_ = cidx // 2, cidx % 2
        xpc = xp[ps_, pq_ * 2048:(pq_ + 1) * 2048]
        xsc = xsm[ss_, sh_ * 2048:(sh_ + 1) * 2048]
        xc = np.concatenate([xpc, xsc], 0)
        xt = xc.reshape(NT, T, KC, 128).transpose(0, 3, 2, 1)
        halo = np.zeros((4, D), np.float32)
        if pq_ > 0:
            halo[0] = xp[ps_, pq_ * 2048 - 1]
        if pq_ < 3:
            halo[1] = xp[ps_, (pq_ + 1) * 2048]
        if sh_ > 0:
            halo[2] = xsm[ss_, sh_ * 2048 - 1]
        if sh_ < 1:
            halo[3] = xsm[ss_, (sh_ + 1) * 2048]
        hl = halo.reshape(4, KC, 128).transpose(2, 1, 0)
        memc = np.stack([mp[ps_], ms[ss_]], 0)
        memt = memc.reshape(2, 256, KC, 128).transpose(0, 3, 2, 1)
        pos = np.concatenate([np.arange(pq_ * 2048, (pq_ + 1) * 2048), np.arange(sh_ * 2048, (sh_ + 1) * 2048)])
        m = {
            "xT": np.ascontiguousarray(xt).reshape(NT, 128, KC * T),
            "xh": np.ascontiguousarray(hl).reshape(128, KC * 4),
            "memT": np.ascontiguousarray(memt).reshape(2, 128, KC * 256),
            "ropeT": _rope_table(pos),
        }
        m.update(W)
        in_maps.append(m)
    res = run_bass_kernel_spmd(nc, in_maps, core_ids=list(range(8)))
    yp = np.empty_like(xp)
    ysm = np.empty_like(xsm)
    for cidx in range(8):
        ps_, pq_ = cidx // 4, cidx % 4
        ss_, sh_ = cidx // 2, cidx % 2
        y = np.asarray(res.results[cidx]["yT"]).reshape(NT, 128, KC, T).transpose(0, 3, 2, 1).reshape(NT * T, D)
        yp[ps_, pq_ * 2048:(pq_ + 1) * 2048] = y[:2048]
        ysm[ss_, sh_ * 2048:(sh_ + 1) * 2048] = y[2048:]
    return (yp, ysm)
```

```python
import numpy as np
from contextlib import ExitStack
import concourse.bass as bass
import concourse.mybir as mybir
from concourse.bass_utils import run_bass_kernel_spmd

F32 = mybir.dt.float32
BF16 = mybir.dt.bfloat16
I32 = mybir.dt.int32
AF = mybir.ActivationFunctionType
ALU = mybir.AluOpType

D = 1024
KC = 8
T = 512
NT = 8
FF = 2816
NJ = 22
EPS = 1e-6
SAME_ENG_SYNC = True
MAGIC = 1597463007.0

G_FFN1, G_MIX, G_FFN2, G_MEM, G_QL, G_KVL = 0, 8, 16, 24, 32, 35
G_MQ, G_MQS, G_MK, G_KR, G_KRS, G_XQ, G_XK, G_CONV = 37, 38, 39, 40, 41, 42, 43, 44
NG = 56


class Buf:
    __slots__ = ("w", "r", "name")

    def __init__(self, name=""):
        self.w = None
        self.r = {}
        self.name = name


class Eng:
    def __init__(self, e, sem, name, is_pe=False):
        self.e = e
        self.sem = sem
        self.n = 0
        self.seen = {}
        self.name = name
        self.is_pe = is_pe

    def wait(self, tok):
        if tok is None:
            return
        sem, val = tok
        if sem is self.sem and (self.is_pe or not SAME_ENG_SYNC):
            return
        k = id(sem)
        if self.seen.get(k, 0) >= val:
            return
        self.e.wait_ge(sem, val)
        self.seen[k] = val


class Prog:
    def __init__(self):
        self.nc = bass.Bass("TRN2", target_bir_lowering=False)
        self.es = ExitStack()
        nc = self.nc
        self.semcount = {}
        self.sems = []
        self.PE = Eng(nc.tensor, self.sem("pe"), "pe", is_pe=True)
        self.ACT = Eng(nc.scalar, self.sem("act"), "act")
        self.DVE = Eng(nc.vector, self.sem("dve"), "dve")
        self.POOL = Eng(nc.gpsimd, self.sem("pool"), "pool")
        self.SP = Eng(nc.sync, self.sem("sp"), "sp")
        self.engs = [self.PE, self.ACT, self.DVE, self.POOL, self.SP]
        self.bank_rr = 0

    def sem(self, name):
        s = self.es.enter_context(self.nc.semaphore(f"{name}_n{len(self.sems)}"))
        self.semcount[id(s)] = 0
        self.sems.append(s)
        return s

    def deps(self, E, reads, writes):
        for b in reads:
            E.wait(b.w)
        for b in writes:
            E.wait(b.w)
            for t in list(b.r.values()):
                E.wait(t)

    def done(self, tok, reads, writes):
        for b in reads:
            b.r[id(tok[0])] = tok
        for b in writes:
            b.w = tok
            b.r = {}

    def op(self, E, fn, reads=(), writes=()):
        self.deps(E, reads, writes)
        ins = fn()
        E.n += 1
        ins.then_inc(E.sem, 1)
        self.semcount[id(E.sem)] = E.n
        self.done((E.sem, E.n), reads, writes)

    def mm(self, fns, reads, writes):
        E = self.PE
        self.deps(E, reads, writes)
        ins = None
        for f in fns:
            ins = f()
        E.n += 1
        ins.then_inc(E.sem, 1)
        self.semcount[id(E.sem)] = E.n
        self.done((E.sem, E.n), reads, writes)

    def mmf(self, items, reads, writes):
        E = self.PE
        self.deps(E, reads, writes)
        allr = list(reads)
        ins = None
        for f, rb in items:
            for b in rb:
                E.wait(b.w)
            allr += rb
            ins = f()
        E.n += 1
        ins.then_inc(E.sem, 1)
        self.semcount[id(E.sem)] = E.n
        self.done((E.sem, E.n), allr, writes)

    def dma(self, Q, out, in_, sem, reads=(), writes=(), **kw):
        self.deps(Q, reads, writes)
        Q.e.dma_start(out=out, in_=in_, **kw).then_inc(sem, 16)
        self.semcount[id(sem)] += 16
        self.done((sem, self.semcount[id(sem)]), reads, writes)

    def barrier(self):
        for E in self.engs:
            for s in self.sems:
                c = self.semcount[id(s)]
                if c > 0:
                    E.wait((s, c)) if s is not E.sem else None


class Stream:
    def __init__(self, P, es, name, width, nslots, srcs):
        self.P = P
        self.srcs = srcs
        self.n = nslots
        self.slots = [es.enter_context(P.nc.sbuf_tensor(f"{name}_s{i}_{len(P.sems)}", [128, width], BF16)) for i in range(nslots)]
        self.bufs = [Buf(f"{name}{i}") for i in range(nslots)]
        self.sems = [P.sem(f"{name}_q{i}") for i in range(nslots)]
        self.issued = 0
        self.pos = 0

    def _issue(self):
        i = self.issued
        s = i % self.n
        ap, db = self.srcs[i]
        self.P.dma(self.P.SP, self.slots[s][:, :], ap, self.sems[s], reads=[db], writes=[self.bufs[s]])
        self.issued += 1

    def next(self):
        i = self.pos
        while self.issued < min(i + self.n, len(self.srcs)):
            self._issue()
        self.pos += 1
        s = i % self.n
        return self.slots[s], self.bufs[s]


import os as _os
STOP = float(_os.environ.get("KSTOP", "9"))
KSUB = int(_os.environ.get("KSUB", "99"))
KATT = int(_os.environ.get("KATT", "99"))
KK = int(_os.environ.get("KK", "99"))


class _Stop(Exception):
    pass


def build():
    stacks = []
    P = Prog()
    try:
        _build(P, stacks)
    except _Stop:
        P.barrier()
        for st in reversed(stacks):
            st.close()
        P.es.close()
    return P.nc


def _build(P, stacks):
    nc = P.nc
    es = P.es
    PE, ACT, DVE, POOL, SP = P.PE, P.ACT, P.DVE, P.POOL, P.SP

    def din(name, shape, dt=F32):
        return nc.dram_tensor(name, shape, dt, kind="ExternalInput")

    xT = din("xT", [NT, 128, KC * T])
    xh = din("xh", [128, KC * 4])
    memT = din("memT", [2, 128, KC * 256])
    gains_d = din("gains", [128, NG])
    rope_d = din("ropeT", [128, 2, NT * T])
    wsh = {
        "w1gu": [NJ, 128, KC * 256], "w1d": [8, 128, NJ * 128],
        "w2gu": [NJ, 128, KC * 256], "w2d": [8, 128, NJ * 128],
        "wina": [14, 128, KC * 128], "winb": [32, 128, KC * 128],
        "wuq": [8, 128, 3 * 192], "wuk": [1, 128, 2 * 512], "wuv": [1, 128, 2 * 512],
        "wo3": [8, 128, 16 * 128], "wout": [8, 128, KC * 128],
        "wmk": [4, 128, KC * 128], "wmv": [1, 128, KC * 512],
    }
    wf = {k: din(k, v) for k, v in wsh.items()}
    wb = {k: nc.dram_tensor(k + "_b", v, BF16) for k, v in wsh.items()}
    wbuf = {k: Buf(k) for k in wsh}
    yT = nc.dram_tensor("yT", [NT, 128, KC * T], F32, kind="ExternalOutput")

    x1s = nc.dram_tensor("x1s", [NT, 128, KC * T], F32)
    x1s_buf = [Buf(f"x1s{i}") for i in range(NT)]
    Us = nc.dram_tensor("Us", [128, 4, 2, 2048], BF16)
    Us_buf = Buf("Us")
    AOs = nc.dram_tensor("AOs", [64, 8, NT * T], BF16)
    AOs_buf = Buf("AOs")
    LROWS = 320
    latP = [nc.dram_tensor(f"latP{i}", [LROWS, 1024], BF16) for i in range(2)]
    latS = [nc.dram_tensor(f"latS{i}", [LROWS, 1024], BF16) for i in range(2)]
    gatP = [nc.dram_tensor(f"gatP{i}", [4 * LROWS, 1024], BF16) for i in range(2)]
    gatS = [nc.dram_tensor(f"gatS{i}", [2 * LROWS, 1024], BF16) for i in range(2)]
    lat_buf = [Buf("latP"), Buf("latS")]
    gat_buf = [Buf("gatP"), Buf("gatS")]

    uniq = [0]

    def sb(stack, name, shape, dt):
        uniq[0] += 1
        return stack.enter_context(nc.sbuf_tensor(f"{name}_u{uniq[0]}", shape, dt))

    def emit_cast(k):
        s = P.sem("c_" + k)
        n0, _, wd = wsh[k]
        bb = max(d_ for d_ in range(1, 1025) if wd % d_ == 0)
        src = wf[k].ap().rearrange("n p (a b) -> (n p a) b", b=bb)
        dst = wb[k].ap().rearrange("n p (a b) -> (n p a) b", b=bb)
        rows = src.shape[0]
        step = 4096
        for r0 in range(0, rows, step):
            r1 = min(rows, r0 + step)
            P.dma(POOL, dst[r0:r1, :], src[r0:r1, :], s, writes=[wbuf[k]] if r0 + step >= rows else [], max_dma_last_dim=4096)

    for k in ["w1gu", "w1d", "wina"]:
        emit_cast(k)
    late_casts = {1: ["wmk", "wmv", "wuq", "wuk", "wuv"], 2: ["winb"], 3: ["wo3", "wout"], 4: ["w2gu"], 5: ["w2d"]}

    if STOP <= 0.1:
        raise _Stop()
    gains = sb(es, "gains_sb", [128, NG], F32)
    gains_b = Buf("gains")
    ONESB = sb(es, "onesb", [128, 128], BF16)
    ONESF = sb(es, "onesf", [128, 64], F32)
    ones_b = Buf("ones")
    s_misc = P.sem("misc")
    P.dma(SP, gains[:, :], gains_d.ap(), s_misc, writes=[gains_b])
    P.op(DVE, lambda: nc.vector.memset(ONESB[:, :], 1.0), writes=[ones_b])
    P.op(DVE, lambda: nc.vector.memset(ONESF[:, :], 1.0), writes=[ones_b])
    MK = sb(es, "MK", [128, 2, 4, 256], BF16)
    MV = sb(es, "MV", [128, 2, 2, 512], BF16)
    MK_b, MV_b = Buf("MK"), Buf("MV")
    UH = sb(es, "UH", [128, 4, 4], BF16)
    UHb = Buf()

    psum = [es.enter_context(nc.psum_tensor(f"ps{i}", [128, 512], F32)) for i in range(8)]
    psb = [Buf(f"ps{i}") for i in range(8)]
    bank_pool = list(range(8))

    def bank():
        i = bank_pool[P.bank_rr % len(bank_pool)]
        P.bank_rr += 1
        return psum[i], psb[i]

    def g(col, p0=0, p1=128):
        return gains[p0:p1, col:col + 1]

    def rsqrt(tmp, ps_ap, ps_b, out_ap, out_b, inv_n, post=None):
        V, Y, TT = tmp["V"], tmp["Y"], tmp["T"]
        vb, yb, tb = tmp["Vb"], tmp["Yb"], tmp["Tb"]
        shp = ps_ap.shape
        np_, n = shp[0], shp[1]
        p0 = tmp.get("p0", 0)
        v = V[p0:p0 + np_, 0:n]
        y = Y[p0:p0 + np_, 0:n]
        t = TT[p0:p0 + np_, 0:n]
        P.op(DVE, lambda: nc.vector.tensor_scalar(out=v, in0=ps_ap, scalar1=inv_n, scalar2=EPS, op0=ALU.mult, op1=ALU.add),
             reads=[ps_b], writes=[vb])
        P.op(DVE, lambda: nc.vector.tensor_scalar(out=y.bitcast(I32), in0=v.bitcast(I32), scalar1=-0.5, scalar2=MAGIC,
                                                  op0=ALU.mult, op1=ALU.add), reads=[vb], writes=[yb])
        for it in range(2):
            P.op(DVE, lambda: nc.vector.tensor_tensor(out=t, in0=y, in1=y, op=ALU.mult), reads=[yb], writes=[tb])
            P.op(DVE, lambda: nc.vector.scalar_tensor_tensor(out=t, in0=t, scalar=-0.5, in1=v, op0=ALU.mult, op1=ALU.mult),
                 reads=[tb, vb], writes=[tb])
            last = it == 1
            o = out_ap if last else y
            ob = out_b if last else yb
            if last and post is not None:
                P.op(DVE, lambda: nc.vector.scalar_tensor_tensor(out=y, in0=t, scalar=1.5, in1=y, op0=ALU.add, op1=ALU.mult),
                     reads=[tb, yb], writes=[yb])
                P.op(DVE, lambda: nc.vector.tensor_scalar(out=o, in0=y, scalar1=post, scalar2=None, op0=ALU.mult),
                     reads=[yb], writes=[ob])
            else:
                P.op(DVE, lambda: nc.vector.scalar_tensor_tensor(out=o, in0=t, scalar=1.5, in1=y, op0=ALU.add, op1=ALU.mult),
                     reads=[tb, yb], writes=[ob])

    def mk_tmp(stack, tag, n=T):
        return {"V": sb(stack, "tV" + tag, [128, n], F32), "Y": sb(stack, "tY" + tag, [128, n], F32),
                "T": sb(stack, "tT" + tag, [128, n], F32), "Vb": Buf(), "Yb": Buf(), "Tb": Buf()}

    class TileCtx:
        pass

    def alloc_tile_ctx(stack):
        c = TileCtx()
        c.X = [sb(stack, f"X{i}", [128, KC, T], F32) for i in range(2)]
        c.Xb = [[Buf(f"X{i}_{k}") for k in range(KC)] for i in range(2)]
        c.Xsem = [P.sem(f"X{i}") for i in range(2)]
        c.XsemS = [P.sem(f"XS{i}") for i in range(2)]
        c.XNs = [sb(stack, f"XN{i}", [128, KC, T], BF16) for i in range(2)]
        c.XNbs = [[Buf(f"XN{i}_{k}") for k in range(KC)] for i in range(2)]
        c.XN = c.XNs[0]
        c.XNb = c.XNbs[0]
        c.SQ = sb(stack, "SQn", [128, KC, T], BF16)
        c.SQb = [Buf(f"SQ{k}") for k in range(KC)]
        c.H = sb(stack, "H", [128, NJ, T], BF16)
        c.Hb = [Buf(f"H{j}") for j in range(NJ)]
        c.SG = [sb(stack, f"SG{i}", [128, T], F32) for i in range(2)]
        c.SGb = [Buf(), Buf()]
        c.RINV = sb(stack, "RINV", [128, T], F32)
        c.RINVb = Buf("rinv")
        c.tmp = mk_tmp(stack, "a")
        return c

    def norm_to_xn(c, xs, n, gcol, xi=0):
        X, Xb = c.X[xs], c.Xb[xs]
        XN, XNb = c.XNs[xi], c.XNbs[xi]
        for k0 in range(0, KC, 4):
            P.op(ACT, lambda k0=k0: nc.scalar.activation(out=c.SQ[:, k0:k0 + 4, 0:n], in_=X[:, k0:k0 + 4, 0:n], func=AF.Square),
                 reads=Xb[k0:k0 + 4], writes=c.SQb[k0:k0 + 4])
        ps, pb = bank()
        P.mmf([(lambda k=k: nc.tensor.matmul(ps[:, 0:n], lhsT=ONESB[:, :], rhs=c.SQ[:, k, 0:n], start=(k == 0), stop=(k == KC - 1)), [c.SQb[k]])
               for k in range(KC)], reads=[ones_b], writes=[pb])
        rsqrt(c.tmp, ps[:, 0:n], pb, c.RINV[:, 0:n], c.RINVb, 1.0 / D)
        for k in range(KC):
            P.op(DVE, lambda k=k: nc.vector.scalar_tensor_tensor(out=XN[:, k, 0:n], in0=X[:, k, 0:n], scalar=g(gcol + k),
                                                                 in1=c.RINV[:, 0:n], op0=ALU.mult, op1=ALU.mult),
                 reads=[Xb[k], c.RINVb, gains_b], writes=[XNb[k]])

    def ffn(c, xs, n, gcol, gu_stream, d_stream, xi=0, do_norm=True, mid_hook=None):
        X, Xb = c.X[xs], c.Xb[xs]
        XN, XNb = c.XNs[xi], c.XNbs[xi]
        if do_norm:
            norm_to_xn(c, xs, n, gcol, xi)
        for j in range(NJ):
            wt, wtb = gu_stream.next()
            w3 = wt[:, :].rearrange("p (k c) -> p k c", c=256)
            pg, pgb = bank()
            pu, pub = bank()
            P.mmf([(lambda k=k: nc.tensor.matmul(pg[:, 0:n], lhsT=w3[:, k, 0:128], rhs=XN[:, k, 0:n], start=(k == 0), stop=(k == KC - 1)), [XNb[k]])
                   for k in range(KC)], reads=[wtb], writes=[pgb])
            P.mmf([(lambda k=k: nc.tensor.matmul(pu[:, 0:n], lhsT=w3[:, k, 128:256], rhs=XN[:, k, 0:n], start=(k == 0), stop=(k == KC - 1)), [XNb[k]])
                   for k in range(KC)], reads=[wtb], writes=[pub])
            sg, sgb = c.SG[j % 2], c.SGb[j % 2]
            P.op(ACT, lambda: nc.scalar.activation(out=sg[:, 0:n], in_=pg[:, 0:n], func=AF.Silu), reads=[pgb], writes=[sgb])
            P.op(DVE, lambda: nc.vector.tensor_tensor(out=c.H[:, j, 0:n], in0=sg[:, 0:n], in1=pu[:, 0:n], op=ALU.mult),
                 reads=[sgb, pub], writes=[c.Hb[j]])
        if mid_hook is not None:
            mid_hook()
        for fc in range(KC):
            wt, wtb = d_stream.next()
            w3 = wt[:, :].rearrange("p (j c) -> p j c", c=128)
            pd, pdb = bank()
            P.mmf([(lambda j=j: nc.tensor.matmul(pd[:, 0:n], lhsT=w3[:, j, :], rhs=c.H[:, j, 0:n], start=(j == 0), stop=(j == NJ - 1)), [c.Hb[j]])
                   for j in range(NJ)], reads=[wtb], writes=[pdb])
            P.op(DVE, lambda: nc.vector.scalar_tensor_tensor(out=X[:, fc, 0:n], in0=pd[:, 0:n], scalar=0.5, in1=X[:, fc, 0:n],
                                                             op0=ALU.mult, op1=ALU.add), reads=[pdb, Xb[fc]], writes=[Xb[fc]])

    def proj(c, blk_stream, n, xi=0):
        wt, wtb = blk_stream.next()
        w3 = wt[:, :].rearrange("p (k c) -> p k c", c=128)
        ps, pb = bank()
        P.mmf([(lambda k=k: nc.tensor.matmul(ps[:, 0:n], lhsT=w3[:, k, :], rhs=c.XNs[xi][:, k, 0:n], start=(k == 0), stop=(k == KC - 1)), [c.XNbs[xi][k]])
               for k in range(KC)], reads=[wtb], writes=[pb])
        return ps, pb

    esA = ExitStack()
    stacks.append(esA)
    CQ = sb(esA, "CQ", [128, 3, NT * T], BF16)
    CQb = [Buf(f"CQ{i}") for i in range(NT)]
    es1 = ExitStack()
    stacks.append(es1)
    c = alloc_tile_ctx(es1)
    order1 = (["h"] if not _os.environ.get("KSKIPH") else []) + list(range(NT))
    gu_srcs, d_srcs, blk_srcs = [], [], []
    for ti in order1:
        gu_srcs += [(wb["w1gu"][j], wbuf["w1gu"]) for j in range(NJ)]
        d_srcs += [(wb["w1d"][f], wbuf["w1d"]) for f in range(8)]
        if ti == "h":
            blk_srcs += [(wb["wina"][m], wbuf["wina"]) for m in range(6, 14)]
        else:
            blk_srcs += [(wb["wina"][m], wbuf["wina"]) for m in range(14)]
    mem_blk = []
    for ctx in range(2):
        mem_blk += [(wb["wmk"][h], wbuf["wmk"]) for h in range(4)]
    blk_srcs = blk_srcs + mem_blk
    gu1 = Stream(P, es1, "gu", KC * 256, 3, gu_srcs)
    d1 = Stream(P, es1, "wd", NJ * 128, 2, d_srcs)
    blk1 = Stream(P, es1, "blk", KC * 128, 4, blk_srcs)
    RAW = sb(es1, "RAW", [128, 3, T], F32)
    RAWb = [Buf() for _ in range(3)]
    SQ3 = sb(es1, "SQ3", [128, 3, T], BF16)
    SQ3b = [Buf() for _ in range(3)]
    RV2 = c.RINV
    RV2b = c.RINVb
    tmp2 = c.tmp
    CKVN = sb(es1, "CKVN", [128, 2, T], BF16)
    CKVNb = Buf()
    s_ckvn = P.sem("ckvn")
    KRO = sb(es1, "KRO", [32, 2, T], BF16)
    KROb = Buf()
    s_kro = P.sem("kro")
    KA = sb(es1, "KA", [32, T], F32)
    KB = sb(es1, "KB", [32, T], F32)
    KAb, KBb = Buf(), Buf()
    ROPE = sb(es1, "ROPE", [32, 2, T], F32)
    ROPEb = Buf()
    s_rope = P.sem("rope")
    UT = sb(es1, "UT", [128, 4, T], BF16)
    UTb = Buf()
    s_ut = P.sem("ut")
    CCT = sb(es1, "CCT", [128, T], F32)
    CCTb = Buf()
    if STOP <= 0.3:
        raise _Stop()
    for idx, ti in enumerate(order1):
        if (STOP <= 0.4 and idx == 1) or (STOP <= 0.5 and idx == 2):
            raise _Stop()
        halo = ti == "h"
        n = 128 if halo else T
        xs = idx % 2
        X, Xb = c.X[xs], c.Xb[xs]
        def load_x(idx_, ti_):
            xs_ = idx_ % 2
            if ti_ == "h":
                P.op(DVE, lambda: nc.vector.memset(c.X[xs_][:, :, 0:128], 0.0), writes=c.Xb[xs_])
                P.dma(SP, c.X[xs_][:, :, 0:4], xh.ap().rearrange("p (k t) -> p k t", t=4), c.Xsem[xs_], writes=c.Xb[xs_])
            else:
                P.dma(SP, c.X[xs_][:, :, :], xT[ti_].rearrange("p (k t) -> p k t", t=T), c.Xsem[xs_], writes=c.Xb[xs_])
        if idx == 0:
            load_x(0, order1[0])
        if idx + 1 < len(order1):
            load_x(idx + 1, order1[idx + 1])
        if not halo:
            P.dma(SP, ROPE[:, :, :], rope_d[0:32, :, ti * T:(ti + 1) * T], s_rope, writes=[ROPEb])
        xi = idx % 2
        if idx == 0:
            norm_to_xn(c, xs, n, G_FFN1, xi)

        def hook1(idx=idx):
            if idx + 1 < len(order1):
                norm_to_xn(c, (idx + 1) % 2, T, G_FFN1, (idx + 1) % 2)
        ffn(c, xs, n, G_FFN1, gu1, d1, xi=xi, do_norm=False, mid_hook=hook1)
        if KSUB <= 4:
            raise _Stop()
        if not halo:
            P.dma(POOL, x1s[ti].rearrange("p (k t) -> p k t", t=T), X[:, :, :], c.XsemS[xs], reads=Xb, writes=[x1s_buf[ti]])
        for kk in late_casts.get(idx, []):
            emit_cast(kk)
        if KSUB <= 5:
            raise _Stop()
        norm_to_xn(c, xs, n, G_MIX, xi)
        if KSUB <= 6:
            raise _Stop()
        if not halo:
            ch, tl = ti // 4, ti % 4
            for i in range(3):
                ps, pb = proj(c, blk1, n, xi)
                P.op(ACT, lambda: nc.scalar.activation(out=SQ3[:, i, :], in_=ps[:, :], func=AF.Square), reads=[pb], writes=[SQ3b[i]])
                P.op(ACT, lambda: nc.scalar.activation(out=RAW[:, i, :], in_=ps[:, :], func=AF.Copy), reads=[pb], writes=[RAWb[i]])
            p2, p2b = bank()
            P.mm([lambda i=i: nc.tensor.matmul(p2[:, :], lhsT=ONESB[:, :], rhs=SQ3[:, i, :], start=(i == 0), stop=(i == 2)) for i in range(3)],
                 reads=[ones_b] + SQ3b, writes=[p2b])
            rsqrt(tmp2, p2[:, :], p2b, RV2[:, :], RV2b, 1.0 / 384)
            for i in range(3):
                P.op(DVE, lambda i=i: nc.vector.scalar_tensor_tensor(out=CQ[:, i, ti * T:(ti + 1) * T], in0=RAW[:, i, :], scalar=g(G_QL + i),
                                                                     in1=RV2[:, :], op0=ALU.mult, op1=ALU.mult),
                     reads=[RAWb[i], RV2b, gains_b], writes=[CQb[ti]])
            if KSUB <= 7:
                raise _Stop()
            for i in range(2):
                ps, pb = proj(c, blk1, n, xi)
                P.op(ACT, lambda: nc.scalar.activation(out=SQ3[:, i, :], in_=ps[:, :], func=AF.Square), reads=[pb], writes=[SQ3b[i]])
                P.op(ACT, lambda: nc.scalar.activation(out=RAW[:, i, :], in_=ps[:, :], func=AF.Copy), reads=[pb], writes=[RAWb[i]])
            p2, p2b = bank()
            P.mm([lambda i=i: nc.tensor.matmul(p2[:, :], lhsT=ONESB[:, :], rhs=SQ3[:, i, :], start=(i == 0), stop=(i == 1)) for i in range(2)],
                 reads=[ones_b] + SQ3b[0:2], writes=[p2b])
            rsqrt(tmp2, p2[:, :], p2b, RV2[:, :], RV2b, 1.0 / 256)
            for i in range(2):
                P.op(DVE, lambda i=i: nc.vector.scalar_tensor_tensor(out=CKVN[:, i, :], in0=RAW[:, i, :], scalar=g(G_KVL + i),
                                                                     in1=RV2[:, :], op0=ALU.mult, op1=ALU.mult),
                     reads=[RAWb[i], RV2b, gains_b], writes=[CKVNb])
            lat = [latP, latS][ch][tl // 2]
            tl2 = tl % 2
            P.dma(POOL, lat[0:256, tl2 * T:(tl2 + 1) * T].rearrange("(k p) t -> p k t", p=128), CKVN[:, :, :], s_ckvn,
                  reads=[CKVNb], writes=[lat_buf[ch]])
            if KSUB <= 8:
                raise _Stop()
            wt, wtb = blk1.next()
            w3 = wt[:, :].rearrange("p (k c) -> p k c", c=128)
            pk, pkb = bank()
            pq, pqb = bank()
            P.mm([lambda k=k: nc.tensor.matmul(pk[0:32, :], lhsT=w3[:, k, 0:32], rhs=c.XNs[xi][:, k, :], start=(k == 0), stop=(k == KC - 1))
                  for k in range(KC)], reads=[wtb] + c.XNbs[xi], writes=[pkb])
            P.mm([lambda k=k: nc.tensor.matmul(pq[0:32, :], lhsT=w3[:, k, 32:64], rhs=c.XNs[xi][:, k, :], start=(k == 0), stop=(k == KC - 1))
                  for k in range(KC)], reads=[wtb] + c.XNbs[xi], writes=[pqb])
            P.op(ACT, lambda: nc.scalar.activation(out=KRO[:, 1, :], in_=pk[0:32, :], func=AF.Copy), reads=[pkb], writes=[KROb])
            P.op(DVE, lambda: nc.vector.scalar_tensor_tensor(out=KA[:, :], in0=pk[0:32, :], scalar=g(G_KR, 0, 32), in1=ROPE[:, 0, :],
                                                             op0=ALU.mult, op1=ALU.mult), reads=[pkb, ROPEb, gains_b, KROb], writes=[KAb])
            P.op(DVE, lambda: nc.vector.scalar_tensor_tensor(out=KB[:, :], in0=pq[0:32, :], scalar=g(G_KRS, 0, 32), in1=ROPE[:, 1, :],
                                                             op0=ALU.mult, op1=ALU.mult), reads=[pqb, ROPEb, gains_b], writes=[KBb])
            P.op(DVE, lambda: nc.vector.tensor_tensor(out=KRO[:, 0, :], in0=KA[:, :], in1=KB[:, :], op=ALU.add),
                 reads=[KAb, KBb], writes=[KROb])
            P.dma(POOL, lat[256:320, tl2 * T:(tl2 + 1) * T].rearrange("(a p) t -> p a t", p=32), KRO[:, :, :], s_kro,
                  reads=[KROb], writes=[lat_buf[ch]])
        if KSUB <= 9:
            raise _Stop()
        pcc = []
        for i in range(4):
            pcc.append(proj(c, blk1, n, xi))
            if i >= 1:
                pass
        for i in range(4):
            pc, pcb = pcc[i]
            px, pxb = proj(c, blk1, n, xi)
            P.op(ACT, lambda: nc.scalar.activation(out=CCT[:, 0:n], in_=pc[:, 0:n], func=AF.Copy), reads=[pcb], writes=[CCTb])
            if halo:
                P.op(DVE, lambda: nc.vector.tensor_tensor(out=UH[:, i, :], in0=CCT[:, 0:4], in1=px[:, 0:4], op=ALU.mult),
                     reads=[CCTb, pxb], writes=[UHb])
            else:
                P.op(DVE, lambda: nc.vector.tensor_tensor(out=UT[:, i, :], in0=CCT[:, :], in1=px[:, :], op=ALU.mult),
                     reads=[CCTb, pxb], writes=[UTb])
        if not halo:
            P.dma(POOL, Us[:, :, ch, tl * T:(tl + 1) * T], UT[:, :, :], s_ut, reads=[UTb], writes=[Us_buf])

    if STOP <= 1:
        raise _Stop()
    s_cc = P.sem("cc")
    P.deps(POOL, [lat_buf[0], lat_buf[1]], [gat_buf[0], gat_buf[1]])
    for hf in range(2):
        nc.gpsimd.collective_compute("AllGather", ALU.bypass, replica_groups=[[0, 1, 2, 3], [4, 5, 6, 7]],
                                     ins=[latP[hf].ap().opt()], outs=[gatP[hf].ap().opt()]).then_inc(s_cc)
        P.semcount[id(s_cc)] += 1
    gat_buf[0].w = (s_cc, P.semcount[id(s_cc)])
    for hf in range(2):
        nc.gpsimd.collective_compute("AllGather", ALU.bypass, replica_groups=[[0, 1], [2, 3], [4, 5], [6, 7]],
                                     ins=[latS[hf].ap().opt()], outs=[gatS[hf].ap().opt()]).then_inc(s_cc)
        P.semcount[id(s_cc)] += 1
    gat_buf[1].w = (s_cc, P.semcount[id(s_cc)])

    s_wmv = P.sem("wmv")
    WMVbs = c.Hb[8:16]
    P.dma(SP, c.H[:, 8:16, :], wb["wmv"][0].rearrange("p (k c) -> p k c", c=512), s_wmv, reads=[wbuf["wmv"]], writes=WMVbs)

    for ctx in range(2):
        MEMX = c.X[1][:, :, 0:256]
        MEMXb = c.Xb[1][0]
        P.dma(SP, MEMX, memT[ctx].rearrange("p (k m) -> p k m", m=256), c.Xsem[1], writes=c.Xb[1])
        for k in range(KC):
            P.op(ACT, lambda k=k: nc.scalar.activation(out=c.H[:, k, 0:256], in_=MEMX[:, k, :], func=AF.Square),
                 reads=[MEMXb], writes=[c.Hb[k]])
        ps, pb = bank()
        P.mm([lambda k=k: nc.tensor.matmul(ps[:, 0:256], lhsT=ONESB[:, :], rhs=c.H[:, k, 0:256], start=(k == 0), stop=(k == KC - 1))
              for k in range(KC)], reads=[ones_b] + c.Hb[0:KC], writes=[pb])
        rsqrt(c.tmp, ps[:, 0:256], pb, c.RINV[:, 0:256], c.RINVb, 1.0 / D)
        for k in range(KC):
            P.op(DVE, lambda k=k: nc.vector.scalar_tensor_tensor(out=c.XN[:, k, 0:256], in0=MEMX[:, k, :], scalar=g(G_MEM + k),
                                                                 in1=c.RINV[:, 0:256], op0=ALU.mult, op1=ALU.mult),
                 reads=[MEMXb, c.RINVb, gains_b], writes=[c.XNb[k]])
        for h in range(4):
            ps, pb = proj(c, blk1, 256)
            P.op(ACT, lambda: nc.scalar.activation(out=SQ3[:, 0, 0:256], in_=ps[:, 0:256], func=AF.Square), reads=[pb], writes=[SQ3b[0]])
            P.op(ACT, lambda: nc.scalar.activation(out=RAW[:, 0, 0:256], in_=ps[:, 0:256], func=AF.Copy), reads=[pb], writes=[RAWb[0]])
            p2, p2b = bank()
            P.mm([lambda: nc.tensor.matmul(p2[:, 0:256], lhsT=ONESB[:, :], rhs=SQ3[:, 0, 0:256], start=True, stop=True)],
                 reads=[ones_b, SQ3b[0]], writes=[p2b])
            rsqrt(tmp2, p2[:, 0:256], p2b, RV2[:, 0:256], RV2b, 1.0 / 128)
            P.op(DVE, lambda: nc.vector.scalar_tensor_tensor(out=MK[:, ctx, h, :], in0=RAW[:, 0, 0:256], scalar=g(G_XK), in1=RV2[:, 0:256],
                                                             op0=ALU.mult, op1=ALU.mult), reads=[RAWb[0], RV2b, gains_b], writes=[MK_b])
        wmv3 = c.H[:, 8:16, :]
        for mc in range(2):
            ps, pb = bank()
            P.mm([lambda k=k: nc.tensor.matmul(ps[:, :], lhsT=c.XN[:, k, mc * 128:(mc + 1) * 128], rhs=wmv3[:, k, :],
                                               start=(k == 0), stop=(k == KC - 1)) for k in range(KC)],
                 reads=WMVbs + c.XNb, writes=[pb])
            P.op(ACT, lambda: nc.scalar.activation(out=MV[:, ctx, mc, :], in_=ps[:, :], func=AF.Copy), reads=[pb], writes=[MV_b])

    P.barrier()
    es1.close()
    stacks.pop()
    if STOP <= 2:
        raise _Stop()

    es2 = ExitStack()
    stacks.append(es2)
    SMAX = 8192
    CKV = sb(es2, "CKV", [128, 2, SMAX], BF16)
    CKVb = Buf()
    s_ckv = P.sem("ckv")
    KT = [sb(es2, f"KT{i}", [96, SMAX], BF16) for i in range(2)]
    KTb = [Buf(), Buf()]
    KTrb = [Buf(), Buf()]
    s_kt = [P.sem("kt0"), P.sem("kt1")]
    KRR = sb(es2, "KRR", [96, 2048], BF16)
    KRRb = Buf()
    s_krr = P.sem("krr")
    VG = sb(es2, "VG", [128, SMAX // 128, 4, 65], BF16)
    VGb = Buf()
    QT = [sb(es2, f"QT{i}", [96, T], BF16) for i in range(2)]
    QTb = [Buf(), Buf()]
    NPT = 4
    PT = [sb(es2, f"PT{i}", [128, T], BF16) for i in range(NPT)]
    PTb = [Buf() for _ in range(NPT)]
    SQK = [sb(es2, f"SQK{i}", [64, T], BF16) for i in range(2)]
    SQKb = [Buf(), Buf()]
    SQQ = sb(es2, "SQQ", [96, T], BF16)
    SQQb = Buf()
    tq = mk_tmp(es2, "q")
    RQ = sb(es2, "RQ", [96, T], F32)
    RQb = Buf()
    QA = sb(es2, "QA", [96, T], F32)
    QB = sb(es2, "QB", [96, T], F32)
    QAb, QBb = Buf(), Buf()
    ROQ = sb(es2, "ROQ", [96, 2, T], F32)
    ROQb = Buf()
    s_roq = P.sem("roq")
    KRSS = sb(es2, "KRSS", [128, 64], F32)
    KRSSb = Buf()
    RK = [sb(es2, f"RK{i}", [128, 64], F32) for i in range(2)]
    RKb = [Buf(), Buf()]
    tk = mk_tmp(es2, "k", 64)
    SSK = sb(es2, "SSK", [128, 64], F32)
    SSKb = Buf()
    REC = sb(es2, "REC", [65, T], F32)
    RECb = Buf()
    BC = sb(es2, "BC", [64, T], F32)
    BCb = Buf()
    AOT = [sb(es2, f"AOT{i}", [64, T], BF16) for i in range(2)]
    AOTb = [Buf(), Buf()]
    s_aot = [P.sem("aot0"), P.sem("aot1")]
    P.op(DVE, lambda: nc.vector.memset(VG[:, :, :, 64:65], 1.0), writes=[VGb])
    WUQ = sb(es2, "WUQ", [128, 8, 3 * 192], BF16)
    WUK = sb(es2, "WUK", [128, 2 * 512], BF16)
    WUV = sb(es2, "WUV", [128, 2 * 512], BF16)
    wsm_b = Buf("wsmall")
    s_ws = P.sem("wsm")
    P.dma(SP, WUQ[:, :, :], wb["wuq"].ap().rearrange("h p x -> p h x"), s_ws, reads=[wbuf["wuq"]], writes=[wsm_b])
    P.dma(SP, WUK[:, :], wb["wuk"][0], s_ws, reads=[wbuf["wuk"]], writes=[wsm_b])
    P.dma(SP, WUV[:, :], wb["wuv"][0], s_ws, reads=[wbuf["wuv"]], writes=[wsm_b])

    wuk3 = WUK[:, :].rearrange("p (k c) -> p k c", c=512)
    wuv3 = WUV[:, :].rearrange("p (k c) -> p k c", c=512)
    ST_BANKS = [0, 1, 2]
    O_BANKS = [3, 4]
    bank_pool[:] = [5, 6, 7]
    st_rr = 0
    o_rr = 0
    pt_rr = 0
    qt_rr = 0
    kt_rr = 0
    aot_rr = 0
    SCALE = 96.0 ** -0.5

    rr = {"st": 0, "o": 0, "pt": 0, "qt": 0, "aot": 0}
    pend_q = [None]
    pend_fin = [None]
    LOOK = 2

    def q_gen(h, ti):
        P.dma(SP, ROQ[64:96, :, :], rope_d[64:96, :, ti * T:(ti + 1) * T], s_roq, writes=[ROQb])
        pq, pqb = bank()
        pqs, pqsb = bank()
        P.mm([lambda k=k: nc.tensor.matmul(pq[0:96, :], lhsT=WUQ[:, h, k * 192:k * 192 + 96], rhs=CQ[:, k, ti * T:(ti + 1) * T],
                                           start=(k == 0), stop=(k == 2)) for k in range(3)],
             reads=[wsm_b, CQb[ti]], writes=[pqb])
        P.mm([lambda k=k: nc.tensor.matmul(pqs[0:96, :], lhsT=WUQ[:, h, k * 192 + 96:k * 192 + 192], rhs=CQ[:, k, ti * T:(ti + 1) * T],
                                           start=(k == 0), stop=(k == 2)) for k in range(3)],
             reads=[wsm_b, CQb[ti]], writes=[pqsb])
        P.op(ACT, lambda: nc.scalar.activation(out=SQQ[:, :], in_=pq[0:96, :], func=AF.Square), reads=[pqb], writes=[SQQb])
        p2, p2b = bank()
        P.mm([lambda: nc.tensor.matmul(p2[0:96, :], lhsT=ONESB[0:96, 0:96], rhs=SQQ[:, :], start=True, stop=True)],
             reads=[ones_b, SQQb], writes=[p2b])
        rsqrt(tq, p2[0:96, :], p2b, RQ[:, :], RQb, 1.0 / 96)
        qi = rr["qt"] % 2
        rr["qt"] += 1
        qtile, qtb = QT[qi], QTb[qi]
        P.op(DVE, lambda: nc.vector.scalar_tensor_tensor(out=qtile[0:64, :], in0=pq[0:64, :], scalar=g(G_MQ, 0, 64), in1=RQ[0:64, :],
                                                         op0=ALU.mult, op1=ALU.mult), reads=[pqb, RQb, gains_b], writes=[qtb])
        P.op(DVE, lambda: nc.vector.scalar_tensor_tensor(out=QA[64:96, :], in0=pq[64:96, :], scalar=g(G_MQ, 64, 96), in1=ROQ[64:96, 0, :],
                                                         op0=ALU.mult, op1=ALU.mult), reads=[pqb, ROQb, gains_b], writes=[QAb])
        P.op(DVE, lambda: nc.vector.scalar_tensor_tensor(out=QB[64:96, :], in0=pqs[64:96, :], scalar=g(G_MQS, 64, 96), in1=ROQ[64:96, 1, :],
                                                         op0=ALU.mult, op1=ALU.mult), reads=[pqsb, ROQb, gains_b], writes=[QBb])
        P.op(DVE, lambda: nc.vector.tensor_tensor(out=QA[64:96, :], in0=QA[64:96, :], in1=QB[64:96, :], op=ALU.add),
             reads=[QAb, QBb], writes=[QAb])
        P.op(DVE, lambda: nc.vector.tensor_tensor(out=qtile[64:96, :], in0=QA[64:96, :], in1=RQ[64:96, :], op=ALU.mult),
             reads=[QAb, RQb], writes=[qtb])
        return qtile, qtb

    def main_store(h, hh, ti, NCH, ktile, ktb, ki, rk, rkb, qtile, qtb):
        ob = O_BANKS[rr["o"] % 2]
        rr["o"] += 1
        po, pob = psum[ob], psb[ob]
        P.deps(PE, [], [pob])
        pend = []
        for step in range(NCH + LOOK):
            if step < NCH:
                cc = step
                sbk = ST_BANKS[rr["st"] % 3]
                rr["st"] += 1
                pst, pstb = psum[sbk], psb[sbk]
                P.mm([lambda: nc.tensor.matmul(pst[:, :], lhsT=ktile[0:96, cc * 128:(cc + 1) * 128], rhs=qtile[0:96, :], start=True, stop=True)],
                     reads=[ktb, KTrb[ki], qtb], writes=[pstb])
                pi = rr["pt"] % NPT
                rr["pt"] += 1
                P.op(ACT, lambda: nc.scalar.activation(out=PT[pi][:, :], in_=pst[:, :], func=AF.Exp, scale=rk[:, cc:cc + 1]),
                     reads=[pstb, rkb], writes=[PTb[pi]])
                pend.append((cc, pi))
            if step >= LOOK:
                cc, pi = pend.pop(0)
                last = cc == NCH - 1
                if not last:
                    PE.wait(PTb[pi].w)
                    PE.wait(VGb.w)
                    nc.tensor.matmul(po[0:65, :], lhsT=VG[:, cc, hh, :], rhs=PT[pi][:, :], start=(cc == 0), stop=False)
                    PTb[pi].r[id(PE.sem)] = (PE.sem, PE.n + 1)
                else:
                    P.mm([lambda: nc.tensor.matmul(po[0:65, :], lhsT=VG[:, cc, hh, :], rhs=PT[pi][:, :], start=(cc == 0), stop=True)],
                         reads=[PTb[pi], VGb], writes=[pob])
            if step == 22 and pend_fin[0] is not None:
                finish(*pend_fin[0])
                pend_fin[0] = None
        pend_fin[0] = (h, ti, po, pob)

    def finish(h, ti, po, pob):
        P.op(DVE, lambda: nc.vector.reciprocal(out=REC[64:65, :], in_=po[64:65, :]), reads=[pob], writes=[RECb])
        pbc, pbcb = bank()
        P.mm([lambda: nc.tensor.matmul(pbc[0:64, :], lhsT=ONESF[64:65, 0:64], rhs=REC[64:65, :], start=True, stop=True)],
             reads=[ones_b, RECb], writes=[pbcb])
        P.op(ACT, lambda: nc.scalar.activation(out=BC[:, :], in_=pbc[0:64, :], func=AF.Copy), reads=[pbcb], writes=[BCb])
        ai = rr["aot"] % 2
        rr["aot"] += 1
        P.op(DVE, lambda: nc.vector.tensor_tensor(out=AOT[ai][:, :], in0=po[0:64, :], in1=BC[:, :], op=ALU.mult),
             reads=[pob, BCb], writes=[AOTb[ai]])
        P.dma(POOL, AOs[:, h, ti * T:(ti + 1) * T], AOT[ai][:, :], s_aot[ai], reads=[AOTb[ai]], writes=[AOs_buf])

    for ctx in range(2):
        S = 8192 if ctx == 0 else 4096
        R = 4 if ctx == 0 else 2
        NCH = S // 128
        NKT = S // T
        gat = [gatP, gatS][ctx]
        gb = gat_buf[ctx]
        gvs = [gat[hf].ap().rearrange("(r x) t -> r x t", x=LROWS) for hf in range(2)]
        for r in range(R):
            for hf in range(2):
                c0 = r * 2048 + hf * 1024
                P.dma(SP, CKV[:, :, c0:c0 + 1024], gvs[hf][r, 0:256, :].rearrange("(k p) t -> p k t", p=128), s_ckv,
                      reads=[gb], writes=[CKVb])
                for i in range(2):
                    P.dma(SP, KT[i][64:96, c0:c0 + 1024], gvs[hf][r, 256:288, :], s_kt[i], reads=[gb], writes=[KTrb[i], KTb[i]])
        for r in range(R):
            for hf in range(2):
                P.dma(SP, KRR[64:96, hf * 1024:(hf + 1) * 1024], gvs[hf][r, 288:320, :], s_krr, reads=[gb], writes=[KRRb])
            for kt in range(4):
                P.op(ACT, lambda kt=kt: nc.scalar.activation(out=KRR[64:96, kt * T:(kt + 1) * T], in_=KRR[64:96, kt * T:(kt + 1) * T], func=AF.Square),
                     reads=[KRRb], writes=[KRRb])
            ps, pb = bank()
            fns = [lambda cc=cc: nc.tensor.matmul(ps[:, cc:cc + 1], lhsT=KRR[64:96, cc * 128:(cc + 1) * 128], rhs=ONESB[64:96, 0:1], start=True, stop=True)
                   for cc in range(16)]
            P.mm(fns, reads=[KRRb, ones_b], writes=[pb])
            P.op(DVE, lambda: nc.vector.tensor_copy(out=KRSS[:, r * 16:(r + 1) * 16], in_=ps[:, 0:16]), reads=[pb], writes=[KRSSb])

        if KATT <= 1:
            raise _Stop()
        for hg in range(2):
            for cc in range(NCH):
                ps, pb = bank()
                P.mm([lambda k=k: nc.tensor.matmul(ps[:, 0:256], lhsT=CKV[:, k, cc * 128:(cc + 1) * 128], rhs=wuv3[:, k, hg * 256:(hg + 1) * 256],
                                                   start=(k == 0), stop=(k == 1)) for k in range(2)],
                     reads=[CKVb, wsm_b], writes=[pb])
                eng = ACT if cc % 2 == 0 else DVE
                if eng is ACT:
                    P.op(ACT, lambda: nc.scalar.activation(out=VG[:, cc, :, 0:64], in_=ps[:, 0:256].rearrange("p (h d) -> p h d", d=64), func=AF.Copy),
                         reads=[pb], writes=[VGb])
                else:
                    P.op(DVE, lambda: nc.vector.tensor_copy(out=VG[:, cc, :, 0:64], in_=ps[:, 0:256].rearrange("p (h d) -> p h d", d=64)),
                         reads=[pb], writes=[VGb])
            if KATT <= 2:
                raise _Stop()
            for hh in range(4):
                h = hg * 4 + hh
                ki = kt_rr % 2
                kt_rr += 1
                ktile, ktb = KT[ki], KTb[ki]
                pss, pssb = psum[0], psb[0]
                def ss_mm(kt_):
                    sq_, sqb_ = SQK[kt_ % 2], SQKb[kt_ % 2]
                    P.mm([lambda a=a: nc.tensor.matmul(pss[:, kt_ * 4 + a:kt_ * 4 + a + 1], lhsT=sq_[0:64, a * 128:(a + 1) * 128], rhs=ONESB[0:64, 0:1],
                                                       start=True, stop=True) for a in range(4)],
                         reads=[sqb_, ones_b], writes=[pssb])
                prev_kt = None
                for kt in range(NKT):
                    ps, pb = bank()
                    P.mm([lambda k=k: nc.tensor.matmul(ps[0:64, :], lhsT=wuk3[:, k, h * 64:(h + 1) * 64], rhs=CKV[:, k, kt * T:(kt + 1) * T],
                                                       start=(k == 0), stop=(k == 1)) for k in range(2)],
                         reads=[CKVb, wsm_b], writes=[pb])
                    sq, sqb = SQK[kt % 2], SQKb[kt % 2]
                    P.op(ACT, lambda: nc.scalar.activation(out=sq[:, :], in_=ps[0:64, :], func=AF.Square), reads=[pb], writes=[sqb])
                    P.op(DVE, lambda: nc.vector.tensor_scalar(out=ktile[0:64, kt * T:(kt + 1) * T], in0=ps[0:64, :], scalar1=g(G_MK, 0, 64),
                                                              scalar2=None, op0=ALU.mult), reads=[pb, gains_b, sqb], writes=[ktb])
                    if prev_kt is not None:
                        ss_mm(prev_kt)
                    prev_kt = kt
                ss_mm(prev_kt)
                if KK >= 4:
                    P.op(DVE, lambda: nc.vector.tensor_tensor(out=SSK[:, 0:NCH], in0=pss[:, 0:NCH], in1=KRSS[:, 0:NCH], op=ALU.add),
                         reads=[pssb, KRSSb], writes=[SSKb])
                rk, rkb = RK[ki], RKb[ki]
                if KK >= 5:
                    rsqrt(tk, SSK[:, 0:NCH], SSKb, rk[:, 0:NCH], rkb, 1.0 / 96, post=SCALE)
                if KATT <= 3:
                    raise _Stop()
                for qt in range(4):
                    ti = ctx * 4 + qt
                    if pend_q[0] is None:
                        pend_q[0] = q_gen(h, ti)
                    qtile, qtb = pend_q[0]
                    nxt = None
                    if qt < 3:
                        nxt = (h, ti + 1)
                    elif h < 7:
                        nxt = (h + 1, ctx * 4)
                    pend_q[0] = q_gen(*nxt) if nxt is not None else None
                    main_store(h, hh, ti, NCH, ktile, ktb, ki, rk, rkb, qtile, qtb)
                if KATT <= 7:
                    raise _Stop()

    if pend_fin[0] is not None:
        finish(*pend_fin[0])
        pend_fin[0] = None
    P.barrier()
    es2.close()
    esA.close()
    stacks.pop()
    stacks.pop()
    if STOP <= 3:
        raise _Stop()
    bank_pool[:] = list(range(8))

    es3 = ExitStack()
    stacks.append(es3)
    c = alloc_tile_ctx(es3)
    gu_srcs, d_srcs, blk_srcs, wo_srcs = [], [], [], []
    for ti in range(NT):
        gu_srcs += [(wb["w2gu"][j], wbuf["w2gu"]) for j in range(NJ)]
        d_srcs += [(wb["w2d"][f], wbuf["w2d"]) for f in range(8)]
        blk_srcs += [(wb["winb"][m], wbuf["winb"]) for m in range(8)]
        for fc in range(8):
            blk_srcs += [(wb["winb"][8 + gg * 8 + fc], wbuf["winb"]) for gg in range(3)]
        blk_srcs += [(wb["wout"][f], wbuf["wout"]) for f in range(8)]
        wo_srcs += [(wb["wo3"][f], wbuf["wo3"]) for f in range(8)]
    gu3 = Stream(P, es3, "gu3", KC * 256, 3, gu_srcs)
    d3 = Stream(P, es3, "wd3", NJ * 128, 2, d_srcs)
    blk3 = Stream(P, es3, "blk3", KC * 128, 4, blk_srcs)
    wo3s = Stream(P, es3, "wo3", 16 * 128, 2, wo_srcs)
    UW = sb(es3, "UW", [128, 4, T + 2], BF16)
    UWb = Buf()
    s_uw = P.sem("uw")
    AOX = sb(es3, "AOX", [64, 8, T], BF16)
    AOXb = Buf()
    s_aox = P.sem("aox")
    CVT = sb(es3, "CVT", [128, T], F32)
    CVTb = Buf()
    CV = sb(es3, "CV", [128, 4, T], BF16)
    CVb = Buf()
    XQ = sb(es3, "XQ", [128, 4, T], BF16)
    XQb = Buf()
    RAWX = sb(es3, "RAWX", [128, T], F32)
    RAWXb = Buf()
    SQX = sb(es3, "SQX", [128, T], BF16)
    SQXb = Buf()
    RV3 = c.RINV
    RV3b = c.RINVb
    tmp3 = c.tmp
    XA = sb(es3, "XA", [128, 4, T], BF16)
    XAb = Buf()
    PM = [sb(es3, f"PM{i}", [128, T], BF16) for i in range(4)]
    PMb = [Buf() for _ in range(4)]
    RD = sb(es3, "RD", [128, T], F32)
    RDb = Buf()
    TH = [sb(es3, f"TH{i}", [128, T], F32) for i in range(3)]
    THb = [Buf() for _ in range(3)]
    M0 = sb(es3, "M0", [128, T], F32)
    M1 = sb(es3, "M1", [128, T], F32)
    M0b, M1b = Buf(), Buf()
    MG = sb(es3, "MG", [128, KC, T], BF16)
    MGb = [Buf() for _ in range(KC)]
    s_y = [P.sem("y0"), P.sem("y1")]
    XSC = 128.0 ** -0.5

    for ti in range(NT):
        ctx, tl = ti // 4, ti % 4
        xs = ti % 2
        X, Xb = c.X[xs], c.Xb[xs]
        def load_x3(ti_):
            xs_ = ti_ % 2
            P.dma(SP, c.X[xs_][:, :, :], x1s[ti_].rearrange("p (k t) -> p k t", t=T), c.Xsem[xs_], reads=[x1s_buf[ti_]], writes=c.Xb[xs_])
        if ti == 0:
            load_x3(0)
        if ti + 1 < NT:
            load_x3(ti + 1)
        lo = max(tl * T - 1, 0)
        hi = min(tl * T + T + 1, 2048)
        d0 = lo - (tl * T - 1)
        P.dma(SP, UW[:, :, d0:d0 + (hi - lo)], Us[:, :, ctx, lo:hi], s_uw, reads=[Us_buf], writes=[UWb])
        if tl == 0:
            P.op(DVE, lambda: nc.vector.tensor_copy(out=UW[:, :, 0:1], in_=UH[:, :, 2 * ctx:2 * ctx + 1]), reads=[UHb], writes=[UWb])
        if tl == 3:
            P.op(DVE, lambda: nc.vector.tensor_copy(out=UW[:, :, T + 1:T + 2], in_=UH[:, :, 2 * ctx + 1:2 * ctx + 2]), reads=[UHb], writes=[UWb])
        P.dma(SP, AOX[:, :, :], AOs[:, :, ti * T:(ti + 1) * T], s_aox, reads=[AOs_buf], writes=[AOXb])
        xi = ti % 2
        if ti == 0:
            norm_to_xn(c, xs, T, G_MIX, xi)
        for i in range(4):
            ps, pb = proj(c, blk3, T, xi)
            P.op(DVE, lambda: nc.vector.tensor_scalar(out=CVT[:, :], in0=UW[:, i, 0:T], scalar1=g(G_CONV + i * 3 + 0), scalar2=None, op0=ALU.mult),
                 reads=[UWb, gains_b], writes=[CVTb])
            P.op(DVE, lambda: nc.vector.scalar_tensor_tensor(out=CVT[:, :], in0=UW[:, i, 1:T + 1], scalar=g(G_CONV + i * 3 + 1), in1=CVT[:, :],
                                                             op0=ALU.mult, op1=ALU.add), reads=[UWb, CVTb, gains_b], writes=[CVTb])
            P.op(DVE, lambda: nc.vector.scalar_tensor_tensor(out=CVT[:, :], in0=UW[:, i, 2:T + 2], scalar=g(G_CONV + i * 3 + 2), in1=CVT[:, :],
                                                             op0=ALU.mult, op1=ALU.add), reads=[UWb, CVTb, gains_b], writes=[CVTb])
            P.op(DVE, lambda: nc.vector.tensor_tensor(out=CV[:, i, :], in0=CVT[:, :], in1=ps[:, :], op=ALU.mult),
                 reads=[CVTb, pb], writes=[CVb])
        for h in range(4):
            ps, pb = proj(c, blk3, T, xi)
            P.op(ACT, lambda: nc.scalar.activation(out=SQX[:, :], in_=ps[:, :], func=AF.Square), reads=[pb], writes=[SQXb])
            P.op(ACT, lambda: nc.scalar.activation(out=RAWX[:, :], in_=ps[:, :], func=AF.Copy), reads=[pb], writes=[RAWXb])
            p2, p2b = bank()
            P.mm([lambda: nc.tensor.matmul(p2[:, :], lhsT=ONESB[:, :], rhs=SQX[:, :], start=True, stop=True)], reads=[ones_b, SQXb], writes=[p2b])
            rsqrt(tmp3, p2[:, :], p2b, RV3[:, :], RV3b, 1.0 / 128)
            P.op(DVE, lambda: nc.vector.scalar_tensor_tensor(out=XQ[:, h, :], in0=RAWX[:, :], scalar=g(G_XQ), in1=RV3[:, :],
                                                             op0=ALU.mult, op1=ALU.mult), reads=[RAWXb, RV3b, gains_b], writes=[XQb])
        def xa_scores(h):
            for mc in range(2):
                ps, pb = bank()
                P.mm([lambda: nc.tensor.matmul(ps[:, :], lhsT=MK[:, ctx, h, mc * 128:(mc + 1) * 128], rhs=XQ[:, h, :], start=True, stop=True)],
                     reads=[MK_b, XQb], writes=[pb])
                pm, pmb = PM[(h % 2) * 2 + mc], PMb[(h % 2) * 2 + mc]
                P.op(ACT, lambda: nc.scalar.activation(out=pm[:, :], in_=ps[:, :], func=AF.Exp, scale=XSC), reads=[pb], writes=[pmb])

        def xa_pv(h):
            pms = [PM[(h % 2) * 2 + mc] for mc in range(2)]
            pmbs = [PMb[(h % 2) * 2 + mc] for mc in range(2)]
            po, pob = bank()
            pdn, pdnb = bank()
            P.mm([lambda mc=mc: nc.tensor.matmul(po[:, :], lhsT=MV[:, ctx, mc, h * 128:(h + 1) * 128], rhs=pms[mc][:, :], start=(mc == 0), stop=(mc == 1))
                  for mc in range(2)], reads=[MV_b] + pmbs, writes=[pob])
            P.mm([lambda mc=mc: nc.tensor.matmul(pdn[:, :], lhsT=ONESB[:, :], rhs=pms[mc][:, :], start=(mc == 0), stop=(mc == 1))
                  for mc in range(2)], reads=[ones_b] + pmbs, writes=[pdnb])
            P.op(DVE, lambda: nc.vector.reciprocal(out=RD[:, :], in_=pdn[:, :]), reads=[pdnb], writes=[RDb])
            P.op(DVE, lambda: nc.vector.tensor_tensor(out=XA[:, h, :], in0=po[:, :], in1=RD[:, :], op=ALU.mult), reads=[pob, RDb], writes=[XAb])

        xa_scores(0)
        for h in range(4):
            if h + 1 < 4:
                xa_scores(h + 1)
            xa_pv(h)
        for fc in range(KC):
            wt, wtb = wo3s.next()
            w3 = wt[:, :].rearrange("p (k c) -> p k c", c=128)
            ys = []
            py, pyb = bank()
            P.mm([lambda h=h: nc.tensor.matmul(py[:, :], lhsT=w3[0:64, h, :], rhs=AOX[:, h, :], start=(h == 0), stop=(h == 7)) for h in range(8)],
                 reads=[wtb, AOXb], writes=[pyb])
            ys.append((py, pyb))
            py, pyb = bank()
            P.mm([lambda i=i: nc.tensor.matmul(py[:, :], lhsT=w3[:, 8 + i, :], rhs=CV[:, i, :], start=(i == 0), stop=(i == 3)) for i in range(4)],
                 reads=[wtb, CVb], writes=[pyb])
            ys.append((py, pyb))
            py, pyb = bank()
            P.mm([lambda i=i: nc.tensor.matmul(py[:, :], lhsT=w3[:, 12 + i, :], rhs=XA[:, i, :], start=(i == 0), stop=(i == 3)) for i in range(4)],
                 reads=[wtb, XAb], writes=[pyb])
            ys.append((py, pyb))
            for gg in range(3):
                pg, pgb = proj(c, blk3, T, xi)
                P.op(ACT, lambda: nc.scalar.activation(out=TH[gg][:, :], in_=pg[:, :], func=AF.Tanh, scale=0.5), reads=[pgb], writes=[THb[gg]])
            P.op(DVE, lambda: nc.vector.scalar_tensor_tensor(out=M0[:, :], in0=TH[0][:, :], scalar=1.0, in1=ys[0][0][:, :], op0=ALU.add, op1=ALU.mult),
                 reads=[THb[0], ys[0][1]], writes=[M0b])
            P.op(DVE, lambda: nc.vector.scalar_tensor_tensor(out=M1[:, :], in0=TH[1][:, :], scalar=1.0, in1=ys[1][0][:, :], op0=ALU.add, op1=ALU.mult),
                 reads=[THb[1], ys[1][1]], writes=[M1b])
            P.op(DVE, lambda: nc.vector.tensor_tensor(out=M0[:, :], in0=M0[:, :], in1=M1[:, :], op=ALU.add), reads=[M0b, M1b], writes=[M0b])
            P.op(DVE, lambda: nc.vector.scalar_tensor_tensor(out=M1[:, :], in0=TH[2][:, :], scalar=1.0, in1=ys[2][0][:, :], op0=ALU.add, op1=ALU.mult),
                 reads=[THb[2], ys[2][1]], writes=[M1b])
            P.op(DVE, lambda: nc.vector.tensor_tensor(out=MG[:, fc, :], in0=M0[:, :], in1=M1[:, :], op=ALU.add), reads=[M0b, M1b], writes=[MGb[fc]])
        for fo in range(KC):
            wt, wtb = blk3.next()
            w3 = wt[:, :].rearrange("p (k c) -> p k c", c=128)
            ps, pb = bank()
            P.mm([lambda k=k: nc.tensor.matmul(ps[:, :], lhsT=w3[:, k, :], rhs=MG[:, k, :], start=(k == 0), stop=(k == KC - 1)) for k in range(KC)],
                 reads=[wtb] + MGb, writes=[pb])
            P.op(DVE, lambda: nc.vector.scalar_tensor_tensor(out=X[:, fo, :], in0=ps[:, :], scalar=0.5, in1=X[:, fo, :], op0=ALU.mult, op1=ALU.add),
                 reads=[pb, Xb[fo]], writes=[Xb[fo]])
        def hook3(ti=ti):
            if ti + 1 < NT:
                norm_to_xn(c, (ti + 1) % 2, T, G_MIX, (ti + 1) % 2)
        ffn(c, xs, T, G_FFN2, gu3, d3, xi=xi, do_norm=True, mid_hook=hook3)
        P.dma(POOL, yT[ti].rearrange("p (k t) -> p k t", t=T), X[:, :, :], c.XsemS[xs], reads=Xb, writes=[])

    P.barrier()
    es3.close()
    es.close()


def _kblocks(W, cols):
    K = W.shape[0]
    kc = K // 128
    out = np.empty((len(cols), 128, kc, 128), np.float32)
    Wr = W.reshape(kc, 128, W.shape[1])
    for m, c0 in enumerate(cols):
        out[m] = Wr[:, :, c0:c0 + 128].transpose(1, 0, 2)
    return out.reshape(len(cols), 128, kc * 128)


def _prep_weights(inp):
    f = lambda a: np.ascontiguousarray(np.asarray(a, np.float32))
    out = {}
    for tag, gu, dn in (("w1", "ffn1_w_gu", "ffn1_w_down"), ("w2", "ffn2_w_gu", "ffn2_w_down")):
        W = f(inp[gu][0]).reshape(KC, 128, 2 * FF)
        gate = W[:, :, :FF].reshape(KC, 128, NJ, 128)
        up = W[:, :, FF:].reshape(KC, 128, NJ, 128)
        st = np.stack([gate, up], axis=3)
        out[tag + "gu"] = np.ascontiguousarray(st.transpose(2, 1, 0, 3, 4)).reshape(NJ, 128, KC * 256)
        Wd = f(inp[dn][0]).reshape(NJ, 128, 8, 128)
        out[tag + "d"] = np.ascontiguousarray(Wd.transpose(2, 1, 0, 3)).reshape(8, 128, NJ * 128)
    Win = f(inp["w_in"][0])
    perm = np.concatenate([np.arange(16, 32), np.arange(0, 16)])
    kr = Win[:, 640:672]
    Wkr = np.concatenate([kr, kr[:, perm], np.zeros((D, 64), np.float32)], axis=1)
    Wa = np.concatenate([Win[:, 0:640], Wkr, Win[:, 1184:2208]], axis=1)
    out["wina"] = _kblocks(Wa, [i * 128 for i in range(14)])
    Wb = np.concatenate([Win[:, 672:1184], Win[:, 2208:2720], Win[:, 2720:5792]], axis=1)
    out["winb"] = _kblocks(Wb, [i * 128 for i in range(32)])
    Wuq = f(inp["w_uq"][0])
    wuq = np.empty((8, 128, 3, 192), np.float32)
    Wr = Wuq.reshape(3, 128, 768)
    for h in range(8):
        blk = Wr[:, :, h * 96:(h + 1) * 96]
        sw = np.concatenate([blk[:, :, :64], blk[:, :, 64:][:, :, perm]], axis=2)
        wuq[h] = np.concatenate([blk, sw], axis=2).transpose(1, 0, 2)
    out["wuq"] = wuq.reshape(8, 128, 3 * 192)
    out["wuk"] = np.ascontiguousarray(f(inp["w_uk"][0]).reshape(2, 128, 512).transpose(1, 0, 2)).reshape(1, 128, 1024)
    out["wuv"] = np.ascontiguousarray(f(inp["w_uv"][0]).reshape(2, 128, 512).transpose(1, 0, 2)).reshape(1, 128, 1024)
    wo3 = np.zeros((8, 128, 16, 128), np.float32)
    Wm = f(inp["w_o_mla"][0]).reshape(8, 64, 8, 128)
    Wc = f(inp["w_o_conv"][0]).reshape(4, 128, 8, 128)
    Wx = f(inp["w_o_mem"][0]).reshape(4, 128, 8, 128)
    wo3[:, 0:64, 0:8, :] = Wm.transpose(2, 1, 0, 3)
    wo3[:, :, 8:12, :] = Wc.transpose(2, 1, 0, 3)
    wo3[:, :, 12:16, :] = Wx.transpose(2, 1, 0, 3)
    out["wo3"] = wo3.reshape(8, 128, 16 * 128)
    out["wout"] = _kblocks(f(inp["w_out"][0]), [i * 128 for i in range(8)])
    Wmkv = f(inp["w_mem_kv"][0])
    out["wmk"] = _kblocks(Wmkv, [i * 128 for i in range(4)])
    out["wmv"] = np.ascontiguousarray(Wmkv[:, 512:].reshape(8, 128, 512).transpose(1, 0, 2)).reshape(1, 128, 8 * 512)
    G = np.zeros((128, NG), np.float32)
    def colk(v, c0):
        v = f(v).reshape(-1, 128)
        for k in range(v.shape[0]):
            G[:, c0 + k] = v[k]
    colk(inp["ffn1_norm"][0], G_FFN1)
    colk(inp["mix_norm"][0], G_MIX)
    colk(inp["ffn2_norm"][0], G_FFN2)
    colk(inp["mem_norm"][0], G_MEM)
    colk(inp["q_lora_norm"][0], G_QL)
    colk(inp["kv_lora_norm"][0], G_KVL)
    mq = f(inp["mla_q_norm"][0])
    mk = f(inp["mla_k_norm"][0])
    G[0:96, G_MQ] = mq
    G[0:64, G_MQS] = mq[:64]
    G[64:96, G_MQS] = mq[64:][perm]
    G[0:64, G_MK] = mk[:64]
    G[0:32, G_KR] = mk[64:]
    G[0:32, G_KRS] = mk[64:][perm]
    G[:, G_XQ] = f(inp["xa_q_norm"][0])
    G[:, G_XK] = f(inp["xa_k_norm"][0])
    cw = f(inp["conv_w"][0])
    for i in range(4):
        for tap in range(3):
            G[:, G_CONV + i * 3 + tap] = cw[tap, i * 128:(i + 1) * 128]
    out["gains"] = G
    return out


def _rope_table(pos):
    half = 16
    inv_freq = (10000.0 ** (-np.arange(half, dtype=np.float32) / half)).astype(np.float32)
    ang = pos.astype(np.float32)[None, :] * inv_freq[:, None]
    cos = np.cos(ang).astype(np.float32)
    sin = np.sin(ang).astype(np.float32)
    c32 = np.concatenate([cos, cos], 0)
    s32 = np.concatenate([-sin, sin], 0)
    tab = np.stack([c32, s32], axis=1)
    return np.ascontiguousarray(np.tile(tab, (4, 1, 1)))


_NC_CACHE = {}


def kernel(**inputs):
    xp = np.asarray(inputs["x_prompt"], np.float32)
    xsm = np.asarray(inputs["x_sample"], np.float32)
    mp = np.asarray(inputs["mem_prompt"], np.float32)
    ms = np.asarray(inputs["mem_sample"], np.float32)
    W = _prep_weights(inputs)
    if "nc" not in _NC_CACHE:
        _NC_CACHE["nc"] = build()
    nc = _NC_CACHE["nc"]
    in_maps = []
    for cidx in range(8):
        ps_, pq_ = cidx // 4, cidx % 4
        ss_, sh_ = cidx // 2, cidx % 2
        xpc = xp[ps_, pq_ * 2048:(pq_ + 1) * 2048]
        xsc = xsm[ss_, sh_ * 2048:(sh_ + 1) * 2048]
        xc = np.concatenate([xpc, xsc], 0)
        xt = xc.reshape(NT, T, KC, 128).transpose(0, 3, 2, 1)
        halo = np.zeros((4, D), np.float32)
        if pq_ > 0:
            halo[0] = xp[ps_, pq_ * 2048 - 1]
        if pq_ < 3:
            halo[1] = xp[ps_, (pq_ + 1) * 2048]
        if sh_ > 0:
            halo[2] = xsm[ss_, sh_ * 2048 - 1]
        if sh_ < 1:
            halo[3] = xsm[ss_, (sh_ + 1) * 2048]
        hl = halo.reshape(4, KC, 128).transpose(2, 1, 0)
        memc = np.stack([mp[ps_], ms[ss_]], 0)
        memt = memc.reshape(2, 256, KC, 128).transpose(0, 3, 2, 1)
        pos = np.concatenate([np.arange(pq_ * 2048, (pq_ + 1) * 2048), np.arange(sh_ * 2048, (sh_ + 1) * 2048)])
        m = {
            "xT": np.ascontiguousarray(xt).reshape(NT, 128, KC * T),
            "xh": np.ascontiguousarray(hl).reshape(128, KC * 4),
            "memT": np.ascontiguousarray(memt).reshape(2, 128, KC * 256),
            "ropeT": _rope_table(pos),
        }
        m.update(W)
        in_maps.append(m)
    res = run_bass_kernel_spmd(nc, in_maps, core_ids=list(range(8)))
    yp = np.empty_like(xp)
    ysm = np.empty_like(xsm)
    for cidx in range(8):
        ps_, pq_ = cidx // 4, cidx % 4
        ss_, sh_ = cidx // 2, cidx % 2
        y = np.asarray(res.results[cidx]["yT"]).reshape(NT, 128, KC, T).transpose(0, 3, 2, 1).reshape(NT * T, D)
        yp[ps_, pq_ * 2048:(pq_ + 1) * 2048] = y[:2048]
        ysm[ss_, sh_ * 2048:(sh_ + 1) * 2048] = y[2048:]
    return (yp, ysm)
```

```python
import numpy as np
from contextlib import ExitStack
import concourse.bass as bass
import concourse.mybir as mybir
from concourse.bass_utils import run_bass_kernel_spmd

F32 = mybir.dt.float32
BF16 = mybir.dt.bfloat16
I32 = mybir.dt.int32
AF = mybir.ActivationFunctionType
ALU = mybir.AluOpType

D = 1024
KC = 8
T = 512
NT = 8
FF = 2816
NJ = 22
EPS = 1e-6
SAME_ENG_SYNC = True
MAGIC = 1597463007.0

G_FFN1, G_MIX, G_FFN2, G_MEM, G_QL, G_KVL = 0, 8, 16, 24, 32, 35
G_MQ, G_MQS, G_MK, G_KR, G_KRS, G_XQ, G_XK, G_CONV = 37, 38, 39, 40, 41, 42, 43, 44
NG = 56


class Buf:
    __slots__ = ("w", "r", "name")

    def __init__(self, name=""):
        self.w = None
        self.r = {}
        self.name = name


class Eng:
    def __init__(self, e, sem, name, is_pe=False):
        self.e = e
        self.sem = sem
        self.n = 0
        self.seen = {}
        self.name = name
        self.is_pe = is_pe

    def wait(self, tok):
        if tok is None:
            return
        sem, val = tok
        if sem is self.sem and (self.is_pe or not SAME_ENG_SYNC):
            return
        k = id(sem)
        if self.seen.get(k, 0) >= val:
            return
        self.e.wait_ge(sem, val)
        self.seen[k] = val


class Prog:
    def __init__(self):
        self.nc = bass.Bass("TRN2", target_bir_lowering=False)
        self.es = ExitStack()
        nc = self.nc
        self.semcount = {}
        self.sems = []
        self.PE = Eng(nc.tensor, self.sem("pe"), "pe", is_pe=True)
        self.ACT = Eng(nc.scalar, self.sem("act"), "act")
        self.DVE = Eng(nc.vector, self.sem("dve"), "dve")
        self.POOL = Eng(nc.gpsimd, self.sem("pool"), "pool")
        self.SP = Eng(nc.sync, self.sem("sp"), "sp")
        self.engs = [self.PE, self.ACT, self.DVE, self.POOL, self.SP]
        self.bank_rr = 0

    def sem(self, name):
        s = self.es.enter_context(self.nc.semaphore(f"{name}_n{len(self.sems)}"))
        self.semcount[id(s)] = 0
        self.sems.append(s)
        return s

    def deps(self, E, reads, writes):
        for b in reads:
            E.wait(b.w)
        for b in writes:
            E.wait(b.w)
            for t in list(b.r.values()):
                E.wait(t)

    def done(self, tok, reads, writes):
        for b in reads:
            b.r[id(tok[0])] = tok
        for b in writes:
            b.w = tok
            b.r = {}

    def op(self, E, fn, reads=(), writes=()):
        self.deps(E, reads, writes)
        ins = fn()
        E.n += 1
        ins.then_inc(E.sem, 1)
        self.semcount[id(E.sem)] = E.n
        self.done((E.sem, E.n), reads, writes)

    def mm(self, fns, reads, writes):
        E = self.PE
        self.deps(E, reads, writes)
        ins = None
        for f in fns:
            ins = f()
        E.n += 1
        ins.then_inc(E.sem, 1)
        self.semcount[id(E.sem)] = E.n
        self.done((E.sem, E.n), reads, writes)

    def mmf(self, items, reads, writes):
        E = self.PE
        self.deps(E, reads, writes)
        allr = list(reads)
        ins = None
        for f, rb in items:
            for b in rb:
                E.wait(b.w)
            allr += rb
            ins = f()
        E.n += 1
        ins.then_inc(E.sem, 1)
        self.semcount[id(E.sem)] = E.n
        self.done((E.sem, E.n), allr, writes)

    def dma(self, Q, out, in_, sem, reads=(), writes=(), **kw):
        self.deps(Q, reads, writes)
        Q.e.dma_start(out=out, in_=in_, **kw).then_inc(sem, 16)
        self.semcount[id(sem)] += 16
        self.done((sem, self.semcount[id(sem)]), reads, writes)

    def barrier(self):
        for E in self.engs:
            for s in self.sems:
                c = self.semcount[id(s)]
                if c > 0:
                    E.wait((s, c)) if s is not E.sem else None


class Stream:
    def __init__(self, P, es, name, width, nslots, srcs):
        self.P = P
        self.srcs = srcs
        self.n = nslots
        self.slots = [es.enter_context(P.nc.sbuf_tensor(f"{name}_s{i}_{len(P.sems)}", [128, width], BF16)) for i in range(nslots)]
        self.bufs = [Buf(f"{name}{i}") for i in range(nslots)]
        self.sems = [P.sem(f"{name}_q{i}") for i in range(nslots)]
        self.issued = 0
        self.pos = 0

    def _issue(self):
        i = self.issued
        s = i % self.n
        ap, db = self.srcs[i]
        self.P.dma(self.P.SP, self.slots[s][:, :], ap, self.sems[s], reads=[db], writes=[self.bufs[s]])
        self.issued += 1

    def next(self):
        i = self.pos
        while self.issued < min(i + self.n, len(self.srcs)):
            self._issue()
        self.pos += 1
        s = i % self.n
        return self.slots[s], self.bufs[s]


import os as _os
STOP = float(_os.environ.get("KSTOP", "9"))
KSUB = int(_os.environ.get("KSUB", "99"))
KATT = int(_os.environ.get("KATT", "99"))
KK = int(_os.environ.get("KK", "99"))


class _Stop(Exception):
    pass


def build():
    stacks = []
    P = Prog()
    try:
        _build(P, stacks)
    except _Stop:
        P.barrier()
        for st in reversed(stacks):
            st.close()
        P.es.close()
    return P.nc


def _build(P, stacks):
    nc = P.nc
    es = P.es
    PE, ACT, DVE, POOL, SP = P.PE, P.ACT, P.DVE, P.POOL, P.SP

    def din(name, shape, dt=F32):
        return nc.dram_tensor(name, shape, dt, kind="ExternalInput")

    xT = din("xT", [NT, 128, KC * T])
    xh = din("xh", [128, KC * 4])
    memT = din("memT", [2, 128, KC * 256])
    gains_d = din("gains", [128, NG])
    rope_d = din("ropeT", [128, 2, NT * T])
    wsh = {
        "w1gu": [NJ, 128, KC * 256], "w1d": [8, 128, NJ * 128],
        "w2gu": [NJ, 128, KC * 256], "w2d": [8, 128, NJ * 128],
        "wina": [14, 128, KC * 128], "winb": [32, 128, KC * 128],
        "wuq": [8, 128, 3 * 192], "wuk": [1, 128, 2 * 512], "wuv": [1, 128, 2 * 512],
        "wo3": [8, 128, 16 * 128], "wout": [8, 128, KC * 128],
        "wmk": [4, 128, KC * 128], "wmv": [1, 128, KC * 512],
    }
    wf = {k: din(k, v) for k, v in wsh.items()}
    wb = {k: nc.dram_tensor(k + "_b", v, BF16) for k, v in wsh.items()}
    wbuf = {k: Buf(k) for k in wsh}
    yT = nc.dram_tensor("yT", [NT, 128, KC * T], F32, kind="ExternalOutput")

    x1s = nc.dram_tensor("x1s", [NT, 128, KC * T], F32)
    x1s_buf = [Buf(f"x1s{i}") for i in range(NT)]
    Us = nc.dram_tensor("Us", [128, 4, 2, 2048], BF16)
    Us_buf = Buf("Us")
    AOs = nc.dram_tensor("AOs", [64, 8, NT * T], BF16)
    AOs_buf = Buf("AOs")
    LROWS = 320
    latP = [nc.dram_tensor(f"latP{i}", [LROWS, 1024], BF16) for i in range(2)]
    latS = [nc.dram_tensor(f"latS{i}", [LROWS, 1024], BF16) for i in range(2)]
    gatP = [nc.dram_tensor(f"gatP{i}", [4 * LROWS, 1024], BF16) for i in range(2)]
    gatS = [nc.dram_tensor(f"gatS{i}", [2 * LROWS, 1024], BF16) for i in range(2)]
    lat_buf = [Buf("latP"), Buf("latS")]
    gat_buf = [Buf("gatP"), Buf("gatS")]

    uniq = [0]

    def sb(stack, name, shape, dt):
        uniq[0] += 1
        return stack.enter_context(nc.sbuf_tensor(f"{name}_u{uniq[0]}", shape, dt))

    wchunks = {}

    def emit_cast(k, step=4096):
        n0, _, wd = wsh[k]
        bb = max(d_ for d_ in range(1, 1025) if wd % d_ == 0)
        src = wf[k].ap().rearrange("n p (a b) -> (n p a) b", b=bb)
        dst = wb[k].ap().rearrange("n p (a b) -> (n p a) b", b=bb)
        rows = src.shape[0]
        rpb = rows // n0
        fine = step < rows and step % rpb == 0 and k == "w1gu"
        s = None if fine else P.sem("c_" + k)
        wchunks[k] = []
        for r0 in range(0, rows, step):
            r1 = min(rows, r0 + step)
            if fine:
                sc = P.sem(f"c_{k}_{r0}")
                cb = Buf(f"{k}_{r0}")
                P.dma(POOL, dst[r0:r1, :], src[r0:r1, :], sc, writes=[cb], max_dma_last_dim=4096)
                wchunks[k].append((r0 // rpb, (r1 + rpb - 1) // rpb, cb))
            else:
                P.dma(POOL, dst[r0:r1, :], src[r0:r1, :], s, writes=[wbuf[k]] if r0 + step >= rows else [], max_dma_last_dim=4096)

    def wsrcbuf(k, i):
        for b0, b1, cb in wchunks.get(k, []):
            if b0 <= i < b1:
                return cb
        return wbuf[k]

    emit_cast("w1gu", step=1024)
    emit_cast("w1d")
    emit_cast("wina")
    late_casts = {0: ["wmk", "wmv", "wuq", "wuk", "wuv"], 1: ["winb"], 2: ["wo3", "wout"], 3: ["w2gu"], 4: ["w2d"]}

    if STOP <= 0.1:
        raise _Stop()
    gains = sb(es, "gains_sb", [128, NG], F32)
    gains_b = Buf("gains")
    ONESB = sb(es, "onesb", [128, 128], BF16)
    ONESF = sb(es, "onesf", [128, 64], F32)
    ones_b = Buf("ones")
    s_misc = P.sem("misc")
    P.dma(SP, gains[:, :], gains_d.ap(), s_misc, writes=[gains_b])
    P.op(DVE, lambda: nc.vector.memset(ONESB[:, :], 1.0), writes=[ones_b])
    P.op(DVE, lambda: nc.vector.memset(ONESF[:, :], 1.0), writes=[ones_b])
    MK = sb(es, "MK", [128, 2, 4, 256], BF16)
    MV = sb(es, "MV", [128, 2, 2, 512], BF16)
    MK_b, MV_b = Buf("MK"), Buf("MV")
    UH = sb(es, "UH", [128, 4, 4], BF16)
    UHb = Buf()

    psum = [es.enter_context(nc.psum_tensor(f"ps{i}", [128, 512], F32)) for i in range(8)]
    psb = [Buf(f"ps{i}") for i in range(8)]
    bank_pool = list(range(8))

    def bank():
        i = bank_pool[P.bank_rr % len(bank_pool)]
        P.bank_rr += 1
        return psum[i], psb[i]

    def g(col, p0=0, p1=128):
        return gains[p0:p1, col:col + 1]

    def rsqrt(tmp, ps_ap, ps_b, out_ap, out_b, inv_n, post=None):
        V, Y, TT = tmp["V"], tmp["Y"], tmp["T"]
        vb, yb, tb = tmp["Vb"], tmp["Yb"], tmp["Tb"]
        shp = ps_ap.shape
        np_, n = shp[0], shp[1]
        p0 = tmp.get("p0", 0)
        v = V[p0:p0 + np_, 0:n]
        y = Y[p0:p0 + np_, 0:n]
        t = TT[p0:p0 + np_, 0:n]
        P.op(DVE, lambda: nc.vector.tensor_scalar(out=v, in0=ps_ap, scalar1=inv_n, scalar2=EPS, op0=ALU.mult, op1=ALU.add),
             reads=[ps_b], writes=[vb])
        P.op(DVE, lambda: nc.vector.tensor_scalar(out=y.bitcast(I32), in0=v.bitcast(I32), scalar1=-0.5, scalar2=MAGIC,
                                                  op0=ALU.mult, op1=ALU.add), reads=[vb], writes=[yb])
        for it in range(2):
            P.op(DVE, lambda: nc.vector.tensor_tensor(out=t, in0=y, in1=y, op=ALU.mult), reads=[yb], writes=[tb])
            P.op(DVE, lambda: nc.vector.scalar_tensor_tensor(out=t, in0=t, scalar=-0.5, in1=v, op0=ALU.mult, op1=ALU.mult),
                 reads=[tb, vb], writes=[tb])
            last = it == 1
            o = out_ap if last else y
            ob = out_b if last else yb
            if last and post is not None:
                P.op(DVE, lambda: nc.vector.scalar_tensor_tensor(out=y, in0=t, scalar=1.5, in1=y, op0=ALU.add, op1=ALU.mult),
                     reads=[tb, yb], writes=[yb])
                P.op(DVE, lambda: nc.vector.tensor_scalar(out=o, in0=y, scalar1=post, scalar2=None, op0=ALU.mult),
                     reads=[yb], writes=[ob])
            else:
                P.op(DVE, lambda: nc.vector.scalar_tensor_tensor(out=o, in0=t, scalar=1.5, in1=y, op0=ALU.add, op1=ALU.mult),
                     reads=[tb, yb], writes=[ob])

    def mk_tmp(stack, tag, n=T):
        return {"V": sb(stack, "tV" + tag, [128, n], F32), "Y": sb(stack, "tY" + tag, [128, n], F32),
                "T": sb(stack, "tT" + tag, [128, n], F32), "Vb": Buf(), "Yb": Buf(), "Tb": Buf()}

    class TileCtx:
        pass

    def alloc_tile_ctx(stack):
        c = TileCtx()
        c.X = [sb(stack, f"X{i}", [128, KC, T], F32) for i in range(2)]
        c.Xb = [[Buf(f"X{i}_{k}") for k in range(KC)] for i in range(2)]
        c.Xsem = [P.sem(f"X{i}") for i in range(2)]
        c.XsemS = [P.sem(f"XS{i}") for i in range(2)]
        c.XNs = [sb(stack, f"XN{i}", [128, KC, T], BF16) for i in range(2)]
        c.XNbs = [[Buf(f"XN{i}_{k}") for k in range(KC)] for i in range(2)]
        c.XN = c.XNs[0]
        c.XNb = c.XNbs[0]
        c.SQ = sb(stack, "SQn", [128, KC, T], BF16)
        c.SQb = [Buf(f"SQ{k}") for k in range(KC)]
        c.H = sb(stack, "H", [128, NJ, T], BF16)
        c.Hb = [Buf(f"H{j}") for j in range(NJ)]
        c.SG = [sb(stack, f"SG{i}", [128, T], F32) for i in range(2)]
        c.SGb = [Buf(), Buf()]
        c.RINV = sb(stack, "RINV", [128, T], F32)
        c.RINVb = Buf("rinv")
        c.tmp = mk_tmp(stack, "a")
        return c

    def norm_to_xn(c, xs, n, gcol, xi=0):
        X, Xb = c.X[xs], c.Xb[xs]
        XN, XNb = c.XNs[xi], c.XNbs[xi]
        for k0 in range(0, KC, 4):
            P.op(ACT, lambda k0=k0: nc.scalar.activation(out=c.SQ[:, k0:k0 + 4, 0:n], in_=X[:, k0:k0 + 4, 0:n], func=AF.Square),
                 reads=Xb[k0:k0 + 4], writes=c.SQb[k0:k0 + 4])
        ps, pb = bank()
        P.mmf([(lambda k=k: nc.tensor.matmul(ps[:, 0:n], lhsT=ONESB[:, :], rhs=c.SQ[:, k, 0:n], start=(k == 0), stop=(k == KC - 1)), [c.SQb[k]])
               for k in range(KC)], reads=[ones_b], writes=[pb])
        rsqrt(c.tmp, ps[:, 0:n], pb, c.RINV[:, 0:n], c.RINVb, 1.0 / D)
        for k in range(KC):
            P.op(DVE, lambda k=k: nc.vector.scalar_tensor_tensor(out=XN[:, k, 0:n], in0=X[:, k, 0:n], scalar=g(gcol + k),
                                                                 in1=c.RINV[:, 0:n], op0=ALU.mult, op1=ALU.mult),
                 reads=[Xb[k], c.RINVb, gains_b], writes=[XNb[k]])

    def ffn(c, xs, n, gcol, gu_stream, d_stream, xi=0, do_norm=True, mid_hook=None):
        X, Xb = c.X[xs], c.Xb[xs]
        XN, XNb = c.XNs[xi], c.XNbs[xi]
        if do_norm:
            norm_to_xn(c, xs, n, gcol, xi)
        for j in range(NJ):
            wt, wtb = gu_stream.next()
            w3 = wt[:, :].rearrange("p (k c) -> p k c", c=256)
            pg, pgb = bank()
            pu, pub = bank()
            P.mmf([(lambda k=k: nc.tensor.matmul(pg[:, 0:n], lhsT=w3[:, k, 0:128], rhs=XN[:, k, 0:n], start=(k == 0), stop=(k == KC - 1)), [XNb[k]])
                   for k in range(KC)], reads=[wtb], writes=[pgb])
            P.mmf([(lambda k=k: nc.tensor.matmul(pu[:, 0:n], lhsT=w3[:, k, 128:256], rhs=XN[:, k, 0:n], start=(k == 0), stop=(k == KC - 1)), [XNb[k]])
                   for k in range(KC)], reads=[wtb], writes=[pub])
            sg, sgb = c.SG[j % 2], c.SGb[j % 2]
            P.op(ACT, lambda: nc.scalar.activation(out=sg[:, 0:n], in_=pg[:, 0:n], func=AF.Silu), reads=[pgb], writes=[sgb])
            P.op(DVE, lambda: nc.vector.tensor_tensor(out=c.H[:, j, 0:n], in0=sg[:, 0:n], in1=pu[:, 0:n], op=ALU.mult),
                 reads=[sgb, pub], writes=[c.Hb[j]])
        if mid_hook is not None:
            mid_hook()
        for fc in range(KC):
            wt, wtb = d_stream.next()
            w3 = wt[:, :].rearrange("p (j c) -> p j c", c=128)
            pd, pdb = bank()
            P.mmf([(lambda j=j: nc.tensor.matmul(pd[:, 0:n], lhsT=w3[:, j, :], rhs=c.H[:, j, 0:n], start=(j == 0), stop=(j == NJ - 1)), [c.Hb[j]])
                   for j in range(NJ)], reads=[wtb], writes=[pdb])
            P.op(DVE, lambda: nc.vector.scalar_tensor_tensor(out=X[:, fc, 0:n], in0=pd[:, 0:n], scalar=0.5, in1=X[:, fc, 0:n],
                                                             op0=ALU.mult, op1=ALU.add), reads=[pdb, Xb[fc]], writes=[Xb[fc]])

    def proj(c, blk_stream, n, xi=0):
        wt, wtb = blk_stream.next()
        w3 = wt[:, :].rearrange("p (k c) -> p k c", c=128)
        ps, pb = bank()
        P.mmf([(lambda k=k: nc.tensor.matmul(ps[:, 0:n], lhsT=w3[:, k, :], rhs=c.XNs[xi][:, k, 0:n], start=(k == 0), stop=(k == KC - 1)), [c.XNbs[xi][k]])
               for k in range(KC)], reads=[wtb], writes=[pb])
        return ps, pb

    esA = ExitStack()
    stacks.append(esA)
    CQ = sb(esA, "CQ", [128, 3, NT * T], BF16)
    CQb = [Buf(f"CQ{i}") for i in range(NT)]
    es1 = ExitStack()
    stacks.append(es1)
    c = alloc_tile_ctx(es1)
    order1 = list(range(NT)) + (["h"] if not _os.environ.get("KSKIPH") else [])
    gu_srcs, d_srcs, blk_srcs = [], [], []
    for ti in order1:
        gu_srcs += [(wb["w1gu"][j], wsrcbuf("w1gu", j)) for j in range(NJ)]
        d_srcs += [(wb["w1d"][f], wbuf["w1d"]) for f in range(8)]
        if ti == "h":
            blk_srcs += [(wb["wina"][m], wbuf["wina"]) for m in range(6, 14)]
        else:
            blk_srcs += [(wb["wina"][m], wbuf["wina"]) for m in range(14)]
    mem_blk = []
    for ctx in range(2):
        mem_blk += [(wb["wmk"][h], wbuf["wmk"]) for h in range(4)]
    blk_srcs = blk_srcs + mem_blk
    gu1 = Stream(P, es1, "gu", KC * 256, 3, gu_srcs)
    d1 = Stream(P, es1, "wd", NJ * 128, 2, d_srcs)
    blk1 = Stream(P, es1, "blk", KC * 128, 4, blk_srcs)
    RAW = sb(es1, "RAW", [128, 3, T], F32)
    RAWb = [Buf() for _ in range(3)]
    SQ3 = sb(es1, "SQ3", [128, 3, T], BF16)
    SQ3b = [Buf() for _ in range(3)]
    RV2 = c.RINV
    RV2b = c.RINVb
    tmp2 = c.tmp
    CKVN = sb(es1, "CKVN", [128, 2, T], BF16)
    CKVNb = Buf()
    s_ckvn = P.sem("ckvn")
    KRO = sb(es1, "KRO", [32, 2, T], BF16)
    KROb = Buf()
    s_kro = P.sem("kro")
    KA = sb(es1, "KA", [32, T], F32)
    KB = sb(es1, "KB", [32, T], F32)
    KAb, KBb = Buf(), Buf()
    ROPE = sb(es1, "ROPE", [32, 2, T], F32)
    ROPEb = Buf()
    s_rope = P.sem("rope")
    UT = sb(es1, "UT", [128, 4, T], BF16)
    UTb = Buf()
    s_ut = P.sem("ut")
    CCT = sb(es1, "CCT", [128, T], F32)
    CCTb = Buf()
    if STOP <= 0.3:
        raise _Stop()
    for idx, ti in enumerate(order1):
        if (STOP <= 0.4 and idx == 1) or (STOP <= 0.5 and idx == 2):
            raise _Stop()
        halo = ti == "h"
        n = 128 if halo else T
        xs = idx % 2
        X, Xb = c.X[xs], c.Xb[xs]
        def load_x(idx_, ti_):
            xs_ = idx_ % 2
            if ti_ == "h":
                P.op(DVE, lambda: nc.vector.memset(c.X[xs_][:, :, 0:128], 0.0), writes=c.Xb[xs_])
                P.dma(SP, c.X[xs_][:, :, 0:4], xh.ap().rearrange("p (k t) -> p k t", t=4), c.Xsem[xs_], writes=c.Xb[xs_])
            else:
                P.dma(SP, c.X[xs_][:, :, :], xT[ti_].rearrange("p (k t) -> p k t", t=T), c.Xsem[xs_], writes=c.Xb[xs_])
        if idx == 0:
            load_x(0, order1[0])
        if idx + 1 < len(order1):
            load_x(idx + 1, order1[idx + 1])
        if not halo:
            P.dma(SP, ROPE[:, :, :], rope_d[0:32, :, ti * T:(ti + 1) * T], s_rope, writes=[ROPEb])
        xi = idx % 2
        if idx == 0:
            norm_to_xn(c, xs, n, G_FFN1, xi)

        def hook1(idx=idx):
            if idx + 1 < len(order1):
                norm_to_xn(c, (idx + 1) % 2, 128 if order1[idx + 1] == "h" else T, G_FFN1, (idx + 1) % 2)
        ffn(c, xs, n, G_FFN1, gu1, d1, xi=xi, do_norm=False, mid_hook=hook1)
        if KSUB <= 4:
            raise _Stop()
        if not halo:
            P.dma(POOL, x1s[ti].rearrange("p (k t) -> p k t", t=T), X[:, :, :], c.XsemS[xs], reads=Xb, writes=[x1s_buf[ti]])
        for kk in late_casts.get(idx, []):
            emit_cast(kk)
        if KSUB <= 5:
            raise _Stop()
        norm_to_xn(c, xs, n, G_MIX, xi)
        if KSUB <= 6:
            raise _Stop()
        if not halo:
            ch, tl = ti // 4, ti % 4
            for i in range(3):
                ps, pb = proj(c, blk1, n, xi)
                P.op(ACT, lambda: nc.scalar.activation(out=SQ3[:, i, :], in_=ps[:, :], func=AF.Square), reads=[pb], writes=[SQ3b[i]])
                P.op(ACT, lambda: nc.scalar.activation(out=RAW[:, i, :], in_=ps[:, :], func=AF.Copy), reads=[pb], writes=[RAWb[i]])
            p2, p2b = bank()
            P.mm([lambda i=i: nc.tensor.matmul(p2[:, :], lhsT=ONESB[:, :], rhs=SQ3[:, i, :], start=(i == 0), stop=(i == 2)) for i in range(3)],
                 reads=[ones_b] + SQ3b, writes=[p2b])
            rsqrt(tmp2, p2[:, :], p2b, RV2[:, :], RV2b, 1.0 / 384)
            for i in range(3):
                P.op(DVE, lambda i=i: nc.vector.scalar_tensor_tensor(out=CQ[:, i, ti * T:(ti + 1) * T], in0=RAW[:, i, :], scalar=g(G_QL + i),
                                                                     in1=RV2[:, :], op0=ALU.mult, op1=ALU.mult),
                     reads=[RAWb[i], RV2b, gains_b], writes=[CQb[ti]])
            if KSUB <= 7:
                raise _Stop()
            for i in range(2):
                ps, pb = proj(c, blk1, n, xi)
                P.op(ACT, lambda: nc.scalar.activation(out=SQ3[:, i, :], in_=ps[:, :], func=AF.Square), reads=[pb], writes=[SQ3b[i]])
                P.op(ACT, lambda: nc.scalar.activation(out=RAW[:, i, :], in_=ps[:, :], func=AF.Copy), reads=[pb], writes=[RAWb[i]])
            p2, p2b = bank()
            P.mm([lambda i=i: nc.tensor.matmul(p2[:, :], lhsT=ONESB[:, :], rhs=SQ3[:, i, :], start=(i == 0), stop=(i == 1)) for i in range(2)],
                 reads=[ones_b] + SQ3b[0:2], writes=[p2b])
            rsqrt(tmp2, p2[:, :], p2b, RV2[:, :], RV2b, 1.0 / 256)
            for i in range(2):
                P.op(DVE, lambda i=i: nc.vector.scalar_tensor_tensor(out=CKVN[:, i, :], in0=RAW[:, i, :], scalar=g(G_KVL + i),
                                                                     in1=RV2[:, :], op0=ALU.mult, op1=ALU.mult),
                     reads=[RAWb[i], RV2b, gains_b], writes=[CKVNb])
            lat = [latP, latS][ch][tl // 2]
            tl2 = tl % 2
            P.dma(POOL, lat[0:256, tl2 * T:(tl2 + 1) * T].rearrange("(k p) t -> p k t", p=128), CKVN[:, :, :], s_ckvn,
                  reads=[CKVNb], writes=[lat_buf[ch]])
            if KSUB <= 8:
                raise _Stop()
            wt, wtb = blk1.next()
            w3 = wt[:, :].rearrange("p (k c) -> p k c", c=128)
            pk, pkb = bank()
            pq, pqb = bank()
            P.mm([lambda k=k: nc.tensor.matmul(pk[0:32, :], lhsT=w3[:, k, 0:32], rhs=c.XNs[xi][:, k, :], start=(k == 0), stop=(k == KC - 1))
                  for k in range(KC)], reads=[wtb] + c.XNbs[xi], writes=[pkb])
            P.mm([lambda k=k: nc.tensor.matmul(pq[0:32, :], lhsT=w3[:, k, 32:64], rhs=c.XNs[xi][:, k, :], start=(k == 0), stop=(k == KC - 1))
                  for k in range(KC)], reads=[wtb] + c.XNbs[xi], writes=[pqb])
            P.op(ACT, lambda: nc.scalar.activation(out=KRO[:, 1, :], in_=pk[0:32, :], func=AF.Copy), reads=[pkb], writes=[KROb])
            P.op(DVE, lambda: nc.vector.scalar_tensor_tensor(out=KA[:, :], in0=pk[0:32, :], scalar=g(G_KR, 0, 32), in1=ROPE[:, 0, :],
                                                             op0=ALU.mult, op1=ALU.mult), reads=[pkb, ROPEb, gains_b, KROb], writes=[KAb])
            P.op(DVE, lambda: nc.vector.scalar_tensor_tensor(out=KB[:, :], in0=pq[0:32, :], scalar=g(G_KRS, 0, 32), in1=ROPE[:, 1, :],
                                                             op0=ALU.mult, op1=ALU.mult), reads=[pqb, ROPEb, gains_b], writes=[KBb])
            P.op(DVE, lambda: nc.vector.tensor_tensor(out=KRO[:, 0, :], in0=KA[:, :], in1=KB[:, :], op=ALU.add),
                 reads=[KAb, KBb], writes=[KROb])
            P.dma(POOL, lat[256:320, tl2 * T:(tl2 + 1) * T].rearrange("(a p) t -> p a t", p=32), KRO[:, :, :], s_kro,
                  reads=[KROb], writes=[lat_buf[ch]])
        if KSUB <= 9:
            raise _Stop()
        pcc = []
        for i in range(4):
            pcc.append(proj(c, blk1, n, xi))
            if i >= 1:
                pass
        for i in range(4):
            pc, pcb = pcc[i]
            px, pxb = proj(c, blk1, n, xi)
            P.op(ACT, lambda: nc.scalar.activation(out=CCT[:, 0:n], in_=pc[:, 0:n], func=AF.Copy), reads=[pcb], writes=[CCTb])
            if halo:
                P.op(DVE, lambda: nc.vector.tensor_tensor(out=UH[:, i, :], in0=CCT[:, 0:4], in1=px[:, 0:4], op=ALU.mult),
                     reads=[CCTb, pxb], writes=[UHb])
            else:
                P.op(DVE, lambda: nc.vector.tensor_tensor(out=UT[:, i, :], in0=CCT[:, :], in1=px[:, :], op=ALU.mult),
                     reads=[CCTb, pxb], writes=[UTb])
        if not halo:
            P.dma(POOL, Us[:, :, ch, tl * T:(tl + 1) * T], UT[:, :, :], s_ut, reads=[UTb], writes=[Us_buf])

    if STOP <= 1:
        raise _Stop()
    s_cc = P.sem("cc")
    P.deps(POOL, [lat_buf[0], lat_buf[1]], [gat_buf[0], gat_buf[1]])
    for hf in range(2):
        nc.gpsimd.collective_compute("AllGather", ALU.bypass, replica_groups=[[0, 1, 2, 3], [4, 5, 6, 7]],
                                     ins=[latP[hf].ap().opt()], outs=[gatP[hf].ap().opt()]).then_inc(s_cc)
        P.semcount[id(s_cc)] += 1
    gat_buf[0].w = (s_cc, P.semcount[id(s_cc)])
    for hf in range(2):
        nc.gpsimd.collective_compute("AllGather", ALU.bypass, replica_groups=[[0, 1], [2, 3], [4, 5], [6, 7]],
                                     ins=[latS[hf].ap().opt()], outs=[gatS[hf].ap().opt()]).then_inc(s_cc)
        P.semcount[id(s_cc)] += 1
    gat_buf[1].w = (s_cc, P.semcount[id(s_cc)])

    s_wmv = P.sem("wmv")
    WMVbs = c.Hb[8:16]
    P.dma(SP, c.H[:, 8:16, :], wb["wmv"][0].rearrange("p (k c) -> p k c", c=512), s_wmv, reads=[wbuf["wmv"]], writes=WMVbs)

    for ctx in range(2):
        MEMX = c.X[1][:, :, 0:256]
        MEMXb = c.Xb[1][0]
        P.dma(SP, MEMX, memT[ctx].rearrange("p (k m) -> p k m", m=256), c.Xsem[1], writes=c.Xb[1])
        for k in range(KC):
            P.op(ACT, lambda k=k: nc.scalar.activation(out=c.H[:, k, 0:256], in_=MEMX[:, k, :], func=AF.Square),
                 reads=[MEMXb], writes=[c.Hb[k]])
        ps, pb = bank()
        P.mm([lambda k=k: nc.tensor.matmul(ps[:, 0:256], lhsT=ONESB[:, :], rhs=c.H[:, k, 0:256], start=(k == 0), stop=(k == KC - 1))
              for k in range(KC)], reads=[ones_b] + c.Hb[0:KC], writes=[pb])
        rsqrt(c.tmp, ps[:, 0:256], pb, c.RINV[:, 0:256], c.RINVb, 1.0 / D)
        for k in range(KC):
            P.op(DVE, lambda k=k: nc.vector.scalar_tensor_tensor(out=c.XN[:, k, 0:256], in0=MEMX[:, k, :], scalar=g(G_MEM + k),
                                                                 in1=c.RINV[:, 0:256], op0=ALU.mult, op1=ALU.mult),
                 reads=[MEMXb, c.RINVb, gains_b], writes=[c.XNb[k]])
        for h in range(4):
            ps, pb = proj(c, blk1, 256)
            P.op(ACT, lambda: nc.scalar.activation(out=SQ3[:, 0, 0:256], in_=ps[:, 0:256], func=AF.Square), reads=[pb], writes=[SQ3b[0]])
            P.op(ACT, lambda: nc.scalar.activation(out=RAW[:, 0, 0:256], in_=ps[:, 0:256], func=AF.Copy), reads=[pb], writes=[RAWb[0]])
            p2, p2b = bank()
            P.mm([lambda: nc.tensor.matmul(p2[:, 0:256], lhsT=ONESB[:, :], rhs=SQ3[:, 0, 0:256], start=True, stop=True)],
                 reads=[ones_b, SQ3b[0]], writes=[p2b])
            rsqrt(tmp2, p2[:, 0:256], p2b, RV2[:, 0:256], RV2b, 1.0 / 128)
            P.op(DVE, lambda: nc.vector.scalar_tensor_tensor(out=MK[:, ctx, h, :], in0=RAW[:, 0, 0:256], scalar=g(G_XK), in1=RV2[:, 0:256],
                                                             op0=ALU.mult, op1=ALU.mult), reads=[RAWb[0], RV2b, gains_b], writes=[MK_b])
        wmv3 = c.H[:, 8:16, :]
        for mc in range(2):
            ps, pb = bank()
            P.mm([lambda k=k: nc.tensor.matmul(ps[:, :], lhsT=c.XN[:, k, mc * 128:(mc + 1) * 128], rhs=wmv3[:, k, :],
                                               start=(k == 0), stop=(k == KC - 1)) for k in range(KC)],
                 reads=WMVbs + c.XNb, writes=[pb])
            P.op(ACT, lambda: nc.scalar.activation(out=MV[:, ctx, mc, :], in_=ps[:, :], func=AF.Copy), reads=[pb], writes=[MV_b])

    P.barrier()
    es1.close()
    stacks.pop()
    if STOP <= 2:
        raise _Stop()

    es2 = ExitStack()
    stacks.append(es2)
    SMAX = 8192
    CKV = sb(es2, "CKV", [128, 2, SMAX], BF16)
    CKVb = Buf()
    s_ckv = P.sem("ckv")
    KT = [sb(es2, f"KT{i}", [96, SMAX], BF16) for i in range(2)]
    KTb = [Buf(), Buf()]
    KTrb = [Buf(), Buf()]
    s_kt = [P.sem("kt0"), P.sem("kt1")]
    KRR = sb(es2, "KRR", [96, 2048], BF16)
    KRRb = Buf()
    s_krr = P.sem("krr")
    VG = sb(es2, "VG", [128, SMAX // 128, 4, 65], BF16)
    VGb = Buf()
    QT = [sb(es2, f"QT{i}", [96, T], BF16) for i in range(2)]
    QTb = [Buf(), Buf()]
    NPT = 4
    PT = [sb(es2, f"PT{i}", [128, T], BF16) for i in range(NPT)]
    PTb = [Buf() for _ in range(NPT)]
    SQK = [sb(es2, f"SQK{i}", [64, T], BF16) for i in range(2)]
    SQKb = [Buf(), Buf()]
    SQQ = sb(es2, "SQQ", [96, T], BF16)
    SQQb = Buf()
    tq = mk_tmp(es2, "q")
    RQ = sb(es2, "RQ", [96, T], F32)
    RQb = Buf()
    QA = sb(es2, "QA", [96, T], F32)
    QB = sb(es2, "QB", [96, T], F32)
    QAb, QBb = Buf(), Buf()
    ROQ = sb(es2, "ROQ", [96, 2, T], F32)
    ROQb = Buf()
    s_roq = P.sem("roq")
    KRSS = sb(es2, "KRSS", [128, 64], F32)
    KRSSb = Buf()
    RK = [sb(es2, f"RK{i}", [128, 64], F32) for i in range(2)]
    RKb = [Buf(), Buf()]
    tk = mk_tmp(es2, "k", 64)
    SSK = sb(es2, "SSK", [128, 64], F32)
    SSKb = Buf()
    REC = sb(es2, "REC", [65, T], F32)
    RECb = Buf()
    BC = sb(es2, "BC", [64, T], F32)
    BCb = Buf()
    AOT = [sb(es2, f"AOT{i}", [64, T], BF16) for i in range(2)]
    AOTb = [Buf(), Buf()]
    s_aot = [P.sem("aot0"), P.sem("aot1")]
    P.op(DVE, lambda: nc.vector.memset(VG[:, :, :, 64:65], 1.0), writes=[VGb])
    WUQ = sb(es2, "WUQ", [128, 8, 3 * 192], BF16)
    WUK = sb(es2, "WUK", [128, 2 * 512], BF16)
    WUV = sb(es2, "WUV", [128, 2 * 512], BF16)
    wsm_b = Buf("wsmall")
    s_ws = P.sem("wsm")
    P.dma(SP, WUQ[:, :, :], wb["wuq"].ap().rearrange("h p x -> p h x"), s_ws, reads=[wbuf["wuq"]], writes=[wsm_b])
    P.dma(SP, WUK[:, :], wb["wuk"][0], s_ws, reads=[wbuf["wuk"]], writes=[wsm_b])
    P.dma(SP, WUV[:, :], wb["wuv"][0], s_ws, reads=[wbuf["wuv"]], writes=[wsm_b])

    wuk3 = WUK[:, :].rearrange("p (k c) -> p k c", c=512)
    wuv3 = WUV[:, :].rearrange("p (k c) -> p k c", c=512)
    ST_BANKS = [0, 1, 2]
    O_BANKS = [3, 4]
    bank_pool[:] = [5, 6, 7]
    st_rr = 0
    o_rr = 0
    pt_rr = 0
    qt_rr = 0
    kt_rr = 0
    aot_rr = 0
    SCALE = 96.0 ** -0.5

    rr = {"st": 0, "o": 0, "pt": 0, "qt": 0, "aot": 0}
    pend_q = [None]
    pend_fin = [None]
    LOOK = 2

    def q_gen(h, ti):
        P.dma(SP, ROQ[64:96, :, :], rope_d[64:96, :, ti * T:(ti + 1) * T], s_roq, writes=[ROQb])
        pq, pqb = bank()
        pqs, pqsb = bank()
        P.mm([lambda k=k: nc.tensor.matmul(pq[0:96, :], lhsT=WUQ[:, h, k * 192:k * 192 + 96], rhs=CQ[:, k, ti * T:(ti + 1) * T],
                                           start=(k == 0), stop=(k == 2)) for k in range(3)],
             reads=[wsm_b, CQb[ti]], writes=[pqb])
        P.mm([lambda k=k: nc.tensor.matmul(pqs[0:96, :], lhsT=WUQ[:, h, k * 192 + 96:k * 192 + 192], rhs=CQ[:, k, ti * T:(ti + 1) * T],
                                           start=(k == 0), stop=(k == 2)) for k in range(3)],
             reads=[wsm_b, CQb[ti]], writes=[pqsb])
        P.op(ACT, lambda: nc.scalar.activation(out=SQQ[:, :], in_=pq[0:96, :], func=AF.Square), reads=[pqb], writes=[SQQb])
        p2, p2b = bank()
        P.mm([lambda: nc.tensor.matmul(p2[0:96, :], lhsT=ONESB[0:96, 0:96], rhs=SQQ[:, :], start=True, stop=True)],
             reads=[ones_b, SQQb], writes=[p2b])
        rsqrt(tq, p2[0:96, :], p2b, RQ[:, :], RQb, 1.0 / 96)
        qi = rr["qt"] % 2
        rr["qt"] += 1
        qtile, qtb = QT[qi], QTb[qi]
        P.op(DVE, lambda: nc.vector.scalar_tensor_tensor(out=qtile[0:64, :], in0=pq[0:64, :], scalar=g(G_MQ, 0, 64), in1=RQ[0:64, :],
                                                         op0=ALU.mult, op1=ALU.mult), reads=[pqb, RQb, gains_b], writes=[qtb])
        P.op(DVE, lambda: nc.vector.scalar_tensor_tensor(out=QA[64:96, :], in0=pq[64:96, :], scalar=g(G_MQ, 64, 96), in1=ROQ[64:96, 0, :],
                                                         op0=ALU.mult, op1=ALU.mult), reads=[pqb, ROQb, gains_b], writes=[QAb])
        P.op(DVE, lambda: nc.vector.scalar_tensor_tensor(out=QB[64:96, :], in0=pqs[64:96, :], scalar=g(G_MQS, 64, 96), in1=ROQ[64:96, 1, :],
                                                         op0=ALU.mult, op1=ALU.mult), reads=[pqsb, ROQb, gains_b], writes=[QBb])
        P.op(DVE, lambda: nc.vector.tensor_tensor(out=QA[64:96, :], in0=QA[64:96, :], in1=QB[64:96, :], op=ALU.add),
             reads=[QAb, QBb], writes=[QAb])
        P.op(DVE, lambda: nc.vector.tensor_tensor(out=qtile[64:96, :], in0=QA[64:96, :], in1=RQ[64:96, :], op=ALU.mult),
             reads=[QAb, RQb], writes=[qtb])
        return qtile, qtb

    def main_store(h, hh, ti, NCH, ktile, ktb, ki, rk, rkb, qtile, qtb):
        ob = O_BANKS[rr["o"] % 2]
        rr["o"] += 1
        po, pob = psum[ob], psb[ob]
        P.deps(PE, [], [pob])
        pend = []
        for step in range(NCH + LOOK):
            if step < NCH:
                cc = step
                sbk = ST_BANKS[rr["st"] % 3]
                rr["st"] += 1
                pst, pstb = psum[sbk], psb[sbk]
                P.mm([lambda: nc.tensor.matmul(pst[:, :], lhsT=ktile[0:96, cc * 128:(cc + 1) * 128], rhs=qtile[0:96, :], start=True, stop=True)],
                     reads=[ktb, KTrb[ki], qtb], writes=[pstb])
                pi = rr["pt"] % NPT
                rr["pt"] += 1
                P.op(ACT, lambda: nc.scalar.activation(out=PT[pi][:, :], in_=pst[:, :], func=AF.Exp, scale=rk[:, cc:cc + 1]),
                     reads=[pstb, rkb], writes=[PTb[pi]])
                pend.append((cc, pi))
            if step >= LOOK:
                cc, pi = pend.pop(0)
                last = cc == NCH - 1
                if not last:
                    PE.wait(PTb[pi].w)
                    PE.wait(VGb.w)
                    nc.tensor.matmul(po[0:65, :], lhsT=VG[:, cc, hh, :], rhs=PT[pi][:, :], start=(cc == 0), stop=False)
                    PTb[pi].r[id(PE.sem)] = (PE.sem, PE.n + 1)
                else:
                    P.mm([lambda: nc.tensor.matmul(po[0:65, :], lhsT=VG[:, cc, hh, :], rhs=PT[pi][:, :], start=(cc == 0), stop=True)],
                         reads=[PTb[pi], VGb], writes=[pob])
            if step == 22 and pend_fin[0] is not None:
                finish(*pend_fin[0])
                pend_fin[0] = None
        pend_fin[0] = (h, ti, po, pob)

    def finish(h, ti, po, pob):
        P.op(DVE, lambda: nc.vector.reciprocal(out=REC[64:65, :], in_=po[64:65, :]), reads=[pob], writes=[RECb])
        pbc, pbcb = bank()
        P.mm([lambda: nc.tensor.matmul(pbc[0:64, :], lhsT=ONESF[64:65, 0:64], rhs=REC[64:65, :], start=True, stop=True)],
             reads=[ones_b, RECb], writes=[pbcb])
        P.op(ACT, lambda: nc.scalar.activation(out=BC[:, :], in_=pbc[0:64, :], func=AF.Copy), reads=[pbcb], writes=[BCb])
        ai = rr["aot"] % 2
        rr["aot"] += 1
        P.op(DVE, lambda: nc.vector.tensor_tensor(out=AOT[ai][:, :], in0=po[0:64, :], in1=BC[:, :], op=ALU.mult),
             reads=[pob, BCb], writes=[AOTb[ai]])
        P.dma(POOL, AOs[:, h, ti * T:(ti + 1) * T], AOT[ai][:, :], s_aot[ai], reads=[AOTb[ai]], writes=[AOs_buf])

    for ctx in range(2):
        S = 8192 if ctx == 0 else 4096
        R = 4 if ctx == 0 else 2
        NCH = S // 128
        NKT = S // T
        gat = [gatP, gatS][ctx]
        gb = gat_buf[ctx]
        gvs = [gat[hf].ap().rearrange("(r x) t -> r x t", x=LROWS) for hf in range(2)]
        for r in range(R):
            for hf in range(2):
                c0 = r * 2048 + hf * 1024
                P.dma(SP, CKV[:, :, c0:c0 + 1024], gvs[hf][r, 0:256, :].rearrange("(k p) t -> p k t", p=128), s_ckv,
                      reads=[gb], writes=[CKVb])
                for i in range(2):
                    P.dma(SP, KT[i][64:96, c0:c0 + 1024], gvs[hf][r, 256:288, :], s_kt[i], reads=[gb], writes=[KTrb[i], KTb[i]])
        for r in range(R):
            for hf in range(2):
                P.dma(SP, KRR[64:96, hf * 1024:(hf + 1) * 1024], gvs[hf][r, 288:320, :], s_krr, reads=[gb], writes=[KRRb])
            for kt in range(4):
                P.op(ACT, lambda kt=kt: nc.scalar.activation(out=KRR[64:96, kt * T:(kt + 1) * T], in_=KRR[64:96, kt * T:(kt + 1) * T], func=AF.Square),
                     reads=[KRRb], writes=[KRRb])
            ps, pb = bank()
            fns = [lambda cc=cc: nc.tensor.matmul(ps[:, cc:cc + 1], lhsT=KRR[64:96, cc * 128:(cc + 1) * 128], rhs=ONESB[64:96, 0:1], start=True, stop=True)
                   for cc in range(16)]
            P.mm(fns, reads=[KRRb, ones_b], writes=[pb])
            P.op(DVE, lambda: nc.vector.tensor_copy(out=KRSS[:, r * 16:(r + 1) * 16], in_=ps[:, 0:16]), reads=[pb], writes=[KRSSb])

        if KATT <= 1:
            raise _Stop()
        for hg in range(2):
            for cc in range(NCH):
                ps, pb = bank()
                P.mm([lambda k=k: nc.tensor.matmul(ps[:, 0:256], lhsT=CKV[:, k, cc * 128:(cc + 1) * 128], rhs=wuv3[:, k, hg * 256:(hg + 1) * 256],
                                                   start=(k == 0), stop=(k == 1)) for k in range(2)],
                     reads=[CKVb, wsm_b], writes=[pb])
                eng = ACT if cc % 2 == 0 else DVE
                if eng is ACT:
                    P.op(ACT, lambda: nc.scalar.activation(out=VG[:, cc, :, 0:64], in_=ps[:, 0:256].rearrange("p (h d) -> p h d", d=64), func=AF.Copy),
                         reads=[pb], writes=[VGb])
                else:
                    P.op(DVE, lambda: nc.vector.tensor_copy(out=VG[:, cc, :, 0:64], in_=ps[:, 0:256].rearrange("p (h d) -> p h d", d=64)),
                         reads=[pb], writes=[VGb])
            if KATT <= 2:
                raise _Stop()
            for hh in range(4):
                h = hg * 4 + hh
                ki = kt_rr % 2
                kt_rr += 1
                ktile, ktb = KT[ki], KTb[ki]
                pss, pssb = psum[0], psb[0]
                def ss_mm(kt_):
                    sq_, sqb_ = SQK[kt_ % 2], SQKb[kt_ % 2]
                    P.mm([lambda a=a: nc.tensor.matmul(pss[:, kt_ * 4 + a:kt_ * 4 + a + 1], lhsT=sq_[0:64, a * 128:(a + 1) * 128], rhs=ONESB[0:64, 0:1],
                                                       start=True, stop=True) for a in range(4)],
                         reads=[sqb_, ones_b], writes=[pssb])
                prev_kt = None
                for kt in range(NKT):
                    ps, pb = bank()
                    P.mm([lambda k=k: nc.tensor.matmul(ps[0:64, :], lhsT=wuk3[:, k, h * 64:(h + 1) * 64], rhs=CKV[:, k, kt * T:(kt + 1) * T],
                                                       start=(k == 0), stop=(k == 1)) for k in range(2)],
                         reads=[CKVb, wsm_b], writes=[pb])
                    sq, sqb = SQK[kt % 2], SQKb[kt % 2]
                    P.op(ACT, lambda: nc.scalar.activation(out=sq[:, :], in_=ps[0:64, :], func=AF.Square), reads=[pb], writes=[sqb])
                    P.op(DVE, lambda: nc.vector.tensor_scalar(out=ktile[0:64, kt * T:(kt + 1) * T], in0=ps[0:64, :], scalar1=g(G_MK, 0, 64),
                                                              scalar2=None, op0=ALU.mult), reads=[pb, gains_b, sqb], writes=[ktb])
                    if prev_kt is not None:
                        ss_mm(prev_kt)
                    prev_kt = kt
                ss_mm(prev_kt)
                if KK >= 4:
                    P.op(DVE, lambda: nc.vector.tensor_tensor(out=SSK[:, 0:NCH], in0=pss[:, 0:NCH], in1=KRSS[:, 0:NCH], op=ALU.add),
                         reads=[pssb, KRSSb], writes=[SSKb])
                rk, rkb = RK[ki], RKb[ki]
                if KK >= 5:
                    rsqrt(tk, SSK[:, 0:NCH], SSKb, rk[:, 0:NCH], rkb, 1.0 / 96, post=SCALE)
                if KATT <= 3:
                    raise _Stop()
                for qt in range(4):
                    ti = ctx * 4 + qt
                    if pend_q[0] is None:
                        pend_q[0] = q_gen(h, ti)
                    qtile, qtb = pend_q[0]
                    nxt = None
                    if qt < 3:
                        nxt = (h, ti + 1)
                    elif h < 7:
                        nxt = (h + 1, ctx * 4)
                    pend_q[0] = q_gen(*nxt) if nxt is not None else None
                    main_store(h, hh, ti, NCH, ktile, ktb, ki, rk, rkb, qtile, qtb)
                if KATT <= 7:
                    raise _Stop()

    if pend_fin[0] is not None:
        finish(*pend_fin[0])
        pend_fin[0] = None
    P.barrier()
    es2.close()
    esA.close()
    stacks.pop()
    stacks.pop()
    if STOP <= 3:
        raise _Stop()
    bank_pool[:] = list(range(8))

    es3 = ExitStack()
    stacks.append(es3)
    c = alloc_tile_ctx(es3)
    gu_srcs, d_srcs, blk_srcs, wo_srcs = [], [], [], []
    for ti in range(NT):
        gu_srcs += [(wb["w2gu"][j], wbuf["w2gu"]) for j in range(NJ)]
        d_srcs += [(wb["w2d"][f], wbuf["w2d"]) for f in range(8)]
        blk_srcs += [(wb["winb"][m], wbuf["winb"]) for m in range(8)]
        for fc in range(8):
            blk_srcs += [(wb["winb"][8 + gg * 8 + fc], wbuf["winb"]) for gg in range(3)]
        blk_srcs += [(wb["wout"][f], wbuf["wout"]) for f in range(8)]
        wo_srcs += [(wb["wo3"][f], wbuf["wo3"]) for f in range(8)]
    gu3 = Stream(P, es3, "gu3", KC * 256, 3, gu_srcs)
    d3 = Stream(P, es3, "wd3", NJ * 128, 2, d_srcs)
    blk3 = Stream(P, es3, "blk3", KC * 128, 4, blk_srcs)
    wo3s = Stream(P, es3, "wo3", 16 * 128, 2, wo_srcs)
    UW = sb(es3, "UW", [128, 4, T + 2], BF16)
    UWb = Buf()
    s_uw = P.sem("uw")
    AOX = sb(es3, "AOX", [64, 8, T], BF16)
    AOXb = Buf()
    s_aox = P.sem("aox")
    CVT = sb(es3, "CVT", [128, T], F32)
    CVTb = Buf()
    CV = sb(es3, "CV", [128, 4, T], BF16)
    CVb = Buf()
    XQ = sb(es3, "XQ", [128, 4, T], BF16)
    XQb = Buf()
    RAWX = sb(es3, "RAWX", [128, T], F32)
    RAWXb = Buf()
    SQX = sb(es3, "SQX", [128, T], BF16)
    SQXb = Buf()
    RV3 = c.RINV
    RV3b = c.RINVb
    tmp3 = c.tmp
    XA = sb(es3, "XA", [128, 4, T], BF16)
    XAb = Buf()
    PM = [sb(es3, f"PM{i}", [128, T], BF16) for i in range(4)]
    PMb = [Buf() for _ in range(4)]
    RD = sb(es3, "RD", [128, T], F32)
    RDb = Buf()
    TH = [sb(es3, f"TH{i}", [128, T], F32) for i in range(3)]
    THb = [Buf() for _ in range(3)]
    M0 = sb(es3, "M0", [128, T], F32)
    M1 = sb(es3, "M1", [128, T], F32)
    M0b, M1b = Buf(), Buf()
    MG = sb(es3, "MG", [128, KC, T], BF16)
    MGb = [Buf() for _ in range(KC)]
    s_y = [P.sem("y0"), P.sem("y1")]
    XSC = 128.0 ** -0.5

    for ti in range(NT):
        ctx, tl = ti // 4, ti % 4
        xs = ti % 2
        X, Xb = c.X[xs], c.Xb[xs]
        def load_x3(ti_):
            xs_ = ti_ % 2
            P.dma(SP, c.X[xs_][:, :, :], x1s[ti_].rearrange("p (k t) -> p k t", t=T), c.Xsem[xs_], reads=[x1s_buf[ti_]], writes=c.Xb[xs_])
        if ti == 0:
            load_x3(0)
        if ti + 1 < NT:
            load_x3(ti + 1)
        lo = max(tl * T - 1, 0)
        hi = min(tl * T + T + 1, 2048)
        d0 = lo - (tl * T - 1)
        P.dma(SP, UW[:, :, d0:d0 + (hi - lo)], Us[:, :, ctx, lo:hi], s_uw, reads=[Us_buf], writes=[UWb])
        if tl == 0:
            P.op(DVE, lambda: nc.vector.tensor_copy(out=UW[:, :, 0:1], in_=UH[:, :, 2 * ctx:2 * ctx + 1]), reads=[UHb], writes=[UWb])
        if tl == 3:
            P.op(DVE, lambda: nc.vector.tensor_copy(out=UW[:, :, T + 1:T + 2], in_=UH[:, :, 2 * ctx + 1:2 * ctx + 2]), reads=[UHb], writes=[UWb])
        P.dma(SP, AOX[:, :, :], AOs[:, :, ti * T:(ti + 1) * T], s_aox, reads=[AOs_buf], writes=[AOXb])
        xi = ti % 2
        if ti == 0:
            norm_to_xn(c, xs, T, G_MIX, xi)
        for i in range(4):
            ps, pb = proj(c, blk3, T, xi)
            P.op(DVE, lambda: nc.vector.tensor_scalar(out=CVT[:, :], in0=UW[:, i, 0:T], scalar1=g(G_CONV + i * 3 + 0), scalar2=None, op0=ALU.mult),
                 reads=[UWb, gains_b], writes=[CVTb])
            P.op(DVE, lambda: nc.vector.scalar_tensor_tensor(out=CVT[:, :], in0=UW[:, i, 1:T + 1], scalar=g(G_CONV + i * 3 + 1), in1=CVT[:, :],
                                                             op0=ALU.mult, op1=ALU.add), reads=[UWb, CVTb, gains_b], writes=[CVTb])
            P.op(DVE, lambda: nc.vector.scalar_tensor_tensor(out=CVT[:, :], in0=UW[:, i, 2:T + 2], scalar=g(G_CONV + i * 3 + 2), in1=CVT[:, :],
                                                             op0=ALU.mult, op1=ALU.add), reads=[UWb, CVTb, gains_b], writes=[CVTb])
            P.op(DVE, lambda: nc.vector.tensor_tensor(out=CV[:, i, :], in0=CVT[:, :], in1=ps[:, :], op=ALU.mult),
                 reads=[CVTb, pb], writes=[CVb])
        for h in range(4):
            ps, pb = proj(c, blk3, T, xi)
            P.op(ACT, lambda: nc.scalar.activation(out=SQX[:, :], in_=ps[:, :], func=AF.Square), reads=[pb], writes=[SQXb])
            P.op(ACT, lambda: nc.scalar.activation(out=RAWX[:, :], in_=ps[:, :], func=AF.Copy), reads=[pb], writes=[RAWXb])
            p2, p2b = bank()
            P.mm([lambda: nc.tensor.matmul(p2[:, :], lhsT=ONESB[:, :], rhs=SQX[:, :], start=True, stop=True)], reads=[ones_b, SQXb], writes=[p2b])
            rsqrt(tmp3, p2[:, :], p2b, RV3[:, :], RV3b, 1.0 / 128)
            P.op(DVE, lambda: nc.vector.scalar_tensor_tensor(out=XQ[:, h, :], in0=RAWX[:, :], scalar=g(G_XQ), in1=RV3[:, :],
                                                             op0=ALU.mult, op1=ALU.mult), reads=[RAWXb, RV3b, gains_b], writes=[XQb])
        def xa_scores(h):
            for mc in range(2):
                ps, pb = bank()
                P.mm([lambda: nc.tensor.matmul(ps[:, :], lhsT=MK[:, ctx, h, mc * 128:(mc + 1) * 128], rhs=XQ[:, h, :], start=True, stop=True)],
                     reads=[MK_b, XQb], writes=[pb])
                pm, pmb = PM[(h % 2) * 2 + mc], PMb[(h % 2) * 2 + mc]
                P.op(ACT, lambda: nc.scalar.activation(out=pm[:, :], in_=ps[:, :], func=AF.Exp, scale=XSC), reads=[pb], writes=[pmb])

        def xa_pv(h):
            pms = [PM[(h % 2) * 2 + mc] for mc in range(2)]
            pmbs = [PMb[(h % 2) * 2 + mc] for mc in range(2)]
            po, pob = bank()
            pdn, pdnb = bank()
            P.mm([lambda mc=mc: nc.tensor.matmul(po[:, :], lhsT=MV[:, ctx, mc, h * 128:(h + 1) * 128], rhs=pms[mc][:, :], start=(mc == 0), stop=(mc == 1))
                  for mc in range(2)], reads=[MV_b] + pmbs, writes=[pob])
            P.mm([lambda mc=mc: nc.tensor.matmul(pdn[:, :], lhsT=ONESB[:, :], rhs=pms[mc][:, :], start=(mc == 0), stop=(mc == 1))
                  for mc in range(2)], reads=[ones_b] + pmbs, writes=[pdnb])
            P.op(DVE, lambda: nc.vector.reciprocal(out=RD[:, :], in_=pdn[:, :]), reads=[pdnb], writes=[RDb])
            P.op(DVE, lambda: nc.vector.tensor_tensor(out=XA[:, h, :], in0=po[:, :], in1=RD[:, :], op=ALU.mult), reads=[pob, RDb], writes=[XAb])

        xa_scores(0)
        for h in range(4):
            if h + 1 < 4:
                xa_scores(h + 1)
            xa_pv(h)
        for fc in range(KC):
            wt, wtb = wo3s.next()
            w3 = wt[:, :].rearrange("p (k c) -> p k c", c=128)
            ys = []
            py, pyb = bank()
            P.mm([lambda h=h: nc.tensor.matmul(py[:, :], lhsT=w3[0:64, h, :], rhs=AOX[:, h, :], start=(h == 0), stop=(h == 7)) for h in range(8)],
                 reads=[wtb, AOXb], writes=[pyb])
            ys.append((py, pyb))
            py, pyb = bank()
            P.mm([lambda i=i: nc.tensor.matmul(py[:, :], lhsT=w3[:, 8 + i, :], rhs=CV[:, i, :], start=(i == 0), stop=(i == 3)) for i in range(4)],
                 reads=[wtb, CVb], writes=[pyb])
            ys.append((py, pyb))
            py, pyb = bank()
            P.mm([lambda i=i: nc.tensor.matmul(py[:, :], lhsT=w3[:, 12 + i, :], rhs=XA[:, i, :], start=(i == 0), stop=(i == 3)) for i in range(4)],
                 reads=[wtb, XAb], writes=[pyb])
            ys.append((py, pyb))
            for gg in range(3):
                pg, pgb = proj(c, blk3, T, xi)
                P.op(ACT, lambda: nc.scalar.activation(out=TH[gg][:, :], in_=pg[:, :], func=AF.Tanh, scale=0.5), reads=[pgb], writes=[THb[gg]])
            P.op(DVE, lambda: nc.vector.scalar_tensor_tensor(out=M0[:, :], in0=TH[0][:, :], scalar=1.0, in1=ys[0][0][:, :], op0=ALU.add, op1=ALU.mult),
                 reads=[THb[0], ys[0][1]], writes=[M0b])
            P.op(DVE, lambda: nc.vector.scalar_tensor_tensor(out=M1[:, :], in0=TH[1][:, :], scalar=1.0, in1=ys[1][0][:, :], op0=ALU.add, op1=ALU.mult),
                 reads=[THb[1], ys[1][1]], writes=[M1b])
            P.op(DVE, lambda: nc.vector.tensor_tensor(out=M0[:, :], in0=M0[:, :], in1=M1[:, :], op=ALU.add), reads=[M0b, M1b], writes=[M0b])
            P.op(DVE, lambda: nc.vector.scalar_tensor_tensor(out=M1[:, :], in0=TH[2][:, :], scalar=1.0, in1=ys[2][0][:, :], op0=ALU.add, op1=ALU.mult),
                 reads=[THb[2], ys[2][1]], writes=[M1b])
            P.op(DVE, lambda: nc.vector.tensor_tensor(out=MG[:, fc, :], in0=M0[:, :], in1=M1[:, :], op=ALU.add), reads=[M0b, M1b], writes=[MGb[fc]])
        for fo in range(KC):
            wt, wtb = blk3.next()
            w3 = wt[:, :].rearrange("p (k c) -> p k c", c=128)
            ps, pb = bank()
            P.mm([lambda k=k: nc.tensor.matmul(ps[:, :], lhsT=w3[:, k, :], rhs=MG[:, k, :], start=(k == 0), stop=(k == KC - 1)) for k in range(KC)],
                 reads=[wtb] + MGb, writes=[pb])
            P.op(DVE, lambda: nc.vector.scalar_tensor_tensor(out=X[:, fo, :], in0=ps[:, :], scalar=0.5, in1=X[:, fo, :], op0=ALU.mult, op1=ALU.add),
                 reads=[pb, Xb[fo]], writes=[Xb[fo]])
        def hook3(ti=ti):
            if ti + 1 < NT:
                norm_to_xn(c, (ti + 1) % 2, T, G_MIX, (ti + 1) % 2)
        ffn(c, xs, T, G_FFN2, gu3, d3, xi=xi, do_norm=True, mid_hook=hook3)
        P.dma(POOL, yT[ti].rearrange("p (k t) -> p k t", t=T), X[:, :, :], c.XsemS[xs], reads=Xb, writes=[])

    P.barrier()
    es3.close()
    es.close()


def _kblocks(W, cols):
    K = W.shape[0]
    kc = K // 128
    out = np.empty((len(cols), 128, kc, 128), np.float32)
    Wr = W.reshape(kc, 128, W.shape[1])
    for m, c0 in enumerate(cols):
        out[m] = Wr[:, :, c0:c0 + 128].transpose(1, 0, 2)
    return out.reshape(len(cols), 128, kc * 128)


def _prep_weights(inp):
    f = lambda a: np.ascontiguousarray(np.asarray(a, np.float32))
    out = {}
    for tag, gu, dn in (("w1", "ffn1_w_gu", "ffn1_w_down"), ("w2", "ffn2_w_gu", "ffn2_w_down")):
        W = f(inp[gu][0]).reshape(KC, 128, 2 * FF)
        gate = W[:, :, :FF].reshape(KC, 128, NJ, 128)
        up = W[:, :, FF:].reshape(KC, 128, NJ, 128)
        st = np.stack([gate, up], axis=3)
        out[tag + "gu"] = np.ascontiguousarray(st.transpose(2, 1, 0, 3, 4)).reshape(NJ, 128, KC * 256)
        Wd = f(inp[dn][0]).reshape(NJ, 128, 8, 128)
        out[tag + "d"] = np.ascontiguousarray(Wd.transpose(2, 1, 0, 3)).reshape(8, 128, NJ * 128)
    Win = f(inp["w_in"][0])
    perm = np.concatenate([np.arange(16, 32), np.arange(0, 16)])
    kr = Win[:, 640:672]
    Wkr = np.concatenate([kr, kr[:, perm], np.zeros((D, 64), np.float32)], axis=1)
    Wa = np.concatenate([Win[:, 0:640], Wkr, Win[:, 1184:2208]], axis=1)
    out["wina"] = _kblocks(Wa, [i * 128 for i in range(14)])
    Wb = np.concatenate([Win[:, 672:1184], Win[:, 2208:2720], Win[:, 2720:5792]], axis=1)
    out["winb"] = _kblocks(Wb, [i * 128 for i in range(32)])
    Wuq = f(inp["w_uq"][0])
    wuq = np.empty((8, 128, 3, 192), np.float32)
    Wr = Wuq.reshape(3, 128, 768)
    for h in range(8):
        blk = Wr[:, :, h * 96:(h + 1) * 96]
        sw = np.concatenate([blk[:, :, :64], blk[:, :, 64:][:, :, perm]], axis=2)
        wuq[h] = np.concatenate([blk, sw], axis=2).transpose(1, 0, 2)
    out["wuq"] = wuq.reshape(8, 128, 3 * 192)
    out["wuk"] = np.ascontiguousarray(f(inp["w_uk"][0]).reshape(2, 128, 512).transpose(1, 0, 2)).reshape(1, 128, 1024)
    out["wuv"] = np.ascontiguousarray(f(inp["w_uv"][0]).reshape(2, 128, 512).transpose(1, 0, 2)).reshape(1, 128, 1024)
    wo3 = np.zeros((8, 128, 16, 128), np.float32)
    Wm = f(inp["w_o_mla"][0]).reshape(8, 64, 8, 128)
    Wc = f(inp["w_o_conv"][0]).reshape(4, 128, 8, 128)
    Wx = f(inp["w_o_mem"][0]).reshape(4, 128, 8, 128)
    wo3[:, 0:64, 0:8, :] = Wm.transpose(2, 1, 0, 3)
    wo3[:, :, 8:12, :] = Wc.transpose(2, 1, 0, 3)
    wo3[:, :, 12:16, :] = Wx.transpose(2, 1, 0, 3)
    out["wo3"] = wo3.reshape(8, 128, 16 * 128)
    out["wout"] = _kblocks(f(inp["w_out"][0]), [i * 128 for i in range(8)])
    Wmkv = f(inp["w_mem_kv"][0])
    out["wmk"] = _kblocks(Wmkv, [i * 128 for i in range(4)])
    out["wmv"] = np.ascontiguousarray(Wmkv[:, 512:].reshape(8, 128, 512).transpose(1, 0, 2)).reshape(1, 128, 8 * 512)
    G = np.zeros((128, NG), np.float32)
    def colk(v, c0):
        v = f(v).reshape(-1, 128)
        for k in range(v.shape[0]):
            G[:, c0 + k] = v[k]
    colk(inp["ffn1_norm"][0], G_FFN1)
    colk(inp["mix_norm"][0], G_MIX)
    colk(inp["ffn2_norm"][0], G_FFN2)
    colk(inp["mem_norm"][0], G_MEM)
    colk(inp["q_lora_norm"][0], G_QL)
    colk(inp["kv_lora_norm"][0], G_KVL)
    mq = f(inp["mla_q_norm"][0])
    mk = f(inp["mla_k_norm"][0])
    G[0:96, G_MQ] = mq
    G[0:64, G_MQS] = mq[:64]
    G[64:96, G_MQS] = mq[64:][perm]
    G[0:64, G_MK] = mk[:64]
    G[0:32, G_KR] = mk[64:]
    G[0:32, G_KRS] = mk[64:][perm]
    G[:, G_XQ] = f(inp["xa_q_norm"][0])
    G[:, G_XK] = f(inp["xa_k_norm"][0])
    cw = f(inp["conv_w"][0])
    for i in range(4):
        for tap in range(3):
            G[:, G_CONV + i * 3 + tap] = cw[tap, i * 128:(i + 1) * 128]
    out["gains"] = G
    return out


def _rope_table(pos):
    half = 16
    inv_freq = (10000.0 ** (-np.arange(half, dtype=np.float32) / half)).astype(np.float32)
    ang = pos.astype(np.float32)[None, :] * inv_freq[:, None]
    cos = np.cos(ang).astype(np.float32)
    sin = np.sin(ang).astype(np.float32)
    c32 = np.concatenate([cos, cos], 0)
    s32 = np.concatenate([-sin, sin], 0)
    tab = np.stack([c32, s32], axis=1)
    return np.ascontiguousarray(np.tile(tab, (4, 1, 1)))


_NC_CACHE = {}


def kernel(**inputs):
    xp = np.asarray(inputs["x_prompt"], np.float32)
    xsm = np.asarray(inputs["x_sample"], np.float32)
    mp = np.asarray(inputs["mem_prompt"], np.float32)
    ms = np.asarray(inputs["mem_sample"], np.float32)
    W = _prep_weights(inputs)
    if "nc" not in _NC_CACHE:
        _NC_CACHE["nc"] = build()
    nc = _NC_CACHE["nc"]
    in_maps = []
    for cidx in range(8):
        ps_, pq_ = cidx // 4, cidx % 4
        ss_, sh_ = cidx // 2, cidx % 2
        xpc = xp[ps_, pq_ * 2048:(pq_ + 1) * 2048]
        xsc = xsm[ss_, sh_ * 2048:(sh_ + 1) * 2048]
        xc = np.concatenate([xpc, xsc], 0)
        xt = xc.reshape(NT, T, KC, 128).transpose(0, 3, 2, 1)
        halo = np.zeros((4, D), np.float32)
        if pq_ > 0:
            halo[0] = xp[ps_, pq_ * 2048 - 1]
        if pq_ < 3:
            halo[1] = xp[ps_, (pq_ + 1) * 2048]
        if sh_ > 0:
            halo[2] = xsm[ss_, sh_ * 2048 - 1]
        if sh_ < 1:
            halo[3] = xsm[ss_, (sh_ + 1) * 2048]
        hl = halo.reshape(4, KC, 128).transpose(2, 1, 0)
        memc = np.stack([mp[ps_], ms[ss_]], 0)
        memt = memc.reshape(2, 256, KC, 128).transpose(0, 3, 2, 1)
        pos = np.concatenate([np.arange(pq_ * 2048, (pq_ + 1) * 2048), np.arange(sh_ * 2048, (sh_ + 1) * 2048)])
        m = {
            "xT": np.ascontiguousarray(xt).reshape(NT, 128, KC * T),
            "xh": np.ascontiguousarray(hl).reshape(128, KC * 4),
            "memT": np.ascontiguousarray(memt).reshape(2, 128, KC * 256),
            "ropeT": _rope_table(pos),
        }
        m.update(W)
        in_maps.append(m)
    res = run_bass_kernel_spmd(nc, in_maps, core_ids=list(range(8)))
    yp = np.empty_like(xp)
    ysm = np.empty_like(xsm)
    for cidx in range(8):
        ps_, pq_ = cidx // 4, cidx % 4
        ss_, sh_ = cidx // 2, cidx % 2
        y = np.asarray(res.results[cidx]["yT"]).reshape(NT, 128, KC, T).transpose(0, 3, 2, 1).reshape(NT * T, D)
        yp[ps_, pq_ * 2048:(pq_ + 1) * 2048] = y[:2048]
        ysm[ss_, sh_ * 2048:(sh_ + 1) * 2048] = y[2048:]
    return (yp, ysm)
```

```python
import numpy as np
from contextlib import ExitStack
import concourse.bass as bass
import concourse.mybir as mybir
from concourse.bass_utils import run_bass_kernel_spmd

F32 = mybir.dt.float32
BF16 = mybir.dt.bfloat16
I32 = mybir.dt.int32
AF = mybir.ActivationFunctionType
ALU = mybir.AluOpType

D = 1024
KC = 8
T = 512
NT = 8
FF = 2816
NJ = 22
EPS = 1e-6
SAME_ENG_SYNC = True
MAGIC = 1597463007.0

G_FFN1, G_MIX, G_FFN2, G_MEM, G_QL, G_KVL = 0, 8, 16, 24, 32, 35
G_MQ, G_MQS, G_MK, G_KR, G_KRS, G_XQ, G_XK, G_CONV = 37, 38, 39, 40, 41, 42, 43, 44
NG = 56


class Buf:
    __slots__ = ("w", "r", "name")

    def __init__(self, name=""):
        self.w = None
        self.r = {}
        self.name = name


class Eng:
    def __init__(self, e, sem, name, is_pe=False):
        self.e = e
        self.sem = sem
        self.n = 0
        self.seen = {}
        self.name = name
        self.is_pe = is_pe

    def wait(self, tok):
        if tok is None:
            return
        sem, val = tok
        if sem is self.sem and (self.is_pe or not SAME_ENG_SYNC):
            return
        k = id(sem)
        if self.seen.get(k, 0) >= val:
            return
        self.e.wait_ge(sem, val)
        self.seen[k] = val


class Prog:
    def __init__(self):
        self.nc = bass.Bass("TRN2", target_bir_lowering=False)
        self.es = ExitStack()
        nc = self.nc
        self.semcount = {}
        self.sems = []
        self.PE = Eng(nc.tensor, self.sem("pe"), "pe", is_pe=True)
        self.ACT = Eng(nc.scalar, self.sem("act"), "act")
        self.DVE = Eng(nc.vector, self.sem("dve"), "dve")
        self.POOL = Eng(nc.gpsimd, self.sem("pool"), "pool")
        self.SP = Eng(nc.sync, self.sem("sp"), "sp")
        self.engs = [self.PE, self.ACT, self.DVE, self.POOL, self.SP]
        self.bank_rr = 0

    def sem(self, name):
        s = self.es.enter_context(self.nc.semaphore(f"{name}_n{len(self.sems)}"))
        self.semcount[id(s)] = 0
        self.sems.append(s)
        return s

    def deps(self, E, reads, writes):
        for b in reads:
            E.wait(b.w)
        for b in writes:
            E.wait(b.w)
            for t in list(b.r.values()):
                E.wait(t)

    def done(self, tok, reads, writes):
        for b in reads:
            b.r[id(tok[0])] = tok
        for b in writes:
            b.w = tok
            b.r = {}

    def op(self, E, fn, reads=(), writes=()):
        self.deps(E, reads, writes)
        ins = fn()
        E.n += 1
        ins.then_inc(E.sem, 1)
        self.semcount[id(E.sem)] = E.n
        self.done((E.sem, E.n), reads, writes)

    def mm(self, fns, reads, writes):
        E = self.PE
        self.deps(E, reads, writes)
        ins = None
        for f in fns:
            ins = f()
        E.n += 1
        ins.then_inc(E.sem, 1)
        self.semcount[id(E.sem)] = E.n
        self.done((E.sem, E.n), reads, writes)

    def mmf(self, items, reads, writes):
        E = self.PE
        self.deps(E, reads, writes)
        allr = list(reads)
        ins = None
        for f, rb in items:
            for b in rb:
                E.wait(b.w)
            allr += rb
            ins = f()
        E.n += 1
        ins.then_inc(E.sem, 1)
        self.semcount[id(E.sem)] = E.n
        self.done((E.sem, E.n), allr, writes)

    def dma(self, Q, out, in_, sem, reads=(), writes=(), **kw):
        self.deps(Q, reads, writes)
        Q.e.dma_start(out=out, in_=in_, **kw).then_inc(sem, 16)
        self.semcount[id(sem)] += 16
        self.done((sem, self.semcount[id(sem)]), reads, writes)

    def barrier(self):
        for E in self.engs:
            for s in self.sems:
                c = self.semcount[id(s)]
                if c > 0:
                    E.wait((s, c)) if s is not E.sem else None


class Stream:
    def __init__(self, P, es, name, width, nslots, srcs):
        self.P = P
        self.srcs = srcs
        self.n = nslots
        self.slots = [es.enter_context(P.nc.sbuf_tensor(f"{name}_s{i}_{len(P.sems)}", [128, width], BF16)) for i in range(nslots)]
        self.bufs = [Buf(f"{name}{i}") for i in range(nslots)]
        self.sems = [P.sem(f"{name}_q{i}") for i in range(nslots)]
        self.issued = 0
        self.pos = 0

    def _issue(self):
        i = self.issued
        s = i % self.n
        ap, db = self.srcs[i]
        self.P.dma(self.P.SP, self.slots[s][:, :], ap, self.sems[s], reads=[db], writes=[self.bufs[s]])
        self.issued += 1

    def next(self):
        i = self.pos
        while self.issued < min(i + self.n, len(self.srcs)):
            self._issue()
        self.pos += 1
        s = i % self.n
        return self.slots[s], self.bufs[s]


import os as _os
STOP = float(_os.environ.get("KSTOP", "9"))
KSUB = int(_os.environ.get("KSUB", "99"))
KATT = int(_os.environ.get("KATT", "99"))
KK = int(_os.environ.get("KK", "99"))


class _Stop(Exception):
    pass


def build():
    stacks = []
    P = Prog()
    try:
        _build(P, stacks)
    except _Stop:
        P.barrier()
        for st in reversed(stacks):
            st.close()
        P.es.close()
    return P.nc


def _build(P, stacks):
    nc = P.nc
    es = P.es
    PE, ACT, DVE, POOL, SP = P.PE, P.ACT, P.DVE, P.POOL, P.SP

    def din(name, shape, dt=F32):
        return nc.dram_tensor(name, shape, dt, kind="ExternalInput")

    xT = din("xT", [NT, 128, KC * T])
    xh = din("xh", [128, KC * 4])
    memT = din("memT", [2, 128, KC * 256])
    gains_d = din("gains", [128, NG])
    rope_d = din("ropeT", [128, 2, NT * T])
    wsh = {
        "w1gu": [NJ, 128, KC * 256], "w1d": [8, 128, NJ * 128],
        "w2gu": [NJ, 128, KC * 256], "w2d": [8, 128, NJ * 128],
        "wina": [14, 128, KC * 128], "winb": [32, 128, KC * 128],
        "wuq": [8, 128, 3 * 192], "wuk": [1, 128, 2 * 512], "wuv": [1, 128, 2 * 512],
        "wo3": [8, 128, 16 * 128], "wout": [8, 128, KC * 128],
        "wmk": [4, 128, KC * 128], "wmv": [1, 128, KC * 512],
    }
    wf = {k: din(k, v) for k, v in wsh.items()}
    wb = {k: nc.dram_tensor(k + "_b", v, BF16) for k, v in wsh.items()}
    wbuf = {k: Buf(k) for k in wsh}
    yT = nc.dram_tensor("yT", [NT, 128, KC * T], F32, kind="ExternalOutput")

    x1s = nc.dram_tensor("x1s", [NT, 128, KC * T], F32)
    x1s_buf = [Buf(f"x1s{i}") for i in range(NT)]
    Us = nc.dram_tensor("Us", [128, 4, 2, 2048], BF16)
    Us_buf = Buf("Us")
    AOs = nc.dram_tensor("AOs", [64, 8, NT * T], BF16)
    AOs_buf = Buf("AOs")
    LROWS = 320
    latP = [nc.dram_tensor(f"latP{i}", [LROWS, 1024], BF16) for i in range(2)]
    latS = [nc.dram_tensor(f"latS{i}", [LROWS, 1024], BF16) for i in range(2)]
    gatP = [nc.dram_tensor(f"gatP{i}", [4 * LROWS, 1024], BF16) for i in range(2)]
    gatS = [nc.dram_tensor(f"gatS{i}", [2 * LROWS, 1024], BF16) for i in range(2)]
    lat_buf = [Buf("latP"), Buf("latS")]
    gat_buf = [Buf("gatP"), Buf("gatS")]

    uniq = [0]

    def sb(stack, name, shape, dt):
        uniq[0] += 1
        return stack.enter_context(nc.sbuf_tensor(f"{name}_u{uniq[0]}", shape, dt))

    wchunks = {}

    def emit_cast(k, step=4096):
        n0, _, wd = wsh[k]
        bb = max(d_ for d_ in range(1, 1025) if wd % d_ == 0)
        src = wf[k].ap().rearrange("n p (a b) -> (n p a) b", b=bb)
        dst = wb[k].ap().rearrange("n p (a b) -> (n p a) b", b=bb)
        rows = src.shape[0]
        rpb = rows // n0
        fine = step < rows and step % rpb == 0 and k == "w1gu"
        s = None if fine else P.sem("c_" + k)
        wchunks[k] = []
        for r0 in range(0, rows, step):
            r1 = min(rows, r0 + step)
            if fine:
                sc = P.sem(f"c_{k}_{r0}")
                cb = Buf(f"{k}_{r0}")
                P.dma(POOL, dst[r0:r1, :], src[r0:r1, :], sc, writes=[cb], max_dma_last_dim=4096)
                wchunks[k].append((r0 // rpb, (r1 + rpb - 1) // rpb, cb))
            else:
                P.dma(POOL, dst[r0:r1, :], src[r0:r1, :], s, writes=[wbuf[k]] if r0 + step >= rows else [], max_dma_last_dim=4096)

    def wsrcbuf(k, i):
        for b0, b1, cb in wchunks.get(k, []):
            if b0 <= i < b1:
                return cb
        return wbuf[k]

    emit_cast("w1gu", step=1024)
    emit_cast("w1d")
    emit_cast("wina")
    late_casts = {0: ["wmk", "wmv", "wuq", "wuk", "wuv"], 1: ["winb"], 2: ["wo3", "wout"], 3: ["w2gu"], 4: ["w2d"]}

    if STOP <= 0.1:
        raise _Stop()
    gains = sb(es, "gains_sb", [128, NG], F32)
    gains_b = Buf("gains")
    ONESB = sb(es, "onesb", [128, 128], BF16)
    ONESF = sb(es, "onesf", [128, 64], F32)
    ones_b = Buf("ones")
    s_misc = P.sem("misc")
    P.dma(SP, gains[:, :], gains_d.ap(), s_misc, writes=[gains_b])
    P.op(DVE, lambda: nc.vector.memset(ONESB[:, :], 1.0), writes=[ones_b])
    P.op(DVE, lambda: nc.vector.memset(ONESF[:, :], 1.0), writes=[ones_b])
    MK = sb(es, "MK", [128, 2, 4, 256], BF16)
    MV = sb(es, "MV", [128, 2, 2, 512], BF16)
    MK_b, MV_b = Buf("MK"), Buf("MV")
    UH = sb(es, "UH", [128, 4, 4], BF16)
    UHb = Buf()

    psum = [es.enter_context(nc.psum_tensor(f"ps{i}", [128, 512], F32)) for i in range(8)]
    psb = [Buf(f"ps{i}") for i in range(8)]
    bank_pool = list(range(8))

    def bank():
        i = bank_pool[P.bank_rr % len(bank_pool)]
        P.bank_rr += 1
        return psum[i], psb[i]

    def g(col, p0=0, p1=128):
        return gains[p0:p1, col:col + 1]

    def rsqrt(tmp, ps_ap, ps_b, out_ap, out_b, inv_n, post=None):
        V, Y, TT = tmp["V"], tmp["Y"], tmp["T"]
        vb, yb, tb = tmp["Vb"], tmp["Yb"], tmp["Tb"]
        shp = ps_ap.shape
        np_, n = shp[0], shp[1]
        p0 = tmp.get("p0", 0)
        v = V[p0:p0 + np_, 0:n]
        y = Y[p0:p0 + np_, 0:n]
        t = TT[p0:p0 + np_, 0:n]
        P.op(DVE, lambda: nc.vector.tensor_scalar(out=v, in0=ps_ap, scalar1=inv_n, scalar2=EPS, op0=ALU.mult, op1=ALU.add),
             reads=[ps_b], writes=[vb])
        P.op(DVE, lambda: nc.vector.tensor_scalar(out=y.bitcast(I32), in0=v.bitcast(I32), scalar1=-0.5, scalar2=MAGIC,
                                                  op0=ALU.mult, op1=ALU.add), reads=[vb], writes=[yb])
        for it in range(2):
            P.op(DVE, lambda: nc.vector.tensor_tensor(out=t, in0=y, in1=y, op=ALU.mult), reads=[yb], writes=[tb])
            P.op(DVE, lambda: nc.vector.scalar_tensor_tensor(out=t, in0=t, scalar=-0.5, in1=v, op0=ALU.mult, op1=ALU.mult),
                 reads=[tb, vb], writes=[tb])
            last = it == 1
            o = out_ap if last else y
            ob = out_b if last else yb
            if last and post is not None:
                P.op(DVE, lambda: nc.vector.scalar_tensor_tensor(out=y, in0=t, scalar=1.5, in1=y, op0=ALU.add, op1=ALU.mult),
                     reads=[tb, yb], writes=[yb])
                P.op(DVE, lambda: nc.vector.tensor_scalar(out=o, in0=y, scalar1=post, scalar2=None, op0=ALU.mult),
                     reads=[yb], writes=[ob])
            else:
                P.op(DVE, lambda: nc.vector.scalar_tensor_tensor(out=o, in0=t, scalar=1.5, in1=y, op0=ALU.add, op1=ALU.mult),
                     reads=[tb, yb], writes=[ob])

    def mk_tmp(stack, tag, n=T):
        return {"V": sb(stack, "tV" + tag, [128, n], F32), "Y": sb(stack, "tY" + tag, [128, n], F32),
                "T": sb(stack, "tT" + tag, [128, n], F32), "Vb": Buf(), "Yb": Buf(), "Tb": Buf()}

    class TileCtx:
        pass

    def alloc_tile_ctx(stack):
        c = TileCtx()
        c.X = [sb(stack, f"X{i}", [128, KC, T], F32) for i in range(2)]
        c.Xb = [[Buf(f"X{i}_{k}") for k in range(KC)] for i in range(2)]
        c.Xsem = [P.sem(f"X{i}") for i in range(2)]
        c.XsemS = [P.sem(f"XS{i}") for i in range(2)]
        c.XNs = [sb(stack, f"XN{i}", [128, KC, T], BF16) for i in range(2)]
        c.XNbs = [[Buf(f"XN{i}_{k}") for k in range(KC)] for i in range(2)]
        c.XN = c.XNs[0]
        c.XNb = c.XNbs[0]
        c.SQ = sb(stack, "SQn", [128, KC, T], BF16)
        c.SQb = [Buf(f"SQ{k}") for k in range(KC)]
        c.H = sb(stack, "H", [128, NJ, T], BF16)
        c.Hb = [Buf(f"H{j}") for j in range(NJ)]
        c.SG = [sb(stack, f"SG{i}", [128, T], F32) for i in range(2)]
        c.SGb = [Buf(), Buf()]
        c.RINV = sb(stack, "RINV", [128, T], F32)
        c.RINVb = Buf("rinv")
        c.tmp = mk_tmp(stack, "a")
        return c

    def norm_to_xn(c, xs, n, gcol, xi=0):
        X, Xb = c.X[xs], c.Xb[xs]
        XN, XNb = c.XNs[xi], c.XNbs[xi]
        for k0 in range(0, KC, 4):
            P.op(ACT, lambda k0=k0: nc.scalar.activation(out=c.SQ[:, k0:k0 + 4, 0:n], in_=X[:, k0:k0 + 4, 0:n], func=AF.Square),
                 reads=Xb[k0:k0 + 4], writes=c.SQb[k0:k0 + 4])
        ps, pb = bank()
        P.mmf([(lambda k=k: nc.tensor.matmul(ps[:, 0:n], lhsT=ONESB[:, :], rhs=c.SQ[:, k, 0:n], start=(k == 0), stop=(k == KC - 1)), [c.SQb[k]])
               for k in range(KC)], reads=[ones_b], writes=[pb])
        rsqrt(c.tmp, ps[:, 0:n], pb, c.RINV[:, 0:n], c.RINVb, 1.0 / D)
        for k in range(KC):
            P.op(DVE, lambda k=k: nc.vector.scalar_tensor_tensor(out=XN[:, k, 0:n], in0=X[:, k, 0:n], scalar=g(gcol + k),
                                                                 in1=c.RINV[:, 0:n], op0=ALU.mult, op1=ALU.mult),
                 reads=[Xb[k], c.RINVb, gains_b], writes=[XNb[k]])

    def ffn(c, xs, n, gcol, gu_stream, d_stream, xi=0, do_norm=True, mid_hook=None):
        X, Xb = c.X[xs], c.Xb[xs]
        XN, XNb = c.XNs[xi], c.XNbs[xi]
        if do_norm:
            norm_to_xn(c, xs, n, gcol, xi)
        for j in range(NJ):
            wt, wtb = gu_stream.next()
            w3 = wt[:, :].rearrange("p (k c) -> p k c", c=256)
            pg, pgb = bank()
            pu, pub = bank()
            P.mmf([(lambda k=k: nc.tensor.matmul(pg[:, 0:n], lhsT=w3[:, k, 0:128], rhs=XN[:, k, 0:n], start=(k == 0), stop=(k == KC - 1)), [XNb[k]])
                   for k in range(KC)], reads=[wtb], writes=[pgb])
            P.mmf([(lambda k=k: nc.tensor.matmul(pu[:, 0:n], lhsT=w3[:, k, 128:256], rhs=XN[:, k, 0:n], start=(k == 0), stop=(k == KC - 1)), [XNb[k]])
                   for k in range(KC)], reads=[wtb], writes=[pub])
            sg, sgb = c.SG[j % 2], c.SGb[j % 2]
            P.op(ACT, lambda: nc.scalar.activation(out=sg[:, 0:n], in_=pg[:, 0:n], func=AF.Silu), reads=[pgb], writes=[sgb])
            P.op(DVE, lambda: nc.vector.tensor_tensor(out=c.H[:, j, 0:n], in0=sg[:, 0:n], in1=pu[:, 0:n], op=ALU.mult),
                 reads=[sgb, pub], writes=[c.Hb[j]])
        if mid_hook is not None:
            mid_hook()
        for fc in range(KC):
            wt, wtb = d_stream.next()
            w3 = wt[:, :].rearrange("p (j c) -> p j c", c=128)
            pd, pdb = bank()
            P.mmf([(lambda j=j: nc.tensor.matmul(pd[:, 0:n], lhsT=w3[:, j, :], rhs=c.H[:, j, 0:n], start=(j == 0), stop=(j == NJ - 1)), [c.Hb[j]])
                   for j in range(NJ)], reads=[wtb], writes=[pdb])
            P.op(DVE, lambda: nc.vector.scalar_tensor_tensor(out=X[:, fc, 0:n], in0=pd[:, 0:n], scalar=0.5, in1=X[:, fc, 0:n],
                                                             op0=ALU.mult, op1=ALU.add), reads=[pdb, Xb[fc]], writes=[Xb[fc]])

    def proj(c, blk_stream, n, xi=0):
        wt, wtb = blk_stream.next()
        w3 = wt[:, :].rearrange("p (k c) -> p k c", c=128)
        ps, pb = bank()
        P.mmf([(lambda k=k: nc.tensor.matmul(ps[:, 0:n], lhsT=w3[:, k, :], rhs=c.XNs[xi][:, k, 0:n], start=(k == 0), stop=(k == KC - 1)), [c.XNbs[xi][k]])
               for k in range(KC)], reads=[wtb], writes=[pb])
        return ps, pb

    esA = ExitStack()
    stacks.append(esA)
    CQ = sb(esA, "CQ", [128, 3, NT * T], BF16)
    CQb = [Buf(f"CQ{i}") for i in range(NT)]
    es1 = ExitStack()
    stacks.append(es1)
    c = alloc_tile_ctx(es1)
    order1 = list(range(NT)) + (["h"] if not _os.environ.get("KSKIPH") else [])
    gu_srcs, d_srcs, blk_srcs = [], [], []
    for ti in order1:
        gu_srcs += [(wb["w1gu"][j], wsrcbuf("w1gu", j)) for j in range(NJ)]
        d_srcs += [(wb["w1d"][f], wbuf["w1d"]) for f in range(8)]
        if ti == "h":
            blk_srcs += [(wb["wina"][m], wbuf["wina"]) for m in range(6, 14)]
        else:
            blk_srcs += [(wb["wina"][m], wbuf["wina"]) for m in range(14)]
    mem_blk = []
    for ctx in range(2):
        mem_blk += [(wb["wmk"][h], wbuf["wmk"]) for h in range(4)]
    blk_srcs = blk_srcs + mem_blk
    gu1 = Stream(P, es1, "gu", KC * 256, 3, gu_srcs)
    d1 = Stream(P, es1, "wd", NJ * 128, 2, d_srcs)
    blk1 = Stream(P, es1, "blk", KC * 128, 4, blk_srcs)
    RAW = sb(es1, "RAW", [128, 3, T], F32)
    RAWb = [Buf() for _ in range(3)]
    SQ3 = sb(es1, "SQ3", [128, 3, T], BF16)
    SQ3b = [Buf() for _ in range(3)]
    RV2 = c.RINV
    RV2b = c.RINVb
    tmp2 = c.tmp
    CKVN = sb(es1, "CKVN", [128, 2, T], BF16)
    CKVNb = Buf()
    s_ckvn = P.sem("ckvn")
    KRO = sb(es1, "KRO", [32, 2, T], BF16)
    KROb = Buf()
    s_kro = P.sem("kro")
    KA = sb(es1, "KA", [32, T], F32)
    KB = sb(es1, "KB", [32, T], F32)
    KAb, KBb = Buf(), Buf()
    ROPE = sb(es1, "ROPE", [32, 2, T], F32)
    ROPEb = Buf()
    s_rope = P.sem("rope")
    UT = sb(es1, "UT", [128, 4, T], BF16)
    UTb = Buf()
    s_ut = P.sem("ut")
    CCT = sb(es1, "CCT", [128, T], F32)
    CCTb = Buf()
    if STOP <= 0.3:
        raise _Stop()
    for idx, ti in enumerate(order1):
        if (STOP <= 0.4 and idx == 1) or (STOP <= 0.5 and idx == 2):
            raise _Stop()
        halo = ti == "h"
        n = 128 if halo else T
        xs = idx % 2
        X, Xb = c.X[xs], c.Xb[xs]
        def load_x(idx_, ti_):
            xs_ = idx_ % 2
            if ti_ == "h":
                P.op(DVE, lambda: nc.vector.memset(c.X[xs_][:, :, 0:128], 0.0), writes=c.Xb[xs_])
                P.dma(SP, c.X[xs_][:, :, 0:4], xh.ap().rearrange("p (k t) -> p k t", t=4), c.Xsem[xs_], writes=c.Xb[xs_])
            else:
                P.dma(SP, c.X[xs_][:, :, :], xT[ti_].rearrange("p (k t) -> p k t", t=T), c.Xsem[xs_], writes=c.Xb[xs_])
        if idx == 0:
            load_x(0, order1[0])
        if idx + 1 < len(order1):
            load_x(idx + 1, order1[idx + 1])
        if not halo:
            P.dma(SP, ROPE[:, :, :], rope_d[0:32, :, ti * T:(ti + 1) * T], s_rope, writes=[ROPEb])
        xi = idx % 2
        if idx == 0:
            norm_to_xn(c, xs, n, G_FFN1, xi)

        def hook1(idx=idx):
            if idx + 1 < len(order1):
                norm_to_xn(c, (idx + 1) % 2, 128 if order1[idx + 1] == "h" else T, G_FFN1, (idx + 1) % 2)
        ffn(c, xs, n, G_FFN1, gu1, d1, xi=xi, do_norm=False, mid_hook=hook1)
        if KSUB <= 4:
            raise _Stop()
        if not halo:
            P.dma(POOL, x1s[ti].rearrange("p (k t) -> p k t", t=T), X[:, :, :], c.XsemS[xs], reads=Xb, writes=[x1s_buf[ti]])
        for kk in late_casts.get(idx, []):
            emit_cast(kk)
        if KSUB <= 5:
            raise _Stop()
        norm_to_xn(c, xs, n, G_MIX, xi)
        if KSUB <= 6:
            raise _Stop()
        if not halo:
            ch, tl = ti // 4, ti % 4
            for i in range(3):
                ps, pb = proj(c, blk1, n, xi)
                P.op(ACT, lambda: nc.scalar.activation(out=SQ3[:, i, :], in_=ps[:, :], func=AF.Square), reads=[pb], writes=[SQ3b[i]])
                P.op(ACT, lambda: nc.scalar.activation(out=RAW[:, i, :], in_=ps[:, :], func=AF.Copy), reads=[pb], writes=[RAWb[i]])
            p2, p2b = bank()
            P.mm([lambda i=i: nc.tensor.matmul(p2[:, :], lhsT=ONESB[:, :], rhs=SQ3[:, i, :], start=(i == 0), stop=(i == 2)) for i in range(3)],
                 reads=[ones_b] + SQ3b, writes=[p2b])
            rsqrt(tmp2, p2[:, :], p2b, RV2[:, :], RV2b, 1.0 / 384)
            for i in range(3):
                P.op(DVE, lambda i=i: nc.vector.scalar_tensor_tensor(out=CQ[:, i, ti * T:(ti + 1) * T], in0=RAW[:, i, :], scalar=g(G_QL + i),
                                                                     in1=RV2[:, :], op0=ALU.mult, op1=ALU.mult),
                     reads=[RAWb[i], RV2b, gains_b], writes=[CQb[ti]])
            if KSUB <= 7:
                raise _Stop()
            for i in range(2):
                ps, pb = proj(c, blk1, n, xi)
                P.op(ACT, lambda: nc.scalar.activation(out=SQ3[:, i, :], in_=ps[:, :], func=AF.Square), reads=[pb], writes=[SQ3b[i]])
                P.op(ACT, lambda: nc.scalar.activation(out=RAW[:, i, :], in_=ps[:, :], func=AF.Copy), reads=[pb], writes=[RAWb[i]])
            p2, p2b = bank()
            P.mm([lambda i=i: nc.tensor.matmul(p2[:, :], lhsT=ONESB[:, :], rhs=SQ3[:, i, :], start=(i == 0), stop=(i == 1)) for i in range(2)],
                 reads=[ones_b] + SQ3b[0:2], writes=[p2b])
            rsqrt(tmp2, p2[:, :], p2b, RV2[:, :], RV2b, 1.0 / 256)
            for i in range(2):
                P.op(DVE, lambda i=i: nc.vector.scalar_tensor_tensor(out=CKVN[:, i, :], in0=RAW[:, i, :], scalar=g(G_KVL + i),
                                                                     in1=RV2[:, :], op0=ALU.mult, op1=ALU.mult),
                     reads=[RAWb[i], RV2b, gains_b], writes=[CKVNb])
            lat = [latP, latS][ch][tl // 2]
            tl2 = tl % 2
            P.dma(POOL, lat[0:256, tl2 * T:(tl2 + 1) * T].rearrange("(k p) t -> p k t", p=128), CKVN[:, :, :], s_ckvn,
                  reads=[CKVNb], writes=[lat_buf[ch]])
            if KSUB <= 8:
                raise _Stop()
            wt, wtb = blk1.next()
            w3 = wt[:, :].rearrange("p (k c) -> p k c", c=128)
            pk, pkb = bank()
            pq, pqb = bank()
            P.mm([lambda k=k: nc.tensor.matmul(pk[0:32, :], lhsT=w3[:, k, 0:32], rhs=c.XNs[xi][:, k, :], start=(k == 0), stop=(k == KC - 1))
                  for k in range(KC)], reads=[wtb] + c.XNbs[xi], writes=[pkb])
            P.mm([lambda k=k: nc.tensor.matmul(pq[0:32, :], lhsT=w3[:, k, 32:64], rhs=c.XNs[xi][:, k, :], start=(k == 0), stop=(k == KC - 1))
                  for k in range(KC)], reads=[wtb] + c.XNbs[xi], writes=[pqb])
            P.op(ACT, lambda: nc.scalar.activation(out=KRO[:, 1, :], in_=pk[0:32, :], func=AF.Copy), reads=[pkb], writes=[KROb])
            P.op(DVE, lambda: nc.vector.scalar_tensor_tensor(out=KA[:, :], in0=pk[0:32, :], scalar=g(G_KR, 0, 32), in1=ROPE[:, 0, :],
                                                             op0=ALU.mult, op1=ALU.mult), reads=[pkb, ROPEb, gains_b, KROb], writes=[KAb])
            P.op(DVE, lambda: nc.vector.scalar_tensor_tensor(out=KB[:, :], in0=pq[0:32, :], scalar=g(G_KRS, 0, 32), in1=ROPE[:, 1, :],
                                                             op0=ALU.mult, op1=ALU.mult), reads=[pqb, ROPEb, gains_b], writes=[KBb])
            P.op(DVE, lambda: nc.vector.tensor_tensor(out=KRO[:, 0, :], in0=KA[:, :], in1=KB[:, :], op=ALU.add),
                 reads=[KAb, KBb], writes=[KROb])
            P.dma(POOL, lat[256:320, tl2 * T:(tl2 + 1) * T].rearrange("(a p) t -> p a t", p=32), KRO[:, :, :], s_kro,
                  reads=[KROb], writes=[lat_buf[ch]])
        if KSUB <= 9:
            raise _Stop()
        pcc = []
        for i in range(4):
            pcc.append(proj(c, blk1, n, xi))
            if i >= 1:
                pass
        for i in range(4):
            pc, pcb = pcc[i]
            px, pxb = proj(c, blk1, n, xi)
            P.op(ACT, lambda: nc.scalar.activation(out=CCT[:, 0:n], in_=pc[:, 0:n], func=AF.Copy), reads=[pcb], writes=[CCTb])
            if halo:
                P.op(DVE, lambda: nc.vector.tensor_tensor(out=UH[:, i, :], in0=CCT[:, 0:4], in1=px[:, 0:4], op=ALU.mult),
                     reads=[CCTb, pxb], writes=[UHb])
            else:
                P.op(DVE, lambda: nc.vector.tensor_tensor(out=UT[:, i, :], in0=CCT[:, :], in1=px[:, :], op=ALU.mult),
                     reads=[CCTb, pxb], writes=[UTb])
        if not halo:
            P.dma(POOL, Us[:, :, ch, tl * T:(tl + 1) * T], UT[:, :, :], s_ut, reads=[UTb], writes=[Us_buf])

    if STOP <= 1:
        raise _Stop()
    s_cc = P.sem("cc")
    P.deps(POOL, [lat_buf[0], lat_buf[1]], [gat_buf[0], gat_buf[1]])
    for hf in range(2):
        nc.gpsimd.collective_compute("AllGather", ALU.bypass, replica_groups=[[0, 1, 2, 3], [4, 5, 6, 7]],
                                     ins=[latP[hf].ap().opt()], outs=[gatP[hf].ap().opt()]).then_inc(s_cc)
        P.semcount[id(s_cc)] += 1
    gat_buf[0].w = (s_cc, P.semcount[id(s_cc)])
    for hf in range(2):
        nc.gpsimd.collective_compute("AllGather", ALU.bypass, replica_groups=[[0, 1], [2, 3], [4, 5], [6, 7]],
                                     ins=[latS[hf].ap().opt()], outs=[gatS[hf].ap().opt()]).then_inc(s_cc)
        P.semcount[id(s_cc)] += 1
    gat_buf[1].w = (s_cc, P.semcount[id(s_cc)])

    s_wmv = P.sem("wmv")
    WMVbs = c.Hb[8:16]
    P.dma(SP, c.H[:, 8:16, :], wb["wmv"][0].rearrange("p (k c) -> p k c", c=512), s_wmv, reads=[wbuf["wmv"]], writes=WMVbs)

    for ctx in range(2):
        MEMX = c.X[1][:, :, 0:256]
        MEMXb = c.Xb[1][0]
        P.dma(SP, MEMX, memT[ctx].rearrange("p (k m) -> p k m", m=256), c.Xsem[1], writes=c.Xb[1])
        for k in range(KC):
            P.op(ACT, lambda k=k: nc.scalar.activation(out=c.H[:, k, 0:256], in_=MEMX[:, k, :], func=AF.Square),
                 reads=[MEMXb], writes=[c.Hb[k]])
        ps, pb = bank()
        P.mm([lambda k=k: nc.tensor.matmul(ps[:, 0:256], lhsT=ONESB[:, :], rhs=c.H[:, k, 0:256], start=(k == 0), stop=(k == KC - 1))
              for k in range(KC)], reads=[ones_b] + c.Hb[0:KC], writes=[pb])
        rsqrt(c.tmp, ps[:, 0:256], pb, c.RINV[:, 0:256], c.RINVb, 1.0 / D)
        for k in range(KC):
            P.op(DVE, lambda k=k: nc.vector.scalar_tensor_tensor(out=c.XN[:, k, 0:256], in0=MEMX[:, k, :], scalar=g(G_MEM + k),
                                                                 in1=c.RINV[:, 0:256], op0=ALU.mult, op1=ALU.mult),
                 reads=[MEMXb, c.RINVb, gains_b], writes=[c.XNb[k]])
        for h in range(4):
            ps, pb = proj(c, blk1, 256)
            P.op(ACT, lambda: nc.scalar.activation(out=SQ3[:, 0, 0:256], in_=ps[:, 0:256], func=AF.Square), reads=[pb], writes=[SQ3b[0]])
            P.op(ACT, lambda: nc.scalar.activation(out=RAW[:, 0, 0:256], in_=ps[:, 0:256], func=AF.Copy), reads=[pb], writes=[RAWb[0]])
            p2, p2b = bank()
            P.mm([lambda: nc.tensor.matmul(p2[:, 0:256], lhsT=ONESB[:, :], rhs=SQ3[:, 0, 0:256], start=True, stop=True)],
                 reads=[ones_b, SQ3b[0]], writes=[p2b])
            rsqrt(tmp2, p2[:, 0:256], p2b, RV2[:, 0:256], RV2b, 1.0 / 128)
            P.op(DVE, lambda: nc.vector.scalar_tensor_tensor(out=MK[:, ctx, h, :], in0=RAW[:, 0, 0:256], scalar=g(G_XK), in1=RV2[:, 0:256],
                                                             op0=ALU.mult, op1=ALU.mult), reads=[RAWb[0], RV2b, gains_b], writes=[MK_b])
        wmv3 = c.H[:, 8:16, :]
        for mc in range(2):
            ps, pb = bank()
            P.mm([lambda k=k: nc.tensor.matmul(ps[:, :], lhsT=c.XN[:, k, mc * 128:(mc + 1) * 128], rhs=wmv3[:, k, :],
                                               start=(k == 0), stop=(k == KC - 1)) for k in range(KC)],
                 reads=WMVbs + c.XNb, writes=[pb])
            P.op(ACT, lambda: nc.scalar.activation(out=MV[:, ctx, mc, :], in_=ps[:, :], func=AF.Copy), reads=[pb], writes=[MV_b])

    P.barrier()
    es1.close()
    stacks.pop()
    if STOP <= 2:
        raise _Stop()

    es2 = ExitStack()
    stacks.append(es2)
    SMAX = 8192
    CKV = sb(es2, "CKV", [128, 2, SMAX], BF16)
    CKVb = Buf()
    s_ckv = P.sem("ckv")
    KT = [sb(es2, f"KT{i}", [96, SMAX], BF16) for i in range(2)]
    KTb = [Buf(), Buf()]
    KTrb = [Buf(), Buf()]
    s_kt = [P.sem("kt0"), P.sem("kt1")]
    KRR = sb(es2, "KRR", [96, 2048], BF16)
    KRRb = Buf()
    s_krr = P.sem("krr")
    VG = sb(es2, "VG", [128, SMAX // 128, 4, 65], BF16)
    VGb = Buf()
    QT = [sb(es2, f"QT{i}", [96, T], BF16) for i in range(2)]
    QTb = [Buf(), Buf()]
    NPT = 4
    PT = [sb(es2, f"PT{i}", [128, T], BF16) for i in range(NPT)]
    PTb = [Buf() for _ in range(NPT)]
    SQK = [sb(es2, f"SQK{i}", [64, T], BF16) for i in range(2)]
    SQKb = [Buf(), Buf()]
    SQQ = sb(es2, "SQQ", [96, T], BF16)
    SQQb = Buf()
    tq = mk_tmp(es2, "q")
    RQ = sb(es2, "RQ", [96, T], F32)
    RQb = Buf()
    QA = sb(es2, "QA", [96, T], F32)
    QB = sb(es2, "QB", [96, T], F32)
    QAb, QBb = Buf(), Buf()
    ROQ = sb(es2, "ROQ", [96, 2, T], F32)
    ROQb = Buf()
    s_roq = P.sem("roq")
    KRSS = sb(es2, "KRSS", [128, 64], F32)
    KRSSb = Buf()
    RK = [sb(es2, f"RK{i}", [128, 64], F32) for i in range(2)]
    RKb = [Buf(), Buf()]
    tk = mk_tmp(es2, "k", 64)
    SSK = sb(es2, "SSK", [128, 64], F32)
    SSKb = Buf()
    REC = sb(es2, "REC", [65, T], F32)
    RECb = Buf()
    BC = sb(es2, "BC", [64, T], F32)
    BCb = Buf()
    AOT = [sb(es2, f"AOT{i}", [64, T], BF16) for i in range(2)]
    AOTb = [Buf(), Buf()]
    s_aot = [P.sem("aot0"), P.sem("aot1")]
    P.op(DVE, lambda: nc.vector.memset(VG[:, :, :, 64:65], 1.0), writes=[VGb])
    WUQ = sb(es2, "WUQ", [128, 8, 3 * 192], BF16)
    WUK = sb(es2, "WUK", [128, 2 * 512], BF16)
    WUV = sb(es2, "WUV", [128, 2 * 512], BF16)
    wsm_b = Buf("wsmall")
    s_ws = P.sem("wsm")
    P.dma(SP, WUQ[:, :, :], wb["wuq"].ap().rearrange("h p x -> p h x"), s_ws, reads=[wbuf["wuq"]], writes=[wsm_b])
    P.dma(SP, WUK[:, :], wb["wuk"][0], s_ws, reads=[wbuf["wuk"]], writes=[wsm_b])
    P.dma(SP, WUV[:, :], wb["wuv"][0], s_ws, reads=[wbuf["wuv"]], writes=[wsm_b])

    wuk3 = WUK[:, :].rearrange("p (k c) -> p k c", c=512)
    wuv3 = WUV[:, :].rearrange("p (k c) -> p k c", c=512)
    ST_BANKS = [0, 1, 2]
    O_BANKS = [3, 4]
    bank_pool[:] = [5, 6, 7]
    st_rr = 0
    o_rr = 0
    pt_rr = 0
    qt_rr = 0
    kt_rr = 0
    aot_rr = 0
    SCALE = 96.0 ** -0.5

    rr = {"st": 0, "o": 0, "pt": 0, "qt": 0, "aot": 0}
    pend_q = [None]
    pend_fin = [None]
    LOOK = 2

    def q_gen(h, ti):
        P.dma(SP, ROQ[64:96, :, :], rope_d[64:96, :, ti * T:(ti + 1) * T], s_roq, writes=[ROQb])
        pq, pqb = bank()
        pqs, pqsb = bank()
        P.mm([lambda k=k: nc.tensor.matmul(pq[0:96, :], lhsT=WUQ[:, h, k * 192:k * 192 + 96], rhs=CQ[:, k, ti * T:(ti + 1) * T],
                                           start=(k == 0), stop=(k == 2)) for k in range(3)],
             reads=[wsm_b, CQb[ti]], writes=[pqb])
        P.mm([lambda k=k: nc.tensor.matmul(pqs[0:96, :], lhsT=WUQ[:, h, k * 192 + 96:k * 192 + 192], rhs=CQ[:, k, ti * T:(ti + 1) * T],
                                           start=(k == 0), stop=(k == 2)) for k in range(3)],
             reads=[wsm_b, CQb[ti]], writes=[pqsb])
        P.op(ACT, lambda: nc.scalar.activation(out=SQQ[:, :], in_=pq[0:96, :], func=AF.Square), reads=[pqb], writes=[SQQb])
        p2, p2b = bank()
        P.mm([lambda: nc.tensor.matmul(p2[0:96, :], lhsT=ONESB[0:96, 0:96], rhs=SQQ[:, :], start=True, stop=True)],
             reads=[ones_b, SQQb], writes=[p2b])
        rsqrt(tq, p2[0:96, :], p2b, RQ[:, :], RQb, 1.0 / 96)
        qi = rr["qt"] % 2
        rr["qt"] += 1
        qtile, qtb = QT[qi], QTb[qi]
        P.op(DVE, lambda: nc.vector.scalar_tensor_tensor(out=qtile[0:64, :], in0=pq[0:64, :], scalar=g(G_MQ, 0, 64), in1=RQ[0:64, :],
                                                         op0=ALU.mult, op1=ALU.mult), reads=[pqb, RQb, gains_b], writes=[qtb])
        P.op(DVE, lambda: nc.vector.scalar_tensor_tensor(out=QA[64:96, :], in0=pq[64:96, :], scalar=g(G_MQ, 64, 96), in1=ROQ[64:96, 0, :],
                                                         op0=ALU.mult, op1=ALU.mult), reads=[pqb, ROQb, gains_b], writes=[QAb])
        P.op(DVE, lambda: nc.vector.scalar_tensor_tensor(out=QB[64:96, :], in0=pqs[64:96, :], scalar=g(G_MQS, 64, 96), in1=ROQ[64:96, 1, :],
                                                         op0=ALU.mult, op1=ALU.mult), reads=[pqsb, ROQb, gains_b], writes=[QBb])
        P.op(DVE, lambda: nc.vector.tensor_tensor(out=QA[64:96, :], in0=QA[64:96, :], in1=QB[64:96, :], op=ALU.add),
             reads=[QAb, QBb], writes=[QAb])
        P.op(DVE, lambda: nc.vector.tensor_tensor(out=qtile[64:96, :], in0=QA[64:96, :], in1=RQ[64:96, :], op=ALU.mult),
             reads=[QAb, RQb], writes=[qtb])
        return qtile, qtb

    def main_store(h, hh, ti, NCH, ktile, ktb, ki, rk, rkb, qtile, qtb):
        ob = O_BANKS[rr["o"] % 2]
        rr["o"] += 1
        po, pob = psum[ob], psb[ob]
        P.deps(PE, [], [pob])
        pend = []
        for step in range(NCH + LOOK):
            if step < NCH:
                cc = step
                sbk = ST_BANKS[rr["st"] % 3]
                rr["st"] += 1
                pst, pstb = psum[sbk], psb[sbk]
                P.mm([lambda: nc.tensor.matmul(pst[:, :], lhsT=ktile[0:96, cc * 128:(cc + 1) * 128], rhs=qtile[0:96, :], start=True, stop=True)],
                     reads=[ktb, KTrb[ki], qtb], writes=[pstb])
                pi = rr["pt"] % NPT
                rr["pt"] += 1
                P.op(ACT, lambda: nc.scalar.activation(out=PT[pi][:, :], in_=pst[:, :], func=AF.Exp, scale=rk[:, cc:cc + 1]),
                     reads=[pstb, rkb], writes=[PTb[pi]])
                pend.append((cc, pi))
            if step >= LOOK:
                cc, pi = pend.pop(0)
                last = cc == NCH - 1
                if not last:
                    PE.wait(PTb[pi].w)
                    PE.wait(VGb.w)
                    nc.tensor.matmul(po[0:65, :], lhsT=VG[:, cc, hh, :], rhs=PT[pi][:, :], start=(cc == 0), stop=False)
                    PTb[pi].r[id(PE.sem)] = (PE.sem, PE.n + 1)
                else:
                    P.mm([lambda: nc.tensor.matmul(po[0:65, :], lhsT=VG[:, cc, hh, :], rhs=PT[pi][:, :], start=(cc == 0), stop=True)],
                         reads=[PTb[pi], VGb], writes=[pob])
            if step == 22 and pend_fin[0] is not None:
                finish(*pend_fin[0])
                pend_fin[0] = None
        pend_fin[0] = (h, ti, po, pob)

    def finish(h, ti, po, pob):
        P.op(DVE, lambda: nc.vector.reciprocal(out=REC[64:65, :], in_=po[64:65, :]), reads=[pob], writes=[RECb])
        pbc, pbcb = bank()
        P.mm([lambda: nc.tensor.matmul(pbc[0:64, :], lhsT=ONESF[64:65, 0:64], rhs=REC[64:65, :], start=True, stop=True)],
             reads=[ones_b, RECb], writes=[pbcb])
        P.op(ACT, lambda: nc.scalar.activation(out=BC[:, :], in_=pbc[0:64, :], func=AF.Copy), reads=[pbcb], writes=[BCb])
        ai = rr["aot"] % 2
        rr["aot"] += 1
        P.op(DVE, lambda: nc.vector.tensor_tensor(out=AOT[ai][:, :], in0=po[0:64, :], in1=BC[:, :], op=ALU.mult),
             reads=[pob, BCb], writes=[AOTb[ai]])
        P.dma(POOL, AOs[:, h, ti * T:(ti + 1) * T], AOT[ai][:, :], s_aot[ai], reads=[AOTb[ai]], writes=[AOs_buf])

    for ctx in range(2):
        S = 8192 if ctx == 0 else 4096
        R = 4 if ctx == 0 else 2
        NCH = S // 128
        NKT = S // T
        gat = [gatP, gatS][ctx]
        gb = gat_buf[ctx]
        gvs = [gat[hf].ap().rearrange("(r x) t -> r x t", x=LROWS) for hf in range(2)]
        for r in range(R):
            for hf in range(2):
                c0 = r * 2048 + hf * 1024
                P.dma(SP, CKV[:, :, c0:c0 + 1024], gvs[hf][r, 0:256, :].rearrange("(k p) t -> p k t", p=128), s_ckv,
                      reads=[gb], writes=[CKVb])
                for i in range(2):
                    P.dma(SP, KT[i][64:96, c0:c0 + 1024], gvs[hf][r, 256:288, :], s_kt[i], reads=[gb], writes=[KTrb[i], KTb[i]])
        for r in range(R):
            for hf in range(2):
                P.dma(SP, KRR[64:96, hf * 1024:(hf + 1) * 1024], gvs[hf][r, 288:320, :], s_krr, reads=[gb], writes=[KRRb])
            for kt in range(4):
                P.op(ACT, lambda kt=kt: nc.scalar.activation(out=KRR[64:96, kt * T:(kt + 1) * T], in_=KRR[64:96, kt * T:(kt + 1) * T], func=AF.Square),
                     reads=[KRRb], writes=[KRRb])
            ps, pb = bank()
            fns = [lambda cc=cc: nc.tensor.matmul(ps[:, cc:cc + 1], lhsT=KRR[64:96, cc * 128:(cc + 1) * 128], rhs=ONESB[64:96, 0:1], start=True, stop=True)
                   for cc in range(16)]
            P.mm(fns, reads=[KRRb, ones_b], writes=[pb])
            P.op(DVE, lambda: nc.vector.tensor_copy(out=KRSS[:, r * 16:(r + 1) * 16], in_=ps[:, 0:16]), reads=[pb], writes=[KRSSb])

        if KATT <= 1:
            raise _Stop()
        for hg in range(2):
            for cc in range(NCH):
                ps, pb = bank()
                P.mm([lambda k=k: nc.tensor.matmul(ps[:, 0:256], lhsT=CKV[:, k, cc * 128:(cc + 1) * 128], rhs=wuv3[:, k, hg * 256:(hg + 1) * 256],
                                                   start=(k == 0), stop=(k == 1)) for k in range(2)],
                     reads=[CKVb, wsm_b], writes=[pb])
                eng = ACT if cc % 2 == 0 else DVE
                if eng is ACT:
                    P.op(ACT, lambda: nc.scalar.activation(out=VG[:, cc, :, 0:64], in_=ps[:, 0:256].rearrange("p (h d) -> p h d", d=64), func=AF.Copy),
                         reads=[pb], writes=[VGb])
                else:
                    P.op(DVE, lambda: nc.vector.tensor_copy(out=VG[:, cc, :, 0:64], in_=ps[:, 0:256].rearrange("p (h d) -> p h d", d=64)),
                         reads=[pb], writes=[VGb])
            if KATT <= 2:
                raise _Stop()
            for hh in range(4):
                h = hg * 4 + hh
                ki = kt_rr % 2
                kt_rr += 1
                ktile, ktb = KT[ki], KTb[ki]
                pss, pssb = psum[0], psb[0]
                def ss_mm(kt_):
                    sq_, sqb_ = SQK[kt_ % 2], SQKb[kt_ % 2]
                    P.mm([lambda a=a: nc.tensor.matmul(pss[:, kt_ * 4 + a:kt_ * 4 + a + 1], lhsT=sq_[0:64, a * 128:(a + 1) * 128], rhs=ONESB[0:64, 0:1],
                                                       start=True, stop=True) for a in range(4)],
                         reads=[sqb_, ones_b], writes=[pssb])
                prev_kt = None
                for kt in range(NKT):
                    ps, pb = bank()
                    P.mm([lambda k=k: nc.tensor.matmul(ps[0:64, :], lhsT=wuk3[:, k, h * 64:(h + 1) * 64], rhs=CKV[:, k, kt * T:(kt + 1) * T],
                                                       start=(k == 0), stop=(k == 1)) for k in range(2)],
                         reads=[CKVb, wsm_b], writes=[pb])
                    sq, sqb = SQK[kt % 2], SQKb[kt % 2]
                    P.op(ACT, lambda: nc.scalar.activation(out=sq[:, :], in_=ps[0:64, :], func=AF.Square), reads=[pb], writes=[sqb])
                    P.op(DVE, lambda: nc.vector.tensor_scalar(out=ktile[0:64, kt * T:(kt + 1) * T], in0=ps[0:64, :], scalar1=g(G_MK, 0, 64),
                                                              scalar2=None, op0=ALU.mult), reads=[pb, gains_b, sqb], writes=[ktb])
                    if prev_kt is not None:
                        ss_mm(prev_kt)
                    prev_kt = kt
                ss_mm(prev_kt)
                if KK >= 4:
                    P.op(DVE, lambda: nc.vector.tensor_tensor(out=SSK[:, 0:NCH], in0=pss[:, 0:NCH], in1=KRSS[:, 0:NCH], op=ALU.add),
                         reads=[pssb, KRSSb], writes=[SSKb])
                rk, rkb = RK[ki], RKb[ki]
                if KK >= 5:
                    rsqrt(tk, SSK[:, 0:NCH], SSKb, rk[:, 0:NCH], rkb, 1.0 / 96, post=SCALE)
                if KATT <= 3:
                    raise _Stop()
                for qt in range(4):
                    ti = ctx * 4 + qt
                    if pend_q[0] is None:
                        pend_q[0] = q_gen(h, ti)
                    qtile, qtb = pend_q[0]
                    nxt = None
                    if qt < 3:
                        nxt = (h, ti + 1)
                    elif h < 7:
                        nxt = (h + 1, ctx * 4)
                    pend_q[0] = q_gen(*nxt) if nxt is not None else None
                    main_store(h, hh, ti, NCH, ktile, ktb, ki, rk, rkb, qtile, qtb)
                if KATT <= 7:
                    raise _Stop()

    if pend_fin[0] is not None:
        finish(*pend_fin[0])
        pend_fin[0] = None
    P.barrier()
    es2.close()
    esA.close()
    stacks.pop()
    stacks.pop()
    if STOP <= 3:
        raise _Stop()
    bank_pool[:] = list(range(8))

    es3 = ExitStack()
    stacks.append(es3)
    c = alloc_tile_ctx(es3)
    gu_srcs, d_srcs, blk_srcs, wo_srcs = [], [], [], []
    for ti in range(NT):
        gu_srcs += [(wb["w2gu"][j], wbuf["w2gu"]) for j in range(NJ)]
        d_srcs += [(wb["w2d"][f], wbuf["w2d"]) for f in range(8)]
        blk_srcs += [(wb["winb"][m], wbuf["winb"]) for m in (4, 5, 6, 7, 0, 1, 2, 3)]
        for fc in range(8):
            blk_srcs += [(wb["winb"][8 + gg * 8 + fc], wbuf["winb"]) for gg in range(3)]
        blk_srcs += [(wb["wout"][f], wbuf["wout"]) for f in range(8)]
        wo_srcs += [(wb["wo3"][f], wbuf["wo3"]) for f in range(8)]
    gu3 = Stream(P, es3, "gu3", KC * 256, 3, gu_srcs)
    d3 = Stream(P, es3, "wd3", NJ * 128, 2, d_srcs)
    blk3 = Stream(P, es3, "blk3", KC * 128, 4, blk_srcs)
    wo3s = Stream(P, es3, "wo3", 16 * 128, 2, wo_srcs)
    UW = sb(es3, "UW", [128, 4, T + 2], BF16)
    UWb = Buf()
    s_uw = P.sem("uw")
    AOX = sb(es3, "AOX", [64, 8, T], BF16)
    AOXb = Buf()
    s_aox = P.sem("aox")
    CVT = sb(es3, "CVT", [128, T], F32)
    CVTb = Buf()
    CV = sb(es3, "CV", [128, 4, T], BF16)
    CVb = Buf()
    XQ = sb(es3, "XQ", [128, 4, T], BF16)
    XQb = Buf()
    RAWX = sb(es3, "RAWX", [128, T], F32)
    RAWXb = Buf()
    SQX = sb(es3, "SQX", [128, T], BF16)
    SQXb = Buf()
    RV3 = c.RINV
    RV3b = c.RINVb
    tmp3 = c.tmp
    XA = sb(es3, "XA", [128, 4, T], BF16)
    XAb = Buf()
    PM = [sb(es3, f"PM{i}", [128, T], BF16) for i in range(4)]
    PMb = [Buf() for _ in range(4)]
    RD = sb(es3, "RD", [128, T], F32)
    RDb = Buf()
    TH = [sb(es3, f"TH{i}", [128, T], F32) for i in range(3)]
    THb = [Buf() for _ in range(3)]
    M0 = sb(es3, "M0", [128, T], F32)
    M1 = sb(es3, "M1", [128, T], F32)
    M0b, M1b = Buf(), Buf()
    MG = sb(es3, "MG", [128, KC, T], BF16)
    MGb = [Buf() for _ in range(KC)]
    s_y = [P.sem("y0"), P.sem("y1")]
    XSC = 128.0 ** -0.5

    for ti in range(NT):
        ctx, tl = ti // 4, ti % 4
        xs = ti % 2
        X, Xb = c.X[xs], c.Xb[xs]
        def load_x3(ti_):
            xs_ = ti_ % 2
            P.dma(SP, c.X[xs_][:, :, :], x1s[ti_].rearrange("p (k t) -> p k t", t=T), c.Xsem[xs_], reads=[x1s_buf[ti_]], writes=c.Xb[xs_])
        if ti == 0:
            load_x3(0)
        if ti + 1 < NT:
            load_x3(ti + 1)
        lo = max(tl * T - 1, 0)
        hi = min(tl * T + T + 1, 2048)
        d0 = lo - (tl * T - 1)
        P.dma(SP, UW[:, :, d0:d0 + (hi - lo)], Us[:, :, ctx, lo:hi], s_uw, reads=[Us_buf], writes=[UWb])
        if tl == 0:
            P.op(DVE, lambda: nc.vector.tensor_copy(out=UW[:, :, 0:1], in_=UH[:, :, 2 * ctx:2 * ctx + 1]), reads=[UHb], writes=[UWb])
        if tl == 3:
            P.op(DVE, lambda: nc.vector.tensor_copy(out=UW[:, :, T + 1:T + 2], in_=UH[:, :, 2 * ctx + 1:2 * ctx + 2]), reads=[UHb], writes=[UWb])
        P.dma(SP, AOX[:, :, :], AOs[:, :, ti * T:(ti + 1) * T], s_aox, reads=[AOs_buf], writes=[AOXb])
        xi = ti % 2
        if ti == 0:
            norm_to_xn(c, xs, T, G_MIX, xi)
        for h in range(4):
            ps, pb = proj(c, blk3, T, xi)
            P.op(ACT, lambda: nc.scalar.activation(out=SQX[:, :], in_=ps[:, :], func=AF.Square), reads=[pb], writes=[SQXb])
            P.op(ACT, lambda: nc.scalar.activation(out=RAWX[:, :], in_=ps[:, :], func=AF.Copy), reads=[pb], writes=[RAWXb])
            p2, p2b = bank()
            P.mm([lambda: nc.tensor.matmul(p2[:, :], lhsT=ONESB[:, :], rhs=SQX[:, :], start=True, stop=True)], reads=[ones_b, SQXb], writes=[p2b])
            rsqrt(tmp3, p2[:, :], p2b, RV3[:, :], RV3b, 1.0 / 128)
            P.op(DVE, lambda: nc.vector.scalar_tensor_tensor(out=XQ[:, h, :], in0=RAWX[:, :], scalar=g(G_XQ), in1=RV3[:, :],
                                                             op0=ALU.mult, op1=ALU.mult), reads=[RAWXb, RV3b, gains_b], writes=[XQb])
        for i in range(4):
            ps, pb = proj(c, blk3, T, xi)
            P.op(DVE, lambda: nc.vector.tensor_scalar(out=CVT[:, :], in0=UW[:, i, 0:T], scalar1=g(G_CONV + i * 3 + 0), scalar2=None, op0=ALU.mult),
                 reads=[UWb, gains_b], writes=[CVTb])
            P.op(DVE, lambda: nc.vector.scalar_tensor_tensor(out=CVT[:, :], in0=UW[:, i, 1:T + 1], scalar=g(G_CONV + i * 3 + 1), in1=CVT[:, :],
                                                             op0=ALU.mult, op1=ALU.add), reads=[UWb, CVTb, gains_b], writes=[CVTb])
            P.op(DVE, lambda: nc.vector.scalar_tensor_tensor(out=CVT[:, :], in0=UW[:, i, 2:T + 2], scalar=g(G_CONV + i * 3 + 2), in1=CVT[:, :],
                                                             op0=ALU.mult, op1=ALU.add), reads=[UWb, CVTb, gains_b], writes=[CVTb])
            P.op(DVE, lambda: nc.vector.tensor_tensor(out=CV[:, i, :], in0=CVT[:, :], in1=ps[:, :], op=ALU.mult),
                 reads=[CVTb, pb], writes=[CVb])
        def xa_scores(h):
            for mc in range(2):
                ps, pb = bank()
                P.mm([lambda: nc.tensor.matmul(ps[:, :], lhsT=MK[:, ctx, h, mc * 128:(mc + 1) * 128], rhs=XQ[:, h, :], start=True, stop=True)],
                     reads=[MK_b, XQb], writes=[pb])
                pm, pmb = PM[(h % 2) * 2 + mc], PMb[(h % 2) * 2 + mc]
                P.op(ACT, lambda: nc.scalar.activation(out=pm[:, :], in_=ps[:, :], func=AF.Exp, scale=XSC), reads=[pb], writes=[pmb])

        def xa_pv(h):
            pms = [PM[(h % 2) * 2 + mc] for mc in range(2)]
            pmbs = [PMb[(h % 2) * 2 + mc] for mc in range(2)]
            po, pob = bank()
            pdn, pdnb = bank()
            P.mm([lambda mc=mc: nc.tensor.matmul(po[:, :], lhsT=MV[:, ctx, mc, h * 128:(h + 1) * 128], rhs=pms[mc][:, :], start=(mc == 0), stop=(mc == 1))
                  for mc in range(2)], reads=[MV_b] + pmbs, writes=[pob])
            P.mm([lambda mc=mc: nc.tensor.matmul(pdn[:, :], lhsT=ONESB[:, :], rhs=pms[mc][:, :], start=(mc == 0), stop=(mc == 1))
                  for mc in range(2)], reads=[ones_b] + pmbs, writes=[pdnb])
            P.op(DVE, lambda: nc.vector.reciprocal(out=RD[:, :], in_=pdn[:, :]), reads=[pdnb], writes=[RDb])
            P.op(DVE, lambda: nc.vector.tensor_tensor(out=XA[:, h, :], in0=po[:, :], in1=RD[:, :], op=ALU.mult), reads=[pob, RDb], writes=[XAb])

        xa_scores(0)
        for h in range(4):
            if h + 1 < 4:
                xa_scores(h + 1)
            xa_pv(h)
        for fc in range(KC):
            wt, wtb = wo3s.next()
            w3 = wt[:, :].rearrange("p (k c) -> p k c", c=128)
            ys = []
            py, pyb = bank()
            P.mm([lambda h=h: nc.tensor.matmul(py[:, :], lhsT=w3[0:64, h, :], rhs=AOX[:, h, :], start=(h == 0), stop=(h == 7)) for h in range(8)],
                 reads=[wtb, AOXb], writes=[pyb])
            ys.append((py, pyb))
            py, pyb = bank()
            P.mm([lambda i=i: nc.tensor.matmul(py[:, :], lhsT=w3[:, 8 + i, :], rhs=CV[:, i, :], start=(i == 0), stop=(i == 3)) for i in range(4)],
                 reads=[wtb, CVb], writes=[pyb])
            ys.append((py, pyb))
            py, pyb = bank()
            P.mm([lambda i=i: nc.tensor.matmul(py[:, :], lhsT=w3[:, 12 + i, :], rhs=XA[:, i, :], start=(i == 0), stop=(i == 3)) for i in range(4)],
                 reads=[wtb, XAb], writes=[pyb])
            ys.append((py, pyb))
            for gg in range(3):
                pg, pgb = proj(c, blk3, T, xi)
                P.op(ACT, lambda: nc.scalar.activation(out=TH[gg][:, :], in_=pg[:, :], func=AF.Tanh, scale=0.5), reads=[pgb], writes=[THb[gg]])
            P.op(DVE, lambda: nc.vector.scalar_tensor_tensor(out=M0[:, :], in0=TH[0][:, :], scalar=1.0, in1=ys[0][0][:, :], op0=ALU.add, op1=ALU.mult),
                 reads=[THb[0], ys[0][1]], writes=[M0b])
            P.op(DVE, lambda: nc.vector.scalar_tensor_tensor(out=M1[:, :], in0=TH[1][:, :], scalar=1.0, in1=ys[1][0][:, :], op0=ALU.add, op1=ALU.mult),
                 reads=[THb[1], ys[1][1]], writes=[M1b])
            P.op(DVE, lambda: nc.vector.tensor_tensor(out=M0[:, :], in0=M0[:, :], in1=M1[:, :], op=ALU.add), reads=[M0b, M1b], writes=[M0b])
            P.op(DVE, lambda: nc.vector.scalar_tensor_tensor(out=M1[:, :], in0=TH[2][:, :], scalar=1.0, in1=ys[2][0][:, :], op0=ALU.add, op1=ALU.mult),
                 reads=[THb[2], ys[2][1]], writes=[M1b])
            P.op(DVE, lambda: nc.vector.tensor_tensor(out=MG[:, fc, :], in0=M0[:, :], in1=M1[:, :], op=ALU.add), reads=[M0b, M1b], writes=[MGb[fc]])
        for fo in range(KC):
            wt, wtb = blk3.next()
            w3 = wt[:, :].rearrange("p (k c) -> p k c", c=128)
            ps, pb = bank()
            P.mm([lambda k=k: nc.tensor.matmul(ps[:, :], lhsT=w3[:, k, :], rhs=MG[:, k, :], start=(k == 0), stop=(k == KC - 1)) for k in range(KC)],
                 reads=[wtb] + MGb, writes=[pb])
            P.op(DVE, lambda: nc.vector.scalar_tensor_tensor(out=X[:, fo, :], in0=ps[:, :], scalar=0.5, in1=X[:, fo, :], op0=ALU.mult, op1=ALU.add),
                 reads=[pb, Xb[fo]], writes=[Xb[fo]])
        def hook3(ti=ti):
            if ti + 1 < NT:
                norm_to_xn(c, (ti + 1) % 2, T, G_MIX, (ti + 1) % 2)
        ffn(c, xs, T, G_FFN2, gu3, d3, xi=xi, do_norm=True, mid_hook=hook3)
        P.dma(POOL, yT[ti].rearrange("p (k t) -> p k t", t=T), X[:, :, :], c.XsemS[xs], reads=Xb, writes=[])

    P.barrier()
    es3.close()
    es.close()


def _kblocks(W, cols):
    K = W.shape[0]
    kc = K // 128
    out = np.empty((len(cols), 128, kc, 128), np.float32)
    Wr = W.reshape(kc, 128, W.shape[1])
    for m, c0 in enumerate(cols):
        out[m] = Wr[:, :, c0:c0 + 128].transpose(1, 0, 2)
    return out.reshape(len(cols), 128, kc * 128)


def _prep_weights(inp):
    f = lambda a: np.ascontiguousarray(np.asarray(a, np.float32))
    out = {}
    for tag, gu, dn in (("w1", "ffn1_w_gu", "ffn1_w_down"), ("w2", "ffn2_w_gu", "ffn2_w_down")):
        W = f(inp[gu][0]).reshape(KC, 128, 2 * FF)
        gate = W[:, :, :FF].reshape(KC, 128, NJ, 128)
        up = W[:, :, FF:].reshape(KC, 128, NJ, 128)
        st = np.stack([gate, up], axis=3)
        out[tag + "gu"] = np.ascontiguousarray(st.transpose(2, 1, 0, 3, 4)).reshape(NJ, 128, KC * 256)
        Wd = f(inp[dn][0]).reshape(NJ, 128, 8, 128)
        out[tag + "d"] = np.ascontiguousarray(Wd.transpose(2, 1, 0, 3)).reshape(8, 128, NJ * 128)
    Win = f(inp["w_in"][0])
    perm = np.concatenate([np.arange(16, 32), np.arange(0, 16)])
    kr = Win[:, 640:672]
    Wkr = np.concatenate([kr, kr[:, perm], np.zeros((D, 64), np.float32)], axis=1)
    Wa = np.concatenate([Win[:, 0:640], Wkr, Win[:, 1184:2208]], axis=1)
    out["wina"] = _kblocks(Wa, [i * 128 for i in range(14)])
    Wb = np.concatenate([Win[:, 672:1184], Win[:, 2208:2720], Win[:, 2720:5792]], axis=1)
    out["winb"] = _kblocks(Wb, [i * 128 for i in range(32)])
    Wuq = f(inp["w_uq"][0])
    wuq = np.empty((8, 128, 3, 192), np.float32)
    Wr = Wuq.reshape(3, 128, 768)
    for h in range(8):
        blk = Wr[:, :, h * 96:(h + 1) * 96]
        sw = np.concatenate([blk[:, :, :64], blk[:, :, 64:][:, :, perm]], axis=2)
        wuq[h] = np.concatenate([blk, sw], axis=2).transpose(1, 0, 2)
    out["wuq"] = wuq.reshape(8, 128, 3 * 192)
    out["wuk"] = np.ascontiguousarray(f(inp["w_uk"][0]).reshape(2, 128, 512).transpose(1, 0, 2)).reshape(1, 128, 1024)
    out["wuv"] = np.ascontiguousarray(f(inp["w_uv"][0]).reshape(2, 128, 512).transpose(1, 0, 2)).reshape(1, 128, 1024)
    wo3 = np.zeros((8, 128, 16, 128), np.float32)
    Wm = f(inp["w_o_mla"][0]).reshape(8, 64, 8, 128)
    Wc = f(inp["w_o_conv"][0]).reshape(4, 128, 8, 128)
    Wx = f(inp["w_o_mem"][0]).reshape(4, 128, 8, 128)
    wo3[:, 0:64, 0:8, :] = Wm.transpose(2, 1, 0, 3)
    wo3[:, :, 8:12, :] = Wc.transpose(2, 1, 0, 3)
    wo3[:, :, 12:16, :] = Wx.transpose(2, 1, 0, 3)
    out["wo3"] = wo3.reshape(8, 128, 16 * 128)
    out["wout"] = _kblocks(f(inp["w_out"][0]), [i * 128 for i in range(8)])
    Wmkv = f(inp["w_mem_kv"][0])
    out["wmk"] = _kblocks(Wmkv, [i * 128 for i in range(4)])
    out["wmv"] = np.ascontiguousarray(Wmkv[:, 512:].reshape(8, 128, 512).transpose(1, 0, 2)).reshape(1, 128, 8 * 512)
    G = np.zeros((128, NG), np.float32)
    def colk(v, c0):
        v = f(v).reshape(-1, 128)
        for k in range(v.shape[0]):
            G[:, c0 + k] = v[k]
    colk(inp["ffn1_norm"][0], G_FFN1)
    colk(inp["mix_norm"][0], G_MIX)
    colk(inp["ffn2_norm"][0], G_FFN2)
    colk(inp["mem_norm"][0], G_MEM)
    colk(inp["q_lora_norm"][0], G_QL)
    colk(inp["kv_lora_norm"][0], G_KVL)
    mq = f(inp["mla_q_norm"][0])
    mk = f(inp["mla_k_norm"][0])
    G[0:96, G_MQ] = mq
    G[0:64, G_MQS] = mq[:64]
    G[64:96, G_MQS] = mq[64:][perm]
    G[0:64, G_MK] = mk[:64]
    G[0:32, G_KR] = mk[64:]
    G[0:32, G_KRS] = mk[64:][perm]
    G[:, G_XQ] = f(inp["xa_q_norm"][0])
    G[:, G_XK] = f(inp["xa_k_norm"][0])
    cw = f(inp["conv_w"][0])
    for i in range(4):
        for tap in range(3):
            G[:, G_CONV + i * 3 + tap] = cw[tap, i * 128:(i + 1) * 128]
    out["gains"] = G
    return out


def _rope_table(pos):
    half = 16
    inv_freq = (10000.0 ** (-np.arange(half, dtype=np.float32) / half)).astype(np.float32)
    ang = pos.astype(np.float32)[None, :] * inv_freq[:, None]
    cos = np.cos(ang).astype(np.float32)
    sin = np.sin(ang).astype(np.float32)
    c32 = np.concatenate([cos, cos], 0)
    s32 = np.concatenate([-sin, sin], 0)
    tab = np.stack([c32, s32], axis=1)
    return np.ascontiguousarray(np.tile(tab, (4, 1, 1)))


_NC_CACHE = {}


def kernel(**inputs):
    xp = np.asarray(inputs["x_prompt"], np.float32)
    xsm = np.asarray(inputs["x_sample"], np.float32)
    mp = np.asarray(inputs["mem_prompt"], np.float32)
    ms = np.asarray(inputs["mem_sample"], np.float32)
    W = _prep_weights(inputs)
    if "nc" not in _NC_CACHE:
        _NC_CACHE["nc"] = build()
    nc = _NC_CACHE["nc"]
    in_maps = []
    for cidx in range(8):
        ps_, pq_ = cidx // 4, cidx % 4
        ss_, sh_ = cidx // 2, cidx % 2
        xpc = xp[ps_, pq_ * 2048:(pq_ + 1) * 2048]
        xsc = xsm[ss_, sh_ * 2048:(sh_ + 1) * 2048]
        xc = np.concatenate([xpc, xsc], 0)
        xt = xc.reshape(NT, T, KC, 128).transpose(0, 3, 2, 1)
        halo = np.zeros((4, D), np.float32)
        if pq_ > 0:
            halo[0] = xp[ps_, pq_ * 2048 - 1]
        if pq_ < 3:
            halo[1] = xp[ps_, (pq_ + 1) * 2048]
        if sh_ > 0:
            halo[2] = xsm[ss_, sh_ * 2048 - 1]
        if sh_ < 1:
            halo[3] = xsm[ss_, (sh_ + 1) * 2048]
        hl = halo.reshape(4, KC, 128).transpose(2, 1, 0)
        memc = np.stack([mp[ps_], ms[ss_]], 0)
        memt = memc.reshape(2, 256, KC, 128).transpose(0, 3, 2, 1)
        pos = np.concatenate([np.arange(pq_ * 2048, (pq_ + 1) * 2048), np.arange(sh_ * 2048, (sh_ + 1) * 2048)])
        m = {
            "xT": np.ascontiguousarray(xt).reshape(NT, 128, KC * T),
            "xh": np.ascontiguousarray(hl).reshape(128, KC * 4),
            "memT": np.ascontiguousarray(memt).reshape(2, 128, KC * 256),
            "ropeT": _rope_table(pos),
        }
        m.update(W)
        in_maps.append(m)
    res = run_bass_kernel_spmd(nc, in_maps, core_ids=list(range(8)))
    yp = np.empty_like(xp)
    ysm = np.empty_like(xsm)
    for cidx in range(8):
        ps_, pq_ = cidx // 4, cidx % 4
        ss_, sh_ = cidx // 2, cidx % 2
        y = np.asarray(res.results[cidx]["yT"]).reshape(NT, 128, KC, T).transpose(0, 3, 2, 1).reshape(NT * T, D)
        yp[ps_, pq_ * 2048:(pq_ + 1) * 2048] = y[:2048]
        ysm[ss_, sh_ * 2048:(sh_ + 1) * 2048] = y[2048:]
    return (yp, ysm)
```
